# Optimizing a Trainium2 kernel written in Bass

```python
import math
import jax, jax.numpy as jnp
from jax import lax
import numpy as np

D_MODEL = 2048
BATCH = 4
SEQ = 2048
DEPTH = 4
DEC_BATCH = 8
DEC_SEQ = 1
PAST_LEN = 16384
PAGE_SIZE = 128

N_MIXERS = 3
N_A_LAYERS = (DEPTH + 2) // 3
N_B_LAYERS = (DEPTH + 1) // 3
N_C_LAYERS = DEPTH // 3

EPS = 1e-6
NEG_INF = -1e30

CHUNK = 128
A_WIDTH = D_MODEL
A_GROUP = 128
A_GROUPS = A_WIDTH // A_GROUP

HEAD_DIM = 128
B_HEADS = D_MODEL // HEAD_DIM
B_WINDOWS = (128, 512, 2048)
B_DILATIONS = (1, 4, 16)
B_GROUPS = 3
B_WIDTH = B_HEADS * HEAD_DIM
B_QBLOCK = 128
ATTN_SCALE = HEAD_DIM ** -0.5

N_BUCKETS = 32
BUCKET_MAX_DIST = 2048

C_WIDTH = D_MODEL
C_KERNEL = 31

MEM_LEN = 256
X_HEADS = 4
X_WIDTH = X_HEADS * HEAD_DIM

FFN_HIDDEN = ((8 * D_MODEL + 3 * 256 - 1) // (3 * 256)) * 256

kernel_name = 'hybrid_gmlp_dilated_conformer_decoder_step'


def rms_norm(x, g):
    xf = x.astype(jnp.float32)
    y = xf * lax.rsqrt(jnp.mean(xf * xf, axis=-1, keepdims=True) + EPS)
    return (y * g.astype(jnp.float32)).astype(x.dtype)


def layer_norm(x, g, b):
    xf = x.astype(jnp.float32)
    mu = jnp.mean(xf, axis=-1, keepdims=True)
    xc = xf - mu
    y = xc * lax.rsqrt(jnp.mean(xc * xc, axis=-1, keepdims=True) + EPS)
    return (y * g.astype(jnp.float32) + b.astype(jnp.float32)).astype(x.dtype)


def t5_bucket(dist):
    max_exact = N_BUCKETS // 2
    d = jnp.maximum(dist, 1).astype(jnp.float32)
    large = max_exact + (jnp.log(d / max_exact) / math.log(BUCKET_MAX_DIST / max_exact)
                         * (N_BUCKETS - max_exact)).astype(jnp.int32)
    large = jnp.minimum(large, N_BUCKETS - 1)
    return jnp.where(dist < max_exact, dist, large)


def chunk_gmlp(h, w_in, ln_g, ln_b, w_s, b_s, w_out):
    bsz, L, _ = h.shape
    u, v = jnp.split(jax.nn.gelu(h @ w_in), 2, axis=-1)
    v = layer_norm(v, ln_g, ln_b)
    c = min(L, CHUNK)
    nc = L // c
    ws = jnp.where(jnp.tril(jnp.ones((c, c), dtype=bool)), w_s[:, :c, :c], 0.0)
    vc = v.reshape(bsz, nc, c, A_GROUPS, A_GROUP)
    s = jnp.einsum('gpq,bnqge->bnpge', ws, vc) + b_s[:, :c].T[None, None, :, :, None]
    y = u * s.reshape(bsz, L, A_WIDTH)
    return y @ w_out, v


def dilated_qkv(h, w_qkv, q_g, k_g):
    bsz, L, _ = h.shape
    qkv = (h @ w_qkv).reshape(bsz, L, B_GROUPS, 3, B_HEADS, HEAD_DIM)
    q = rms_norm(qkv[:, :, :, 0], q_g[:, None, :])
    k = rms_norm(qkv[:, :, :, 1], k_g[:, None, :])
    v = qkv[:, :, :, 2]
    return q, k, v


def dilated_attn_prompt(q, k, v, window, dil, tab):
    bsz, S, H, hd = q.shape
    n = window // dil
    Ls = S // dil
    qb = min(B_QBLOCK, Ls)
    nblk = -(-Ls // qb)
    Lpad = nblk * qb

    def sub(a, front):
        a = a.reshape(bsz, Ls, dil, H, hd)
        return jnp.pad(a, ((0, 0), (front, Lpad - Ls), (0, 0), (0, 0), (0, 0)))

    qr = sub(q, 0).reshape(bsz, nblk, qb, dil, H, hd)
    kidx = jnp.arange(nblk)[:, None] * qb + jnp.arange(qb + n)[None, :]
    kb = sub(k, n)[:, kidx]
    vb = sub(v, n)[:, kidx]
    off = jnp.arange(qb)[:, None] + n - jnp.arange(qb + n)[None, :]
    valid = (off >= 0)[None] & (off <= n)[None] & ((kidx - n) >= 0)[:, None, :]
    bias = tab[t5_bucket(jnp.clip(off, 0, n) * dil)].astype(jnp.float32).transpose(2, 0, 1)
    logits = jnp.einsum('bnidhe,bnjdhe->bndhij', qr, kb).astype(jnp.float32) * ATTN_SCALE + bias
    logits = jnp.where(valid[None, :, None, None], logits, NEG_INF)
    m = jnp.max(logits, axis=-1, keepdims=True)
    p = jnp.exp(logits - m)
    s = jnp.sum(p, axis=-1, keepdims=True)
    o = jnp.einsum('bndhij,bnjdhe->bnidhe', (p / s).astype(v.dtype), vb)
    lse = (m + jnp.log(s))[..., 0]
    o = o.reshape(bsz, Lpad, dil, H, hd)[:, :Ls].reshape(bsz, S, H, hd)
    lse = lse.transpose(0, 1, 4, 2, 3).reshape(bsz, Lpad, dil, H)[:, :Ls].reshape(bsz, S, H)
    return o, lse


def dilated_attn_sample(q, k_all, v_all, n_past, window, dil, tab):
    T = q.shape[1]
    n = window // dil
    off = jnp.arange(n + 1)
    idx = n_past + jnp.arange(T)[:, None] - off[None, :] * dil
    valid = idx >= 0
    idx = jnp.maximum(idx, 0)
    kg = k_all[:, idx]
    vg = v_all[:, idx]
    bias = tab[t5_bucket(off * dil)].astype(jnp.float32).T
    logits = jnp.einsum('bthe,btjhe->bhtj', q, kg).astype(jnp.float32) * ATTN_SCALE + bias[:, None, :]
    logits = jnp.where(valid[None, None], logits, NEG_INF)
    m = jnp.max(logits, axis=-1, keepdims=True)
    p = jnp.exp(logits - m)
    s = jnp.sum(p, axis=-1, keepdims=True)
    o = jnp.einsum('bhtj,btjhe->bthe', (p / s).astype(v_all.dtype), vg)
    lse = (m + jnp.log(s))[..., 0].transpose(0, 2, 1)
    return o, lse


def merge_groups(outs, lses, w_out):
    o = jnp.stack(outs, axis=2)
    alpha = jax.nn.softmax(jnp.stack(lses, axis=2), axis=2)
    o = jnp.einsum('blgh,blghe->blhe', alpha.astype(o.dtype), o)
    bsz, L = o.shape[:2]
    return o.reshape(bsz, L, B_WIDTH) @ w_out


def conv_module(h, buf, w_in, b_in, w_dw, b_dw, ln_g, ln_b, w_out):
    a, g = jnp.split(h @ w_in + b_in, 2, axis=-1)
    z = a * jax.nn.sigmoid(g)
    zc = jnp.concatenate([buf, z], axis=1)
    y = lax.conv_general_dilated(zc, w_dw[:, None, :], window_strides=(1,), padding='VALID',
                                 dimension_numbers=('NWC', 'WIO', 'NWC'),
                                 feature_group_count=C_WIDTH) + b_dw
    y = jax.nn.silu(layer_norm(y, ln_g, ln_b))
    return y @ w_out, zc[:, -(C_KERNEL - 1):]


def memory_kv(mem, g_mem, w_kv, k_g):
    bsz = mem.shape[0]
    kv = (rms_norm(mem, g_mem) @ w_kv).reshape(bsz, MEM_LEN, 2, X_HEADS, HEAD_DIM)
    return rms_norm(kv[:, :, 0], k_g), kv[:, :, 1]


def memory_attn(h, k, v, w_q, q_g, w_o):
    bsz, L, _ = h.shape
    q = rms_norm((h @ w_q).reshape(bsz, L, X_HEADS, HEAD_DIM), q_g)
    logits = jnp.einsum('blhe,bmhe->bhlm', q, k).astype(jnp.float32) * ATTN_SCALE
    p = jax.nn.softmax(logits, axis=-1)
    o = jnp.einsum('bhlm,bmhe->blhe', p.astype(v.dtype), v)
    return o.reshape(bsz, L, X_WIDTH) @ w_o


def swiglu(h, w_in, w_out):
    g, u = jnp.split(h @ w_in, 2, axis=-1)
    return (jax.nn.silu(g) * u) @ w_out


def stk(rows):
    return jnp.stack(rows, axis=0)


def setup_inputs(seed: int = 0) -> dict:
    key = jax.random.key(seed)
    keys = iter(jax.random.split(key, 64))

    def nrm(shape, scale=1.0):
        return jax.random.normal(next(keys), shape, jnp.float32) * scale

    def gain(shape):
        return 1.0 + 0.05 * nrm(shape)

    D = D_MODEL
    inp = {}
    inp['x_prompt'] = nrm((BATCH, SEQ, D))
    inp['x_sample'] = nrm((DEC_BATCH, DEC_SEQ, D))
    for w in B_WINDOWS:
        L = min(w, PAST_LEN)
        inp['cache_b_k_w%d' % w] = nrm((N_B_LAYERS, DEC_BATCH, L, B_HEADS, HEAD_DIM))
        inp['cache_b_v_w%d' % w] = nrm((N_B_LAYERS, DEC_BATCH, L, B_HEADS, HEAD_DIM))
    inp['state_c_conv'] = nrm((N_C_LAYERS, DEC_BATCH, C_KERNEL - 1, C_WIDTH), 0.5)
    inp['cache_mem_k'] = nrm((DEPTH, DEC_BATCH, MEM_LEN, X_HEADS, HEAD_DIM))
    inp['cache_mem_v'] = nrm((DEPTH, DEC_BATCH, MEM_LEN, X_HEADS, HEAD_DIM))
    inp['mem_prompt'] = nrm((BATCH, MEM_LEN, D))
    inp['rel_bias'] = nrm((N_BUCKETS, B_GROUPS * B_HEADS), 0.5)
    inp['g_mix'] = gain((DEPTH, D))
    inp['g_xattn'] = gain((DEPTH, D))
    inp['g_mem'] = gain((DEPTH, D))
    inp['g_ffn'] = gain((DEPTH, D))
    inp['a_w_in'] = nrm((N_A_LAYERS, D, 2 * A_WIDTH), D ** -0.5)
    inp['a_ln_g'] = gain((N_A_LAYERS, A_WIDTH))
    inp['a_ln_b'] = nrm((N_A_LAYERS, A_WIDTH), 0.02)
    inp['a_w_s'] = nrm((N_A_LAYERS, A_GROUPS, CHUNK, CHUNK), CHUNK ** -0.5)
    inp['a_b_s'] = 1.0 + 0.1 * nrm((N_A_LAYERS, A_GROUPS, CHUNK))
    inp['a_w_out'] = nrm((N_A_LAYERS, A_WIDTH, D), A_WIDTH ** -0.5)
    inp['b_w_qkv'] = nrm((N_B_LAYERS, D, B_GROUPS * 3 * B_WIDTH), D ** -0.5)
    inp['b_q_norm'] = gain((N_B_LAYERS, B_GROUPS, HEAD_DIM))
    inp['b_k_norm'] = gain((N_B_LAYERS, B_GROUPS, HEAD_DIM))
    inp['b_w_out'] = nrm((N_B_LAYERS, B_WIDTH, D), B_WIDTH ** -0.5)
    inp['c_w_in'] = nrm((N_C_LAYERS, D, 2 * C_WIDTH), D ** -0.5)
    inp['c_b_in'] = nrm((N_C_LAYERS, 2 * C_WIDTH), 0.02)
    inp['c_w_dw'] = nrm((N_C_LAYERS, C_KERNEL, C_WIDTH), C_KERNEL ** -0.5)
    inp['c_b_dw'] = nrm((N_C_LAYERS, C_WIDTH), 0.02)
    inp['c_ln_g'] = gain((N_C_LAYERS, C_WIDTH))
    inp['c_ln_b'] = nrm((N_C_LAYERS, C_WIDTH), 0.02)
    inp['c_w_out'] = nrm((N_C_LAYERS, C_WIDTH, D), C_WIDTH ** -0.5)
    inp['x_w_q'] = nrm((DEPTH, D, X_WIDTH), D ** -0.5)
    inp['x_w_kv'] = nrm((DEPTH, D, 2 * X_WIDTH), D ** -0.5)
    inp['x_q_norm'] = gain((DEPTH, HEAD_DIM))
    inp['x_k_norm'] = gain((DEPTH, HEAD_DIM))
    inp['x_w_o'] = nrm((DEPTH, X_WIDTH, D), X_WIDTH ** -0.5)
    inp['f_w_in'] = nrm((DEPTH, D, 2 * FFN_HIDDEN), D ** -0.5)
    inp['f_w_out'] = nrm((DEPTH, FFN_HIDDEN, D), FFN_HIDDEN ** -0.5)
    return inp


def reference(x_prompt, x_sample, cache_b_k_w128, cache_b_v_w128, cache_b_k_w512, cache_b_v_w512,
              cache_b_k_w2048, cache_b_v_w2048, state_c_conv, cache_mem_k, cache_mem_v, mem_prompt,
              rel_bias, g_mix, g_xattn, g_mem, g_ffn,
              a_w_in, a_ln_g, a_ln_b, a_w_s, a_b_s, a_w_out,
              b_w_qkv, b_q_norm, b_k_norm, b_w_out,
              c_w_in, c_b_in, c_w_dw, c_b_dw, c_ln_g, c_ln_b, c_w_out,
              x_w_q, x_w_kv, x_q_norm, x_k_norm, x_w_o,
              f_w_in, f_w_out):
    b_cache = ((cache_b_k_w128, cache_b_v_w128), (cache_b_k_w512, cache_b_v_w512),
               (cache_b_k_w2048, cache_b_v_w2048))
    xp, xs = x_prompt, x_sample
    bk_p = [[] for _ in range(B_GROUPS)]
    bv_p = [[] for _ in range(B_GROUPS)]
    bk_s = [[] for _ in range(B_GROUPS)]
    bv_s = [[] for _ in range(B_GROUPS)]
    conv_p, conv_s, av_s, mk_p, mv_p = [], [], [], [], []
    for i in range(DEPTH):
        kind, j = i % N_MIXERS, i // N_MIXERS
        hp = rms_norm(xp, g_mix[i])
        hs = rms_norm(xs, g_mix[i])
        if kind == 0:
            a_par = (a_w_in[j], a_ln_g[j], a_ln_b[j], a_w_s[j], a_b_s[j], a_w_out[j])
            dp, _ = chunk_gmlp(hp, *a_par)
            ds, v_rows = chunk_gmlp(hs, *a_par)
            av_s.append(v_rows)
        elif kind == 1:
            qp, kp, vp = dilated_qkv(hp, b_w_qkv[j], b_q_norm[j], b_k_norm[j])
            qs, ks, vs = dilated_qkv(hs, b_w_qkv[j], b_q_norm[j], b_k_norm[j])
            outs_p, lse_p, outs_s, lse_s = [], [], [], []
            for g in range(B_GROUPS):
                win, dil = B_WINDOWS[g], B_DILATIONS[g]
                tab = rel_bias[:, g * B_HEADS:(g + 1) * B_HEADS]
                o, l = dilated_attn_prompt(qp[:, :, g], kp[:, :, g], vp[:, :, g], win, dil, tab)
                outs_p.append(o)
                lse_p.append(l)
                n_keep = min(win, xp.shape[1])
                bk_p[g].append(kp[:, -n_keep:, g])
                bv_p[g].append(vp[:, -n_keep:, g])
                ck, cv = b_cache[g][0][j], b_cache[g][1][j]
                n_past = ck.shape[1]
                k_all = jnp.concatenate([ck, ks[:, :, g]], axis=1)
                v_all = jnp.concatenate([cv, vs[:, :, g]], axis=1)
                o, l = dilated_attn_sample(qs[:, :, g], k_all, v_all, n_past, win, dil, tab)
                outs_s.append(o)
                lse_s.append(l)
                bk_s[g].append(k_all[:, -n_past:])
                bv_s[g].append(v_all[:, -n_past:])
            dp = merge_groups(outs_p, lse_p, b_w_out[j])
            ds = merge_groups(outs_s, lse_s, b_w_out[j])
        else:
            c_par = (c_w_in[j], c_b_in[j], c_w_dw[j], c_b_dw[j], c_ln_g[j], c_ln_b[j], c_w_out[j])
            zero_buf = jnp.zeros((xp.shape[0], C_KERNEL - 1, C_WIDTH), xp.dtype)
            dp, buf_p = conv_module(hp, zero_buf, *c_par)
            ds, buf_s = conv_module(hs, state_c_conv[j], *c_par)
            conv_p.append(buf_p)
            conv_s.append(buf_s)
        xp = xp + dp
        xs = xs + ds
        mk, mv = memory_kv(mem_prompt, g_mem[i], x_w_kv[i], x_k_norm[i])
        mk_p.append(mk)
        mv_p.append(mv)
        xp = xp + memory_attn(rms_norm(xp, g_xattn[i]), mk, mv, x_w_q[i], x_q_norm[i], x_w_o[i])
        xs = xs + memory_attn(rms_norm(xs, g_xattn[i]), cache_mem_k[i], cache_mem_v[i],
                              x_w_q[i], x_q_norm[i], x_w_o[i])
        xp = xp + swiglu(rms_norm(xp, g_ffn[i]), f_w_in[i], f_w_out[i])
        xs = xs + swiglu(rms_norm(xs, g_ffn[i]), f_w_in[i], f_w_out[i])
    return (xp, xs,
            stk(bk_p[0]), stk(bv_p[0]), stk(bk_p[1]), stk(bv_p[1]), stk(bk_p[2]), stk(bv_p[2]),
            stk(conv_p), stk(mk_p), stk(mv_p),
            stk(bk_s[0]), stk(bv_s[0]), stk(bk_s[1]), stk(bv_s[1]), stk(bk_s[2]), stk(bv_s[2]),
            stk(conv_s), stk(av_s))
```

```python
import os
import numpy as np
from contextlib import ExitStack
import concourse.bass as bass
import concourse.mybir as mybir
from concourse.bass_utils import run_bass_kernel_spmd

F32 = mybir.dt.float32
BF16 = mybir.dt.bfloat16
AF = mybir.ActivationFunctionType
ALU = mybir.AluOpType
AX = mybir.AxisListType

D = 2048
KC = 16
TP = 1024
XC = TP + 1
NCORES = 8
FFN_H = 5632
EPS = 1e-6
SCALE = 128 ** -0.5
NSLOT = 2
PAIRS = [[0, 1], [2, 3], [4, 5], [6, 7]]
PAIRS_RUN = PAIRS[:int(os.environ.get("MK_NCORES", 8)) // 2]

V_GMIX, V_GXAT, V_GMEM, V_GFFN = 0, 4, 8, 12
V_ALNG, V_ALNB = 16, 18
V_CBIN, V_CBDW, V_CLNG, V_CLNB, V_CWDW = 20, 22, 23, 24, 25
NV = 56
S_XQ, S_XK, S_BQ, S_BK = 0, 4, 8, 11

BLKS = [(0, 512), (512, 512), (1024, 1)]

PLAN = os.environ.get("MK_PLAN", "")
SKIP = os.environ.get("MK_SKIP", "").split(",")


class Unit:
    __slots__ = ("w", "rs", "excl")

    def __init__(self, excl=False):
        self.w = None
        self.rs = {}
        self.excl = excl


class Eng:
    def __init__(self, name, sem):
        self.name = name
        self.sem = sem
        self.n = 0
        self.seen = {}
        self.prog = []


class DS:
    def __init__(self, sem):
        self.sem = sem
        self.count = 0


class Builder:
    def __init__(self):
        self.nc = bass.Bass("TRN2", target_bir_lowering=False)
        self.es = ExitStack()
        self.sems = []
        self.engs = {}
        self.dss = []
        self.nsb = 0

    def new_sem(self, name):
        s = self.es.enter_context(self.nc.semaphore(name))
        self.sems.append(s)
        return len(self.sems) - 1

    def new_ds(self):
        ds = DS(self.new_sem("d%d" % len(self.dss)))
        self.dss.append(ds)
        return ds

    def sb(self, shape, dt, name=None):
        self.nsb += 1
        return self.es.enter_context(self.nc.sbuf_tensor("s_" + (name or ("sb%d" % self.nsb)), list(shape), dt))

    def dram(self, name, shape, dt, kind=None):
        if kind is None:
            return self.nc.dram_tensor(name, list(shape), dt)
        return self.nc.dram_tensor(name, list(shape), dt, kind=kind)

    def _waits(self, eng, R, W):
        need = {}
        for u in R:
            if u.w is not None and need.get(u.w[0], 0) < u.w[1]:
                need[u.w[0]] = u.w[1]
            if u.excl:
                for s, v in u.rs.items():
                    if s != eng.sem and need.get(s, 0) < v:
                        need[s] = v
        for u in W:
            if u.w is not None and need.get(u.w[0], 0) < u.w[1]:
                need[u.w[0]] = u.w[1]
            for s, v in u.rs.items():
                if need.get(s, 0) < v:
                    need[s] = v
        for s, v in need.items():
            if eng.name == "pe" and s == eng.sem:
                continue
            if eng.seen.get(s, 0) < v:
                eng.prog.append(("wait", s, v))
                eng.seen[s] = v

    def _mark(self, tok, R, W):
        for u in R:
            if u.rs.get(tok[0], 0) < tok[1]:
                u.rs[tok[0]] = tok[1]
        for u in W:
            u.w = tok
            u.rs = {}

    def op(self, eng, meth, R=(), W=(), signal=True, **kw):
        eng = self.engs[eng]
        self._waits(eng, R, W)
        if signal:
            eng.n += 1
            tok = (eng.sem, eng.n)
            eng.prog.append(("ins", meth, kw, eng.sem))
        else:
            tok = (eng.sem, eng.n + 1)
            eng.prog.append(("ins", meth, kw, None))
        self._mark(tok, R, W)

    def dma(self, q, out, in_, ds, R=(), W=(), slow=False):
        eng = self.engs[q]
        self._waits(eng, R, W)
        ds.count += 16
        tok = (ds.sem, ds.count)
        eng.prog.append(("dma", out, in_, ds.sem, slow))
        self._mark(tok, R, W)

    def cc(self, ins, outs, groups, sem, R=(), W=(), count=1):
        eng = self.engs["pool"]
        self._waits(eng, R, W)
        eng.prog.append(("cc", ins, outs, groups, sem))
        self._mark((sem, count), R, W)

    def barrier(self):
        sp = self.engs["sp"]
        for ds in self.dss:
            if ds.count and sp.seen.get(ds.sem, 0) < ds.count:
                sp.prog.append(("wait", ds.sem, ds.count))
                sp.seen[ds.sem] = ds.count
        for e in self.engs.values():
            for x in self.engs.values():
                if x is e or x.n == 0:
                    continue
                if e.seen.get(x.sem, 0) < x.n:
                    e.prog.append(("wait", x.sem, x.n))
                    e.seen[x.sem] = x.n
        sp.n += 1
        sp.prog.append(("seminc", sp.sem))
        for e in self.engs.values():
            if e is not sp:
                e.prog.append(("wait", sp.sem, sp.n))
                e.seen[sp.sem] = sp.n

    def emit(self):
        nc = self.nc
        sems = self.sems
        with nc.Block() as block:
            def run(eng):
                def body(h):
                    for it in eng.prog:
                        if it[0] == "wait":
                            h.wait_ge(sems[it[1]], it[2])
                        elif it[0] == "ins":
                            ins = getattr(h, it[1])(**it[2])
                            if it[3] is not None:
                                ins.then_inc(sems[it[3]], 1)
                        elif it[0] == "dma":
                            if it[4]:
                                h.dma_start(out=it[1], in_=it[2], allow_slow_non_contiguous=True).then_inc(sems[it[3]], 16)
                            else:
                                h.dma_start(out=it[1], in_=it[2]).then_inc(sems[it[3]], 16)
                        elif it[0] == "cc":
                            h.collective_compute("AllGather", ALU.bypass, replica_groups=it[3], ins=[it[1]], outs=[it[2]]).then_inc(sems[it[4]])
                        elif it[0] == "seminc":
                            h.sem_inc(sems[it[1]], 1)
                        elif it[0] == "raw":
                            it[1](h)
                return body
            block.sync(run(self.engs["sp"]))
            block.scalar(run(self.engs["act"]))
            block.vector(run(self.engs["dve"]))
            block.tensor(run(self.engs["pe"]))
            block.gpsimd(run(self.engs["pool"]))


def build_program():
    B = Builder()
    nc = B.nc
    for name in ("pe", "act", "dve", "pool", "sp"):
        B.engs[name] = Eng(name, B.new_sem("e_" + name))
    plan = [s for s in PLAN.split(",") if s]

    def on(tag):
        return (not plan) or (tag in plan)

    class LazyIn:
        def __init__(self, name, shape):
            self.name, self.shape, self.t = name, shape, None

        def handle(self):
            if self.t is None:
                self.t = B.dram(self.name, self.shape, F32, kind="ExternalInput")
                B.used_inputs.append(self.name)
            return self.t

        def ap(self):
            return self.handle().ap()

    B.used_inputs = []

    def din(name, shape):
        return LazyIn(name, shape)

    def dout(name, shape):
        return B.dram(name, shape, F32, kind="ExternalOutput")

    x_p = din("x_p", [TP, D]); x_s = din("x_s", [1, D]); mem = din("mem", [256, D])
    vecs = din("vecs", [NV, D]); gsm = din("gsm", [128, 14]); relb = din("relb", [32, 48])
    flag = din("flag", [128, 1]); ident_d = din("ident", [128, 128]); masku_d = din("masku", [128, 128])
    selg = din("selg", [33, 9 * 255]); sels_d = din("sels", [32, 3 * 128])
    a_w_s = din("a_w_s", [2, 16, 128, 128]); a_b_s = din("a_b_s", [2, 16 * 128])
    ck = [din("ck%d" % g, [w, D]) for g, w in enumerate((128, 512, 2048))]
    cv = [din("cv%d" % g, [w, D]) for g, w in enumerate((128, 512, 2048))]
    cst = din("cst", [30, D]); cmk = din("cmk", [4, 256, 512]); cmv = din("cmv", [4, 256, 512])
    a_w_in = din("a_w_in", [2, D, 4096]); a_w_out = din("a_w_out", [2, D, D])
    b_w_qkv = din("b_w_qkv", [D, 18432]); b_w_out = din("b_w_out", [D, D])
    c_w_in = din("c_w_in", [D, 4096]); c_w_out = din("c_w_out", [D, D])
    x_w_q = din("x_w_q", [4, D, 512]); x_w_kv = din("x_w_kv", [4, D, 1024]); x_w_o = din("x_w_o", [4, 512, D])
    f_w_in = din("f_w_in", [4, D, 2 * FFN_H]); f_w_out = din("f_w_out", [4, FFN_H, D])

    y_p = dout("y_p", [TP, D]); y_s = dout("y_s", [1, D])
    bk_p = [dout("bk%d_p" % g, [n, D]) for g, n in enumerate((128, 512, 1024))]
    bv_p = [dout("bv%d_p" % g, [n, D]) for g, n in enumerate((128, 512, 1024))]
    cconv_p = dout("cconv_p", [30, D]); memk_p = dout("memk_p", [4, 256, 512]); memv_p = dout("memv_p", [4, 256, 512])
    bk_s = [dout("bk%d_s" % g, [w, D]) for g, w in enumerate((128, 512, 2048))]
    bv_s = [dout("bv%d_s" % g, [w, D]) for g, w in enumerate((128, 512, 2048))]
    cconv_s = dout("cconv_s", [30, D]); av_s = dout("av_s", [2, D])

    xT = B.sb([128, KC, XC], F32, "xT"); xU = [[Unit() for _ in BLKS] for _ in range(KC)]
    hT2 = B.sb([128, KC * XC], BF16, "hT"); hU = [Unit() for _ in BLKS]
    hT = hT2[:, :].rearrange("p (k t) -> p k t", t=XC)
    mid = B.sb([128, KC, XC], BF16, "mid"); mU = [[Unit() for _ in BLKS] for _ in range(KC)]
    wsl = [B.sb([128, 8192], BF16, "w%d" % i) for i in range(NSLOT)]
    wU = [Unit() for _ in range(NSLOT)]; wDS = [B.new_ds() for _ in range(NSLOT)]
    ident = B.sb([128, 128], F32, "ident"); onesb = B.sb([128, 128], BF16, "onesb")
    masku = B.sb([128, 128], F32, "masku")
    vecT = B.sb([128, KC, NV], F32, "vecT"); gs = B.sb([128, 14], F32, "gs")
    flg = B.sb([128, 1], F32, "flg"); epsT = B.sb([128, 1], F32, "epsT")
    cU = Unit()
    memhat = B.sb([128, KC, 256], BF16, "memhat"); mhU = Unit()
    ps = [B.es.enter_context(nc.psum_tensor("ps%d" % i, [128, 512], F32)) for i in range(8)]
    pU = [Unit(excl=True) for _ in range(8)]
    AW = int(os.environ.get("MK_AW", 8850))
    arena = B.sb([128, AW], F32, "arena")
    NT = 4
    tmpf = [arena[:, i * 512:(i + 1) * 512] for i in range(NT)]; tU = [Unit() for _ in range(NT)]
    tmpb = [arena[:, NT * 512 + i * 256:NT * 512 + (i + 1) * 256].bitcast(BF16) for i in range(NT)]; bU = [Unit() for _ in range(NT)]
    A0 = NT * 768
    st = {"ps": 0, "tf": 0, "tb": 0, "aoff": 0, "stg": 0}

    def psum():
        i = st["ps"]; st["ps"] = (i + 1) % 8
        return ps[i], pU[i]

    def tf():
        i = st["tf"]; st["tf"] = (i + 1) % NT
        return tmpf[i], tU[i]

    def tb():
        i = st["tb"]; st["tb"] = (i + 1) % NT
        return tmpb[i], bU[i]

    def arena_reset():
        B.barrier()
        st["aoff"] = A0

    def carve(words, dt=F32):
        o = st["aoff"]; st["aoff"] = o + words
        assert st["aoff"] <= AW, ("arena overflow", st["aoff"])
        a = arena[:, o:o + words]
        return a if dt == F32 else a.bitcast(dt)

    op, dma = B.op, B.dma

    def MM(out, lhsT, rhs, start, stop, R, W, signal=True):
        op("pe", "matmul", R=R, W=W, signal=signal, out=out, lhsT=lhsT, rhs=rhs, start=start, stop=stop)

    def TR(out, in_, idn, R, W, signal=True):
        op("pe", "transpose", R=R, W=W, signal=signal, out=out, in_=in_, identity=idn)

    def ACT(out, in_, func, R, W, **kw):
        op("act", "activation", R=R, W=W, out=out, in_=in_, func=func, **kw)

    def DVE(meth, R, W, **kw):
        op("dve", meth, R=R, W=W, **kw)

    def COPY(eng, out, in_, R, W):
        if eng == "act":
            ACT(out, in_, AF.Copy, R, W)
        else:
            DVE("tensor_copy", R, W, out=out, in_=in_)

    cds = B.new_ds()
    dma("sp", ident[:], ident_d.ap(), cds, W=[cU])
    dma("sp", masku[:], masku_d.ap(), cds, W=[cU])
    dma("sp", gs[:], gsm.ap(), cds, W=[cU])
    dma("sp", flg[:], flag.ap(), cds, W=[cU])
    DVE("memset", [], [cU], ap=onesb[:], constant=1.0)
    DVE("memset", [], [cU], ap=epsT[:], constant=EPS)

    ldU = [Unit(), Unit()]; ldDS = [B.new_ds(), B.new_ds()]

    def next_stg():
        i = st["stg"]; st["stg"] ^= 1
        return i

    def rows_to_fm(stg, src_ap, R, dst_fn, dstW, single=False):
        i = 0 if single else next_stg()
        s = stg[i]
        dma("sp", s[0:R, :], src_ap, ldDS[i], W=[ldU[i]])
        for g4 in range(4):
            p, u = psum()
            for j in range(4):
                kc = g4 * 4 + j
                TR(p[:, j * 128:j * 128 + R], s[0:R, kc * 128:(kc + 1) * 128], ident[0:R, 0:R], [ldU[i], cU], [u], signal=(j == 3))
            for j in range(4):
                kc = g4 * 4 + j
                COPY("act" if g4 % 2 else "dve", dst_fn(kc), p[:, j * 128:j * 128 + R], [u], dstW(kc))

    def fm_to_rows(stg, src_fn, srcR, R, dst_ap, single=False):
        i = 0 if single else next_stg()
        s = stg[i]
        for g4 in range(4):
            p, u = psum()
            for j in range(4):
                kc = g4 * 4 + j
                TR(p[0:R, j * 128:(j + 1) * 128], src_fn(kc), ident[:, :], list(srcR(kc)) + [cU], [u], signal=(j == 3))
            COPY("act" if g4 % 2 else "dve", s[0:R, g4 * 512:(g4 + 1) * 512], p[0:R, :], [u], [ldU[i]])
        dma("sp", dst_ap, s[0:R, :], ldDS[i], R=[ldU[i]])

    arena_reset()
    stg = [carve(2048), carve(2048)]
    if "vecs" not in SKIP:
        rows_to_fm(stg, vecs.ap(), NV, lambda kc: vecT[:, kc, :], lambda kc: [cU])
    for t in range(8):
        rows_to_fm(stg, x_p.ap()[t * 128:(t + 1) * 128, :], 128,
                   lambda kc, t=t: xT[:, kc, t * 128:(t + 1) * 128], lambda kc, t=t: [xU[kc][t // 4]])
    if "xs" not in SKIP:
        rows_to_fm(stg, x_s.ap(), 1, lambda kc: xT[:, kc, TP:TP + 1], lambda kc: [xU[kc][2]])

    def rstd_from_psum(p, u, n, inv_n):
        r, ru = tf()
        ACT(r[:, :n], p[:, :n], AF.Sqrt, [u, cU], [ru], bias=epsT[:, 0:1], scale=inv_n)
        DVE("reciprocal", [ru], [ru], out=r[:, :n], in_=r[:, :n])
        return r, ru

    def sumsq_fm(src_fn, srcU, nk, n):
        p, u = psum()
        for kc in range(nk):
            s, su = tb()
            ACT(s[:, :n], src_fn(kc), AF.Square, list(srcU(kc)), [su])
            MM(p[:, :n], onesb[:, :], s[:, :n], kc == 0, kc == nk - 1, [su, cU], [u])
        return p, u

    def rmsnorm(vrow):
        for bi, (c0, n) in enumerate(BLKS):
            p, u = sumsq_fm(lambda kc: xT[:, kc, c0:c0 + n], lambda kc: [xU[kc][bi]], KC, n)
            r, ru = rstd_from_psum(p, u, n, 1.0 / D)
            for kc in range(KC):
                DVE("scalar_tensor_tensor", [xU[kc][bi], ru, cU], [hU[bi]], out=hT[:, kc, c0:c0 + n], in0=xT[:, kc, c0:c0 + n],
                    scalar=vecT[:, kc, vrow:vrow + 1], in1=r[:, :n], op0=ALU.mult, op1=ALU.mult)

    def load_memhat():
        mT = carve(KC * 64).rearrange("p (kc t) -> p kc t", t=64); mTU = [Unit() for _ in range(KC)]
        for t in range(4):
            rows_to_fm(stg, mem.ap()[t * 64:(t + 1) * 64, :], 64, lambda kc: mT[:, kc, :], lambda kc: [mTU[kc]])
            p, u = sumsq_fm(lambda kc: mT[:, kc, :], lambda kc: [mTU[kc]], KC, 64)
            r, ru = rstd_from_psum(p, u, 64, 1.0 / D)
            for kc in range(KC):
                DVE("tensor_tensor", [mTU[kc], ru], [mhU], out=memhat[:, kc, t * 64:(t + 1) * 64], in0=mT[:, kc, :], in1=r[:, :64], op=ALU.mult)
    if "mem" not in SKIP:
        load_memhat()

    steps = []

    def wstep(loads, compute):
        steps.append((loads, compute))

    def wsrc(w_ap, k0, nk, c0, ncol):
        return w_ap[k0 * 128:(k0 + nk) * 128, c0:c0 + ncol].rearrange("(kc p) n -> p kc n", p=128)

    def slot3(slot, nk, ncol):
        return slot[:, 0:nk * ncol].rearrange("p (kc n) -> p kc n", n=ncol)

    def run_steps():
        issued = 0
        for k in range(len(steps)):
            while issued < min(len(steps), k + NSLOT):
                si = issued % NSLOT
                for dst_fn, src in steps[issued][0]:
                    dma("pool", dst_fn(wsl[si]), src, wDS[si], W=[wU[si]])
                issued += 1
            steps[k][1](wsl[k % NSLOT], wU[k % NSLOT])
        steps.clear()

    def mm_fm(p, u, w3, wu, nk, oc, in_t, inU, c0, n):
        for kc in range(nk):
            MM(p[:, :n], w3[:, kc, oc * 128:(oc + 1) * 128], in_t[:, kc, c0:c0 + n], kc == 0, kc == nk - 1, [wu] + list(inU(kc)), [u], signal=(kc == nk - 1))

    def resid_add(p, u, oc, bi, c0, n):
        DVE("tensor_tensor", [u], [xU[oc][bi]], out=xT[:, oc, c0:c0 + n], in0=p[:, :n], in1=xT[:, oc, c0:c0 + n], op=ALU.add)

    def out_proj_steps(w_ap, k0, nk, src=None, srcU=None):
        src = mid if src is None else src
        srcU = (lambda kc, bi: mU[kc][bi]) if srcU is None else srcU

        def mk(cb):
            def comp(slot, wu):
                w3 = slot3(slot, nk, 512)
                for o4 in range(4):
                    for bi, (c0, n) in enumerate(BLKS):
                        p, u = psum()
                        mm_fm(p, u, w3, wu, nk, o4, src, lambda kc: [srcU(kc, bi)], c0, n)
                        resid_add(p, u, cb * 4 + o4, bi, c0, n)
            return comp
        for cb in range(4):
            wstep([(lambda s: slot3(s, nk, 512), wsrc(w_ap, k0, nk, cb * 512, 512))], mk(cb))

    def ffn(i):
        rmsnorm(V_GFFN + i)
        w_in = f_w_in.ap()[i]; w_out = f_w_out.ap()[i]

        def mk_in(c2, c_lo):
            def comp(slot, wu):
                w3 = slot3(slot, KC, 512)
                for j in range(2):
                    lc = c2 + j - c_lo
                    for bi, (c0, n) in enumerate(BLKS):
                        pg, ug = psum(); pu, uu = psum()
                        mm_fm(pg, ug, w3, wu, KC, j, hT, lambda kc: [hU[bi]], c0, n)
                        mm_fm(pu, uu, w3, wu, KC, 2 + j, hT, lambda kc: [hU[bi]], c0, n)
                        s, su = tf()
                        ACT(s[:, :n], pg[:, :n], AF.Silu, [ug], [su])
                        DVE("tensor_tensor", [uu, su], [mU[lc][bi]], out=mid[:, lc, c0:c0 + n], in0=pu[:, :n], in1=s[:, :n], op=ALU.mult)
            return comp
        for c_lo, c_hi in ((0, 16), (16, 32), (32, 44)):
            for c2 in range(c_lo, c_hi, 2):
                wstep([(lambda s: slot3(s, KC, 512)[:, :, 0:256], wsrc(w_in, 0, KC, c2 * 128, 256)),
                       (lambda s: slot3(s, KC, 512)[:, :, 256:512], wsrc(w_in, 0, KC, FFN_H + c2 * 128, 256))], mk_in(c2, c_lo))
            out_proj_steps(w_out, c_lo, c_hi - c_lo)
        run_steps()

    def head_norm(p, u, n, gcol, out_bf, outW, out_f32=None, out32W=()):
        q, qu = tf()
        ACT(q[:, :n], p[:, :n], AF.Copy, [u], [qu])
        s, su = tb()
        DVE("tensor_tensor", [qu], [su], out=s[:, :n], in0=q[:, :n], in1=q[:, :n], op=ALU.mult)
        p2, u2 = psum()
        MM(p2[:, :n], onesb[:, :], s[:, :n], True, True, [su, cU], [u2])
        r, ru = rstd_from_psum(p2, u2, n, 1.0 / 128)
        if out_f32 is not None:
            DVE("scalar_tensor_tensor", [qu, ru, cU], list(out32W), out=out_f32, in0=q[:, :n], scalar=gs[:, gcol:gcol + 1], in1=r[:, :n],
                op0=ALU.mult, op1=ALU.mult)
            ACT(out_bf, out_f32, AF.Copy, list(out32W), list(outW))
        else:
            DVE("scalar_tensor_tensor", [qu, ru, cU], list(outW), out=out_bf, in0=q[:, :n], scalar=gs[:, gcol:gcol + 1], in1=r[:, :n],
                op0=ALU.mult, op1=ALU.mult)

    def xattn(i):
        arena_reset()
        def qT(hh, c0, n):
            return mid[:, 4 + hh, c0:c0 + n]
        kTp = mid[:, 8, 0:1024].rearrange("p (h t) -> p h t", t=256); kpU = [mU[8][0], mU[8][1]]
        kTs = mid[:, 9, 0:1024].rearrange("p (h t) -> p h t", t=256); ksU = [mU[9][0], mU[9][1]]
        vp = mid[:, 10, 0:1024].rearrange("p (m c) -> p m c", c=512); vpU = [mU[10][0], mU[10][1]]
        vs = mid[:, 11, 0:1024].rearrange("p (m c) -> p m c", c=512); vsU = [mU[11][0], mU[11][1]]
        kst = carve(1024).rearrange("p (m c) -> p m c", c=512); kstU = Unit(); kstDS = B.new_ds()
        vsDS = B.new_ds()
        kf = carve(4 * 256).rearrange("p (h t) -> p h t", t=256); kfU = [Unit() for _ in range(4)]
        ost = [carve(512), carve(512)]; ostU = [Unit(), Unit()]; ostDS = [B.new_ds(), B.new_ds()]
        oi = [0]

        def next_o():
            oi[0] ^= 1
            return oi[0]

        dma("sp", kst[:, :, :], cmk.ap()[i].rearrange("(m p) c -> p m c", p=128), kstDS, W=[kstU])
        for hh in range(4):
            p, u = psum()
            for m in range(2):
                TR(p[:, m * 128:(m + 1) * 128], kst[:, m, hh * 128:(hh + 1) * 128], ident[:, :], [kstU, cU], [u], signal=(m == 1))
            ACT(kTs[:, hh, :], p[:, 0:256], AF.Copy, [u], ksU)
        dma("pool", vs[:, :, :], cmv.ap()[i].rearrange("(m p) c -> p m c", p=128), vsDS, W=vsU)

        rmsnorm(V_GXAT + i)

        def comp_q(slot, wu):
            w3 = slot3(slot, KC, 512)
            for hh in range(4):
                for bi, (c0, n) in enumerate(BLKS):
                    p, u = psum()
                    mm_fm(p, u, w3, wu, KC, hh, hT, lambda kc: [hU[bi]], c0, n)
                    head_norm(p, u, n, S_XQ + i, qT(hh, c0, n), [mU[4 + hh][bi]])
        wstep([(lambda s: slot3(s, KC, 512), wsrc(x_w_q.ap()[i], 0, KC, 0, 512))], comp_q)

        def fold_gain(w3, wu):
            g = vecT[:, :, V_GMEM + i:V_GMEM + i + 1].to_broadcast([128, KC, 512])
            DVE("tensor_tensor", [wu, cU], [wu], out=w3, in0=w3, in1=g, op=ALU.mult)

        def comp_k(slot, wu):
            w3 = slot3(slot, KC, 512)
            fold_gain(w3, wu)
            for hh in range(4):
                p, u = psum()
                mm_fm(p, u, w3, wu, KC, hh, memhat, lambda kc: [mhU], 0, 256)
                head_norm(p, u, 256, S_XK + i, kTp[:, hh, :], kpU, out_f32=kf[:, hh, :], out32W=[kfU[hh]])
            for m in range(2):
                si = next_o()
                p, u = psum()
                for hh in range(4):
                    TR(p[:, hh * 128:(hh + 1) * 128], kf[:, hh, m * 128:(m + 1) * 128], ident[:, :], [kfU[hh], cU], [u], signal=(hh == 3))
                DVE("tensor_copy", [u], [ostU[si]], out=ost[si][:, :], in_=p[:, :])
                dma("sp", memk_p.ap()[i, m * 128:(m + 1) * 128, :], ost[si][:, :], ostDS[si], R=[ostU[si]])
        wstep([(lambda s: slot3(s, KC, 512), wsrc(x_w_kv.ap()[i], 0, KC, 0, 512))], comp_k)

        def comp_v(slot, wu):
            w3 = slot3(slot, KC, 512)
            fold_gain(w3, wu)
            for m in range(2):
                si = next_o()
                p, u = psum()
                for kc in range(KC):
                    MM(p[:, :], memhat[:, kc, m * 128:(m + 1) * 128], w3[:, kc, :], kc == 0, kc == KC - 1, [wu, mhU], [u], signal=(kc == KC - 1))
                DVE("tensor_copy", [u], [ostU[si]], out=ost[si][:, :], in_=p[:, :])
                ACT(vp[:, m, :], ost[si][:, :], AF.Copy, [ostU[si]], vpU)
                dma("sp", memv_p.ap()[i, m * 128:(m + 1) * 128, :], ost[si][:, :], ostDS[si], R=[ostU[si]])
        wstep([(lambda s: slot3(s, KC, 512), wsrc(x_w_kv.ap()[i], 0, KC, 512, 512))], comp_v)

        def comp_o(slot, wu):
            for bi, (c0, n) in enumerate(BLKS):
                kT, kU, vv, vU = (kTs, ksU, vs, vsU) if bi == 2 else (kTp, kpU, vp, vpU)
                for hh in range(4):
                    po, uo = psum(); pd, ud = psum()
                    for m in range(2):
                        p, u = psum()
                        MM(p[:, :n], kT[:, hh, m * 128:(m + 1) * 128], qT(hh, c0, n), True, True, kU + [mU[4 + hh][bi]], [u])
                        e, eu = tb()
                        ACT(e[:, :n], p[:, :n], AF.Exp, [u], [eu], scale=SCALE)
                        MM(po[:, :n], vv[:, m, hh * 128:(hh + 1) * 128], e[:, :n], m == 0, m == 1, vU + [eu], [uo])
                        MM(pd[:, :n], onesb[:, :], e[:, :n], m == 0, m == 1, [eu, cU], [ud])
                    r, ru = tf()
                    DVE("reciprocal", [ud], [ru], out=r[:, :n], in_=pd[:, :n])
                    DVE("tensor_tensor", [uo, ru], [mU[hh][bi]], out=mid[:, hh, c0:c0 + n], in0=po[:, :n], in1=r[:, :n], op=ALU.mult)
            w3 = slot[:, 0:4 * D].rearrange("p (kc n) -> p kc n", n=D)
            for oc in range(KC):
                for bi, (c0, n) in enumerate(BLKS):
                    p, u = psum()
                    mm_fm(p, u, w3, wu, 4, oc, mid, lambda kc: [mU[kc][bi]], c0, n)
                    resid_add(p, u, oc, bi, c0, n)
        wstep([(lambda s: s[:, 0:4 * D].rearrange("p (kc n) -> p kc n", n=D), wsrc(x_w_o.ap()[i], 0, 4, 0, D))], comp_o)
        run_steps()

    def gmlp(i, j):
        arena_reset()
        w_in = a_w_in.ap()[j]; w_out = a_w_out.ap()[j]
        NTL = 2
        gv = carve(NTL * D // 2, BF16).rearrange("p (t c) -> p t c", c=D); gvU = [Unit() for _ in range(NTL)]
        wsT = carve(16 * 128 // 2, BF16).rearrange("p (g q) -> p g q", q=128); wsU = Unit()
        Cb = carve(16 * 128).rearrange("p (g q) -> p g q", q=128); CU = Unit(); CDS = B.new_ds()
        wst = carve(512).rearrange("p (g q) -> p g q", q=128); wstU = Unit(); wstDS = B.new_ds()
        sm = carve(64); smU = Unit(); smDS = B.new_ds()
        ws00, bs0, gvs, vln = sm[:, 0:16], sm[:, 16:32], sm[:, 32:48], sm[:, 48:64]
        stat = carve(16); statU = Unit()
        rmsnorm(V_GMIX + i)
        dma("sp", Cb[:, :, :], a_b_s.ap()[j].partition_broadcast(128).rearrange("p (g q) -> p g q", q=128), CDS, W=[CU])
        dma("sp", ws00, bass.AP(a_w_s.handle(), j * 16 * 16384, [[0, 128], [16384, 16]]), smDS, W=[smU], slow=True)
        dma("sp", bs0, bass.AP(a_b_s.handle(), j * 2048, [[0, 128], [128, 16]]), smDS, W=[smU], slow=True)
        for g4 in range(4):
            dma("sp", wst[:, :, :], a_w_s.ap()[j, g4 * 4:(g4 + 1) * 4].rearrange("g p q -> p g q"), wstDS, W=[wstU])
            p, u = psum()
            for k in range(4):
                TR(p[:, k * 128:(k + 1) * 128], wst[:, k, :], ident[:, :], [wstU, cU], [u], signal=(k == 3))
            for k in range(4):
                g = g4 * 4 + k
                DVE("tensor_tensor", [u, cU], [wsU], out=wsT[:, g, :], in0=p[:, k * 128:(k + 1) * 128], in1=masku[:, :], op=ALU.mult)
        for g4 in range(4):
            p, u = psum()
            for k in range(4):
                g = g4 * 4 + k
                MM(p[:, k * 128:(k + 1) * 128], onesb[:, :], wsT[:, g, :], True, True, [wsU, cU], [u], signal=(k == 3))
            for k in range(4):
                g = g4 * 4 + k
                DVE("scalar_tensor_tensor", [u, CU, cU], [CU], out=Cb[:, g, :], in0=p[:, k * 128:(k + 1) * 128], scalar=vecT[:, g, V_ALNB + j:V_ALNB + j + 1],
                    in1=Cb[:, g, :], op0=ALU.mult, op1=ALU.add)

        def mk_v(cb, t0, last):
            def comp(slot, wu):
                w3 = slot3(slot, KC, 512)
                for tl in range(NTL):
                    t = t0 + tl
                    p, u = psum()
                    for kc in range(KC):
                        MM(p[:, :], hT[:, kc, t * 128:(t + 1) * 128], w3[:, kc, :], kc == 0, kc == KC - 1, [wu, hU[t // 4]], [u], signal=(kc == KC - 1))
                    ACT(gv[:, tl, cb * 512:(cb + 1) * 512], p[:, :], AF.Gelu_apprx_tanh, [u], [gvU[tl]])
                if not last:
                    return
                for tl in range(NTL):
                    t = t0 + tl
                    bi = t // 4
                    j1, j1u = tb(); j2, j2u = tb()
                    s1 = stat[:, 0:1]; s2 = stat[:, 1:2]; mu = stat[:, 2:3]; var = stat[:, 3:4]; rs = stat[:, 4:5]; nb = stat[:, 5:6]
                    DVE("memset", [], [statU], ap=stat[:, 8:16], constant=0.0)
                    for q4 in range(4):
                        ACT(j1[:, :], gv[:, tl, q4 * 512:(q4 + 1) * 512], AF.Copy, [gvU[tl]], [j1u, statU], accum_out=stat[:, 8 + q4:9 + q4])
                        ACT(j2[:, :], gv[:, tl, q4 * 512:(q4 + 1) * 512], AF.Square, [gvU[tl]], [j2u, statU], accum_out=stat[:, 12 + q4:13 + q4])
                    DVE("tensor_reduce", [statU], [statU], out=s1, in_=stat[:, 8:12], axis=AX.X, op=ALU.add)
                    DVE("tensor_reduce", [statU], [statU], out=s2, in_=stat[:, 12:16], axis=AX.X, op=ALU.add)
                    DVE("tensor_scalar_mul", [statU], [statU], out=mu, in0=s1, scalar1=1.0 / D)
                    DVE("tensor_tensor", [statU], [statU], out=var, in0=mu, in1=mu, op=ALU.mult)
                    DVE("scalar_tensor_tensor", [statU], [statU], out=var, in0=s2, scalar=1.0 / D, in1=var, op0=ALU.mult, op1=ALU.subtract)
                    ACT(rs, var, AF.Sqrt, [statU, cU], [statU], bias=epsT[:, 0:1], scale=1.0)
                    DVE("reciprocal", [statU], [statU], out=rs, in_=rs)
                    DVE("scalar_tensor_tensor", [statU], [statU], out=nb, in0=mu, scalar=-1.0, in1=rs, op0=ALU.mult, op1=ALU.mult)
                    ACT(gv[:, tl, :], gv[:, tl, :], AF.Identity, [gvU[tl], statU], [gvU[tl]], bias=nb, scale=rs)
                    for g4 in range(4):
                        p, u = psum()
                        for k in range(4):
                            g = g4 * 4 + k
                            MM(p[:, k * 128:(k + 1) * 128], gv[:, tl, g * 128:(g + 1) * 128], wsT[:, g, :], True, True, [gvU[tl], wsU], [u], signal=(k == 3))
                        for k in range(4):
                            g = g4 * 4 + k
                            DVE("scalar_tensor_tensor", [u, CU, cU], [mU[g][bi]], out=mid[:, g, t * 128:(t + 1) * 128], in0=p[:, k * 128:(k + 1) * 128],
                                scalar=vecT[:, g, V_ALNG + j:V_ALNG + j + 1], in1=Cb[:, g, :], op0=ALU.mult, op1=ALU.add)
            return comp
        for t0 in range(0, 8, NTL):
            for cb in range(4):
                wstep([(lambda s: slot3(s, KC, 512), wsrc(w_in, 0, KC, D + cb * 512, 512))], mk_v(cb, t0, cb == 3))

        def mk_vs(cb):
            def comp(slot, wu):
                w3 = slot3(slot, KC, 512)
                for o4 in range(4):
                    p, u = psum()
                    mm_fm(p, u, w3, wu, KC, o4, hT, lambda kc: [hU[2]], TP, 1)
                    ACT(gvs[:, cb * 4 + o4:cb * 4 + o4 + 1], p[:, 0:1], AF.Gelu_apprx_tanh, [u], [smU])
                if cb != 3:
                    return
                sq, squ = tf()
                DVE("tensor_copy", [smU], [squ], out=sq[:, 0:16], in_=gvs)
                DVE("tensor_tensor", [smU], [squ], out=sq[:, 16:32], in0=gvs, in1=gvs, op=ALU.mult)
                hb, hbu = tb(); lb_, lbu = tb()
                DVE("tensor_copy", [squ], [hbu], out=hb[:, 0:32], in_=sq[:, 0:32])
                DVE("tensor_tensor", [squ, hbu], [squ], out=sq[:, 32:64], in0=sq[:, 0:32], in1=hb[:, 0:32], op=ALU.subtract)
                DVE("tensor_copy", [squ], [lbu], out=lb_[:, 0:32], in_=sq[:, 32:64])
                p, u = psum()
                MM(p[:, 0:32], onesb[:, :], hb[:, 0:32], True, False, [hbu, cU], [u], signal=False)
                MM(p[:, 0:32], onesb[:, :], lb_[:, 0:32], False, True, [lbu, cU], [u])
                s1 = stat[:, 0:1]; s2 = stat[:, 1:2]; mu = stat[:, 2:3]; var = stat[:, 3:4]; rs = stat[:, 4:5]; nb = stat[:, 5:6]
                DVE("tensor_reduce", [u], [statU], out=s1, in_=p[:, 0:16], axis=AX.X, op=ALU.add)
                DVE("tensor_reduce", [u], [statU], out=s2, in_=p[:, 16:32], axis=AX.X, op=ALU.add)
                DVE("tensor_scalar_mul", [statU], [statU], out=mu, in0=s1, scalar1=1.0 / D)
                DVE("tensor_tensor", [statU], [statU], out=var, in0=mu, in1=mu, op=ALU.mult)
                DVE("scalar_tensor_tensor", [statU], [statU], out=var, in0=s2, scalar=1.0 / D, in1=var, op0=ALU.mult, op1=ALU.subtract)
                ACT(rs, var, AF.Sqrt, [statU, cU], [statU], bias=epsT[:, 0:1], scale=1.0)
                DVE("reciprocal", [statU], [statU], out=rs, in_=rs)
                DVE("scalar_tensor_tensor", [statU], [statU], out=nb, in0=mu, scalar=-1.0, in1=rs, op0=ALU.mult, op1=ALU.mult)
                ACT(vln, gvs, AF.Identity, [smU, statU], [smU], bias=nb, scale=rs)
                DVE("tensor_tensor", [smU, cU], [smU], out=vln, in0=vln, in1=vecT[:, :, V_ALNG + j], op=ALU.mult)
                DVE("tensor_tensor", [smU, cU], [smU], out=vln, in0=vln, in1=vecT[:, :, V_ALNB + j], op=ALU.add)
                p2, u2 = psum()
                TR(p2[0:16, 0:128], vln, ident[:, :], [smU, cU], [u2])
                o, ou = tf()
                DVE("tensor_copy", [u2], [ou], out=o[0:16, 0:128], in_=p2[0:16, 0:128])
                dma("sp", av_s.ap()[j].rearrange("(g e) -> g e", e=128), o[0:16, 0:128], smDS, R=[ou])
                DVE("tensor_tensor", [smU], [smU], out=gvs, in0=vln, in1=ws00, op=ALU.mult)
                DVE("tensor_tensor", [smU], [mU[g][2] for g in range(KC)], out=mid[:, :, TP], in0=gvs, in1=bs0, op=ALU.add)
            return comp
        for cb in range(4):
            wstep([(lambda s: slot3(s, KC, 512), wsrc(w_in, 0, KC, D + cb * 512, 512))], mk_vs(cb))

        def mk_u(cb):
            def comp(slot, wu):
                w3 = slot3(slot, KC, 512)
                for o4 in range(4):
                    oc = cb * 4 + o4
                    for bi, (c0, n) in enumerate(BLKS):
                        p, u = psum()
                        mm_fm(p, u, w3, wu, KC, o4, hT, lambda kc: [hU[bi]], c0, n)
                        g_, gu = tf()
                        ACT(g_[:, :n], p[:, :n], AF.Gelu_apprx_tanh, [u], [gu])
                        DVE("tensor_tensor", [gu], [mU[oc][bi]], out=mid[:, oc, c0:c0 + n], in0=g_[:, :n], in1=mid[:, oc, c0:c0 + n], op=ALU.mult)
            return comp
        for cb in range(4):
            wstep([(lambda s: slot3(s, KC, 512), wsrc(w_in, 0, KC, cb * 512, 512))], mk_u(cb))
        out_proj_steps(w_out, 0, KC)
        run_steps()

    def sample_attn(qs_f, ks_f, vs_f, smpU, hTb):
        B.barrier()
        hTf = hTb[:, 0:16400].bitcast(F32)
        kcf = hTf[:, 0:2048]; prod = hTf[:, 2048:2560]; sc = hTf[:, 2560:2608]; pf = hTf[:, 2608:2656]
        pn = hTf[:, 2656:2704]; t48 = hTf[:, 2704:2752]; t48b = hTf[:, 2752:2800]; b0 = hTf[:, 2800:2848]
        tabs = hTf[:, 2848:2896]; sels = hTf[:, 2896:3280]; o16 = hTf[:, 3280:3312]; rowst = hTf[:, 3312:3440]
        bfv = hTb[:, 6880:16400]
        Qd = bfv[:, 0:2048]; vcb = [bfv[:, 2048 * (1 + g):2048 * (2 + g)] for g in range(3)]
        pb48 = bfv[:, 8192:8240]; hb = bfv[:, 8240:8288]; lb_ = bfv[:, 8288:8336]
        kU_, vU_, sU, cDS_, vDS_, oDS_, rDS_ = Unit(), [Unit(), Unit(), Unit()], Unit(), B.new_ds(), B.new_ds(), B.new_ds(), B.new_ds()
        qdU, prU, rsU = Unit(), Unit(), Unit()
        dma("sp", tabs[0:32, :], relb.ap(), cDS_, W=[sU])
        dma("sp", sels[0:32, :], sels_d.ap(), cDS_, W=[sU])
        dma("sp", b0, relb.ap()[0:1, :].partition_broadcast(128), cDS_, W=[sU])
        for g in range(3):
            dil = DIL[g]
            dma("pool", vcb[g], bass.AP(cv[g].handle(), 0, [[dil * D, 128], [1, D]]), vDS_, W=[vU_[g]])
        for g in range(3):
            dil = DIL[g]
            dma("sp", kcf, bass.AP(ck[g].handle(), 0, [[dil * D, 128], [1, D]]), cDS_, W=[kU_])
            DVE("tensor_tensor", [cU, smpU], [qdU], out=Qd.rearrange("p (h e) -> p h e", e=128), in0=ident[:, :].unsqueeze(1).to_broadcast([128, 16, 128]),
                in1=qs_f[:, g * 16:(g + 1) * 16].unsqueeze(2).to_broadcast([128, 16, 128]), op=ALU.mult)
            for c4 in range(4):
                p, u = psum()
                MM(p[:, :], onesb[:, :], Qd[:, c4 * 512:(c4 + 1) * 512], True, True, [qdU, cU], [u])
                DVE("tensor_tensor", [u, kU_], [prU], out=prod, in0=kcf[:, c4 * 512:(c4 + 1) * 512], in1=p[:, :], op=ALU.mult)
                DVE("tensor_reduce", [prU], [sU], out=sc[:, g * 16 + c4 * 4:g * 16 + c4 * 4 + 4], in_=prod.rearrange("p (h e) -> p h e", e=128), axis=AX.X, op=ALU.add)
            p, u = psum()
            MM(p[:, 0:16], sels[0:32, g * 128:(g + 1) * 128], tabs[0:32, g * 16:(g + 1) * 16], True, True, [sU], [u])
            DVE("scalar_tensor_tensor", [u, sU], [sU], out=sc[:, g * 16:(g + 1) * 16], in0=sc[:, g * 16:(g + 1) * 16], scalar=SCALE, in1=p[:, 0:16], op0=ALU.mult, op1=ALU.add)
        ACT(pf, sc, AF.Exp, [sU], [sU])
        DVE("tensor_copy", [sU], [sU], out=pb48, in_=pf)
        DVE("tensor_tensor", [smpU], [sU], out=t48, in0=qs_f, in1=ks_f, op=ALU.mult)
        DVE("tensor_copy", [sU], [sU], out=hb, in_=t48)
        DVE("tensor_tensor", [sU], [sU], out=t48b, in0=t48, in1=hb, op=ALU.subtract)
        DVE("tensor_copy", [sU], [sU], out=lb_, in_=t48b)
        p, u = psum()
        MM(p[:, 0:48], onesb[:, :], hb, True, False, [sU, cU], [u])
        MM(p[:, 0:48], onesb[:, :], lb_, False, True, [sU, cU], [u])
        DVE("scalar_tensor_tensor", [u, sU], [sU], out=t48, in0=p[:, 0:48], scalar=SCALE, in1=b0, op0=ALU.mult, op1=ALU.add)
        ACT(pn, t48, AF.Exp, [sU], [sU])
        pso, uso = psum(); psd, usd = psum()
        for h_ in range(16):
            for g in range(3):
                MM(pso[:, h_:h_ + 1], vcb[g][:, h_ * 128:(h_ + 1) * 128], pb48[:, g * 16 + h_:g * 16 + h_ + 1], g == 0, g == 2, [vU_[g], sU], [uso])
        for g in range(3):
            MM(psd[:, 0:16], onesb[:, :], pb48[:, g * 16:(g + 1) * 16], g == 0, g == 2, [sU, cU], [usd])
        DVE("tensor_tensor", [sU, smpU], [sU], out=t48, in0=pn, in1=vs_f, op=ALU.mult)
        DVE("tensor_reduce", [sU], [sU], out=o16[:, 0:16], in_=t48.rearrange("p (g h) -> p h g", g=3), axis=AX.X, op=ALU.add)
        DVE("tensor_reduce", [sU], [sU], out=o16[:, 16:32], in_=pn.rearrange("p (g h) -> p h g", g=3), axis=AX.X, op=ALU.add)
        DVE("tensor_tensor", [uso, sU], [sU], out=o16[:, 0:16], in0=pso[:, 0:16], in1=o16[:, 0:16], op=ALU.add)
        DVE("tensor_tensor", [usd, sU], [sU], out=o16[:, 16:32], in0=psd[:, 0:16], in1=o16[:, 16:32], op=ALU.add)
        DVE("reciprocal", [sU], [sU], out=o16[:, 16:32], in_=o16[:, 16:32])
        DVE("tensor_tensor", [sU], [mU[k][2] for k in range(KC)], out=mid[:, :, TP], in0=o16[:, 0:16], in1=o16[:, 16:32], op=ALU.mult)
        for g, W_ in enumerate((128, 512, 2048)):
            n16 = (W_ - 1) * 16
            for src_t, dst_t, col in ((ck[g], bk_s[g], ks_f), (cv[g], bv_s[g], vs_f)):
                dma("sp", bass.AP(dst_t, 0, [[n16, 128], [1, n16]]), bass.AP(src_t.handle(), D, [[n16, 128], [1, n16]]), oDS_)
                p, u = psum()
                TR(p[0:16, 0:128], col[:, g * 16:(g + 1) * 16], ident[:, :], [smpU, cU], [u])
                DVE("tensor_copy", [u, rsU], [rsU], out=rowst[0:16, :], in_=p[0:16, 0:128])
                dma("sp", dst_t.ap()[W_ - 1:W_, :].rearrange("o (h e) -> (o h) e", e=128), rowst[0:16, :], rDS_, R=[rsU])

    DIL = (1, 4, 16)
    KEEP = (128, 512, 1024)

    def dilattn(i):
        qs_d = B.dram("qs_d", [6144, 1024], BF16)
        kvi = B.dram("kvi", [12288, 1024], BF16)
        kvo = B.dram("kvo", [24576, 1024], BF16)

        def kvo_off(elem_off):
            row = elem_off // 1024
            return ((row // 1024) * 2048 + (row % 1024)) * 1024 + (elem_off % 1024)
        eg_d = B.dram("eg_d", [144, 255], F32)
        re_d = B.dram("re_d", [144 * 128, 255], F32)
        VB = 6144 * 1024
        RANK = 12288 * 1024
        kvW = Unit(); qsW = Unit(); reU = Unit(); kvoU = Unit()
        ccsem = B.new_sem("cc_kv")
        w_qkv = b_w_qkv.ap(); w_o = b_w_out.ap()

        arena_reset()
        tabx = carve(48); selS = carve(9 * 255); egs = carve(9 * 255); tU_ = Unit(); tDS = B.new_ds()
        dma("sp", tabx[0:32, :], relb.ap(), tDS, W=[tU_])
        DVE("memset", [], [tU_], ap=tabx[32:33, :], constant=-1e30)
        dma("sp", selS[0:33, :], selg.ap(), tDS, W=[tU_])
        for g in range(3):
            for v in range(3):
                c = (g * 3 + v) * 255
                p, u = psum()
                MM(p[0:16, 0:255], tabx[0:33, g * 16:(g + 1) * 16], selS[0:33, c:c + 255], True, True, [tU_], [u])
                ACT(egs[0:16, c:c + 255], p[0:16, 0:255], AF.Exp, [u], [tU_])
                if v == 2:
                    DVE("tensor_scalar_mul", [tU_, cU], [tU_], out=egs[0:16, c:c + 255], in0=egs[0:16, c:c + 255], scalar1=flg[0:16, 0:1])
        dma("sp", eg_d.ap().rearrange("(g h v) c -> h g v c", g=3, h=16, v=3), egs[0:16, :].rearrange("p (g v c) -> p g v c", g=3, v=3), tDS, R=[tU_], W=[reU])
        for k in range(9):
            dma("sp", re_d.ap()[k * 2048:(k + 1) * 2048, :].rearrange("(r j) c -> r j c", j=128),
                bass.AP(eg_d, k * 16 * 255, [[255, 16], [0, 128], [1, 255]]), tDS, R=[reU], W=[reU])

        arena_reset()
        smp = carve(3 * 48); smpU = Unit()
        rmsnorm(V_GMIX + i)
        hk = [carve(512, BF16), carve(512, BF16)]; hkU = [Unit(), Unit()]; hkDS = [B.new_ds(), B.new_ds()]
        kst = carve(2048).rearrange("p (t c) -> p t c", c=512); kstU = Unit(); kstDS = B.new_ds()
        vst = [carve(512), carve(512)]; vstU = [Unit(), Unit()]; vstDS = [B.new_ds(), B.new_ds()]
        vbs = [carve(256, BF16), carve(256, BF16)]; vbsU = [Unit(), Unit()]; vbsDS = [B.new_ds(), B.new_ds()]
        qs_f, ks_f, vs_f = smp[:, 0:48], smp[:, 48:96], smp[:, 96:144]
        cnt = {"hk": 0, "v": 0}

        def k_out_dma(g, hq, bi):
            for tt in range(4):
                t = bi * 4 + tt
                if t * 128 >= TP - KEEP[g]:
                    r0 = t * 128 - (TP - KEEP[g])
                    dma("sp", bk_p[g].ap()[r0:r0 + 128, hq * 512:(hq + 1) * 512], kst[:, tt, :], kstDS, R=[kstU])

        def mk_k(g, hq):
            dil = DIL[g]

            def comp(slot, wu):
                w3 = slot3(slot, KC, 512)
                gcol = S_BK + g
                for bi, (c0, n) in enumerate(BLKS):
                    for o4 in range(4):
                        h_ = hq * 4 + o4
                        p, u = psum()
                        mm_fm(p, u, w3, wu, KC, o4, hT, lambda kc: [hU[bi]], c0, n)
                        q, qu = tf()
                        ACT(q[:, :n], p[:, :n], AF.Copy, [u], [qu])
                        s_, su = tb()
                        DVE("tensor_tensor", [qu], [su], out=s_[:, :n], in0=q[:, :n], in1=q[:, :n], op=ALU.mult)
                        p2, u2 = psum()
                        MM(p2[:, :n], onesb[:, :], s_[:, :n], True, True, [su, cU], [u2])
                        r, ru = rstd_from_psum(p2, u2, n, 1.0 / 128)
                        if bi == 2:
                            DVE("scalar_tensor_tensor", [qu, ru, cU], [smpU], out=ks_f[:, g * 16 + h_:g * 16 + h_ + 1], in0=q[:, :1], scalar=gs[:, gcol:gcol + 1],
                                in1=r[:, :1], op0=ALU.mult, op1=ALU.mult)
                            continue
                        DVE("scalar_tensor_tensor", [qu, ru, cU], [qu], out=q[:, :n], in0=q[:, :n], scalar=gs[:, gcol:gcol + 1], in1=r[:, :n], op0=ALU.mult, op1=ALU.mult)
                        hi = cnt["hk"] % 2; cnt["hk"] += 1
                        nu = 512 // dil
                        ACT(hk[hi][:, 0:512].rearrange("p (r u) -> p r u", r=dil), q[:, :].rearrange("p (u r) -> p r u", r=dil), AF.Copy, [qu], [hkU[hi]])
                        row = (g * 16 + h_) * 128
                        dma("sp", kvi.ap()[row:row + 128, :].rearrange("p (r u) -> p r u", r=dil)[:, :, bi * nu:(bi + 1) * nu],
                            hk[hi][:, 0:512].rearrange("p (r u) -> p r u", r=dil), hkDS[hi], R=[hkU[hi], kvW])
                        tiles = [tt for tt in range(4) if (bi * 4 + tt) * 128 >= TP - KEEP[g]]
                        if tiles:
                            pt, ut = psum()
                            for tt in tiles:
                                TR(pt[:, tt * 128:(tt + 1) * 128], q[:, tt * 128:(tt + 1) * 128], ident[:, :], [qu, cU], [ut], signal=(tt == tiles[-1]))
                            for tt in tiles:
                                DVE("tensor_copy", [ut], [kstU], out=kst[:, tt, o4 * 128:(o4 + 1) * 128], in_=pt[:, tt * 128:(tt + 1) * 128])
                    if bi < 2:
                        k_out_dma(g, hq, bi)
            return comp

        def mk_q(g, hq):
            dil = DIL[g]

            def comp(slot, wu):
                w3 = slot3(slot, KC, 512)
                gcol = S_BQ + g
                for bi, (c0, n) in enumerate(BLKS):
                    for o4 in range(4):
                        h_ = hq * 4 + o4
                        p, u = psum()
                        mm_fm(p, u, w3, wu, KC, o4, hT, lambda kc: [hU[bi]], c0, n)
                        q, qu = tf()
                        ACT(q[:, :n], p[:, :n], AF.Copy, [u], [qu])
                        s_, su = tb()
                        DVE("tensor_tensor", [qu], [su], out=s_[:, :n], in0=q[:, :n], in1=q[:, :n], op=ALU.mult)
                        p2, u2 = psum()
                        MM(p2[:, :n], onesb[:, :], s_[:, :n], True, True, [su, cU], [u2])
                        r, ru = rstd_from_psum(p2, u2, n, 1.0 / 128)
                        if bi == 2:
                            DVE("scalar_tensor_tensor", [qu, ru, cU], [smpU], out=qs_f[:, g * 16 + h_:g * 16 + h_ + 1], in0=q[:, :1], scalar=gs[:, gcol:gcol + 1],
                                in1=r[:, :1], op0=ALU.mult, op1=ALU.mult)
                            continue
                        hi = cnt["hk"] % 2; cnt["hk"] += 1
                        nu = 512 // dil
                        hv = hk[hi][:, 0:512].rearrange("p (r u) -> p r u", r=dil)
                        DVE("scalar_tensor_tensor", [qu, ru, cU], [hkU[hi]], out=hv, in0=q[:, :].rearrange("p (u r) -> p r u", r=dil), scalar=gs[:, gcol:gcol + 1],
                            in1=r[:, :].rearrange("p (u r) -> p r u", r=dil), op0=ALU.mult, op1=ALU.mult)
                        row = (g * 16 + h_) * 128
                        dma("sp", qs_d.ap()[row:row + 128, :].rearrange("p (r u) -> p r u", r=dil)[:, :, bi * nu:(bi + 1) * nu], hv, hkDS[hi], R=[hkU[hi], qsW])
            return comp

        def mk_v(g, hq):
            def comp(slot, wu):
                w3 = slot3(slot, KC, 512)
                for t in range(8):
                    vi = cnt["v"] % 2; cnt["v"] += 1
                    p, u = psum()
                    for kc in range(KC):
                        MM(p[:, :], hT[:, kc, t * 128:(t + 1) * 128], w3[:, kc, :], kc == 0, kc == KC - 1, [wu, hU[t // 4]], [u], signal=(kc == KC - 1))
                    DVE("tensor_copy", [u], [vstU[vi]], out=vst[vi][:, :], in_=p[:, :])
                    ACT(vbs[vi][:, :], vst[vi][:, :], AF.Copy, [vstU[vi]], [vbsU[vi]])
                    if t * 128 >= TP - KEEP[g]:
                        r0 = t * 128 - (TP - KEEP[g])
                        dma("sp", bv_p[g].ap()[r0:r0 + 128, hq * 512:(hq + 1) * 512], vst[vi][:, :], vstDS[vi], R=[vstU[vi]])
                    off = VB + ((g * 16 + hq * 4) * 1024 + t * 128) * 128
                    dma("sp", bass.AP(kvi, off, [[128, 128], [1024 * 128, 4], [1, 128]]), vbs[vi][:, :].rearrange("p (o e) -> p o e", e=128), vbsDS[vi], R=[vbsU[vi], kvW])
                for o4 in range(4):
                    p, u = psum()
                    mm_fm(p, u, w3, wu, KC, o4, hT, lambda kc: [hU[2]], TP, 1)
                    DVE("tensor_copy", [u], [smpU], out=vs_f[:, g * 16 + hq * 4 + o4:g * 16 + hq * 4 + o4 + 1], in_=p[:, 0:1])
            return comp

        def qkv_src(g, which, hq):
            return wsrc(w_qkv, 0, KC, g * 6144 + which * 2048 + hq * 512, 512)
        for g in range(3):
            for hq in range(4):
                wstep([(lambda s: slot3(s, KC, 512), qkv_src(g, 1, hq))], mk_k(g, hq))
        for g in range(3):
            for hq in range(4):
                wstep([(lambda s: slot3(s, KC, 512), qkv_src(g, 2, hq))], mk_v(g, hq))
        run_steps()
        for k in range(12):
            B.cc(kvi.ap()[k * 1024:(k + 1) * 1024, :].opt(), kvo.ap()[k * 2048:(k + 1) * 2048, :].opt(), PAIRS_RUN, ccsem, W=[kvW, kvoU], count=k + 1)
        for g in range(3):
            for hq in range(4):
                wstep([(lambda s: slot3(s, KC, 512), qkv_src(g, 0, hq))], mk_q(g, hq))
        run_steps()

        arena_reset()
        carve(3 * 48)
        hTb = hT2[:, :]
        hoff = [0]

        def hcarve(n_bf):
            o = hoff[0]; hoff[0] = o + n_bf + (n_bf % 2)
            assert hoff[0] <= KC * XC
            return hTb[:, o:o + n_bf]
        q3 = [hcarve(1024) for _ in range(3)]; ko = [hcarve(1024) for _ in range(3)]
        kp = [hcarve(128), hcarve(512), hcarve(1024)]
        vo = [hcarve(8 * 128).rearrange("p (t e) -> p t e", e=128), hcarve(8 * 128).rearrange("p (t e) -> p t e", e=128), hcarve(16 * 128).rearrange("p (t e) -> p t e", e=128)]
        vpv = [hcarve(128).rearrange("p (t e) -> p t e", e=128), hcarve(4 * 128).rearrange("p (t e) -> p t e", e=128), hcarve(16 * 128).rearrange("p (t e) -> p t e", e=128)]
        et = carve(8 * 128).rearrange("p (k i) -> p k i", i=128)
        acc = carve(1024); den = carve(1024); accU = Unit(); denU = Unit()
        ldq = Unit(); ldk = Unit(); ldv = Unit(); lde = Unit()
        qDS, kDS, vDS, eDS = B.new_ds(), B.new_ds(), B.new_ds(), B.new_ds()

        def head(h_):
            for g in range(3):
                dil = DIL[g]; U = TP // dil
                row = (g * 16 + h_) * 128
                dma("sp", q3[g], qs_d.ap()[row:row + 128, :], qDS, R=[qsW], W=[ldq])
                dma("sp", ko[g], kvi.ap()[row:row + 128, :], kDS, R=[kvW], W=[ldk])
                nprev = 128 // dil if g == 0 else (128 if g == 1 else 64)
                orow = kvo_off(row * 1024) // 1024
                if g == 0:
                    dma("sp", kp[0], kvo.ap()[orow:orow + 128, 896:1024], kDS, R=[kvoU], W=[ldk])
                elif g == 1:
                    dma("sp", kp[1].rearrange("p (r u) -> p r u", r=4), kvo.ap()[orow:orow + 128, :].rearrange("p (r u) -> p r u", r=4)[:, :, 128:256], kDS, R=[kvoU], W=[ldk])
                else:
                    dma("sp", kp[2], kvo.ap()[orow:orow + 128, :], kDS, R=[kvoU], W=[ldk])
                vbase = VB + (g * 16 + h_) * 1024 * 128
                if g == 0:
                    dma("sp", vo[0], bass.AP(kvi, vbase, [[128, 128], [128 * 128, 8], [1, 128]]), vDS, R=[kvW], W=[ldv])
                    dma("sp", vpv[0], bass.AP(kvo, kvo_off(vbase) + 896 * 128, [[128, 128], [128 * 128, 1], [1, 128]]), vDS, R=[kvoU], W=[ldv])
                elif g == 1:
                    for r in range(4):
                        dma("sp", vo[1][:, r * 2:r * 2 + 2, :], bass.AP(kvi, vbase + r * 128, [[4 * 128, 128], [512 * 128, 2], [1, 128]]), vDS, R=[kvW], W=[ldv])
                    dma("sp", vpv[1], bass.AP(kvo, kvo_off(vbase) + 512 * 128, [[4 * 128, 128], [128, 4], [1, 128]]), vDS, R=[kvoU], W=[ldv])
                else:
                    dma("sp", vo[2][0:64, :, :], bass.AP(kvi, vbase, [[16 * 128, 64], [128, 16], [1, 128]]), vDS, R=[kvW], W=[ldv])
                    dma("sp", vpv[2][0:64, :, :], bass.AP(kvo, kvo_off(vbase), [[16 * 128, 64], [128, 16], [1, 128]]), vDS, R=[kvoU], W=[ldv])
                erow = (g * 16 + h_) * 3
                if g < 2:
                    for v in range(3):
                        dma("sp", et[:, g * 3 + v, :], bass.AP(re_d, (erow + v) * 128 * 255 + 127, [[254, 128], [1, 128]]), eDS, R=[reU], W=[lde])
                else:
                    dma("sp", et[0:64, 6, 0:64], bass.AP(re_d, (erow + 0) * 128 * 255 + 127, [[254, 64], [1, 64]]), eDS, R=[reU], W=[lde])
                    dma("sp", et[0:64, 7, 0:64], bass.AP(re_d, (erow + 2) * 128 * 255 + 191, [[254, 64], [1, 64]]), eDS, R=[reU], W=[lde])
            for g in range(3):
                dil = DIL[g]; U = TP // dil
                QB = 128 if g < 2 else 64
                nqb = U // QB
                acc3 = acc.rearrange("p (u r) -> p r u", r=dil); den3 = den.rearrange("p (u r) -> p r u", r=dil)
                for r in range(dil):
                    for qb in range(nqb):
                        qcols = q3[g][:, r * U + qb * QB:r * U + (qb + 1) * QB]
                        po, uo = psum(); pd, ud = psum()
                        for kt in range(2):
                            if kt == 0:
                                if qb > 0:
                                    kT = ko[g][:, r * U + (qb - 1) * QB:r * U + qb * QB]
                                    vt = vo[g][0:QB, (r * nqb + qb - 1) if g else (qb - 1), :]
                                    ei = g * 3 + 1
                                else:
                                    if g == 0:
                                        kT = kp[0][:, :]; vt = vpv[0][:, 0, :]
                                    elif g == 1:
                                        kT = kp[1][:, r * 128:(r + 1) * 128]; vt = vpv[1][:, r, :]
                                    else:
                                        kT = kp[2][:, r * 64:(r + 1) * 64]; vt = vpv[2][0:64, r, :]
                                    ei = (g * 3 + 2) if g < 2 else 7
                            else:
                                kT = ko[g][:, r * U + qb * QB:r * U + (qb + 1) * QB]
                                vt = vo[g][0:QB, (r * nqb + qb) if g else qb, :]
                                ei = g * 3 if g < 2 else 6
                            p, u = psum()
                            MM(p[0:QB, 0:QB], kT, qcols, True, True, [ldk, ldq], [u])
                            e_, eu = tf()
                            ACT(e_[0:QB, 0:QB], p[0:QB, 0:QB], AF.Exp, [u], [eu], scale=SCALE)
                            pb, pbu = tb()
                            DVE("tensor_tensor", [eu, lde], [pbu], out=pb[0:QB, 0:QB], in0=e_[0:QB, 0:QB], in1=et[0:QB, ei, 0:QB], op=ALU.mult)
                            MM(po[:, 0:QB], vt, pb[0:QB, 0:QB], kt == 0, kt == 1, [ldv, pbu], [uo])
                            MM(pd[:, 0:QB], onesb[0:QB, :], pb[0:QB, 0:QB], kt == 0, kt == 1, [pbu, cU], [ud])
                        a_out = acc3[:, r, qb * QB:(qb + 1) * QB]; d_out = den3[:, r, qb * QB:(qb + 1) * QB]
                        if g == 0:
                            DVE("tensor_copy", [uo], [accU], out=a_out, in_=po[:, 0:QB])
                            ACT(d_out, pd[:, 0:QB], AF.Copy, [ud], [denU])
                        else:
                            DVE("tensor_tensor", [uo, accU], [accU], out=a_out, in0=po[:, 0:QB], in1=a_out, op=ALU.add)
                            DVE("tensor_tensor", [ud, denU], [denU], out=d_out, in0=pd[:, 0:QB], in1=d_out, op=ALU.add)
            DVE("reciprocal", [denU], [denU], out=den[:, :], in_=den[:, :])
            DVE("tensor_tensor", [accU, denU], [mU[h_][0], mU[h_][1]], out=mid[:, h_, 0:TP], in0=acc[:, :], in1=den[:, :], op=ALU.mult)
        for h_ in range(16):
            head(h_)

        sample_attn(qs_f, ks_f, vs_f, smpU, hTb)
        out_proj_steps(w_o, 0, KC)
        run_steps()

    def col_ln(src, srcU, grow, brow, out, outW, scr, scrU):
        sq = scr[:, 0:32]; lo = scr[:, 32:64]; stt = scr[:, 64:70]
        DVE("tensor_copy", list(srcU), [scrU], out=sq[:, 0:16], in_=src)
        DVE("tensor_tensor", list(srcU), [scrU], out=sq[:, 16:32], in0=src, in1=src, op=ALU.mult)
        hb, hbu = tb(); lb_, lbu = tb()
        DVE("tensor_copy", [scrU], [hbu], out=hb[:, 0:32], in_=sq)
        DVE("tensor_tensor", [scrU, hbu], [scrU], out=lo, in0=sq, in1=hb[:, 0:32], op=ALU.subtract)
        DVE("tensor_copy", [scrU], [lbu], out=lb_[:, 0:32], in_=lo)
        p, u = psum()
        MM(p[:, 0:32], onesb[:, :], hb[:, 0:32], True, False, [hbu, cU], [u])
        MM(p[:, 0:32], onesb[:, :], lb_[:, 0:32], False, True, [lbu, cU], [u])
        s1 = stt[:, 0:1]; s2 = stt[:, 1:2]; mu = stt[:, 2:3]; var = stt[:, 3:4]; rs = stt[:, 4:5]; nb = stt[:, 5:6]
        DVE("tensor_reduce", [u], [scrU], out=s1, in_=p[:, 0:16], axis=AX.X, op=ALU.add)
        DVE("tensor_reduce", [u], [scrU], out=s2, in_=p[:, 16:32], axis=AX.X, op=ALU.add)
        DVE("tensor_scalar_mul", [scrU], [scrU], out=mu, in0=s1, scalar1=1.0 / D)
        DVE("tensor_tensor", [scrU], [scrU], out=var, in0=mu, in1=mu, op=ALU.mult)
        DVE("scalar_tensor_tensor", [scrU], [scrU], out=var, in0=s2, scalar=1.0 / D, in1=var, op0=ALU.mult, op1=ALU.subtract)
        ACT(rs, var, AF.Sqrt, [scrU, cU], [scrU], bias=epsT[:, 0:1], scale=1.0)
        DVE("reciprocal", [scrU], [scrU], out=rs, in_=rs)
        DVE("scalar_tensor_tensor", [scrU], [scrU], out=nb, in0=mu, scalar=-1.0, in1=rs, op0=ALU.mult, op1=ALU.mult)
        ACT(out, src, AF.Identity, list(srcU) + [scrU], list(outW), bias=nb, scale=rs)
        DVE("tensor_tensor", list(outW) + [cU], list(outW), out=out, in0=out, in1=vecT[:, :, grow], op=ALU.mult)
        DVE("tensor_tensor", list(outW) + [cU], list(outW), out=out, in0=out, in1=vecT[:, :, brow], op=ALU.add)

    def convmod(i):
        arena_reset()
        cxi = B.dram("cxi", [128, 480], F32); cxo = B.dram("cxo", [256, 480], F32)
        ccs = B.new_sem("cc_cv")
        w_in = c_w_in.ap(); w_out = c_w_out.ap()
        ztail = carve(480).rearrange("p (c t) -> p c t", t=30); ztU = Unit(); ztDS = B.new_ds()
        halo = carve(480).rearrange("p (c t) -> p c t", t=30); haU = Unit(); haDS = B.new_ds()
        zcs = carve(16 * 31).rearrange("p (c t) -> p c t", t=31); zsU = Unit()
        zh = carve(16 * 60 // 2, BF16).rearrange("p (c t) -> p c t", t=60); zhU = Unit()
        stg1 = carve(2048)
        yacc = [stg1[:, 0:1024], stg1[:, 1024:2048]]; yU = [Unit(), Unit()]
        scr = carve(80); scrU = Unit()
        ysm = carve(32); ysU = Unit()
        lnmu = carve(512); lnrs = carve(512)
        rmsnorm(V_GMIX + i)
        rows_to_fm([stg1, stg1], cst.ap(), 30, lambda kc: zcs[:, kc, 0:30], lambda kc: [zsU], single=True)

        def mk_in(c2):
            def comp(slot, wu):
                w3 = slot3(slot, KC, 512)
                for j in range(2):
                    c = c2 + j
                    for bi, (c0, n) in enumerate(BLKS):
                        pa, ua = psum(); pg, ug = psum()
                        mm_fm(pa, ua, w3, wu, KC, j, hT, lambda kc: [hU[bi]], c0, n)
                        mm_fm(pg, ug, w3, wu, KC, 2 + j, hT, lambda kc: [hU[bi]], c0, n)
                        s_, su = tf()
                        ACT(s_[:, :n], pg[:, :n], AF.Sigmoid, [ug, cU], [su], bias=vecT[:, c, V_CBIN + 1:V_CBIN + 2], scale=1.0)
                        if bi == 2:
                            DVE("scalar_tensor_tensor", [ua, su, cU], [zsU], out=zcs[:, c, 30:31], in0=pa[:, :1], scalar=vecT[:, c, V_CBIN:V_CBIN + 1], in1=s_[:, :1], op0=ALU.add, op1=ALU.mult)
                            continue
                        DVE("scalar_tensor_tensor", [ua, su, cU], [mU[c][bi]], out=mid[:, c, c0:c0 + n], in0=pa[:, :n], scalar=vecT[:, c, V_CBIN:V_CBIN + 1], in1=s_[:, :n], op0=ALU.add, op1=ALU.mult)
                        if bi == 1:
                            DVE("scalar_tensor_tensor", [ua, su, cU], [ztU], out=ztail[:, c, :], in0=pa[:, 482:512], scalar=vecT[:, c, V_CBIN:V_CBIN + 1], in1=s_[:, 482:512], op0=ALU.add, op1=ALU.mult)
            return comp
        for c2 in range(0, KC, 2):
            wstep([(lambda s: slot3(s, KC, 512)[:, :, 0:256], wsrc(w_in, 0, KC, c2 * 128, 256)),
                   (lambda s: slot3(s, KC, 512)[:, :, 256:512], wsrc(w_in, 0, KC, D + c2 * 128, 256))], mk_in(c2))
        run_steps()
        fm_to_rows([stg1, stg1], lambda kc: ztail[:, kc, :], lambda kc: [ztU], 30, cconv_p.ap(), single=True)
        fm_to_rows([stg1, stg1], lambda kc: zcs[:, kc, 1:31], lambda kc: [zsU], 30, cconv_s.ap(), single=True)
        cxU = Unit()
        dma("sp", cxi.ap(), ztail[:, :, :].rearrange("p c t -> p (c t)"), ztDS, R=[ztU], W=[cxU])
        B.cc(cxi.ap().opt(), cxo.ap().opt(), PAIRS_RUN, ccs, W=[cxU])
        dma("sp", halo[:, :, :].rearrange("p c t -> p (c t)"), cxo.ap()[0:128, :], haDS, R=[cxU], W=[haU])
        DVE("tensor_scalar_mul", [haU, cU], [haU], out=halo[:, :, :], in0=halo[:, :, :], scalar1=flg[:, 0:1])
        DVE("tensor_copy", [haU], [zhU], out=zh[:, :, 0:30], in_=halo[:, :, :])
        DVE("tensor_copy", [mU[c][0] for c in range(KC)], [zhU], out=zh[:, :, 30:60], in_=mid[:, :, 0:30])

        B.barrier()
        def wtap(c, k):
            return vecT[:, c, V_CWDW + k:V_CWDW + k + 1]
        for c in range(KC):
            eng = "dve"
            ya, yu = yacc[c % 2], yU[c % 2]
            zsrc = [mU[c][0], mU[c][1]]
            op(eng, "tensor_scalar_mul", R=zsrc + [cU], W=[yu], out=ya[:, 30:TP], in0=mid[:, c, 0:TP - 30], scalar1=wtap(c, 0))
            for k in range(1, 31):
                op(eng, "scalar_tensor_tensor", R=zsrc + [cU, yu], W=[yu], out=ya[:, 30:TP], in0=mid[:, c, k:k + TP - 30], scalar=wtap(c, k), in1=ya[:, 30:TP], op0=ALU.mult, op1=ALU.add)
            op(eng, "tensor_scalar_mul", R=[zhU, cU], W=[yu], out=ya[:, 0:30], in0=zh[:, c, 0:30], scalar1=wtap(c, 0))
            for k in range(1, 31):
                op(eng, "scalar_tensor_tensor", R=[zhU, cU, yu], W=[yu], out=ya[:, 0:30], in0=zh[:, c, k:k + 30], scalar=wtap(c, k), in1=ya[:, 0:30], op0=ALU.mult, op1=ALU.add)
            ACT(hT[:, c, 0:TP], ya[:, :], AF.Identity, [yu, cU], [hU[0], hU[1]], bias=vecT[:, c, V_CBDW:V_CBDW + 1], scale=1.0)
        for bi, (c0, n) in enumerate(BLKS[:2]):
            p1, u1 = psum()
            for c in range(KC):
                MM(p1[:, :n], onesb[:, :], hT[:, c, c0:c0 + n], c == 0, c == KC - 1, [hU[bi], cU], [u1], signal=(c == KC - 1))
            p2, u2 = sumsq_fm(lambda kc: hT[:, kc, c0:c0 + n], lambda kc: [hU[bi]], KC, n)
            mu, muU, rs, rsU2 = lnmu, Unit(), lnrs, Unit()
            DVE("tensor_scalar_mul", [u1], [muU], out=mu[:, :n], in0=p1[:, :n], scalar1=1.0 / D)
            DVE("tensor_tensor", [muU], [rsU2], out=rs[:, :n], in0=mu[:, :n], in1=mu[:, :n], op=ALU.mult)
            DVE("scalar_tensor_tensor", [u2, rsU2], [rsU2], out=rs[:, :n], in0=p2[:, :n], scalar=1.0 / D, in1=rs[:, :n], op0=ALU.mult, op1=ALU.subtract)
            ACT(rs[:, :n], rs[:, :n], AF.Sqrt, [rsU2, cU], [rsU2], bias=epsT[:, 0:1], scale=1.0)
            DVE("reciprocal", [rsU2], [rsU2], out=rs[:, :n], in_=rs[:, :n])
            for c in range(KC):
                t_, tu_ = tf()
                DVE("tensor_tensor", [hU[bi], muU], [tu_], out=t_[:, :n], in0=hT[:, c, c0:c0 + n], in1=mu[:, :n], op=ALU.subtract)
                DVE("tensor_tensor", [tu_, rsU2], [tu_], out=t_[:, :n], in0=t_[:, :n], in1=rs[:, :n], op=ALU.mult)
                ACT(hT[:, c, c0:c0 + n], t_[:, :n], AF.Silu, [tu_, cU], [hU[bi]], bias=vecT[:, c, V_CLNB:V_CLNB + 1], scale=vecT[:, c, V_CLNG:V_CLNG + 1])
        tmp = scr
        prod31 = yacc[0][:, 0:16 * 31].rearrange("p (c t) -> p c t", t=31)
        DVE("tensor_tensor", [zsU, cU, yU[0]], [yU[0]], out=prod31, in0=zcs[:, :, :], in1=vecT[:, :, V_CWDW:V_CWDW + 31], op=ALU.mult)
        DVE("tensor_reduce", [yU[0]], [ysU], out=ysm[:, 0:16], in_=prod31, axis=AX.X, op=ALU.add)
        DVE("tensor_tensor", [ysU, cU], [ysU], out=ysm[:, 0:16], in0=ysm[:, 0:16], in1=vecT[:, :, V_CBDW], op=ALU.add)
        col_ln(ysm[:, 0:16], [ysU], V_CLNG, V_CLNB, ysm[:, 16:32], [ysU], scr, scrU)
        ACT(hT[:, :, TP], ysm[:, 16:32], AF.Silu, [ysU], [hU[2]])
        out_proj_steps(w_out, 0, KC, src=hT, srcU=lambda kc, bi: hU[bi])
        run_steps()

    for i in range(4):
        if on("mix%d" % i):
            if i % 3 == 0:
                gmlp(i, i // 3)
            elif i % 3 == 1:
                dilattn(i)
            else:
                convmod(i)
        if on("xat%d" % i):
            xattn(i)
        if on("ffn%d" % i):
            ffn(i)

    arena_reset()
    stg = [carve(2048), carve(2048)]
    for t in range(8):
        fm_to_rows(stg, lambda kc, t=t: xT[:, kc, t * 128:(t + 1) * 128], lambda kc, t=t: [xU[kc][t // 4]], 128, y_p.ap()[t * 128:(t + 1) * 128, :])
    if "xs" not in SKIP:
        fm_to_rows(stg, lambda kc: xT[:, kc, TP:TP + 1], lambda kc: [xU[kc][2]], 1, y_s.ap())

    sp = B.engs["sp"]
    for ds in B.dss:
        if ds.count:
            sp.prog.append(("wait", ds.sem, ds.count))
    B.emit()
    return B


def t5_bucket_np(dist):
    import math
    max_exact = 16
    d = np.maximum(dist, 1).astype(np.float32)
    large = max_exact + (np.log(d / max_exact) / math.log(2048 / max_exact) * (32 - max_exact)).astype(np.int32)
    large = np.minimum(large, 31)
    return np.where(dist < max_exact, dist, large)


_CACHE = {}


def kernel(**inp):
    if "B" not in _CACHE:
        _CACHE["B"] = build_program()
    B = _CACHE["B"]
    f = lambda a: np.ascontiguousarray(a, dtype=np.float32)
    vec_rows = [inp["g_mix"], inp["g_xattn"], inp["g_mem"], inp["g_ffn"], inp["a_ln_g"], inp["a_ln_b"],
                inp["c_b_in"].reshape(2, D), inp["c_b_dw"], inp["c_ln_g"], inp["c_ln_b"], inp["c_w_dw"][0]]
    vecs = f(np.concatenate([np.asarray(v).reshape(-1, D) for v in vec_rows], axis=0))
    assert vecs.shape[0] == NV
    gsm = f(np.concatenate([inp["x_q_norm"], inp["x_k_norm"], inp["b_q_norm"][0], inp["b_k_norm"][0]], axis=0).T)
    ident = np.eye(128, dtype=np.float32)
    masku = np.triu(np.ones((128, 128), np.float32))
    selg = np.zeros((33, 9, 255), np.float32)
    u = np.arange(255)
    for g, dil in enumerate((1, 4, 16)):
        cur = np.where(u >= 127, t5_bucket_np(np.maximum(u - 127, 0) * dil), 32)
        prev = np.where(u <= 127, t5_bucket_np((u + 1) * dil), 32)
        third = prev if g < 2 else cur
        for v, idx in enumerate((cur, prev, third)):
            selg[idx, g * 3 + v, u] = 1.0
    selg = selg.reshape(33, 9 * 255)
    sels = np.zeros((32, 3, 128), np.float32)
    jj = np.arange(128)
    for g, dil in enumerate((1, 4, 16)):
        sels[t5_bucket_np((128 - jj) * dil), g, jj] = 1.0
    sels = sels.reshape(32, 384)
    shared = dict(vecs=vecs, gsm=gsm, relb=f(inp["rel_bias"]), ident=ident, masku=masku, selg=selg, sels=sels,
                  a_w_s=f(inp["a_w_s"]), a_b_s=f(inp["a_b_s"]).reshape(2, 2048),
                  a_w_in=f(inp["a_w_in"]), a_w_out=f(inp["a_w_out"]), b_w_qkv=f(inp["b_w_qkv"][0]), b_w_out=f(inp["b_w_out"][0]),
                  c_w_in=f(inp["c_w_in"][0]), c_w_out=f(inp["c_w_out"][0]), x_w_q=f(inp["x_w_q"]), x_w_kv=f(inp["x_w_kv"]),
                  x_w_o=f(inp["x_w_o"]), f_w_in=f(inp["f_w_in"]), f_w_out=f(inp["f_w_out"]))
    in_maps = []
    for c in range(NCORES):
        b, half = c // 2, c % 2
        m = dict(shared)
        m["x_p"] = f(inp["x_prompt"][b, half * TP:(half + 1) * TP])
        m["x_s"] = f(inp["x_sample"][c])
        m["mem"] = f(inp["mem_prompt"][b])
        m["flag"] = np.full((128, 1), float(half), np.float32)
        for g, w in enumerate((128, 512, 2048)):
            m["ck%d" % g] = f(inp["cache_b_k_w%d" % w][0, c]).reshape(w, D)
            m["cv%d" % g] = f(inp["cache_b_v_w%d" % w][0, c]).reshape(w, D)
        m["cst"] = f(inp["state_c_conv"][0, c])
        m["cmk"] = f(inp["cache_mem_k"][:, c]).reshape(4, 256, 512)
        m["cmv"] = f(inp["cache_mem_v"][:, c]).reshape(4, 256, 512)
        in_maps.append(m)
    nrun = int(os.environ.get("MK_NCORES", NCORES))
    in_maps = [{k: m[k] for k in B.used_inputs} for m in in_maps]
    res = run_bass_kernel_spmd(B.nc, in_maps[:nrun], core_ids=list(range(nrun)))
    R = list(res.results) + [res.results[c % nrun] for c in range(nrun, NCORES)]
    o = lambda c, k: np.asarray(R[c][k], dtype=np.float32)
    y_prompt = np.stack([np.concatenate([o(2 * b, "y_p"), o(2 * b + 1, "y_p")], axis=0) for b in range(4)])
    y_sample = np.stack([o(c, "y_s") for c in range(8)])
    outs = [y_prompt, y_sample]
    for g, n in enumerate((128, 512, 2048)):
        for kv in ("bk", "bv"):
            if g < 2:
                a = np.stack([o(2 * b + 1, "%s%d_p" % (kv, g)) for b in range(4)])
            else:
                a = np.stack([np.concatenate([o(2 * b, "%s2_p" % kv), o(2 * b + 1, "%s2_p" % kv)], axis=0) for b in range(4)])
            outs.append(a.reshape(1, 4, n, 16, 128))
    outs.append(np.stack([o(2 * b + 1, "cconv_p") for b in range(4)])[None])
    outs.append(np.stack([o(2 * b, "memk_p") for b in range(4)], axis=1).reshape(4, 4, 256, 4, 128))
    outs.append(np.stack([o(2 * b, "memv_p") for b in range(4)], axis=1).reshape(4, 4, 256, 4, 128))
    for g, w in enumerate((128, 512, 2048)):
        for kv in ("bk", "bv"):
            outs.append(np.stack([o(c, "%s%d_s" % (kv, g)) for c in range(8)]).reshape(1, 8, w, 16, 128))
    outs.append(np.stack([o(c, "cconv_s") for c in range(8)])[None])
    outs.append(np.stack([o(c, "av_s") for c in range(8)], axis=1).reshape(2, 8, 1, D))
    return tuple(outs)
```

```python
import os
import numpy as np
from contextlib import ExitStack
import concourse.bass as bass
import concourse.mybir as mybir
from concourse.bass_utils import run_bass_kernel_spmd

F32 = mybir.dt.float32
BF16 = mybir.dt.bfloat16
AF = mybir.ActivationFunctionType
ALU = mybir.AluOpType
AX = mybir.AxisListType

D = 2048
KC = 16
TP = 1024
XC = TP + 1
NCORES = 8
FFN_H = 5632
EPS = 1e-6
SCALE = 128 ** -0.5
NSLOT = 2
PAIRS = [[0, 1], [2, 3], [4, 5], [6, 7]]
PAIRS_RUN = PAIRS[:int(os.environ.get("MK_NCORES", 8)) // 2]

V_GMIX, V_GXAT, V_GMEM, V_GFFN = 0, 4, 8, 12
V_ALNG, V_ALNB = 16, 18
V_CBIN, V_CBDW, V_CLNG, V_CLNB, V_CWDW = 20, 22, 23, 24, 25
NV = 56
S_XQ, S_XK, S_BQ, S_BK = 0, 4, 8, 11

BLKS = [(0, 512), (512, 512), (1024, 1)]

PLAN = os.environ.get("MK_PLAN", "")
SKIP = os.environ.get("MK_SKIP", "").split(",")


class Unit:
    __slots__ = ("w", "rs", "excl")

    def __init__(self, excl=False):
        self.w = None
        self.rs = {}
        self.excl = excl


class Eng:
    def __init__(self, name, sem):
        self.name = name
        self.sem = sem
        self.n = 0
        self.seen = {}
        self.prog = []


class DS:
    def __init__(self, sem):
        self.sem = sem
        self.count = 0


class Builder:
    def __init__(self):
        self.nc = bass.Bass("TRN2", target_bir_lowering=False)
        self.es = ExitStack()
        self.sems = []
        self.engs = {}
        self.dss = []
        self.nsb = 0

    def new_sem(self, name):
        s = self.es.enter_context(self.nc.semaphore(name))
        self.sems.append(s)
        return len(self.sems) - 1

    def new_ds(self):
        ds = DS(self.new_sem("d%d" % len(self.dss)))
        self.dss.append(ds)
        return ds

    def sb(self, shape, dt, name=None):
        self.nsb += 1
        return self.es.enter_context(self.nc.sbuf_tensor("s_" + (name or ("sb%d" % self.nsb)), list(shape), dt))

    def dram(self, name, shape, dt, kind=None):
        if kind is None:
            return self.nc.dram_tensor(name, list(shape), dt)
        return self.nc.dram_tensor(name, list(shape), dt, kind=kind)

    def _waits(self, eng, R, W):
        need = {}
        for u in R:
            if u.w is not None and need.get(u.w[0], 0) < u.w[1]:
                need[u.w[0]] = u.w[1]
            if u.excl:
                for s, v in u.rs.items():
                    if s != eng.sem and need.get(s, 0) < v:
                        need[s] = v
        for u in W:
            if u.w is not None and need.get(u.w[0], 0) < u.w[1]:
                need[u.w[0]] = u.w[1]
            for s, v in u.rs.items():
                if need.get(s, 0) < v:
                    need[s] = v
        for s, v in need.items():
            if eng.name == "pe" and s == eng.sem:
                continue
            if eng.seen.get(s, 0) < v:
                eng.prog.append(("wait", s, v))
                eng.seen[s] = v

    def _mark(self, tok, R, W):
        for u in R:
            if u.rs.get(tok[0], 0) < tok[1]:
                u.rs[tok[0]] = tok[1]
        for u in W:
            u.w = tok
            u.rs = {}

    def op(self, eng, meth, R=(), W=(), signal=True, **kw):
        eng = self.engs[eng]
        self._waits(eng, R, W)
        if signal:
            eng.n += 1
            tok = (eng.sem, eng.n)
            eng.prog.append(("ins", meth, kw, eng.sem))
        else:
            tok = (eng.sem, eng.n + 1)
            eng.prog.append(("ins", meth, kw, None))
        self._mark(tok, R, W)

    def dma(self, q, out, in_, ds, R=(), W=(), slow=False):
        eng = self.engs[q]
        self._waits(eng, R, W)
        ds.count += 16
        tok = (ds.sem, ds.count)
        eng.prog.append(("dma", out, in_, ds.sem, slow))
        self._mark(tok, R, W)

    def cc(self, ins, outs, groups, sem, R=(), W=(), count=1):
        eng = self.engs["pool"]
        self._waits(eng, R, W)
        eng.prog.append(("cc", ins, outs, groups, sem))
        self._mark((sem, count), R, W)

    def barrier(self):
        sp = self.engs["sp"]
        for ds in self.dss:
            if ds.count and sp.seen.get(ds.sem, 0) < ds.count:
                sp.prog.append(("wait", ds.sem, ds.count))
                sp.seen[ds.sem] = ds.count
        for e in self.engs.values():
            for x in self.engs.values():
                if x is e or x.n == 0:
                    continue
                if e.seen.get(x.sem, 0) < x.n:
                    e.prog.append(("wait", x.sem, x.n))
                    e.seen[x.sem] = x.n
        sp.n += 1
        sp.prog.append(("seminc", sp.sem))
        for e in self.engs.values():
            if e is not sp:
                e.prog.append(("wait", sp.sem, sp.n))
                e.seen[sp.sem] = sp.n

    def emit(self):
        nc = self.nc
        sems = self.sems
        with nc.Block() as block:
            def run(eng):
                def body(h):
                    for it in eng.prog:
                        if it[0] == "wait":
                            h.wait_ge(sems[it[1]], it[2])
                        elif it[0] == "ins":
                            ins = getattr(h, it[1])(**it[2])
                            if it[3] is not None:
                                ins.then_inc(sems[it[3]], 1)
                        elif it[0] == "dma":
                            if it[4]:
                                h.dma_start(out=it[1], in_=it[2], allow_slow_non_contiguous=True).then_inc(sems[it[3]], 16)
                            else:
                                h.dma_start(out=it[1], in_=it[2]).then_inc(sems[it[3]], 16)
                        elif it[0] == "cc":
                            h.collective_compute("AllGather", ALU.bypass, replica_groups=it[3], ins=[it[1]], outs=[it[2]]).then_inc(sems[it[4]])
                        elif it[0] == "seminc":
                            h.sem_inc(sems[it[1]], 1)
                        elif it[0] == "raw":
                            it[1](h)
                return body
            block.sync(run(self.engs["sp"]))
            block.scalar(run(self.engs["act"]))
            block.vector(run(self.engs["dve"]))
            block.tensor(run(self.engs["pe"]))
            block.gpsimd(run(self.engs["pool"]))


def build_program():
    B = Builder()
    nc = B.nc
    for name in ("pe", "act", "dve", "pool", "sp"):
        B.engs[name] = Eng(name, B.new_sem("e_" + name))
    plan = [s for s in PLAN.split(",") if s]

    def on(tag):
        return (not plan) or (tag in plan)

    class LazyIn:
        def __init__(self, name, shape):
            self.name, self.shape, self.t = name, shape, None

        def handle(self):
            if self.t is None:
                self.t = B.dram(self.name, self.shape, F32, kind="ExternalInput")
                B.used_inputs.append(self.name)
            return self.t

        def ap(self):
            return self.handle().ap()

    B.used_inputs = []

    def din(name, shape):
        return LazyIn(name, shape)

    def dout(name, shape):
        return B.dram(name, shape, F32, kind="ExternalOutput")

    x_p = din("x_p", [TP, D]); x_s = din("x_s", [1, D]); mem = din("mem", [256, D])
    vecs = din("vecs", [NV, D]); gsm = din("gsm", [128, 14]); relb = din("relb", [32, 48])
    flag = din("flag", [128, 1]); ident_d = din("ident", [128, 128]); masku_d = din("masku", [128, 128])
    selg = din("selg", [33, 9 * 255]); sels_d = din("sels", [32, 3 * 128])
    a_w_s = din("a_w_s", [2, 16, 128, 128]); a_b_s = din("a_b_s", [2, 16 * 128])
    ck = [din("ck%d" % g, [w, D]) for g, w in enumerate((128, 512, 2048))]
    cv = [din("cv%d" % g, [w, D]) for g, w in enumerate((128, 512, 2048))]
    cst = din("cst", [30, D]); cmk = din("cmk", [4, 256, 512]); cmv = din("cmv", [4, 256, 512])
    a_w_in = din("a_w_in", [2, D, 4096]); a_w_out = din("a_w_out", [2, D, D])
    b_w_qkv = din("b_w_qkv", [D, 18432]); b_w_out = din("b_w_out", [D, D])
    c_w_in = din("c_w_in", [D, 4096]); c_w_out = din("c_w_out", [D, D])
    x_w_q = din("x_w_q", [4, D, 512]); x_w_kv = din("x_w_kv", [4, D, 1024]); x_w_o = din("x_w_o", [4, 512, D])
    f_w_in = din("f_w_in", [4, D, 2 * FFN_H]); f_w_out = din("f_w_out", [4, FFN_H, D])

    y_p = dout("y_p", [TP, D]); y_s = dout("y_s", [1, D])
    bk_p = [dout("bk%d_p" % g, [n, D]) for g, n in enumerate((128, 512, 1024))]
    bv_p = [dout("bv%d_p" % g, [n, D]) for g, n in enumerate((128, 512, 1024))]
    cconv_p = dout("cconv_p", [30, D]); memk_p = dout("memk_p", [4, 256, 512]); memv_p = dout("memv_p", [4, 256, 512])
    bk_s = [dout("bk%d_s" % g, [w, D]) for g, w in enumerate((128, 512, 2048))]
    bv_s = [dout("bv%d_s" % g, [w, D]) for g, w in enumerate((128, 512, 2048))]
    cconv_s = dout("cconv_s", [30, D]); av_s = dout("av_s", [2, D])

    xT = B.sb([128, KC, XC], F32, "xT"); xU = [[Unit() for _ in BLKS] for _ in range(KC)]
    hT2 = B.sb([128, KC * XC], BF16, "hT"); hU = [Unit() for _ in BLKS]
    hT = hT2[:, :].rearrange("p (k t) -> p k t", t=XC)
    mid = B.sb([128, KC, XC], BF16, "mid"); mU = [[Unit() for _ in BLKS] for _ in range(KC)]
    wsl = [B.sb([128, 8192], BF16, "w%d" % i) for i in range(NSLOT)]
    wU = [Unit() for _ in range(NSLOT)]; wDS = [B.new_ds() for _ in range(NSLOT)]
    ident = B.sb([128, 128], F32, "ident"); onesb = B.sb([128, 128], BF16, "onesb")
    masku = B.sb([128, 128], F32, "masku")
    vecT = B.sb([128, KC, NV], F32, "vecT"); gs = B.sb([128, 14], F32, "gs")
    flg = B.sb([128, 1], F32, "flg"); epsT = B.sb([128, 1], F32, "epsT")
    cU = Unit()
    memhat = B.sb([128, KC, 256], BF16, "memhat"); mhU = Unit()
    ps = [B.es.enter_context(nc.psum_tensor("ps%d" % i, [128, 512], F32)) for i in range(8)]
    pU = [Unit(excl=True) for _ in range(8)]
    AW = int(os.environ.get("MK_AW", 8850))
    arena = B.sb([128, AW], F32, "arena")
    NT = 4
    tmpf = [arena[:, i * 512:(i + 1) * 512] for i in range(NT)]; tU = [Unit() for _ in range(NT)]
    tmpb = [arena[:, NT * 512 + i * 256:NT * 512 + (i + 1) * 256].bitcast(BF16) for i in range(NT)]; bU = [Unit() for _ in range(NT)]
    A0 = NT * 768
    st = {"ps": 0, "tf": 0, "tb": 0, "aoff": 0, "stg": 0}

    def psum():
        i = st["ps"]; st["ps"] = (i + 1) % 8
        return ps[i], pU[i]

    def tf():
        i = st["tf"]; st["tf"] = (i + 1) % NT
        return tmpf[i], tU[i]

    def tb():
        i = st["tb"]; st["tb"] = (i + 1) % NT
        return tmpb[i], bU[i]

    def arena_reset():
        B.barrier()
        st["aoff"] = A0

    def carve(words, dt=F32):
        o = st["aoff"]; st["aoff"] = o + words
        assert st["aoff"] <= AW, ("arena overflow", st["aoff"])
        a = arena[:, o:o + words]
        return a if dt == F32 else a.bitcast(dt)

    op, dma = B.op, B.dma

    def MM(out, lhsT, rhs, start, stop, R, W, signal=True):
        op("pe", "matmul", R=R, W=W, signal=signal, out=out, lhsT=lhsT, rhs=rhs, start=start, stop=stop)

    def TR(out, in_, idn, R, W, signal=True):
        op("pe", "transpose", R=R, W=W, signal=signal, out=out, in_=in_, identity=idn)

    def ACT(out, in_, func, R, W, **kw):
        op("act", "activation", R=R, W=W, out=out, in_=in_, func=func, **kw)

    def DVE(meth, R, W, **kw):
        op("dve", meth, R=R, W=W, **kw)

    def COPY(eng, out, in_, R, W):
        if eng == "act":
            ACT(out, in_, AF.Copy, R, W)
        else:
            DVE("tensor_copy", R, W, out=out, in_=in_)

    cds = B.new_ds()
    dma("sp", ident[:], ident_d.ap(), cds, W=[cU])
    dma("sp", masku[:], masku_d.ap(), cds, W=[cU])
    dma("sp", gs[:], gsm.ap(), cds, W=[cU])
    dma("sp", flg[:], flag.ap(), cds, W=[cU])
    DVE("memset", [], [cU], ap=onesb[:], constant=1.0)
    DVE("memset", [], [cU], ap=epsT[:], constant=EPS)

    ldU = [Unit(), Unit()]; ldDS = [B.new_ds(), B.new_ds()]

    def next_stg():
        i = st["stg"]; st["stg"] ^= 1
        return i

    def rows_to_fm(stg, src_ap, R, dst_fn, dstW, single=False):
        i = 0 if single else next_stg()
        s = stg[i]
        dma("sp", s[0:R, :], src_ap, ldDS[i], W=[ldU[i]])
        for g4 in range(4):
            p, u = psum()
            for j in range(4):
                kc = g4 * 4 + j
                TR(p[:, j * 128:j * 128 + R], s[0:R, kc * 128:(kc + 1) * 128], ident[0:R, 0:R], [ldU[i], cU], [u], signal=(j == 3))
            for j in range(4):
                kc = g4 * 4 + j
                COPY("act" if g4 % 2 else "dve", dst_fn(kc), p[:, j * 128:j * 128 + R], [u], dstW(kc))

    def fm_to_rows(stg, src_fn, srcR, R, dst_ap, single=False):
        i = 0 if single else next_stg()
        s = stg[i]
        for g4 in range(4):
            p, u = psum()
            for j in range(4):
                kc = g4 * 4 + j
                TR(p[0:R, j * 128:(j + 1) * 128], src_fn(kc), ident[:, :], list(srcR(kc)) + [cU], [u], signal=(j == 3))
            COPY("act" if g4 % 2 else "dve", s[0:R, g4 * 512:(g4 + 1) * 512], p[0:R, :], [u], [ldU[i]])
        dma("sp", dst_ap, s[0:R, :], ldDS[i], R=[ldU[i]])

    arena_reset()
    stg = [carve(2048), carve(2048)]
    if "vecs" not in SKIP:
        rows_to_fm(stg, vecs.ap(), NV, lambda kc: vecT[:, kc, :], lambda kc: [cU])
    for t in range(8):
        rows_to_fm(stg, x_p.ap()[t * 128:(t + 1) * 128, :], 128,
                   lambda kc, t=t: xT[:, kc, t * 128:(t + 1) * 128], lambda kc, t=t: [xU[kc][t // 4]])
    if "xs" not in SKIP:
        rows_to_fm(stg, x_s.ap(), 1, lambda kc: xT[:, kc, TP:TP + 1], lambda kc: [xU[kc][2]])

    def rstd_from_psum(p, u, n, inv_n):
        r, ru = tf()
        ACT(r[:, :n], p[:, :n], AF.Sqrt, [u, cU], [ru], bias=epsT[:, 0:1], scale=inv_n)
        DVE("reciprocal", [ru], [ru], out=r[:, :n], in_=r[:, :n])
        return r, ru

    def sumsq_fm(src_fn, srcU, nk, n):
        p, u = psum()
        for kc in range(nk):
            s, su = tb()
            ACT(s[:, :n], src_fn(kc), AF.Square, list(srcU(kc)), [su])
            MM(p[:, :n], onesb[:, :], s[:, :n], kc == 0, kc == nk - 1, [su, cU], [u])
        return p, u

    def rmsnorm(vrow):
        for bi, (c0, n) in enumerate(BLKS):
            p, u = sumsq_fm(lambda kc: xT[:, kc, c0:c0 + n], lambda kc: [xU[kc][bi]], KC, n)
            r, ru = rstd_from_psum(p, u, n, 1.0 / D)
            for kc in range(KC):
                DVE("scalar_tensor_tensor", [xU[kc][bi], ru, cU], [hU[bi]], out=hT[:, kc, c0:c0 + n], in0=xT[:, kc, c0:c0 + n],
                    scalar=vecT[:, kc, vrow:vrow + 1], in1=r[:, :n], op0=ALU.mult, op1=ALU.mult)

    def load_memhat():
        mT = carve(KC * 64).rearrange("p (kc t) -> p kc t", t=64); mTU = [Unit() for _ in range(KC)]
        for t in range(4):
            rows_to_fm(stg, mem.ap()[t * 64:(t + 1) * 64, :], 64, lambda kc: mT[:, kc, :], lambda kc: [mTU[kc]])
            p, u = sumsq_fm(lambda kc: mT[:, kc, :], lambda kc: [mTU[kc]], KC, 64)
            r, ru = rstd_from_psum(p, u, 64, 1.0 / D)
            for kc in range(KC):
                DVE("tensor_tensor", [mTU[kc], ru], [mhU], out=memhat[:, kc, t * 64:(t + 1) * 64], in0=mT[:, kc, :], in1=r[:, :64], op=ALU.mult)
    if "mem" not in SKIP:
        load_memhat()

    steps = []

    def wstep(loads, compute):
        steps.append((loads, compute))

    def wsrc(w_ap, k0, nk, c0, ncol):
        return w_ap[k0 * 128:(k0 + nk) * 128, c0:c0 + ncol].rearrange("(kc p) n -> p kc n", p=128)

    def slot3(slot, nk, ncol):
        return slot[:, 0:nk * ncol].rearrange("p (kc n) -> p kc n", n=ncol)

    def run_steps():
        issued = 0
        for k in range(len(steps)):
            while issued < min(len(steps), k + NSLOT):
                si = issued % NSLOT
                for dst_fn, src in steps[issued][0]:
                    dma("pool", dst_fn(wsl[si]), src, wDS[si], W=[wU[si]])
                issued += 1
            steps[k][1](wsl[k % NSLOT], wU[k % NSLOT])
        steps.clear()

    def mm_fm(p, u, w3, wu, nk, oc, in_t, inU, c0, n):
        for kc in range(nk):
            MM(p[:, :n], w3[:, kc, oc * 128:(oc + 1) * 128], in_t[:, kc, c0:c0 + n], kc == 0, kc == nk - 1, [wu] + list(inU(kc)), [u], signal=(kc == nk - 1))

    def resid_add(p, u, oc, bi, c0, n):
        DVE("tensor_tensor", [u], [xU[oc][bi]], out=xT[:, oc, c0:c0 + n], in0=p[:, :n], in1=xT[:, oc, c0:c0 + n], op=ALU.add)

    def out_proj_steps(w_ap, k0, nk, src=None, srcU=None):
        src = mid if src is None else src
        srcU = (lambda kc, bi: mU[kc][bi]) if srcU is None else srcU

        def mk(cb):
            def comp(slot, wu):
                w3 = slot3(slot, nk, 512)
                for o4 in range(4):
                    for bi, (c0, n) in enumerate(BLKS):
                        p, u = psum()
                        mm_fm(p, u, w3, wu, nk, o4, src, lambda kc: [srcU(kc, bi)], c0, n)
                        resid_add(p, u, cb * 4 + o4, bi, c0, n)
            return comp
        for cb in range(4):
            wstep([(lambda s: slot3(s, nk, 512), wsrc(w_ap, k0, nk, cb * 512, 512))], mk(cb))

    def ffn(i):
        rmsnorm(V_GFFN + i)
        w_in = f_w_in.ap()[i]; w_out = f_w_out.ap()[i]

        def mk_in(c2, c_lo):
            def comp(slot, wu):
                w3 = slot3(slot, KC, 512)
                for j in range(2):
                    lc = c2 + j - c_lo
                    for bi, (c0, n) in enumerate(BLKS):
                        pg, ug = psum(); pu, uu = psum()
                        mm_fm(pg, ug, w3, wu, KC, j, hT, lambda kc: [hU[bi]], c0, n)
                        mm_fm(pu, uu, w3, wu, KC, 2 + j, hT, lambda kc: [hU[bi]], c0, n)
                        s, su = tf()
                        ACT(s[:, :n], pg[:, :n], AF.Silu, [ug], [su])
                        DVE("tensor_tensor", [uu, su], [mU[lc][bi]], out=mid[:, lc, c0:c0 + n], in0=pu[:, :n], in1=s[:, :n], op=ALU.mult)
            return comp
        for c_lo, c_hi in ((0, 16), (16, 32), (32, 44)):
            for c2 in range(c_lo, c_hi, 2):
                wstep([(lambda s: slot3(s, KC, 512)[:, :, 0:256], wsrc(w_in, 0, KC, c2 * 128, 256)),
                       (lambda s: slot3(s, KC, 512)[:, :, 256:512], wsrc(w_in, 0, KC, FFN_H + c2 * 128, 256))], mk_in(c2, c_lo))
            out_proj_steps(w_out, c_lo, c_hi - c_lo)
        run_steps()

    def head_norm(p, u, n, gcol, out_bf, outW, out_f32=None, out32W=()):
        q, qu = tf()
        ACT(q[:, :n], p[:, :n], AF.Copy, [u], [qu])
        s, su = tb()
        DVE("tensor_tensor", [qu], [su], out=s[:, :n], in0=q[:, :n], in1=q[:, :n], op=ALU.mult)
        p2, u2 = psum()
        MM(p2[:, :n], onesb[:, :], s[:, :n], True, True, [su, cU], [u2])
        r, ru = rstd_from_psum(p2, u2, n, 1.0 / 128)
        if out_f32 is not None:
            DVE("scalar_tensor_tensor", [qu, ru, cU], list(out32W), out=out_f32, in0=q[:, :n], scalar=gs[:, gcol:gcol + 1], in1=r[:, :n],
                op0=ALU.mult, op1=ALU.mult)
            ACT(out_bf, out_f32, AF.Copy, list(out32W), list(outW))
        else:
            DVE("scalar_tensor_tensor", [qu, ru, cU], list(outW), out=out_bf, in0=q[:, :n], scalar=gs[:, gcol:gcol + 1], in1=r[:, :n],
                op0=ALU.mult, op1=ALU.mult)

    def xattn(i):
        arena_reset()
        def qT(hh, c0, n):
            return mid[:, 4 + hh, c0:c0 + n]
        kTp = mid[:, 8, 0:1024].rearrange("p (h t) -> p h t", t=256); kpU = [mU[8][0], mU[8][1]]
        kTs = mid[:, 9, 0:1024].rearrange("p (h t) -> p h t", t=256); ksU = [mU[9][0], mU[9][1]]
        vp = mid[:, 10, 0:1024].rearrange("p (m c) -> p m c", c=512); vpU = [mU[10][0], mU[10][1]]
        vs = mid[:, 11, 0:1024].rearrange("p (m c) -> p m c", c=512); vsU = [mU[11][0], mU[11][1]]
        kst = carve(1024).rearrange("p (m c) -> p m c", c=512); kstU = Unit(); kstDS = B.new_ds()
        vsDS = B.new_ds()
        kf = carve(4 * 256).rearrange("p (h t) -> p h t", t=256); kfU = [Unit() for _ in range(4)]
        ost = [carve(512), carve(512)]; ostU = [Unit(), Unit()]; ostDS = [B.new_ds(), B.new_ds()]
        oi = [0]

        def next_o():
            oi[0] ^= 1
            return oi[0]

        dma("sp", kst[:, :, :], cmk.ap()[i].rearrange("(m p) c -> p m c", p=128), kstDS, W=[kstU])
        for hh in range(4):
            p, u = psum()
            for m in range(2):
                TR(p[:, m * 128:(m + 1) * 128], kst[:, m, hh * 128:(hh + 1) * 128], ident[:, :], [kstU, cU], [u], signal=(m == 1))
            ACT(kTs[:, hh, :], p[:, 0:256], AF.Copy, [u], ksU)
        dma("pool", vs[:, :, :], cmv.ap()[i].rearrange("(m p) c -> p m c", p=128), vsDS, W=vsU)

        rmsnorm(V_GXAT + i)

        def comp_q(slot, wu):
            w3 = slot3(slot, KC, 512)
            for hh in range(4):
                for bi, (c0, n) in enumerate(BLKS):
                    p, u = psum()
                    mm_fm(p, u, w3, wu, KC, hh, hT, lambda kc: [hU[bi]], c0, n)
                    head_norm(p, u, n, S_XQ + i, qT(hh, c0, n), [mU[4 + hh][bi]])
        wstep([(lambda s: slot3(s, KC, 512), wsrc(x_w_q.ap()[i], 0, KC, 0, 512))], comp_q)

        def fold_gain(w3, wu):
            g = vecT[:, :, V_GMEM + i:V_GMEM + i + 1].to_broadcast([128, KC, 512])
            DVE("tensor_tensor", [wu, cU], [wu], out=w3, in0=w3, in1=g, op=ALU.mult)

        def comp_k(slot, wu):
            w3 = slot3(slot, KC, 512)
            fold_gain(w3, wu)
            for hh in range(4):
                p, u = psum()
                mm_fm(p, u, w3, wu, KC, hh, memhat, lambda kc: [mhU], 0, 256)
                head_norm(p, u, 256, S_XK + i, kTp[:, hh, :], kpU, out_f32=kf[:, hh, :], out32W=[kfU[hh]])
            for m in range(2):
                si = next_o()
                p, u = psum()
                for hh in range(4):
                    TR(p[:, hh * 128:(hh + 1) * 128], kf[:, hh, m * 128:(m + 1) * 128], ident[:, :], [kfU[hh], cU], [u], signal=(hh == 3))
                DVE("tensor_copy", [u], [ostU[si]], out=ost[si][:, :], in_=p[:, :])
                dma("sp", memk_p.ap()[i, m * 128:(m + 1) * 128, :], ost[si][:, :], ostDS[si], R=[ostU[si]])
        wstep([(lambda s: slot3(s, KC, 512), wsrc(x_w_kv.ap()[i], 0, KC, 0, 512))], comp_k)

        def comp_v(slot, wu):
            w3 = slot3(slot, KC, 512)
            fold_gain(w3, wu)
            for m in range(2):
                si = next_o()
                p, u = psum()
                for kc in range(KC):
                    MM(p[:, :], memhat[:, kc, m * 128:(m + 1) * 128], w3[:, kc, :], kc == 0, kc == KC - 1, [wu, mhU], [u], signal=(kc == KC - 1))
                DVE("tensor_copy", [u], [ostU[si]], out=ost[si][:, :], in_=p[:, :])
                ACT(vp[:, m, :], ost[si][:, :], AF.Copy, [ostU[si]], vpU)
                dma("sp", memv_p.ap()[i, m * 128:(m + 1) * 128, :], ost[si][:, :], ostDS[si], R=[ostU[si]])
        wstep([(lambda s: slot3(s, KC, 512), wsrc(x_w_kv.ap()[i], 0, KC, 512, 512))], comp_v)

        def comp_o(slot, wu):
            for bi, (c0, n) in enumerate(BLKS):
                kT, kU, vv, vU = (kTs, ksU, vs, vsU) if bi == 2 else (kTp, kpU, vp, vpU)
                for hh in range(4):
                    po, uo = psum(); pd, ud = psum()
                    for m in range(2):
                        p, u = psum()
                        MM(p[:, :n], kT[:, hh, m * 128:(m + 1) * 128], qT(hh, c0, n), True, True, kU + [mU[4 + hh][bi]], [u])
                        e, eu = tb()
                        ACT(e[:, :n], p[:, :n], AF.Exp, [u], [eu], scale=SCALE)
                        MM(po[:, :n], vv[:, m, hh * 128:(hh + 1) * 128], e[:, :n], m == 0, m == 1, vU + [eu], [uo])
                        MM(pd[:, :n], onesb[:, :], e[:, :n], m == 0, m == 1, [eu, cU], [ud])
                    r, ru = tf()
                    DVE("reciprocal", [ud], [ru], out=r[:, :n], in_=pd[:, :n])
                    DVE("tensor_tensor", [uo, ru], [mU[hh][bi]], out=mid[:, hh, c0:c0 + n], in0=po[:, :n], in1=r[:, :n], op=ALU.mult)
            w3 = slot[:, 0:4 * D].rearrange("p (kc n) -> p kc n", n=D)
            for oc in range(KC):
                for bi, (c0, n) in enumerate(BLKS):
                    p, u = psum()
                    mm_fm(p, u, w3, wu, 4, oc, mid, lambda kc: [mU[kc][bi]], c0, n)
                    resid_add(p, u, oc, bi, c0, n)
        wstep([(lambda s: s[:, 0:4 * D].rearrange("p (kc n) -> p kc n", n=D), wsrc(x_w_o.ap()[i], 0, 4, 0, D))], comp_o)
        run_steps()

    def gmlp(i, j):
        arena_reset()
        w_in = a_w_in.ap()[j]; w_out = a_w_out.ap()[j]
        NTL = 2
        gv = carve(NTL * D // 2, BF16).rearrange("p (t c) -> p t c", c=D); gvU = [Unit() for _ in range(NTL)]
        wsT = carve(16 * 128 // 2, BF16).rearrange("p (g q) -> p g q", q=128); wsU = Unit()
        Cb = carve(16 * 128).rearrange("p (g q) -> p g q", q=128); CU = Unit(); CDS = B.new_ds()
        wst = carve(512).rearrange("p (g q) -> p g q", q=128); wstU = Unit(); wstDS = B.new_ds()
        sm = carve(64); smU = Unit(); smDS = B.new_ds()
        ws00, bs0, gvs, vln = sm[:, 0:16], sm[:, 16:32], sm[:, 32:48], sm[:, 48:64]
        stat = carve(16); statU = Unit()
        rmsnorm(V_GMIX + i)
        dma("sp", Cb[:, :, :], a_b_s.ap()[j].partition_broadcast(128).rearrange("p (g q) -> p g q", q=128), CDS, W=[CU])
        dma("sp", ws00, bass.AP(a_w_s.handle(), j * 16 * 16384, [[0, 128], [16384, 16]]), smDS, W=[smU], slow=True)
        dma("sp", bs0, bass.AP(a_b_s.handle(), j * 2048, [[0, 128], [128, 16]]), smDS, W=[smU], slow=True)
        for g4 in range(4):
            dma("sp", wst[:, :, :], a_w_s.ap()[j, g4 * 4:(g4 + 1) * 4].rearrange("g p q -> p g q"), wstDS, W=[wstU])
            p, u = psum()
            for k in range(4):
                TR(p[:, k * 128:(k + 1) * 128], wst[:, k, :], ident[:, :], [wstU, cU], [u], signal=(k == 3))
            for k in range(4):
                g = g4 * 4 + k
                DVE("tensor_tensor", [u, cU], [wsU], out=wsT[:, g, :], in0=p[:, k * 128:(k + 1) * 128], in1=masku[:, :], op=ALU.mult)
        for g4 in range(4):
            p, u = psum()
            for k in range(4):
                g = g4 * 4 + k
                MM(p[:, k * 128:(k + 1) * 128], onesb[:, :], wsT[:, g, :], True, True, [wsU, cU], [u], signal=(k == 3))
            for k in range(4):
                g = g4 * 4 + k
                DVE("scalar_tensor_tensor", [u, CU, cU], [CU], out=Cb[:, g, :], in0=p[:, k * 128:(k + 1) * 128], scalar=vecT[:, g, V_ALNB + j:V_ALNB + j + 1],
                    in1=Cb[:, g, :], op0=ALU.mult, op1=ALU.add)

        def mk_v(cb, t0, last):
            def comp(slot, wu):
                w3 = slot3(slot, KC, 512)
                for tl in range(NTL):
                    t = t0 + tl
                    p, u = psum()
                    for kc in range(KC):
                        MM(p[:, :], hT[:, kc, t * 128:(t + 1) * 128], w3[:, kc, :], kc == 0, kc == KC - 1, [wu, hU[t // 4]], [u], signal=(kc == KC - 1))
                    ACT(gv[:, tl, cb * 512:(cb + 1) * 512], p[:, :], AF.Gelu_apprx_tanh, [u], [gvU[tl]])
                if not last:
                    return
                for tl in range(NTL):
                    t = t0 + tl
                    bi = t // 4
                    j1, j1u = tb(); j2, j2u = tb()
                    s1 = stat[:, 0:1]; s2 = stat[:, 1:2]; mu = stat[:, 2:3]; var = stat[:, 3:4]; rs = stat[:, 4:5]; nb = stat[:, 5:6]
                    DVE("memset", [], [statU], ap=stat[:, 8:16], constant=0.0)
                    for q4 in range(4):
                        ACT(j1[:, :], gv[:, tl, q4 * 512:(q4 + 1) * 512], AF.Copy, [gvU[tl]], [j1u, statU], accum_out=stat[:, 8 + q4:9 + q4])
                        ACT(j2[:, :], gv[:, tl, q4 * 512:(q4 + 1) * 512], AF.Square, [gvU[tl]], [j2u, statU], accum_out=stat[:, 12 + q4:13 + q4])
                    DVE("tensor_reduce", [statU], [statU], out=s1, in_=stat[:, 8:12], axis=AX.X, op=ALU.add)
                    DVE("tensor_reduce", [statU], [statU], out=s2, in_=stat[:, 12:16], axis=AX.X, op=ALU.add)
                    DVE("tensor_scalar_mul", [statU], [statU], out=mu, in0=s1, scalar1=1.0 / D)
                    DVE("tensor_tensor", [statU], [statU], out=var, in0=mu, in1=mu, op=ALU.mult)
                    DVE("scalar_tensor_tensor", [statU], [statU], out=var, in0=s2, scalar=1.0 / D, in1=var, op0=ALU.mult, op1=ALU.subtract)
                    ACT(rs, var, AF.Sqrt, [statU, cU], [statU], bias=epsT[:, 0:1], scale=1.0)
                    DVE("reciprocal", [statU], [statU], out=rs, in_=rs)
                    DVE("scalar_tensor_tensor", [statU], [statU], out=nb, in0=mu, scalar=-1.0, in1=rs, op0=ALU.mult, op1=ALU.mult)
                    ACT(gv[:, tl, :], gv[:, tl, :], AF.Identity, [gvU[tl], statU], [gvU[tl]], bias=nb, scale=rs)
                    for g4 in range(4):
                        p, u = psum()
                        for k in range(4):
                            g = g4 * 4 + k
                            MM(p[:, k * 128:(k + 1) * 128], gv[:, tl, g * 128:(g + 1) * 128], wsT[:, g, :], True, True, [gvU[tl], wsU], [u], signal=(k == 3))
                        for k in range(4):
                            g = g4 * 4 + k
                            DVE("scalar_tensor_tensor", [u, CU, cU], [mU[g][bi]], out=mid[:, g, t * 128:(t + 1) * 128], in0=p[:, k * 128:(k + 1) * 128],
                                scalar=vecT[:, g, V_ALNG + j:V_ALNG + j + 1], in1=Cb[:, g, :], op0=ALU.mult, op1=ALU.add)
            return comp
        for t0 in range(0, 8, NTL):
            for cb in range(4):
                wstep([(lambda s: slot3(s, KC, 512), wsrc(w_in, 0, KC, D + cb * 512, 512))], mk_v(cb, t0, cb == 3))

        def mk_vs(cb):
            def comp(slot, wu):
                w3 = slot3(slot, KC, 512)
                for o4 in range(4):
                    p, u = psum()
                    mm_fm(p, u, w3, wu, KC, o4, hT, lambda kc: [hU[2]], TP, 1)
                    ACT(gvs[:, cb * 4 + o4:cb * 4 + o4 + 1], p[:, 0:1], AF.Gelu_apprx_tanh, [u], [smU])
                if cb != 3:
                    return
                sq, squ = tf()
                DVE("tensor_copy", [smU], [squ], out=sq[:, 0:16], in_=gvs)
                DVE("tensor_tensor", [smU], [squ], out=sq[:, 16:32], in0=gvs, in1=gvs, op=ALU.mult)
                hb, hbu = tb(); lb_, lbu = tb()
                DVE("tensor_copy", [squ], [hbu], out=hb[:, 0:32], in_=sq[:, 0:32])
                DVE("tensor_tensor", [squ, hbu], [squ], out=sq[:, 32:64], in0=sq[:, 0:32], in1=hb[:, 0:32], op=ALU.subtract)
                DVE("tensor_copy", [squ], [lbu], out=lb_[:, 0:32], in_=sq[:, 32:64])
                p, u = psum()
                MM(p[:, 0:32], onesb[:, :], hb[:, 0:32], True, False, [hbu, cU], [u], signal=False)
                MM(p[:, 0:32], onesb[:, :], lb_[:, 0:32], False, True, [lbu, cU], [u])
                s1 = stat[:, 0:1]; s2 = stat[:, 1:2]; mu = stat[:, 2:3]; var = stat[:, 3:4]; rs = stat[:, 4:5]; nb = stat[:, 5:6]
                DVE("tensor_reduce", [u], [statU], out=s1, in_=p[:, 0:16], axis=AX.X, op=ALU.add)
                DVE("tensor_reduce", [u], [statU], out=s2, in_=p[:, 16:32], axis=AX.X, op=ALU.add)
                DVE("tensor_scalar_mul", [statU], [statU], out=mu, in0=s1, scalar1=1.0 / D)
                DVE("tensor_tensor", [statU], [statU], out=var, in0=mu, in1=mu, op=ALU.mult)
                DVE("scalar_tensor_tensor", [statU], [statU], out=var, in0=s2, scalar=1.0 / D, in1=var, op0=ALU.mult, op1=ALU.subtract)
                ACT(rs, var, AF.Sqrt, [statU, cU], [statU], bias=epsT[:, 0:1], scale=1.0)
                DVE("reciprocal", [statU], [statU], out=rs, in_=rs)
                DVE("scalar_tensor_tensor", [statU], [statU], out=nb, in0=mu, scalar=-1.0, in1=rs, op0=ALU.mult, op1=ALU.mult)
                ACT(vln, gvs, AF.Identity, [smU, statU], [smU], bias=nb, scale=rs)
                DVE("tensor_tensor", [smU, cU], [smU], out=vln, in0=vln, in1=vecT[:, :, V_ALNG + j], op=ALU.mult)
                DVE("tensor_tensor", [smU, cU], [smU], out=vln, in0=vln, in1=vecT[:, :, V_ALNB + j], op=ALU.add)
                p2, u2 = psum()
                TR(p2[0:16, 0:128], vln, ident[:, :], [smU, cU], [u2])
                o, ou = tf()
                DVE("tensor_copy", [u2], [ou], out=o[0:16, 0:128], in_=p2[0:16, 0:128])
                dma("sp", av_s.ap()[j].rearrange("(g e) -> g e", e=128), o[0:16, 0:128], smDS, R=[ou])
                DVE("tensor_tensor", [smU], [smU], out=gvs, in0=vln, in1=ws00, op=ALU.mult)
                DVE("tensor_tensor", [smU], [mU[g][2] for g in range(KC)], out=mid[:, :, TP], in0=gvs, in1=bs0, op=ALU.add)
            return comp
        for cb in range(4):
            wstep([(lambda s: slot3(s, KC, 512), wsrc(w_in, 0, KC, D + cb * 512, 512))], mk_vs(cb))

        def mk_u(cb):
            def comp(slot, wu):
                w3 = slot3(slot, KC, 512)
                for o4 in range(4):
                    oc = cb * 4 + o4
                    for bi, (c0, n) in enumerate(BLKS):
                        p, u = psum()
                        mm_fm(p, u, w3, wu, KC, o4, hT, lambda kc: [hU[bi]], c0, n)
                        g_, gu = tf()
                        ACT(g_[:, :n], p[:, :n], AF.Gelu_apprx_tanh, [u], [gu])
                        DVE("tensor_tensor", [gu], [mU[oc][bi]], out=mid[:, oc, c0:c0 + n], in0=g_[:, :n], in1=mid[:, oc, c0:c0 + n], op=ALU.mult)
            return comp
        for cb in range(4):
            wstep([(lambda s: slot3(s, KC, 512), wsrc(w_in, 0, KC, cb * 512, 512))], mk_u(cb))
        out_proj_steps(w_out, 0, KC)
        run_steps()

    def sample_attn(qs_f, ks_f, vs_f, smpU, hTb):
        B.barrier()
        hTf = hTb[:, 0:16400].bitcast(F32)
        kcf = hTf[:, 0:2048]; prod = hTf[:, 2048:2560]; sc = hTf[:, 2560:2608]; pf = hTf[:, 2608:2656]
        pn = hTf[:, 2656:2704]; t48 = hTf[:, 2704:2752]; t48b = hTf[:, 2752:2800]; b0 = hTf[:, 2800:2848]
        tabs = hTf[:, 2848:2896]; sels = hTf[:, 2896:3280]; o16 = hTf[:, 3280:3312]; rowst = hTf[:, 3312:3440]
        bfv = hTb[:, 6880:16400]
        Qd = bfv[:, 0:2048]; vcb = [bfv[:, 2048 * (1 + g):2048 * (2 + g)] for g in range(3)]
        pb48 = bfv[:, 8192:8240]; hb = bfv[:, 8240:8288]; lb_ = bfv[:, 8288:8336]
        kU_, vU_, sU, cDS_, vDS_, oDS_, rDS_ = Unit(), [Unit(), Unit(), Unit()], Unit(), B.new_ds(), B.new_ds(), B.new_ds(), B.new_ds()
        qdU, prU, rsU = Unit(), Unit(), Unit()
        dma("sp", tabs[0:32, :], relb.ap(), cDS_, W=[sU])
        dma("sp", sels[0:32, :], sels_d.ap(), cDS_, W=[sU])
        dma("sp", b0, relb.ap()[0:1, :].partition_broadcast(128), cDS_, W=[sU])
        for g in range(3):
            dil = DIL[g]
            dma("pool", vcb[g], bass.AP(cv[g].handle(), 0, [[dil * D, 128], [1, D]]), vDS_, W=[vU_[g]])
        for g in range(3):
            dil = DIL[g]
            dma("sp", kcf, bass.AP(ck[g].handle(), 0, [[dil * D, 128], [1, D]]), cDS_, W=[kU_])
            DVE("tensor_tensor", [cU, smpU], [qdU], out=Qd.rearrange("p (h e) -> p h e", e=128), in0=ident[:, :].unsqueeze(1).to_broadcast([128, 16, 128]),
                in1=qs_f[:, g * 16:(g + 1) * 16].unsqueeze(2).to_broadcast([128, 16, 128]), op=ALU.mult)
            for c4 in range(4):
                p, u = psum()
                MM(p[:, :], onesb[:, :], Qd[:, c4 * 512:(c4 + 1) * 512], True, True, [qdU, cU], [u])
                DVE("tensor_tensor", [u, kU_], [prU], out=prod, in0=kcf[:, c4 * 512:(c4 + 1) * 512], in1=p[:, :], op=ALU.mult)
                DVE("tensor_reduce", [prU], [sU], out=sc[:, g * 16 + c4 * 4:g * 16 + c4 * 4 + 4], in_=prod.rearrange("p (h e) -> p h e", e=128), axis=AX.X, op=ALU.add)
            p, u = psum()
            MM(p[:, 0:16], sels[0:32, g * 128:(g + 1) * 128], tabs[0:32, g * 16:(g + 1) * 16], True, True, [sU], [u])
            DVE("scalar_tensor_tensor", [u, sU], [sU], out=sc[:, g * 16:(g + 1) * 16], in0=sc[:, g * 16:(g + 1) * 16], scalar=SCALE, in1=p[:, 0:16], op0=ALU.mult, op1=ALU.add)
        ACT(pf, sc, AF.Exp, [sU], [sU])
        DVE("tensor_copy", [sU], [sU], out=pb48, in_=pf)
        DVE("tensor_tensor", [smpU], [sU], out=t48, in0=qs_f, in1=ks_f, op=ALU.mult)
        DVE("tensor_copy", [sU], [sU], out=hb, in_=t48)
        DVE("tensor_tensor", [sU], [sU], out=t48b, in0=t48, in1=hb, op=ALU.subtract)
        DVE("tensor_copy", [sU], [sU], out=lb_, in_=t48b)
        p, u = psum()
        MM(p[:, 0:48], onesb[:, :], hb, True, False, [sU, cU], [u])
        MM(p[:, 0:48], onesb[:, :], lb_, False, True, [sU, cU], [u])
        DVE("scalar_tensor_tensor", [u, sU], [sU], out=t48, in0=p[:, 0:48], scalar=SCALE, in1=b0, op0=ALU.mult, op1=ALU.add)
        ACT(pn, t48, AF.Exp, [sU], [sU])
        pso, uso = psum(); psd, usd = psum()
        for h_ in range(16):
            for g in range(3):
                MM(pso[:, h_:h_ + 1], vcb[g][:, h_ * 128:(h_ + 1) * 128], pb48[:, g * 16 + h_:g * 16 + h_ + 1], g == 0, g == 2, [vU_[g], sU], [uso])
        for g in range(3):
            MM(psd[:, 0:16], onesb[:, :], pb48[:, g * 16:(g + 1) * 16], g == 0, g == 2, [sU, cU], [usd])
        DVE("tensor_tensor", [sU, smpU], [sU], out=t48, in0=pn, in1=vs_f, op=ALU.mult)
        DVE("tensor_reduce", [sU], [sU], out=o16[:, 0:16], in_=t48.rearrange("p (g h) -> p h g", g=3), axis=AX.X, op=ALU.add)
        DVE("tensor_reduce", [sU], [sU], out=o16[:, 16:32], in_=pn.rearrange("p (g h) -> p h g", g=3), axis=AX.X, op=ALU.add)
        DVE("tensor_tensor", [uso, sU], [sU], out=o16[:, 0:16], in0=pso[:, 0:16], in1=o16[:, 0:16], op=ALU.add)
        DVE("tensor_tensor", [usd, sU], [sU], out=o16[:, 16:32], in0=psd[:, 0:16], in1=o16[:, 16:32], op=ALU.add)
        DVE("reciprocal", [sU], [sU], out=o16[:, 16:32], in_=o16[:, 16:32])
        DVE("tensor_tensor", [sU], [mU[k][2] for k in range(KC)], out=mid[:, :, TP], in0=o16[:, 0:16], in1=o16[:, 16:32], op=ALU.mult)
        for g, W_ in enumerate((128, 512, 2048)):
            n16 = (W_ - 1) * 16
            for src_t, dst_t, col in ((ck[g], bk_s[g], ks_f), (cv[g], bv_s[g], vs_f)):
                dma("sp", bass.AP(dst_t, 0, [[n16, 128], [1, n16]]), bass.AP(src_t.handle(), D, [[n16, 128], [1, n16]]), oDS_)
                p, u = psum()
                TR(p[0:16, 0:128], col[:, g * 16:(g + 1) * 16], ident[:, :], [smpU, cU], [u])
                DVE("tensor_copy", [u, rsU], [rsU], out=rowst[0:16, :], in_=p[0:16, 0:128])
                dma("sp", dst_t.ap()[W_ - 1:W_, :].rearrange("o (h e) -> (o h) e", e=128), rowst[0:16, :], rDS_, R=[rsU])

    DIL = (1, 4, 16)
    KEEP = (128, 512, 1024)

    def dilattn(i):
        qs_d = B.dram("qs_d", [6144, 1024], BF16)
        kvi = B.dram("kvi", [12288, 1024], BF16)
        kvo = B.dram("kvo", [24576, 1024], BF16)

        def kvo_off(elem_off):
            row = elem_off // 1024
            return ((row // 1024) * 2048 + (row % 1024)) * 1024 + (elem_off % 1024)
        eg_d = B.dram("eg_d", [144, 255], F32)
        re_d = B.dram("re_d", [144 * 128, 255], F32)
        VB = 6144 * 1024
        RANK = 12288 * 1024
        kvW = Unit(); qsW = Unit(); reU = Unit(); kvoU = Unit()
        ccsem = B.new_sem("cc_kv")
        w_qkv = b_w_qkv.ap(); w_o = b_w_out.ap()

        arena_reset()
        tabx = carve(48); selS = carve(9 * 255); egs = carve(9 * 255); tU_ = Unit(); tDS = B.new_ds()
        dma("sp", tabx[0:32, :], relb.ap(), tDS, W=[tU_])
        DVE("memset", [], [tU_], ap=tabx[32:33, :], constant=-1e30)
        dma("sp", selS[0:33, :], selg.ap(), tDS, W=[tU_])
        for g in range(3):
            for v in range(3):
                c = (g * 3 + v) * 255
                p, u = psum()
                MM(p[0:16, 0:255], tabx[0:33, g * 16:(g + 1) * 16], selS[0:33, c:c + 255], True, True, [tU_], [u])
                ACT(egs[0:16, c:c + 255], p[0:16, 0:255], AF.Exp, [u], [tU_])
                if v == 2:
                    DVE("tensor_scalar_mul", [tU_, cU], [tU_], out=egs[0:16, c:c + 255], in0=egs[0:16, c:c + 255], scalar1=flg[0:16, 0:1])
        dma("sp", eg_d.ap().rearrange("(g h v) c -> h g v c", g=3, h=16, v=3), egs[0:16, :].rearrange("p (g v c) -> p g v c", g=3, v=3), tDS, R=[tU_], W=[reU])
        for k in range(9):
            dma("sp", re_d.ap()[k * 2048:(k + 1) * 2048, :].rearrange("(r j) c -> r j c", j=128),
                bass.AP(eg_d, k * 16 * 255, [[255, 16], [0, 128], [1, 255]]), tDS, R=[reU], W=[reU])

        arena_reset()
        smp = carve(3 * 48); smpU = Unit()
        rmsnorm(V_GMIX + i)
        hk = [carve(512, BF16), carve(512, BF16)]; hkU = [Unit(), Unit()]; hkDS = [B.new_ds(), B.new_ds()]
        kst = carve(2048).rearrange("p (t c) -> p t c", c=512); kstU = Unit(); kstDS = B.new_ds()
        vst = [carve(512), carve(512)]; vstU = [Unit(), Unit()]; vstDS = [B.new_ds(), B.new_ds()]
        vbs = [carve(256, BF16), carve(256, BF16)]; vbsU = [Unit(), Unit()]; vbsDS = [B.new_ds(), B.new_ds()]
        qs_f, ks_f, vs_f = smp[:, 0:48], smp[:, 48:96], smp[:, 96:144]
        cnt = {"hk": 0, "v": 0}

        def k_out_dma(g, hq, bi):
            for tt in range(4):
                t = bi * 4 + tt
                if t * 128 >= TP - KEEP[g]:
                    r0 = t * 128 - (TP - KEEP[g])
                    dma("sp", bk_p[g].ap()[r0:r0 + 128, hq * 512:(hq + 1) * 512], kst[:, tt, :], kstDS, R=[kstU])

        def mk_k(g, hq):
            dil = DIL[g]

            def comp(slot, wu):
                w3 = slot3(slot, KC, 512)
                gcol = S_BK + g
                for bi, (c0, n) in enumerate(BLKS):
                    for o4 in range(4):
                        h_ = hq * 4 + o4
                        p, u = psum()
                        mm_fm(p, u, w3, wu, KC, o4, hT, lambda kc: [hU[bi]], c0, n)
                        q, qu = tf()
                        ACT(q[:, :n], p[:, :n], AF.Copy, [u], [qu])
                        s_, su = tb()
                        DVE("tensor_tensor", [qu], [su], out=s_[:, :n], in0=q[:, :n], in1=q[:, :n], op=ALU.mult)
                        p2, u2 = psum()
                        MM(p2[:, :n], onesb[:, :], s_[:, :n], True, True, [su, cU], [u2])
                        r, ru = rstd_from_psum(p2, u2, n, 1.0 / 128)
                        if bi == 2:
                            DVE("scalar_tensor_tensor", [qu, ru, cU], [smpU], out=ks_f[:, g * 16 + h_:g * 16 + h_ + 1], in0=q[:, :1], scalar=gs[:, gcol:gcol + 1],
                                in1=r[:, :1], op0=ALU.mult, op1=ALU.mult)
                            continue
                        DVE("scalar_tensor_tensor", [qu, ru, cU], [qu], out=q[:, :n], in0=q[:, :n], scalar=gs[:, gcol:gcol + 1], in1=r[:, :n], op0=ALU.mult, op1=ALU.mult)
                        hi = cnt["hk"] % 2; cnt["hk"] += 1
                        nu = 512 // dil
                        ACT(hk[hi][:, 0:512].rearrange("p (r u) -> p r u", r=dil), q[:, :].rearrange("p (u r) -> p r u", r=dil), AF.Copy, [qu], [hkU[hi]])
                        row = (g * 16 + h_) * 128
                        dma("sp", kvi.ap()[row:row + 128, :].rearrange("p (r u) -> p r u", r=dil)[:, :, bi * nu:(bi + 1) * nu],
                            hk[hi][:, 0:512].rearrange("p (r u) -> p r u", r=dil), hkDS[hi], R=[hkU[hi], kvW])
                        tiles = [tt for tt in range(4) if (bi * 4 + tt) * 128 >= TP - KEEP[g]]
                        if tiles:
                            pt, ut = psum()
                            for tt in tiles:
                                TR(pt[:, tt * 128:(tt + 1) * 128], q[:, tt * 128:(tt + 1) * 128], ident[:, :], [qu, cU], [ut], signal=(tt == tiles[-1]))
                            for tt in tiles:
                                DVE("tensor_copy", [ut], [kstU], out=kst[:, tt, o4 * 128:(o4 + 1) * 128], in_=pt[:, tt * 128:(tt + 1) * 128])
                    if bi < 2:
                        k_out_dma(g, hq, bi)
            return comp

        def mk_q(g, hq):
            dil = DIL[g]

            def comp(slot, wu):
                w3 = slot3(slot, KC, 512)
                gcol = S_BQ + g
                for bi, (c0, n) in enumerate(BLKS):
                    for o4 in range(4):
                        h_ = hq * 4 + o4
                        p, u = psum()
                        mm_fm(p, u, w3, wu, KC, o4, hT, lambda kc: [hU[bi]], c0, n)
                        q, qu = tf()
                        ACT(q[:, :n], p[:, :n], AF.Copy, [u], [qu])
                        s_, su = tb()
                        DVE("tensor_tensor", [qu], [su], out=s_[:, :n], in0=q[:, :n], in1=q[:, :n], op=ALU.mult)
                        p2, u2 = psum()
                        MM(p2[:, :n], onesb[:, :], s_[:, :n], True, True, [su, cU], [u2])
                        r, ru = rstd_from_psum(p2, u2, n, 1.0 / 128)
                        if bi == 2:
                            DVE("scalar_tensor_tensor", [qu, ru, cU], [smpU], out=qs_f[:, g * 16 + h_:g * 16 + h_ + 1], in0=q[:, :1], scalar=gs[:, gcol:gcol + 1],
                                in1=r[:, :1], op0=ALU.mult, op1=ALU.mult)
                            continue
                        hi = cnt["hk"] % 2; cnt["hk"] += 1
                        nu = 512 // dil
                        hv = hk[hi][:, 0:512].rearrange("p (r u) -> p r u", r=dil)
                        DVE("scalar_tensor_tensor", [qu, ru, cU], [hkU[hi]], out=hv, in0=q[:, :].rearrange("p (u r) -> p r u", r=dil), scalar=gs[:, gcol:gcol + 1],
                            in1=r[:, :].rearrange("p (u r) -> p r u", r=dil), op0=ALU.mult, op1=ALU.mult)
                        row = (g * 16 + h_) * 128
                        dma("sp", qs_d.ap()[row:row + 128, :].rearrange("p (r u) -> p r u", r=dil)[:, :, bi * nu:(bi + 1) * nu], hv, hkDS[hi], R=[hkU[hi], qsW])
            return comp

        def mk_v(g, hq):
            def comp(slot, wu):
                w3 = slot3(slot, KC, 512)
                for t in range(8):
                    vi = cnt["v"] % 2; cnt["v"] += 1
                    p, u = psum()
                    for kc in range(KC):
                        MM(p[:, :], hT[:, kc, t * 128:(t + 1) * 128], w3[:, kc, :], kc == 0, kc == KC - 1, [wu, hU[t // 4]], [u], signal=(kc == KC - 1))
                    DVE("tensor_copy", [u], [vstU[vi]], out=vst[vi][:, :], in_=p[:, :])
                    ACT(vbs[vi][:, :], vst[vi][:, :], AF.Copy, [vstU[vi]], [vbsU[vi]])
                    if t * 128 >= TP - KEEP[g]:
                        r0 = t * 128 - (TP - KEEP[g])
                        dma("sp", bv_p[g].ap()[r0:r0 + 128, hq * 512:(hq + 1) * 512], vst[vi][:, :], vstDS[vi], R=[vstU[vi]])
                    off = VB + ((g * 16 + hq * 4) * 1024 + t * 128) * 128
                    dma("sp", bass.AP(kvi, off, [[128, 128], [1024 * 128, 4], [1, 128]]), vbs[vi][:, :].rearrange("p (o e) -> p o e", e=128), vbsDS[vi], R=[vbsU[vi], kvW])
                for o4 in range(4):
                    p, u = psum()
                    mm_fm(p, u, w3, wu, KC, o4, hT, lambda kc: [hU[2]], TP, 1)
                    DVE("tensor_copy", [u], [smpU], out=vs_f[:, g * 16 + hq * 4 + o4:g * 16 + hq * 4 + o4 + 1], in_=p[:, 0:1])
            return comp

        def qkv_src(g, which, hq):
            return wsrc(w_qkv, 0, KC, g * 6144 + which * 2048 + hq * 512, 512)
        for g in range(3):
            for hq in range(4):
                wstep([(lambda s: slot3(s, KC, 512), qkv_src(g, 1, hq))], mk_k(g, hq))
        for g in range(3):
            for hq in range(4):
                wstep([(lambda s: slot3(s, KC, 512), qkv_src(g, 2, hq))], mk_v(g, hq))
        run_steps()
        for k in range(12):
            B.cc(kvi.ap()[k * 1024:(k + 1) * 1024, :].opt(), kvo.ap()[k * 2048:(k + 1) * 2048, :].opt(), PAIRS_RUN, ccsem, W=[kvW, kvoU], count=k + 1)
        for g in range(3):
            for hq in range(4):
                wstep([(lambda s: slot3(s, KC, 512), qkv_src(g, 0, hq))], mk_q(g, hq))
        run_steps()

        arena_reset()
        carve(3 * 48)
        hTb = hT2[:, :]
        hoff = [0]

        def hcarve(n_bf):
            o = hoff[0]; hoff[0] = o + n_bf + (n_bf % 2)
            assert hoff[0] <= KC * XC
            return hTb[:, o:o + n_bf]
        q3 = [hcarve(1024) for _ in range(3)]; ko = [hcarve(1024) for _ in range(3)]
        kp = [hcarve(128), hcarve(512), hcarve(1024)]
        vo = [hcarve(8 * 128).rearrange("p (t e) -> p t e", e=128), hcarve(8 * 128).rearrange("p (t e) -> p t e", e=128), hcarve(16 * 128).rearrange("p (t e) -> p t e", e=128)]
        vpv = [hcarve(128).rearrange("p (t e) -> p t e", e=128), hcarve(4 * 128).rearrange("p (t e) -> p t e", e=128), hcarve(16 * 128).rearrange("p (t e) -> p t e", e=128)]
        et = carve(10 * 128).rearrange("p (k i) -> p k i", i=128)
        acc = carve(1024); den = carve(1024); accU = Unit(); denU = Unit()
        ldq = Unit(); ldk = Unit(); ldv = Unit(); lde = Unit()
        qDS, kDS, vDS, eDS = B.new_ds(), B.new_ds(), B.new_ds(), B.new_ds()

        def head(h_):
            for g in range(3):
                dil = DIL[g]; U = TP // dil
                row = (g * 16 + h_) * 128
                dma("sp", q3[g], qs_d.ap()[row:row + 128, :], qDS, R=[qsW], W=[ldq])
                dma("sp", ko[g], kvi.ap()[row:row + 128, :], kDS, R=[kvW], W=[ldk])
                orow = kvo_off(row * 1024) // 1024
                if g == 0:
                    dma("sp", kp[0], kvo.ap()[orow:orow + 128, 896:1024], kDS, R=[kvoU], W=[ldk])
                elif g == 1:
                    dma("sp", kp[1].rearrange("p (r u) -> p r u", r=4), kvo.ap()[orow:orow + 128, :].rearrange("p (r u) -> p r u", r=4)[:, :, 128:256], kDS, R=[kvoU], W=[ldk])
                else:
                    dma("sp", kp[2], kvo.ap()[orow:orow + 128, :], kDS, R=[kvoU], W=[ldk])
                vbase = VB + (g * 16 + h_) * 1024 * 128
                if g == 0:
                    dma("sp", vo[0], bass.AP(kvi, vbase, [[128, 128], [128 * 128, 8], [1, 128]]), vDS, R=[kvW], W=[ldv])
                    dma("sp", vpv[0], bass.AP(kvo, kvo_off(vbase) + 896 * 128, [[128, 128], [128 * 128, 1], [1, 128]]), vDS, R=[kvoU], W=[ldv])
                elif g == 1:
                    for r in range(4):
                        dma("sp", vo[1][:, r * 2:r * 2 + 2, :], bass.AP(kvi, vbase + r * 128, [[4 * 128, 128], [512 * 128, 2], [1, 128]]), vDS, R=[kvW], W=[ldv])
                    dma("sp", vpv[1], bass.AP(kvo, kvo_off(vbase) + 512 * 128, [[4 * 128, 128], [128, 4], [1, 128]]), vDS, R=[kvoU], W=[ldv])
                else:
                    dma("sp", vo[2][0:64, :, :], bass.AP(kvi, vbase, [[16 * 128, 64], [128, 16], [1, 128]]), vDS, R=[kvW], W=[ldv])
                    dma("sp", vpv[2][0:64, :, :], bass.AP(kvo, kvo_off(vbase), [[16 * 128, 64], [128, 16], [1, 128]]), vDS, R=[kvoU], W=[ldv])
                erow = (g * 16 + h_) * 3
                if g < 2:
                    for k_, v in enumerate((1, 0, 2, 0)):
                        dma("sp", et[:, g * 4 + k_, :], bass.AP(re_d, (erow + v) * 128 * 255 + 127, [[254, 128], [1, 128]]), eDS, R=[reU], W=[lde])
                else:
                    dma("sp", et[0:64, 8, 0:64], bass.AP(re_d, (erow + 2) * 128 * 255 + 191, [[254, 64], [1, 64]]), eDS, R=[reU], W=[lde])
                    dma("sp", et[0:64, 9, 0:64], bass.AP(re_d, (erow + 0) * 128 * 255 + 127, [[254, 64], [1, 64]]), eDS, R=[reU], W=[lde])
            units = []
            for g in range(3):
                dil = DIL[g]; U = TP // dil
                QB = 128 if g < 2 else 64
                for r in range(dil):
                    for qb in range(U // QB):
                        units.append((g, r, qb))

            def stageA(un):
                g, r, qb = un
                dil = DIL[g]; U = TP // dil
                QB = 128 if g < 2 else 64
                nqb = U // QB
                qcols = q3[g][:, r * U + qb * QB:r * U + (qb + 1) * QB]
                if qb > 0:
                    kT0 = ko[g][:, r * U + (qb - 1) * QB:r * U + qb * QB]
                    vt0 = vo[g][0:QB, (r * nqb + qb - 1), :]
                    eb = g * 4
                else:
                    if g == 0:
                        kT0 = kp[0][:, :]; vt0 = vpv[0][:, 0, :]
                    elif g == 1:
                        kT0 = kp[1][:, r * 128:(r + 1) * 128]; vt0 = vpv[1][:, r, :]
                    else:
                        kT0 = kp[2][:, r * 64:(r + 1) * 64]; vt0 = vpv[2][0:64, r, :]
                    eb = (g * 4 + 2) if g < 2 else 8
                kT1 = ko[g][:, r * U + qb * QB:r * U + (qb + 1) * QB]
                vt1 = vo[g][0:QB, (r * nqb + qb), :]
                p, u = psum()
                MM(p[0:QB, 0:QB], kT0, qcols, True, True, [ldk, ldq], [u])
                MM(p[0:QB, QB:2 * QB], kT1, qcols, True, True, [ldk, ldq], [u])
                e_, eu = tf()
                ACT(e_[0:QB, 0:2 * QB], p[0:QB, 0:2 * QB], AF.Exp, [u], [eu], scale=SCALE)
                pb, pbu = tb()
                DVE("tensor_tensor", [eu, lde], [pbu], out=pb[0:QB, 0:2 * QB].rearrange("p (k i) -> p k i", k=2), in0=e_[0:QB, 0:2 * QB].rearrange("p (k i) -> p k i", k=2),
                    in1=et[0:QB, eb:eb + 2, 0:QB], op=ALU.mult)
                return (un, pb, pbu, vt0, vt1)

            def stageB(sa):
                un, pb, pbu, vt0, vt1 = sa
                QB = 128 if un[0] < 2 else 64
                po, uo = psum(); pd, ud = psum()
                MM(po[:, 0:QB], vt0, pb[0:QB, 0:QB], True, False, [ldv, pbu], [uo])
                MM(po[:, 0:QB], vt1, pb[0:QB, QB:2 * QB], False, True, [ldv, pbu], [uo])
                MM(pd[:, 0:QB], onesb[0:QB, :], pb[0:QB, 0:QB], True, False, [pbu, cU], [ud])
                MM(pd[:, 0:QB], onesb[0:QB, :], pb[0:QB, QB:2 * QB], False, True, [pbu, cU], [ud])
                return (un, po, uo, pd, ud)

            def stageC(sb):
                (g, r, qb), po, uo, pd, ud = sb
                dil = DIL[g]
                QB = 128 if g < 2 else 64
                a_out = acc.rearrange("p (u r) -> p r u", r=dil)[:, r, qb * QB:(qb + 1) * QB]
                d_out = den.rearrange("p (u r) -> p r u", r=dil)[:, r, qb * QB:(qb + 1) * QB]
                if g == 0:
                    DVE("tensor_copy", [uo], [accU], out=a_out, in_=po[:, 0:QB])
                    ACT(d_out, pd[:, 0:QB], AF.Copy, [ud], [denU])
                else:
                    DVE("tensor_tensor", [uo, accU], [accU], out=a_out, in0=po[:, 0:QB], in1=a_out, op=ALU.add)
                    DVE("tensor_tensor", [ud, denU], [denU], out=d_out, in0=pd[:, 0:QB], in1=d_out, op=ALU.add)
            sa_prev, sb_prev = None, None
            for un in units + [None, None]:
                sa = stageA(un) if un is not None else None
                sb = stageB(sa_prev) if sa_prev is not None else None
                if sb_prev is not None:
                    stageC(sb_prev)
                sa_prev, sb_prev = sa, sb
            DVE("reciprocal", [denU], [denU], out=den[:, :], in_=den[:, :])
            DVE("tensor_tensor", [accU, denU], [mU[h_][0], mU[h_][1]], out=mid[:, h_, 0:TP], in0=acc[:, :], in1=den[:, :], op=ALU.mult)
        for h_ in range(16):
            head(h_)

        sample_attn(qs_f, ks_f, vs_f, smpU, hTb)
        out_proj_steps(w_o, 0, KC)
        run_steps()

    def col_ln(src, srcU, grow, brow, out, outW, scr, scrU):
        sq = scr[:, 0:32]; lo = scr[:, 32:64]; stt = scr[:, 64:70]
        DVE("tensor_copy", list(srcU), [scrU], out=sq[:, 0:16], in_=src)
        DVE("tensor_tensor", list(srcU), [scrU], out=sq[:, 16:32], in0=src, in1=src, op=ALU.mult)
        hb, hbu = tb(); lb_, lbu = tb()
        DVE("tensor_copy", [scrU], [hbu], out=hb[:, 0:32], in_=sq)
        DVE("tensor_tensor", [scrU, hbu], [scrU], out=lo, in0=sq, in1=hb[:, 0:32], op=ALU.subtract)
        DVE("tensor_copy", [scrU], [lbu], out=lb_[:, 0:32], in_=lo)
        p, u = psum()
        MM(p[:, 0:32], onesb[:, :], hb[:, 0:32], True, False, [hbu, cU], [u])
        MM(p[:, 0:32], onesb[:, :], lb_[:, 0:32], False, True, [lbu, cU], [u])
        s1 = stt[:, 0:1]; s2 = stt[:, 1:2]; mu = stt[:, 2:3]; var = stt[:, 3:4]; rs = stt[:, 4:5]; nb = stt[:, 5:6]
        DVE("tensor_reduce", [u], [scrU], out=s1, in_=p[:, 0:16], axis=AX.X, op=ALU.add)
        DVE("tensor_reduce", [u], [scrU], out=s2, in_=p[:, 16:32], axis=AX.X, op=ALU.add)
        DVE("tensor_scalar_mul", [scrU], [scrU], out=mu, in0=s1, scalar1=1.0 / D)
        DVE("tensor_tensor", [scrU], [scrU], out=var, in0=mu, in1=mu, op=ALU.mult)
        DVE("scalar_tensor_tensor", [scrU], [scrU], out=var, in0=s2, scalar=1.0 / D, in1=var, op0=ALU.mult, op1=ALU.subtract)
        ACT(rs, var, AF.Sqrt, [scrU, cU], [scrU], bias=epsT[:, 0:1], scale=1.0)
        DVE("reciprocal", [scrU], [scrU], out=rs, in_=rs)
        DVE("scalar_tensor_tensor", [scrU], [scrU], out=nb, in0=mu, scalar=-1.0, in1=rs, op0=ALU.mult, op1=ALU.mult)
        ACT(out, src, AF.Identity, list(srcU) + [scrU], list(outW), bias=nb, scale=rs)
        DVE("tensor_tensor", list(outW) + [cU], list(outW), out=out, in0=out, in1=vecT[:, :, grow], op=ALU.mult)
        DVE("tensor_tensor", list(outW) + [cU], list(outW), out=out, in0=out, in1=vecT[:, :, brow], op=ALU.add)

    def convmod(i):
        arena_reset()
        cxi = B.dram("cxi", [128, 480], F32); cxo = B.dram("cxo", [256, 480], F32)
        ccs = B.new_sem("cc_cv")
        w_in = c_w_in.ap(); w_out = c_w_out.ap()
        ztail = carve(480).rearrange("p (c t) -> p c t", t=30); ztU = Unit(); ztDS = B.new_ds()
        halo = carve(480).rearrange("p (c t) -> p c t", t=30); haU = Unit(); haDS = B.new_ds()
        zcs = carve(16 * 31).rearrange("p (c t) -> p c t", t=31); zsU = Unit()
        zh = carve(16 * 60 // 2, BF16).rearrange("p (c t) -> p c t", t=60); zhU = Unit()
        stg1 = carve(2048)
        yacc = [stg1[:, 0:1024], stg1[:, 1024:2048]]; yU = [Unit(), Unit()]
        scr = carve(80); scrU = Unit()
        ysm = carve(32); ysU = Unit()
        lnmu = carve(512); lnrs = carve(512)
        rmsnorm(V_GMIX + i)
        rows_to_fm([stg1, stg1], cst.ap(), 30, lambda kc: zcs[:, kc, 0:30], lambda kc: [zsU], single=True)

        def mk_in(c2):
            def comp(slot, wu):
                w3 = slot3(slot, KC, 512)
                for j in range(2):
                    c = c2 + j
                    for bi, (c0, n) in enumerate(BLKS):
                        pa, ua = psum(); pg, ug = psum()
                        mm_fm(pa, ua, w3, wu, KC, j, hT, lambda kc: [hU[bi]], c0, n)
                        mm_fm(pg, ug, w3, wu, KC, 2 + j, hT, lambda kc: [hU[bi]], c0, n)
                        s_, su = tf()
                        ACT(s_[:, :n], pg[:, :n], AF.Sigmoid, [ug, cU], [su], bias=vecT[:, c, V_CBIN + 1:V_CBIN + 2], scale=1.0)
                        if bi == 2:
                            DVE("scalar_tensor_tensor", [ua, su, cU], [zsU], out=zcs[:, c, 30:31], in0=pa[:, :1], scalar=vecT[:, c, V_CBIN:V_CBIN + 1], in1=s_[:, :1], op0=ALU.add, op1=ALU.mult)
                            continue
                        DVE("scalar_tensor_tensor", [ua, su, cU], [mU[c][bi]], out=mid[:, c, c0:c0 + n], in0=pa[:, :n], scalar=vecT[:, c, V_CBIN:V_CBIN + 1], in1=s_[:, :n], op0=ALU.add, op1=ALU.mult)
                        if bi == 1:
                            DVE("scalar_tensor_tensor", [ua, su, cU], [ztU], out=ztail[:, c, :], in0=pa[:, 482:512], scalar=vecT[:, c, V_CBIN:V_CBIN + 1], in1=s_[:, 482:512], op0=ALU.add, op1=ALU.mult)
            return comp
        for c2 in range(0, KC, 2):
            wstep([(lambda s: slot3(s, KC, 512)[:, :, 0:256], wsrc(w_in, 0, KC, c2 * 128, 256)),
                   (lambda s: slot3(s, KC, 512)[:, :, 256:512], wsrc(w_in, 0, KC, D + c2 * 128, 256))], mk_in(c2))
        run_steps()
        fm_to_rows([stg1, stg1], lambda kc: ztail[:, kc, :], lambda kc: [ztU], 30, cconv_p.ap(), single=True)
        fm_to_rows([stg1, stg1], lambda kc: zcs[:, kc, 1:31], lambda kc: [zsU], 30, cconv_s.ap(), single=True)
        cxU = Unit()
        dma("sp", cxi.ap(), ztail[:, :, :].rearrange("p c t -> p (c t)"), ztDS, R=[ztU], W=[cxU])
        B.cc(cxi.ap().opt(), cxo.ap().opt(), PAIRS_RUN, ccs, W=[cxU])
        dma("sp", halo[:, :, :].rearrange("p c t -> p (c t)"), cxo.ap()[0:128, :], haDS, R=[cxU], W=[haU])
        DVE("tensor_scalar_mul", [haU, cU], [haU], out=halo[:, :, :], in0=halo[:, :, :], scalar1=flg[:, 0:1])
        DVE("tensor_copy", [haU], [zhU], out=zh[:, :, 0:30], in_=halo[:, :, :])
        DVE("tensor_copy", [mU[c][0] for c in range(KC)], [zhU], out=zh[:, :, 30:60], in_=mid[:, :, 0:30])

        B.barrier()
        def wtap(c, k):
            return vecT[:, c, V_CWDW + k:V_CWDW + k + 1]
        for c in range(KC):
            eng = "dve"
            ya, yu = yacc[c % 2], yU[c % 2]
            zsrc = [mU[c][0], mU[c][1]]
            op(eng, "tensor_scalar_mul", R=zsrc + [cU], W=[yu], out=ya[:, 30:TP], in0=mid[:, c, 0:TP - 30], scalar1=wtap(c, 0))
            for k in range(1, 31):
                op(eng, "scalar_tensor_tensor", R=zsrc + [cU, yu], W=[yu], out=ya[:, 30:TP], in0=mid[:, c, k:k + TP - 30], scalar=wtap(c, k), in1=ya[:, 30:TP], op0=ALU.mult, op1=ALU.add)
            op(eng, "tensor_scalar_mul", R=[zhU, cU], W=[yu], out=ya[:, 0:30], in0=zh[:, c, 0:30], scalar1=wtap(c, 0))
            for k in range(1, 31):
                op(eng, "scalar_tensor_tensor", R=[zhU, cU, yu], W=[yu], out=ya[:, 0:30], in0=zh[:, c, k:k + 30], scalar=wtap(c, k), in1=ya[:, 0:30], op0=ALU.mult, op1=ALU.add)
            ACT(hT[:, c, 0:TP], ya[:, :], AF.Identity, [yu, cU], [hU[0], hU[1]], bias=vecT[:, c, V_CBDW:V_CBDW + 1], scale=1.0)
        for bi, (c0, n) in enumerate(BLKS[:2]):
            p1, u1 = psum()
            for c in range(KC):
                MM(p1[:, :n], onesb[:, :], hT[:, c, c0:c0 + n], c == 0, c == KC - 1, [hU[bi], cU], [u1], signal=(c == KC - 1))
            p2, u2 = sumsq_fm(lambda kc: hT[:, kc, c0:c0 + n], lambda kc: [hU[bi]], KC, n)
            mu, muU, rs, rsU2 = lnmu, Unit(), lnrs, Unit()
            DVE("tensor_scalar_mul", [u1], [muU], out=mu[:, :n], in0=p1[:, :n], scalar1=1.0 / D)
            DVE("tensor_tensor", [muU], [rsU2], out=rs[:, :n], in0=mu[:, :n], in1=mu[:, :n], op=ALU.mult)
            DVE("scalar_tensor_tensor", [u2, rsU2], [rsU2], out=rs[:, :n], in0=p2[:, :n], scalar=1.0 / D, in1=rs[:, :n], op0=ALU.mult, op1=ALU.subtract)
            ACT(rs[:, :n], rs[:, :n], AF.Sqrt, [rsU2, cU], [rsU2], bias=epsT[:, 0:1], scale=1.0)
            DVE("reciprocal", [rsU2], [rsU2], out=rs[:, :n], in_=rs[:, :n])
            for c in range(KC):
                t_, tu_ = tf()
                DVE("tensor_tensor", [hU[bi], muU], [tu_], out=t_[:, :n], in0=hT[:, c, c0:c0 + n], in1=mu[:, :n], op=ALU.subtract)
                DVE("tensor_tensor", [tu_, rsU2], [tu_], out=t_[:, :n], in0=t_[:, :n], in1=rs[:, :n], op=ALU.mult)
                ACT(hT[:, c, c0:c0 + n], t_[:, :n], AF.Silu, [tu_, cU], [hU[bi]], bias=vecT[:, c, V_CLNB:V_CLNB + 1], scale=vecT[:, c, V_CLNG:V_CLNG + 1])
        tmp = scr
        prod31 = yacc[0][:, 0:16 * 31].rearrange("p (c t) -> p c t", t=31)
        DVE("tensor_tensor", [zsU, cU, yU[0]], [yU[0]], out=prod31, in0=zcs[:, :, :], in1=vecT[:, :, V_CWDW:V_CWDW + 31], op=ALU.mult)
        DVE("tensor_reduce", [yU[0]], [ysU], out=ysm[:, 0:16], in_=prod31, axis=AX.X, op=ALU.add)
        DVE("tensor_tensor", [ysU, cU], [ysU], out=ysm[:, 0:16], in0=ysm[:, 0:16], in1=vecT[:, :, V_CBDW], op=ALU.add)
        col_ln(ysm[:, 0:16], [ysU], V_CLNG, V_CLNB, ysm[:, 16:32], [ysU], scr, scrU)
        ACT(hT[:, :, TP], ysm[:, 16:32], AF.Silu, [ysU], [hU[2]])
        out_proj_steps(w_out, 0, KC, src=hT, srcU=lambda kc, bi: hU[bi])
        run_steps()

    for i in range(4):
        if on("mix%d" % i):
            if i % 3 == 0:
                gmlp(i, i // 3)
            elif i % 3 == 1:
                dilattn(i)
            else:
                convmod(i)
        if on("xat%d" % i):
            xattn(i)
        if on("ffn%d" % i):
            ffn(i)

    arena_reset()
    stg = [carve(2048), carve(2048)]
    for t in range(8):
        fm_to_rows(stg, lambda kc, t=t: xT[:, kc, t * 128:(t + 1) * 128], lambda kc, t=t: [xU[kc][t // 4]], 128, y_p.ap()[t * 128:(t + 1) * 128, :])
    if "xs" not in SKIP:
        fm_to_rows(stg, lambda kc: xT[:, kc, TP:TP + 1], lambda kc: [xU[kc][2]], 1, y_s.ap())

    sp = B.engs["sp"]
    for ds in B.dss:
        if ds.count:
            sp.prog.append(("wait", ds.sem, ds.count))
    B.emit()
    return B


def t5_bucket_np(dist):
    import math
    max_exact = 16
    d = np.maximum(dist, 1).astype(np.float32)
    large = max_exact + (np.log(d / max_exact) / math.log(2048 / max_exact) * (32 - max_exact)).astype(np.int32)
    large = np.minimum(large, 31)
    return np.where(dist < max_exact, dist, large)


_CACHE = {}


def kernel(**inp):
    if "B" not in _CACHE:
        _CACHE["B"] = build_program()
    B = _CACHE["B"]
    f = lambda a: np.ascontiguousarray(a, dtype=np.float32)
    vec_rows = [inp["g_mix"], inp["g_xattn"], inp["g_mem"], inp["g_ffn"], inp["a_ln_g"], inp["a_ln_b"],
                inp["c_b_in"].reshape(2, D), inp["c_b_dw"], inp["c_ln_g"], inp["c_ln_b"], inp["c_w_dw"][0]]
    vecs = f(np.concatenate([np.asarray(v).reshape(-1, D) for v in vec_rows], axis=0))
    assert vecs.shape[0] == NV
    gsm = f(np.concatenate([inp["x_q_norm"], inp["x_k_norm"], inp["b_q_norm"][0], inp["b_k_norm"][0]], axis=0).T)
    ident = np.eye(128, dtype=np.float32)
    masku = np.triu(np.ones((128, 128), np.float32))
    selg = np.zeros((33, 9, 255), np.float32)
    u = np.arange(255)
    for g, dil in enumerate((1, 4, 16)):
        cur = np.where(u >= 127, t5_bucket_np(np.maximum(u - 127, 0) * dil), 32)
        prev = np.where(u <= 127, t5_bucket_np((u + 1) * dil), 32)
        third = prev if g < 2 else cur
        for v, idx in enumerate((cur, prev, third)):
            selg[idx, g * 3 + v, u] = 1.0
    selg = selg.reshape(33, 9 * 255)
    sels = np.zeros((32, 3, 128), np.float32)
    jj = np.arange(128)
    for g, dil in enumerate((1, 4, 16)):
        sels[t5_bucket_np((128 - jj) * dil), g, jj] = 1.0
    sels = sels.reshape(32, 384)
    shared = dict(vecs=vecs, gsm=gsm, relb=f(inp["rel_bias"]), ident=ident, masku=masku, selg=selg, sels=sels,
                  a_w_s=f(inp["a_w_s"]), a_b_s=f(inp["a_b_s"]).reshape(2, 2048),
                  a_w_in=f(inp["a_w_in"]), a_w_out=f(inp["a_w_out"]), b_w_qkv=f(inp["b_w_qkv"][0]), b_w_out=f(inp["b_w_out"][0]),
                  c_w_in=f(inp["c_w_in"][0]), c_w_out=f(inp["c_w_out"][0]), x_w_q=f(inp["x_w_q"]), x_w_kv=f(inp["x_w_kv"]),
                  x_w_o=f(inp["x_w_o"]), f_w_in=f(inp["f_w_in"]), f_w_out=f(inp["f_w_out"]))
    in_maps = []
    for c in range(NCORES):
        b, half = c // 2, c % 2
        m = dict(shared)
        m["x_p"] = f(inp["x_prompt"][b, half * TP:(half + 1) * TP])
        m["x_s"] = f(inp["x_sample"][c])
        m["mem"] = f(inp["mem_prompt"][b])
        m["flag"] = np.full((128, 1), float(half), np.float32)
        for g, w in enumerate((128, 512, 2048)):
            m["ck%d" % g] = f(inp["cache_b_k_w%d" % w][0, c]).reshape(w, D)
            m["cv%d" % g] = f(inp["cache_b_v_w%d" % w][0, c]).reshape(w, D)
        m["cst"] = f(inp["state_c_conv"][0, c])
        m["cmk"] = f(inp["cache_mem_k"][:, c]).reshape(4, 256, 512)
        m["cmv"] = f(inp["cache_mem_v"][:, c]).reshape(4, 256, 512)
        in_maps.append(m)
    nrun = int(os.environ.get("MK_NCORES", NCORES))
    in_maps = [{k: m[k] for k in B.used_inputs} for m in in_maps]
    res = run_bass_kernel_spmd(B.nc, in_maps[:nrun], core_ids=list(range(nrun)))
    R = list(res.results) + [res.results[c % nrun] for c in range(nrun, NCORES)]
    o = lambda c, k: np.asarray(R[c][k], dtype=np.float32)
    y_prompt = np.stack([np.concatenate([o(2 * b, "y_p"), o(2 * b + 1, "y_p")], axis=0) for b in range(4)])
    y_sample = np.stack([o(c, "y_s") for c in range(8)])
    outs = [y_prompt, y_sample]
    for g, n in enumerate((128, 512, 2048)):
        for kv in ("bk", "bv"):
            if g < 2:
                a = np.stack([o(2 * b + 1, "%s%d_p" % (kv, g)) for b in range(4)])
            else:
                a = np.stack([np.concatenate([o(2 * b, "%s2_p" % kv), o(2 * b + 1, "%s2_p" % kv)], axis=0) for b in range(4)])
            outs.append(a.reshape(1, 4, n, 16, 128))
    outs.append(np.stack([o(2 * b + 1, "cconv_p") for b in range(4)])[None])
    outs.append(np.stack([o(2 * b, "memk_p") for b in range(4)], axis=1).reshape(4, 4, 256, 4, 128))
    outs.append(np.stack([o(2 * b, "memv_p") for b in range(4)], axis=1).reshape(4, 4, 256, 4, 128))
    for g, w in enumerate((128, 512, 2048)):
        for kv in ("bk", "bv"):
            outs.append(np.stack([o(c, "%s%d_s" % (kv, g)) for c in range(8)]).reshape(1, 8, w, 16, 128))
    outs.append(np.stack([o(c, "cconv_s") for c in range(8)])[None])
    outs.append(np.stack([o(c, "av_s") for c in range(8)], axis=1).reshape(2, 8, 1, D))
    return tuple(outs)
```

```python
import os
import numpy as np
from contextlib import ExitStack
import concourse.bass as bass
import concourse.mybir as mybir
from concourse.bass_utils import run_bass_kernel_spmd

F32 = mybir.dt.float32
BF16 = mybir.dt.bfloat16
AF = mybir.ActivationFunctionType
ALU = mybir.AluOpType
AX = mybir.AxisListType

D = 2048
KC = 16
TP = 1024
XC = TP + 1
NCORES = 8
FFN_H = 5632
EPS = 1e-6
SCALE = 128 ** -0.5
NSLOT = 2
PAIRS = [[0, 1], [2, 3], [4, 5], [6, 7]]
PAIRS_RUN = PAIRS[:int(os.environ.get("MK_NCORES", 8)) // 2]

V_GMIX, V_GXAT, V_GMEM, V_GFFN = 0, 4, 8, 12
V_ALNG, V_ALNB = 16, 18
V_CBIN, V_CBDW, V_CLNG, V_CLNB, V_CWDW = 20, 22, 23, 24, 25
NV = 56
S_XQ, S_XK, S_BQ, S_BK = 0, 4, 8, 11

BLKS = [(0, 512), (512, 512), (1024, 1)]

PLAN = os.environ.get("MK_PLAN", "")
SKIP = os.environ.get("MK_SKIP", "").split(",")


class Unit:
    __slots__ = ("w", "rs", "excl")

    def __init__(self, excl=False):
        self.w = None
        self.rs = {}
        self.excl = excl


class Eng:
    def __init__(self, name, sem):
        self.name = name
        self.sem = sem
        self.n = 0
        self.seen = {}
        self.prog = []


class DS:
    def __init__(self, sem):
        self.sem = sem
        self.count = 0


class Builder:
    def __init__(self):
        self.nc = bass.Bass("TRN2", target_bir_lowering=False)
        self.es = ExitStack()
        self.sems = []
        self.engs = {}
        self.dss = []
        self.nsb = 0

    def new_sem(self, name):
        s = self.es.enter_context(self.nc.semaphore(name))
        self.sems.append(s)
        return len(self.sems) - 1

    def new_ds(self):
        ds = DS(self.new_sem("d%d" % len(self.dss)))
        self.dss.append(ds)
        return ds

    def sb(self, shape, dt, name=None):
        self.nsb += 1
        return self.es.enter_context(self.nc.sbuf_tensor("s_" + (name or ("sb%d" % self.nsb)), list(shape), dt))

    def dram(self, name, shape, dt, kind=None):
        if kind is None:
            return self.nc.dram_tensor(name, list(shape), dt)
        return self.nc.dram_tensor(name, list(shape), dt, kind=kind)

    def _waits(self, eng, R, W):
        need = {}
        for u in R:
            if u.w is not None and need.get(u.w[0], 0) < u.w[1]:
                need[u.w[0]] = u.w[1]
            if u.excl:
                for s, v in u.rs.items():
                    if s != eng.sem and need.get(s, 0) < v:
                        need[s] = v
        for u in W:
            if u.w is not None and need.get(u.w[0], 0) < u.w[1]:
                need[u.w[0]] = u.w[1]
            for s, v in u.rs.items():
                if need.get(s, 0) < v:
                    need[s] = v
        for s, v in need.items():
            if eng.name == "pe" and s == eng.sem:
                continue
            if eng.seen.get(s, 0) < v:
                eng.prog.append(("wait", s, v))
                eng.seen[s] = v

    def _mark(self, tok, R, W):
        for u in R:
            if u.rs.get(tok[0], 0) < tok[1]:
                u.rs[tok[0]] = tok[1]
        for u in W:
            u.w = tok
            u.rs = {}

    def op(self, eng, meth, R=(), W=(), signal=True, **kw):
        eng = self.engs[eng]
        self._waits(eng, R, W)
        if signal:
            eng.n += 1
            tok = (eng.sem, eng.n)
            eng.prog.append(("ins", meth, kw, eng.sem))
        else:
            tok = (eng.sem, eng.n + 1)
            eng.prog.append(("ins", meth, kw, None))
        self._mark(tok, R, W)

    def dma(self, q, out, in_, ds, R=(), W=(), slow=False):
        eng = self.engs[q]
        self._waits(eng, R, W)
        ds.count += 16
        tok = (ds.sem, ds.count)
        eng.prog.append(("dma", out, in_, ds.sem, slow))
        self._mark(tok, R, W)

    def cc(self, ins, outs, groups, sem, R=(), W=(), count=1):
        eng = self.engs["pool"]
        self._waits(eng, R, W)
        eng.prog.append(("cc", ins, outs, groups, sem))
        self._mark((sem, count), R, W)

    def barrier(self):
        sp = self.engs["sp"]
        for ds in self.dss:
            if ds.count and sp.seen.get(ds.sem, 0) < ds.count:
                sp.prog.append(("wait", ds.sem, ds.count))
                sp.seen[ds.sem] = ds.count
        for e in self.engs.values():
            for x in self.engs.values():
                if x is e or x.n == 0:
                    continue
                if e.seen.get(x.sem, 0) < x.n:
                    e.prog.append(("wait", x.sem, x.n))
                    e.seen[x.sem] = x.n
        sp.n += 1
        sp.prog.append(("seminc", sp.sem))
        for e in self.engs.values():
            if e is not sp:
                e.prog.append(("wait", sp.sem, sp.n))
                e.seen[sp.sem] = sp.n

    def emit(self):
        nc = self.nc
        sems = self.sems
        with nc.Block() as block:
            def run(eng):
                def body(h):
                    for it in eng.prog:
                        if it[0] == "wait":
                            h.wait_ge(sems[it[1]], it[2])
                        elif it[0] == "ins":
                            ins = getattr(h, it[1])(**it[2])
                            if it[3] is not None:
                                ins.then_inc(sems[it[3]], 1)
                        elif it[0] == "dma":
                            if it[4]:
                                h.dma_start(out=it[1], in_=it[2], allow_slow_non_contiguous=True).then_inc(sems[it[3]], 16)
                            else:
                                h.dma_start(out=it[1], in_=it[2]).then_inc(sems[it[3]], 16)
                        elif it[0] == "cc":
                            h.collective_compute("AllGather", ALU.bypass, replica_groups=it[3], ins=[it[1]], outs=[it[2]]).then_inc(sems[it[4]])
                        elif it[0] == "seminc":
                            h.sem_inc(sems[it[1]], 1)
                        elif it[0] == "raw":
                            it[1](h)
                return body
            block.sync(run(self.engs["sp"]))
            block.scalar(run(self.engs["act"]))
            block.vector(run(self.engs["dve"]))
            block.tensor(run(self.engs["pe"]))
            block.gpsimd(run(self.engs["pool"]))


def build_program():
    B = Builder()
    nc = B.nc
    for name in ("pe", "act", "dve", "pool", "sp"):
        B.engs[name] = Eng(name, B.new_sem("e_" + name))
    plan = [s for s in PLAN.split(",") if s]

    def on(tag):
        return (not plan) or (tag in plan)

    class LazyIn:
        def __init__(self, name, shape):
            self.name, self.shape, self.t = name, shape, None

        def handle(self):
            if self.t is None:
                self.t = B.dram(self.name, self.shape, F32, kind="ExternalInput")
                B.used_inputs.append(self.name)
            return self.t

        def ap(self):
            return self.handle().ap()

    B.used_inputs = []

    def din(name, shape):
        return LazyIn(name, shape)

    def dout(name, shape):
        return B.dram(name, shape, F32, kind="ExternalOutput")

    x_p = din("x_p", [TP, D]); x_s = din("x_s", [1, D]); mem = din("mem", [256, D])
    vecs = din("vecs", [NV, D]); gsm = din("gsm", [128, 14]); relb = din("relb", [32, 48])
    flag = din("flag", [128, 1]); ident_d = din("ident", [128, 128]); masku_d = din("masku", [128, 128])
    selg = din("selg", [33, 9 * 255]); sels_d = din("sels", [32, 3 * 128])
    a_w_s = din("a_w_s", [2, 16, 128, 128]); a_b_s = din("a_b_s", [2, 16 * 128])
    ck = [din("ck%d" % g, [w, D]) for g, w in enumerate((128, 512, 2048))]
    cv = [din("cv%d" % g, [w, D]) for g, w in enumerate((128, 512, 2048))]
    cst = din("cst", [30, D]); cmk = din("cmk", [4, 256, 512]); cmv = din("cmv", [4, 256, 512])
    a_w_in = din("a_w_in", [2, D, 4096]); a_w_out = din("a_w_out", [2, D, D])
    b_w_qkv = din("b_w_qkv", [D, 18432]); b_w_out = din("b_w_out", [D, D])
    c_w_in = din("c_w_in", [D, 4096]); c_w_out = din("c_w_out", [D, D])
    x_w_q = din("x_w_q", [4, D, 512]); x_w_kv = din("x_w_kv", [4, D, 1024]); x_w_o = din("x_w_o", [4, 512, D])
    f_w_in = din("f_w_in", [4, D, 2 * FFN_H]); f_w_out = din("f_w_out", [4, FFN_H, D])

    y_p = dout("y_p", [TP, D]); y_s = dout("y_s", [1, D])
    bk_p = [dout("bk%d_p" % g, [n, D]) for g, n in enumerate((128, 512, 1024))]
    bv_p = [dout("bv%d_p" % g, [n, D]) for g, n in enumerate((128, 512, 1024))]
    cconv_p = dout("cconv_p", [30, D]); memk_p = dout("memk_p", [4, 256, 512]); memv_p = dout("memv_p", [4, 256, 512])
    bk_s = [dout("bk%d_s" % g, [w, D]) for g, w in enumerate((128, 512, 2048))]
    bv_s = [dout("bv%d_s" % g, [w, D]) for g, w in enumerate((128, 512, 2048))]
    cconv_s = dout("cconv_s", [30, D]); av_s = dout("av_s", [2, D])

    xT = B.sb([128, KC, XC], F32, "xT"); xU = [[Unit() for _ in BLKS] for _ in range(KC)]
    hT2 = B.sb([128, KC * XC], BF16, "hT"); hU = [Unit() for _ in BLKS]
    hT = hT2[:, :].rearrange("p (k t) -> p k t", t=XC)
    mid = B.sb([128, KC, XC], BF16, "mid"); mU = [[Unit() for _ in BLKS] for _ in range(KC)]
    wsl = [B.sb([128, 8192], BF16, "w%d" % i) for i in range(NSLOT)]
    wU = [Unit() for _ in range(NSLOT)]; wDS = [B.new_ds() for _ in range(NSLOT)]
    ident = B.sb([128, 128], F32, "ident"); onesb = B.sb([128, 128], BF16, "onesb")
    masku = B.sb([128, 128], F32, "masku")
    vecT = B.sb([128, KC, NV], F32, "vecT"); gs = B.sb([128, 14], F32, "gs")
    flg = B.sb([128, 1], F32, "flg"); epsT = B.sb([128, 1], F32, "epsT")
    cU = Unit()
    memhat = B.sb([128, KC, 256], BF16, "memhat"); mhU = Unit()
    ps = [B.es.enter_context(nc.psum_tensor("ps%d" % i, [128, 512], F32)) for i in range(8)]
    pU = [Unit(excl=True) for _ in range(8)]
    AW = int(os.environ.get("MK_AW", 8850))
    arena = B.sb([128, AW], F32, "arena")
    NT = 4
    tmpf = [arena[:, i * 512:(i + 1) * 512] for i in range(NT)]; tU = [Unit() for _ in range(NT)]
    tmpb = [arena[:, NT * 512 + i * 256:NT * 512 + (i + 1) * 256].bitcast(BF16) for i in range(NT)]; bU = [Unit() for _ in range(NT)]
    A0 = NT * 768
    st = {"ps": 0, "tf": 0, "tb": 0, "aoff": 0, "stg": 0}

    def psum():
        i = st["ps"]; st["ps"] = (i + 1) % 8
        return ps[i], pU[i]

    def tf():
        i = st["tf"]; st["tf"] = (i + 1) % NT
        return tmpf[i], tU[i]

    def tb():
        i = st["tb"]; st["tb"] = (i + 1) % NT
        return tmpb[i], bU[i]

    def arena_reset():
        B.barrier()
        st["aoff"] = A0

    def carve(words, dt=F32):
        o = st["aoff"]; st["aoff"] = o + words
        assert st["aoff"] <= AW, ("arena overflow", st["aoff"])
        a = arena[:, o:o + words]
        return a if dt == F32 else a.bitcast(dt)

    op, dma = B.op, B.dma

    def MM(out, lhsT, rhs, start, stop, R, W, signal=True):
        op("pe", "matmul", R=R, W=W, signal=signal, out=out, lhsT=lhsT, rhs=rhs, start=start, stop=stop)

    def TR(out, in_, idn, R, W, signal=True):
        op("pe", "transpose", R=R, W=W, signal=signal, out=out, in_=in_, identity=idn)

    def ACT(out, in_, func, R, W, **kw):
        op("act", "activation", R=R, W=W, out=out, in_=in_, func=func, **kw)

    def DVE(meth, R, W, **kw):
        op("dve", meth, R=R, W=W, **kw)

    def COPY(eng, out, in_, R, W):
        if eng == "act":
            ACT(out, in_, AF.Copy, R, W)
        else:
            DVE("tensor_copy", R, W, out=out, in_=in_)

    cds = B.new_ds()
    dma("sp", ident[:], ident_d.ap(), cds, W=[cU])
    dma("sp", masku[:], masku_d.ap(), cds, W=[cU])
    dma("sp", gs[:], gsm.ap(), cds, W=[cU])
    dma("sp", flg[:], flag.ap(), cds, W=[cU])
    DVE("memset", [], [cU], ap=onesb[:], constant=1.0)
    DVE("memset", [], [cU], ap=epsT[:], constant=EPS)

    ldU = [Unit(), Unit()]; ldDS = [B.new_ds(), B.new_ds()]

    def next_stg():
        i = st["stg"]; st["stg"] ^= 1
        return i

    def rows_to_fm(stg, src_ap, R, dst_fn, dstW, single=False):
        i = 0 if single else next_stg()
        s = stg[i]
        dma("sp", s[0:R, :], src_ap, ldDS[i], W=[ldU[i]])
        for g4 in range(4):
            p, u = psum()
            for j in range(4):
                kc = g4 * 4 + j
                TR(p[:, j * 128:j * 128 + R], s[0:R, kc * 128:(kc + 1) * 128], ident[0:R, 0:R], [ldU[i], cU], [u], signal=(j == 3))
            for j in range(4):
                kc = g4 * 4 + j
                COPY("act" if g4 % 2 else "dve", dst_fn(kc), p[:, j * 128:j * 128 + R], [u], dstW(kc))

    def fm_to_rows(stg, src_fn, srcR, R, dst_ap, single=False):
        i = 0 if single else next_stg()
        s = stg[i]
        for g4 in range(4):
            p, u = psum()
            for j in range(4):
                kc = g4 * 4 + j
                TR(p[0:R, j * 128:(j + 1) * 128], src_fn(kc), ident[:, :], list(srcR(kc)) + [cU], [u], signal=(j == 3))
            COPY("act" if g4 % 2 else "dve", s[0:R, g4 * 512:(g4 + 1) * 512], p[0:R, :], [u], [ldU[i]])
        dma("sp", dst_ap, s[0:R, :], ldDS[i], R=[ldU[i]])

    arena_reset()
    stg = [carve(2048), carve(2048)]
    if "vecs" not in SKIP:
        rows_to_fm(stg, vecs.ap(), NV, lambda kc: vecT[:, kc, :], lambda kc: [cU])
    for t in range(8):
        rows_to_fm(stg, x_p.ap()[t * 128:(t + 1) * 128, :], 128,
                   lambda kc, t=t: xT[:, kc, t * 128:(t + 1) * 128], lambda kc, t=t: [xU[kc][t // 4]])
    if "xs" not in SKIP:
        rows_to_fm(stg, x_s.ap(), 1, lambda kc: xT[:, kc, TP:TP + 1], lambda kc: [xU[kc][2]])

    def rstd_from_psum(p, u, n, inv_n):
        r, ru = tf()
        ACT(r[:, :n], p[:, :n], AF.Sqrt, [u, cU], [ru], bias=epsT[:, 0:1], scale=inv_n)
        DVE("reciprocal", [ru], [ru], out=r[:, :n], in_=r[:, :n])
        return r, ru

    def sumsq_fm(src_fn, srcU, nk, n):
        p, u = psum()
        for kc in range(nk):
            s, su = tb()
            ACT(s[:, :n], src_fn(kc), AF.Square, list(srcU(kc)), [su])
            MM(p[:, :n], onesb[:, :], s[:, :n], kc == 0, kc == nk - 1, [su, cU], [u])
        return p, u

    def rmsnorm(vrow):
        for bi, (c0, n) in enumerate(BLKS):
            p, u = sumsq_fm(lambda kc: xT[:, kc, c0:c0 + n], lambda kc: [xU[kc][bi]], KC, n)
            r, ru = rstd_from_psum(p, u, n, 1.0 / D)
            for kc in range(KC):
                DVE("scalar_tensor_tensor", [xU[kc][bi], ru, cU], [hU[bi]], out=hT[:, kc, c0:c0 + n], in0=xT[:, kc, c0:c0 + n],
                    scalar=vecT[:, kc, vrow:vrow + 1], in1=r[:, :n], op0=ALU.mult, op1=ALU.mult)

    def load_memhat():
        mT = carve(KC * 64).rearrange("p (kc t) -> p kc t", t=64); mTU = [Unit() for _ in range(KC)]
        for t in range(4):
            rows_to_fm(stg, mem.ap()[t * 64:(t + 1) * 64, :], 64, lambda kc: mT[:, kc, :], lambda kc: [mTU[kc]])
            p, u = sumsq_fm(lambda kc: mT[:, kc, :], lambda kc: [mTU[kc]], KC, 64)
            r, ru = rstd_from_psum(p, u, 64, 1.0 / D)
            for kc in range(KC):
                DVE("tensor_tensor", [mTU[kc], ru], [mhU], out=memhat[:, kc, t * 64:(t + 1) * 64], in0=mT[:, kc, :], in1=r[:, :64], op=ALU.mult)
    if "mem" not in SKIP:
        load_memhat()

    steps = []

    def wstep(loads, compute, post_issue=None):
        steps.append((loads, compute, post_issue))

    def wsrc(w_ap, k0, nk, c0, ncol):
        return w_ap[k0 * 128:(k0 + nk) * 128, c0:c0 + ncol].rearrange("(kc p) n -> p kc n", p=128)

    def slot3(slot, nk, ncol):
        return slot[:, 0:nk * ncol].rearrange("p (kc n) -> p kc n", n=ncol)

    def run_steps():
        issued = 0
        for k in range(len(steps)):
            while issued < min(len(steps), k + NSLOT):
                si = issued % NSLOT
                for dst_fn, src in steps[issued][0]:
                    dma("pool", dst_fn(wsl[si]), src, wDS[si], W=[wU[si]])
                if steps[issued][2] is not None:
                    steps[issued][2]()
                issued += 1
            steps[k][1](wsl[k % NSLOT], wU[k % NSLOT])
        steps.clear()

    def mm_fm(p, u, w3, wu, nk, oc, in_t, inU, c0, n):
        for kc in range(nk):
            MM(p[:, :n], w3[:, kc, oc * 128:(oc + 1) * 128], in_t[:, kc, c0:c0 + n], kc == 0, kc == nk - 1, [wu] + list(inU(kc)), [u], signal=(kc == nk - 1))

    def resid_add(p, u, oc, bi, c0, n):
        DVE("tensor_tensor", [u], [xU[oc][bi]], out=xT[:, oc, c0:c0 + n], in0=p[:, :n], in1=xT[:, oc, c0:c0 + n], op=ALU.add)

    def out_proj_steps(w_ap, k0, nk, src=None, srcU=None):
        src = mid if src is None else src
        srcU = (lambda kc, bi: mU[kc][bi]) if srcU is None else srcU

        def mk(cb):
            def comp(slot, wu):
                w3 = slot3(slot, nk, 512)
                for o4 in range(4):
                    for bi, (c0, n) in enumerate(BLKS):
                        p, u = psum()
                        mm_fm(p, u, w3, wu, nk, o4, src, lambda kc: [srcU(kc, bi)], c0, n)
                        resid_add(p, u, cb * 4 + o4, bi, c0, n)
            return comp
        for cb in range(4):
            wstep([(lambda s: slot3(s, nk, 512), wsrc(w_ap, k0, nk, cb * 512, 512))], mk(cb))

    def ffn(i):
        rmsnorm(V_GFFN + i)
        w_in = f_w_in.ap()[i]; w_out = f_w_out.ap()[i]

        def mk_in(c2, c_lo):
            def comp(slot, wu):
                w3 = slot3(slot, KC, 512)
                for j in range(2):
                    lc = c2 + j - c_lo
                    for bi, (c0, n) in enumerate(BLKS):
                        pg, ug = psum(); pu, uu = psum()
                        mm_fm(pg, ug, w3, wu, KC, j, hT, lambda kc: [hU[bi]], c0, n)
                        mm_fm(pu, uu, w3, wu, KC, 2 + j, hT, lambda kc: [hU[bi]], c0, n)
                        s, su = tf()
                        ACT(s[:, :n], pg[:, :n], AF.Silu, [ug], [su])
                        DVE("tensor_tensor", [uu, su], [mU[lc][bi]], out=mid[:, lc, c0:c0 + n], in0=pu[:, :n], in1=s[:, :n], op=ALU.mult)
            return comp
        for c_lo, c_hi in ((0, 16), (16, 32), (32, 44)):
            for c2 in range(c_lo, c_hi, 2):
                wstep([(lambda s: slot3(s, KC, 512)[:, :, 0:256], wsrc(w_in, 0, KC, c2 * 128, 256)),
                       (lambda s: slot3(s, KC, 512)[:, :, 256:512], wsrc(w_in, 0, KC, FFN_H + c2 * 128, 256))], mk_in(c2, c_lo))
            out_proj_steps(w_out, c_lo, c_hi - c_lo)
        run_steps()

    def head_norm(p, u, n, gcol, out_bf, outW, out_f32=None, out32W=()):
        q, qu = tf()
        ACT(q[:, :n], p[:, :n], AF.Copy, [u], [qu])
        s, su = tb()
        DVE("tensor_tensor", [qu], [su], out=s[:, :n], in0=q[:, :n], in1=q[:, :n], op=ALU.mult)
        p2, u2 = psum()
        MM(p2[:, :n], onesb[:, :], s[:, :n], True, True, [su, cU], [u2])
        r, ru = rstd_from_psum(p2, u2, n, 1.0 / 128)
        if out_f32 is not None:
            DVE("scalar_tensor_tensor", [qu, ru, cU], list(out32W), out=out_f32, in0=q[:, :n], scalar=gs[:, gcol:gcol + 1], in1=r[:, :n],
                op0=ALU.mult, op1=ALU.mult)
            ACT(out_bf, out_f32, AF.Copy, list(out32W), list(outW))
        else:
            DVE("scalar_tensor_tensor", [qu, ru, cU], list(outW), out=out_bf, in0=q[:, :n], scalar=gs[:, gcol:gcol + 1], in1=r[:, :n],
                op0=ALU.mult, op1=ALU.mult)

    def xattn(i):
        arena_reset()
        def qT(hh, c0, n):
            return mid[:, 4 + hh, c0:c0 + n]
        kTp = mid[:, 8, 0:1024].rearrange("p (h t) -> p h t", t=256); kpU = [mU[8][0], mU[8][1]]
        kTs = mid[:, 9, 0:1024].rearrange("p (h t) -> p h t", t=256); ksU = [mU[9][0], mU[9][1]]
        vp = mid[:, 10, 0:1024].rearrange("p (m c) -> p m c", c=512); vpU = [mU[10][0], mU[10][1]]
        vs = mid[:, 11, 0:1024].rearrange("p (m c) -> p m c", c=512); vsU = [mU[11][0], mU[11][1]]
        kst = carve(1024).rearrange("p (m c) -> p m c", c=512); kstU = Unit(); kstDS = B.new_ds()
        vsDS = B.new_ds()
        kf = carve(4 * 256).rearrange("p (h t) -> p h t", t=256); kfU = [Unit() for _ in range(4)]
        ost = [carve(512), carve(512)]; ostU = [Unit(), Unit()]; ostDS = [B.new_ds(), B.new_ds()]
        oi = [0]

        def next_o():
            oi[0] ^= 1
            return oi[0]

        dma("sp", kst[:, :, :], cmk.ap()[i].rearrange("(m p) c -> p m c", p=128), kstDS, W=[kstU])
        for hh in range(4):
            p, u = psum()
            for m in range(2):
                TR(p[:, m * 128:(m + 1) * 128], kst[:, m, hh * 128:(hh + 1) * 128], ident[:, :], [kstU, cU], [u], signal=(m == 1))
            ACT(kTs[:, hh, :], p[:, 0:256], AF.Copy, [u], ksU)
        dma("pool", vs[:, :, :], cmv.ap()[i].rearrange("(m p) c -> p m c", p=128), vsDS, W=vsU)

        rmsnorm(V_GXAT + i)

        def comp_q(slot, wu):
            w3 = slot3(slot, KC, 512)
            for hh in range(4):
                for bi, (c0, n) in enumerate(BLKS):
                    p, u = psum()
                    mm_fm(p, u, w3, wu, KC, hh, hT, lambda kc: [hU[bi]], c0, n)
                    head_norm(p, u, n, S_XQ + i, qT(hh, c0, n), [mU[4 + hh][bi]])
        wstep([(lambda s: slot3(s, KC, 512), wsrc(x_w_q.ap()[i], 0, KC, 0, 512))], comp_q)

        def fold_gain(w3, wu):
            g = vecT[:, :, V_GMEM + i:V_GMEM + i + 1].to_broadcast([128, KC, 512])
            DVE("tensor_tensor", [wu, cU], [wu], out=w3, in0=w3, in1=g, op=ALU.mult)

        def comp_k(slot, wu):
            w3 = slot3(slot, KC, 512)
            fold_gain(w3, wu)
            for hh in range(4):
                p, u = psum()
                mm_fm(p, u, w3, wu, KC, hh, memhat, lambda kc: [mhU], 0, 256)
                head_norm(p, u, 256, S_XK + i, kTp[:, hh, :], kpU, out_f32=kf[:, hh, :], out32W=[kfU[hh]])
            for m in range(2):
                si = next_o()
                p, u = psum()
                for hh in range(4):
                    TR(p[:, hh * 128:(hh + 1) * 128], kf[:, hh, m * 128:(m + 1) * 128], ident[:, :], [kfU[hh], cU], [u], signal=(hh == 3))
                DVE("tensor_copy", [u], [ostU[si]], out=ost[si][:, :], in_=p[:, :])
                dma("sp", memk_p.ap()[i, m * 128:(m + 1) * 128, :], ost[si][:, :], ostDS[si], R=[ostU[si]])
        wstep([(lambda s: slot3(s, KC, 512), wsrc(x_w_kv.ap()[i], 0, KC, 0, 512))], comp_k)

        def comp_v(slot, wu):
            w3 = slot3(slot, KC, 512)
            fold_gain(w3, wu)
            for m in range(2):
                si = next_o()
                p, u = psum()
                for kc in range(KC):
                    MM(p[:, :], memhat[:, kc, m * 128:(m + 1) * 128], w3[:, kc, :], kc == 0, kc == KC - 1, [wu, mhU], [u], signal=(kc == KC - 1))
                DVE("tensor_copy", [u], [ostU[si]], out=ost[si][:, :], in_=p[:, :])
                ACT(vp[:, m, :], ost[si][:, :], AF.Copy, [ostU[si]], vpU)
                dma("sp", memv_p.ap()[i, m * 128:(m + 1) * 128, :], ost[si][:, :], ostDS[si], R=[ostU[si]])
        wstep([(lambda s: slot3(s, KC, 512), wsrc(x_w_kv.ap()[i], 0, KC, 512, 512))], comp_v)

        def comp_o(slot, wu):
            for bi, (c0, n) in enumerate(BLKS):
                kT, kU, vv, vU = (kTs, ksU, vs, vsU) if bi == 2 else (kTp, kpU, vp, vpU)
                for hh in range(4):
                    po, uo = psum(); pd, ud = psum()
                    for m in range(2):
                        p, u = psum()
                        MM(p[:, :n], kT[:, hh, m * 128:(m + 1) * 128], qT(hh, c0, n), True, True, kU + [mU[4 + hh][bi]], [u])
                        e, eu = tb()
                        ACT(e[:, :n], p[:, :n], AF.Exp, [u], [eu], scale=SCALE)
                        MM(po[:, :n], vv[:, m, hh * 128:(hh + 1) * 128], e[:, :n], m == 0, m == 1, vU + [eu], [uo])
                        MM(pd[:, :n], onesb[:, :], e[:, :n], m == 0, m == 1, [eu, cU], [ud])
                    r, ru = tf()
                    DVE("reciprocal", [ud], [ru], out=r[:, :n], in_=pd[:, :n])
                    DVE("tensor_tensor", [uo, ru], [mU[hh][bi]], out=mid[:, hh, c0:c0 + n], in0=po[:, :n], in1=r[:, :n], op=ALU.mult)
            w3 = slot[:, 0:4 * D].rearrange("p (kc n) -> p kc n", n=D)
            for oc in range(KC):
                for bi, (c0, n) in enumerate(BLKS):
                    p, u = psum()
                    mm_fm(p, u, w3, wu, 4, oc, mid, lambda kc: [mU[kc][bi]], c0, n)
                    resid_add(p, u, oc, bi, c0, n)
        wstep([(lambda s: s[:, 0:4 * D].rearrange("p (kc n) -> p kc n", n=D), wsrc(x_w_o.ap()[i], 0, 4, 0, D))], comp_o)
        run_steps()

    def gmlp(i, j):
        arena_reset()
        w_in = a_w_in.ap()[j]; w_out = a_w_out.ap()[j]
        NTL = 2
        gv = carve(NTL * D // 2, BF16).rearrange("p (t c) -> p t c", c=D); gvU = [Unit() for _ in range(NTL)]
        wsT = carve(16 * 128 // 2, BF16).rearrange("p (g q) -> p g q", q=128); wsU = Unit()
        Cb = carve(16 * 128).rearrange("p (g q) -> p g q", q=128); CU = Unit(); CDS = B.new_ds()
        wst = carve(512).rearrange("p (g q) -> p g q", q=128); wstU = Unit(); wstDS = B.new_ds()
        sm = carve(64); smU = Unit(); smDS = B.new_ds()
        ws00, bs0, gvs, vln = sm[:, 0:16], sm[:, 16:32], sm[:, 32:48], sm[:, 48:64]
        stat = carve(16); statU = Unit()
        rmsnorm(V_GMIX + i)
        dma("sp", Cb[:, :, :], a_b_s.ap()[j].partition_broadcast(128).rearrange("p (g q) -> p g q", q=128), CDS, W=[CU])
        dma("sp", ws00, bass.AP(a_w_s.handle(), j * 16 * 16384, [[0, 128], [16384, 16]]), smDS, W=[smU], slow=True)
        dma("sp", bs0, bass.AP(a_b_s.handle(), j * 2048, [[0, 128], [128, 16]]), smDS, W=[smU], slow=True)
        for g4 in range(4):
            dma("sp", wst[:, :, :], a_w_s.ap()[j, g4 * 4:(g4 + 1) * 4].rearrange("g p q -> p g q"), wstDS, W=[wstU])
            p, u = psum()
            for k in range(4):
                TR(p[:, k * 128:(k + 1) * 128], wst[:, k, :], ident[:, :], [wstU, cU], [u], signal=(k == 3))
            for k in range(4):
                g = g4 * 4 + k
                DVE("tensor_tensor", [u, cU], [wsU], out=wsT[:, g, :], in0=p[:, k * 128:(k + 1) * 128], in1=masku[:, :], op=ALU.mult)
        for g4 in range(4):
            p, u = psum()
            for k in range(4):
                g = g4 * 4 + k
                MM(p[:, k * 128:(k + 1) * 128], onesb[:, :], wsT[:, g, :], True, True, [wsU, cU], [u], signal=(k == 3))
            for k in range(4):
                g = g4 * 4 + k
                DVE("scalar_tensor_tensor", [u, CU, cU], [CU], out=Cb[:, g, :], in0=p[:, k * 128:(k + 1) * 128], scalar=vecT[:, g, V_ALNB + j:V_ALNB + j + 1],
                    in1=Cb[:, g, :], op0=ALU.mult, op1=ALU.add)

        def mk_v(cb, t0, last):
            def comp(slot, wu):
                w3 = slot3(slot, KC, 512)
                for tl in range(NTL):
                    t = t0 + tl
                    p, u = psum()
                    for kc in range(KC):
                        MM(p[:, :], hT[:, kc, t * 128:(t + 1) * 128], w3[:, kc, :], kc == 0, kc == KC - 1, [wu, hU[t // 4]], [u], signal=(kc == KC - 1))
                    ACT(gv[:, tl, cb * 512:(cb + 1) * 512], p[:, :], AF.Gelu_apprx_tanh, [u], [gvU[tl]])
                if not last:
                    return
                for tl in range(NTL):
                    t = t0 + tl
                    bi = t // 4
                    j1, j1u = tb(); j2, j2u = tb()
                    s1 = stat[:, 0:1]; s2 = stat[:, 1:2]; mu = stat[:, 2:3]; var = stat[:, 3:4]; rs = stat[:, 4:5]; nb = stat[:, 5:6]
                    DVE("memset", [], [statU], ap=stat[:, 8:16], constant=0.0)
                    for q4 in range(4):
                        ACT(j1[:, :], gv[:, tl, q4 * 512:(q4 + 1) * 512], AF.Copy, [gvU[tl]], [j1u, statU], accum_out=stat[:, 8 + q4:9 + q4])
                        ACT(j2[:, :], gv[:, tl, q4 * 512:(q4 + 1) * 512], AF.Square, [gvU[tl]], [j2u, statU], accum_out=stat[:, 12 + q4:13 + q4])
                    DVE("tensor_reduce", [statU], [statU], out=s1, in_=stat[:, 8:12], axis=AX.X, op=ALU.add)
                    DVE("tensor_reduce", [statU], [statU], out=s2, in_=stat[:, 12:16], axis=AX.X, op=ALU.add)
                    DVE("tensor_scalar_mul", [statU], [statU], out=mu, in0=s1, scalar1=1.0 / D)
                    DVE("tensor_tensor", [statU], [statU], out=var, in0=mu, in1=mu, op=ALU.mult)
                    DVE("scalar_tensor_tensor", [statU], [statU], out=var, in0=s2, scalar=1.0 / D, in1=var, op0=ALU.mult, op1=ALU.subtract)
                    ACT(rs, var, AF.Sqrt, [statU, cU], [statU], bias=epsT[:, 0:1], scale=1.0)
                    DVE("reciprocal", [statU], [statU], out=rs, in_=rs)
                    DVE("scalar_tensor_tensor", [statU], [statU], out=nb, in0=mu, scalar=-1.0, in1=rs, op0=ALU.mult, op1=ALU.mult)
                    ACT(gv[:, tl, :], gv[:, tl, :], AF.Identity, [gvU[tl], statU], [gvU[tl]], bias=nb, scale=rs)
                    for g4 in range(4):
                        p, u = psum()
                        for k in range(4):
                            g = g4 * 4 + k
                            MM(p[:, k * 128:(k + 1) * 128], gv[:, tl, g * 128:(g + 1) * 128], wsT[:, g, :], True, True, [gvU[tl], wsU], [u], signal=(k == 3))
                        for k in range(4):
                            g = g4 * 4 + k
                            DVE("scalar_tensor_tensor", [u, CU, cU], [mU[g][bi]], out=mid[:, g, t * 128:(t + 1) * 128], in0=p[:, k * 128:(k + 1) * 128],
                                scalar=vecT[:, g, V_ALNG + j:V_ALNG + j + 1], in1=Cb[:, g, :], op0=ALU.mult, op1=ALU.add)
            return comp
        for t0 in range(0, 8, NTL):
            for cb in range(4):
                wstep([(lambda s: slot3(s, KC, 512), wsrc(w_in, 0, KC, D + cb * 512, 512))], mk_v(cb, t0, cb == 3))

        def mk_vs(cb):
            def comp(slot, wu):
                w3 = slot3(slot, KC, 512)
                for o4 in range(4):
                    p, u = psum()
                    mm_fm(p, u, w3, wu, KC, o4, hT, lambda kc: [hU[2]], TP, 1)
                    ACT(gvs[:, cb * 4 + o4:cb * 4 + o4 + 1], p[:, 0:1], AF.Gelu_apprx_tanh, [u], [smU])
                if cb != 3:
                    return
                sq, squ = tf()
                DVE("tensor_copy", [smU], [squ], out=sq[:, 0:16], in_=gvs)
                DVE("tensor_tensor", [smU], [squ], out=sq[:, 16:32], in0=gvs, in1=gvs, op=ALU.mult)
                hb, hbu = tb(); lb_, lbu = tb()
                DVE("tensor_copy", [squ], [hbu], out=hb[:, 0:32], in_=sq[:, 0:32])
                DVE("tensor_tensor", [squ, hbu], [squ], out=sq[:, 32:64], in0=sq[:, 0:32], in1=hb[:, 0:32], op=ALU.subtract)
                DVE("tensor_copy", [squ], [lbu], out=lb_[:, 0:32], in_=sq[:, 32:64])
                p, u = psum()
                MM(p[:, 0:32], onesb[:, :], hb[:, 0:32], True, False, [hbu, cU], [u], signal=False)
                MM(p[:, 0:32], onesb[:, :], lb_[:, 0:32], False, True, [lbu, cU], [u])
                s1 = stat[:, 0:1]; s2 = stat[:, 1:2]; mu = stat[:, 2:3]; var = stat[:, 3:4]; rs = stat[:, 4:5]; nb = stat[:, 5:6]
                DVE("tensor_reduce", [u], [statU], out=s1, in_=p[:, 0:16], axis=AX.X, op=ALU.add)
                DVE("tensor_reduce", [u], [statU], out=s2, in_=p[:, 16:32], axis=AX.X, op=ALU.add)
                DVE("tensor_scalar_mul", [statU], [statU], out=mu, in0=s1, scalar1=1.0 / D)
                DVE("tensor_tensor", [statU], [statU], out=var, in0=mu, in1=mu, op=ALU.mult)
                DVE("scalar_tensor_tensor", [statU], [statU], out=var, in0=s2, scalar=1.0 / D, in1=var, op0=ALU.mult, op1=ALU.subtract)
                ACT(rs, var, AF.Sqrt, [statU, cU], [statU], bias=epsT[:, 0:1], scale=1.0)
                DVE("reciprocal", [statU], [statU], out=rs, in_=rs)
                DVE("scalar_tensor_tensor", [statU], [statU], out=nb, in0=mu, scalar=-1.0, in1=rs, op0=ALU.mult, op1=ALU.mult)
                ACT(vln, gvs, AF.Identity, [smU, statU], [smU], bias=nb, scale=rs)
                DVE("tensor_tensor", [smU, cU], [smU], out=vln, in0=vln, in1=vecT[:, :, V_ALNG + j], op=ALU.mult)
                DVE("tensor_tensor", [smU, cU], [smU], out=vln, in0=vln, in1=vecT[:, :, V_ALNB + j], op=ALU.add)
                p2, u2 = psum()
                TR(p2[0:16, 0:128], vln, ident[:, :], [smU, cU], [u2])
                o, ou = tf()
                DVE("tensor_copy", [u2], [ou], out=o[0:16, 0:128], in_=p2[0:16, 0:128])
                dma("sp", av_s.ap()[j].rearrange("(g e) -> g e", e=128), o[0:16, 0:128], smDS, R=[ou])
                DVE("tensor_tensor", [smU], [smU], out=gvs, in0=vln, in1=ws00, op=ALU.mult)
                DVE("tensor_tensor", [smU], [mU[g][2] for g in range(KC)], out=mid[:, :, TP], in0=gvs, in1=bs0, op=ALU.add)
            return comp
        for cb in range(4):
            wstep([(lambda s: slot3(s, KC, 512), wsrc(w_in, 0, KC, D + cb * 512, 512))], mk_vs(cb))

        def mk_u(cb):
            def comp(slot, wu):
                w3 = slot3(slot, KC, 512)
                for o4 in range(4):
                    oc = cb * 4 + o4
                    for bi, (c0, n) in enumerate(BLKS):
                        p, u = psum()
                        mm_fm(p, u, w3, wu, KC, o4, hT, lambda kc: [hU[bi]], c0, n)
                        g_, gu = tf()
                        ACT(g_[:, :n], p[:, :n], AF.Gelu_apprx_tanh, [u], [gu])
                        DVE("tensor_tensor", [gu], [mU[oc][bi]], out=mid[:, oc, c0:c0 + n], in0=g_[:, :n], in1=mid[:, oc, c0:c0 + n], op=ALU.mult)
            return comp
        for cb in range(4):
            wstep([(lambda s: slot3(s, KC, 512), wsrc(w_in, 0, KC, cb * 512, 512))], mk_u(cb))
        out_proj_steps(w_out, 0, KC)
        run_steps()

    def sample_attn(qs_f, ks_f, vs_f, smpU, hTb):
        B.barrier()
        hTf = hTb[:, 0:16400].bitcast(F32)
        kcf = hTf[:, 0:2048]; prod = hTf[:, 2048:2560]; sc = hTf[:, 2560:2608]; pf = hTf[:, 2608:2656]
        pn = hTf[:, 2656:2704]; t48 = hTf[:, 2704:2752]; t48b = hTf[:, 2752:2800]; b0 = hTf[:, 2800:2848]
        tabs = hTf[:, 2848:2896]; sels = hTf[:, 2896:3280]; o16 = hTf[:, 3280:3312]; rowst = hTf[:, 3312:3440]
        bfv = hTb[:, 6880:16400]
        Qd = bfv[:, 0:2048]; vcb = [bfv[:, 2048 * (1 + g):2048 * (2 + g)] for g in range(3)]
        pb48 = bfv[:, 8192:8240]; hb = bfv[:, 8240:8288]; lb_ = bfv[:, 8288:8336]
        kU_, vU_, sU, cDS_, vDS_, oDS_, rDS_ = Unit(), [Unit(), Unit(), Unit()], Unit(), B.new_ds(), B.new_ds(), B.new_ds(), B.new_ds()
        qdU, prU, rsU = Unit(), Unit(), Unit()
        dma("sp", tabs[0:32, :], relb.ap(), cDS_, W=[sU])
        dma("sp", sels[0:32, :], sels_d.ap(), cDS_, W=[sU])
        dma("sp", b0, relb.ap()[0:1, :].partition_broadcast(128), cDS_, W=[sU])
        for g in range(3):
            dil = DIL[g]
            dma("pool", vcb[g], bass.AP(cv[g].handle(), 0, [[dil * D, 128], [1, D]]), vDS_, W=[vU_[g]])
        for g in range(3):
            dil = DIL[g]
            dma("sp", kcf, bass.AP(ck[g].handle(), 0, [[dil * D, 128], [1, D]]), cDS_, W=[kU_])
            DVE("tensor_tensor", [cU, smpU], [qdU], out=Qd.rearrange("p (h e) -> p h e", e=128), in0=ident[:, :].unsqueeze(1).to_broadcast([128, 16, 128]),
                in1=qs_f[:, g * 16:(g + 1) * 16].unsqueeze(2).to_broadcast([128, 16, 128]), op=ALU.mult)
            for c4 in range(4):
                p, u = psum()
                MM(p[:, :], onesb[:, :], Qd[:, c4 * 512:(c4 + 1) * 512], True, True, [qdU, cU], [u])
                DVE("tensor_tensor", [u, kU_], [prU], out=prod, in0=kcf[:, c4 * 512:(c4 + 1) * 512], in1=p[:, :], op=ALU.mult)
                DVE("tensor_reduce", [prU], [sU], out=sc[:, g * 16 + c4 * 4:g * 16 + c4 * 4 + 4], in_=prod.rearrange("p (h e) -> p h e", e=128), axis=AX.X, op=ALU.add)
            p, u = psum()
            MM(p[:, 0:16], sels[0:32, g * 128:(g + 1) * 128], tabs[0:32, g * 16:(g + 1) * 16], True, True, [sU], [u])
            DVE("scalar_tensor_tensor", [u, sU], [sU], out=sc[:, g * 16:(g + 1) * 16], in0=sc[:, g * 16:(g + 1) * 16], scalar=SCALE, in1=p[:, 0:16], op0=ALU.mult, op1=ALU.add)
        ACT(pf, sc, AF.Exp, [sU], [sU])
        DVE("tensor_copy", [sU], [sU], out=pb48, in_=pf)
        DVE("tensor_tensor", [smpU], [sU], out=t48, in0=qs_f, in1=ks_f, op=ALU.mult)
        DVE("tensor_copy", [sU], [sU], out=hb, in_=t48)
        DVE("tensor_tensor", [sU], [sU], out=t48b, in0=t48, in1=hb, op=ALU.subtract)
        DVE("tensor_copy", [sU], [sU], out=lb_, in_=t48b)
        p, u = psum()
        MM(p[:, 0:48], onesb[:, :], hb, True, False, [sU, cU], [u])
        MM(p[:, 0:48], onesb[:, :], lb_, False, True, [sU, cU], [u])
        DVE("scalar_tensor_tensor", [u, sU], [sU], out=t48, in0=p[:, 0:48], scalar=SCALE, in1=b0, op0=ALU.mult, op1=ALU.add)
        ACT(pn, t48, AF.Exp, [sU], [sU])
        pso, uso = psum(); psd, usd = psum()
        for h_ in range(16):
            for g in range(3):
                MM(pso[:, h_:h_ + 1], vcb[g][:, h_ * 128:(h_ + 1) * 128], pb48[:, g * 16 + h_:g * 16 + h_ + 1], g == 0, g == 2, [vU_[g], sU], [uso])
        for g in range(3):
            MM(psd[:, 0:16], onesb[:, :], pb48[:, g * 16:(g + 1) * 16], g == 0, g == 2, [sU, cU], [usd])
        DVE("tensor_tensor", [sU, smpU], [sU], out=t48, in0=pn, in1=vs_f, op=ALU.mult)
        DVE("tensor_reduce", [sU], [sU], out=o16[:, 0:16], in_=t48.rearrange("p (g h) -> p h g", g=3), axis=AX.X, op=ALU.add)
        DVE("tensor_reduce", [sU], [sU], out=o16[:, 16:32], in_=pn.rearrange("p (g h) -> p h g", g=3), axis=AX.X, op=ALU.add)
        DVE("tensor_tensor", [uso, sU], [sU], out=o16[:, 0:16], in0=pso[:, 0:16], in1=o16[:, 0:16], op=ALU.add)
        DVE("tensor_tensor", [usd, sU], [sU], out=o16[:, 16:32], in0=psd[:, 0:16], in1=o16[:, 16:32], op=ALU.add)
        DVE("reciprocal", [sU], [sU], out=o16[:, 16:32], in_=o16[:, 16:32])
        DVE("tensor_tensor", [sU], [mU[k][2] for k in range(KC)], out=mid[:, :, TP], in0=o16[:, 0:16], in1=o16[:, 16:32], op=ALU.mult)
        for g, W_ in enumerate((128, 512, 2048)):
            n16 = (W_ - 1) * 16
            for src_t, dst_t, col in ((ck[g], bk_s[g], ks_f), (cv[g], bv_s[g], vs_f)):
                dma("sp", bass.AP(dst_t, 0, [[n16, 128], [1, n16]]), bass.AP(src_t.handle(), D, [[n16, 128], [1, n16]]), oDS_)
                p, u = psum()
                TR(p[0:16, 0:128], col[:, g * 16:(g + 1) * 16], ident[:, :], [smpU, cU], [u])
                DVE("tensor_copy", [u, rsU], [rsU], out=rowst[0:16, :], in_=p[0:16, 0:128])
                dma("sp", dst_t.ap()[W_ - 1:W_, :].rearrange("o (h e) -> (o h) e", e=128), rowst[0:16, :], rDS_, R=[rsU])

    DIL = (1, 4, 16)
    KEEP = (128, 512, 1024)

    def dilattn(i):
        qs_d = B.dram("qs_d", [6144, 1024], BF16)
        kvi = B.dram("kvi", [12288, 1024], BF16)
        kvo = B.dram("kvo", [24576, 1024], BF16)

        def kvo_off(elem_off):
            row = elem_off // 1024
            return ((row // 1024) * 2048 + (row % 1024)) * 1024 + (elem_off % 1024)
        eg_d = B.dram("eg_d", [144, 255], F32)
        re_d = B.dram("re_d", [144 * 128, 255], F32)
        VB = 6144 * 1024
        RANK = 12288 * 1024
        kvW = Unit(); qsW = Unit(); reU = Unit(); kvoU = Unit()
        ccsem = B.new_sem("cc_kv")
        w_qkv = b_w_qkv.ap(); w_o = b_w_out.ap()

        arena_reset()
        tabx = carve(48); selS = carve(9 * 255); egs = carve(9 * 255); tU_ = Unit(); tDS = B.new_ds()
        dma("sp", tabx[0:32, :], relb.ap(), tDS, W=[tU_])
        DVE("memset", [], [tU_], ap=tabx[32:33, :], constant=-1e30)
        dma("sp", selS[0:33, :], selg.ap(), tDS, W=[tU_])
        for g in range(3):
            for v in range(3):
                c = (g * 3 + v) * 255
                p, u = psum()
                MM(p[0:16, 0:255], tabx[0:33, g * 16:(g + 1) * 16], selS[0:33, c:c + 255], True, True, [tU_], [u])
                ACT(egs[0:16, c:c + 255], p[0:16, 0:255], AF.Exp, [u], [tU_])
                if v == 2:
                    DVE("tensor_scalar_mul", [tU_, cU], [tU_], out=egs[0:16, c:c + 255], in0=egs[0:16, c:c + 255], scalar1=flg[0:16, 0:1])
        dma("sp", eg_d.ap().rearrange("(g h v) c -> h g v c", g=3, h=16, v=3), egs[0:16, :].rearrange("p (g v c) -> p g v c", g=3, v=3), tDS, R=[tU_], W=[reU])
        for k in range(9):
            dma("sp", re_d.ap()[k * 2048:(k + 1) * 2048, :].rearrange("(r j) c -> r j c", j=128),
                bass.AP(eg_d, k * 16 * 255, [[255, 16], [0, 128], [1, 255]]), tDS, R=[reU], W=[reU])

        arena_reset()
        smp = carve(3 * 48); smpU = Unit()
        rmsnorm(V_GMIX + i)
        hk = [carve(512, BF16), carve(512, BF16)]; hkU = [Unit(), Unit()]; hkDS = [B.new_ds(), B.new_ds()]
        kst = carve(2048).rearrange("p (t c) -> p t c", c=512); kstU = Unit(); kstDS = B.new_ds()
        vst = [carve(512), carve(512)]; vstU = [Unit(), Unit()]; vstDS = [B.new_ds(), B.new_ds()]
        vbs = [carve(256, BF16), carve(256, BF16)]; vbsU = [Unit(), Unit()]; vbsDS = [B.new_ds(), B.new_ds()]
        qs_f, ks_f, vs_f = smp[:, 0:48], smp[:, 48:96], smp[:, 96:144]
        cnt = {"hk": 0, "v": 0}

        def k_out_dma(g, hq, bi):
            for tt in range(4):
                t = bi * 4 + tt
                if t * 128 >= TP - KEEP[g]:
                    r0 = t * 128 - (TP - KEEP[g])
                    dma("sp", bk_p[g].ap()[r0:r0 + 128, hq * 512:(hq + 1) * 512], kst[:, tt, :], kstDS, R=[kstU])

        pend = {"f": None}

        def flush_pending():
            f = pend["f"]; pend["f"] = None
            if f is not None:
                f()

        def qk_front(w3, wu, o4, bi, c0, n):
            p, u = psum()
            mm_fm(p, u, w3, wu, KC, o4, hT, lambda kc: [hU[bi]], c0, n)
            q, qu = tf()
            ACT(q[:, :n], p[:, :n], AF.Copy, [u], [qu])
            s_, su = tb()
            ACT(s_[:, :n], p[:, :n], AF.Square, [u], [su])
            return q, qu, s_, su

        def qk_sqrt(s_, su, n):
            p2, u2 = psum()
            MM(p2[:, :n], onesb[:, :], s_[:, :n], True, True, [su, cU], [u2])
            r, ru = tf()
            ACT(r[:, :n], p2[:, :n], AF.Sqrt, [u2, cU], [ru], bias=epsT[:, 0:1], scale=1.0 / 128)
            DVE("reciprocal", [ru], [ru], out=r[:, :n], in_=r[:, :n])
            return r, ru

        def mk_k(g, hq):
            dil = DIL[g]

            def comp(slot, wu):
                w3 = slot3(slot, KC, 512)
                gcol = S_BK + g
                for bi, (c0, n) in enumerate(BLKS):
                    for o4 in range(4):
                        h_ = hq * 4 + o4
                        q, qu, s_, su = qk_front(w3, wu, o4, bi, c0, n)
                        flush_pending()
                        if bi == 2:
                            r, ru = qk_sqrt(s_, su, 1)
                            DVE("scalar_tensor_tensor", [qu, ru, cU], [smpU], out=ks_f[:, g * 16 + h_:g * 16 + h_ + 1], in0=q[:, :1], scalar=gs[:, gcol:gcol + 1],
                                in1=r[:, :1], op0=ALU.mult, op1=ALU.mult)
                            continue

                        def tail(q=q, qu=qu, s_=s_, su=su, bi=bi, o4=o4, h_=h_, n=n):
                            r, ru = qk_sqrt(s_, su, n)
                            DVE("scalar_tensor_tensor", [qu, ru, cU], [qu], out=q[:, :n], in0=q[:, :n], scalar=gs[:, gcol:gcol + 1], in1=r[:, :n], op0=ALU.mult, op1=ALU.mult)
                            hi = cnt["hk"] % 2; cnt["hk"] += 1
                            nu = 512 // dil
                            ACT(hk[hi][:, 0:512].rearrange("p (r u) -> p r u", r=dil), q[:, :].rearrange("p (u r) -> p r u", r=dil), AF.Copy, [qu], [hkU[hi]])
                            row = (g * 16 + h_) * 128
                            dma("sp", kvi.ap()[row:row + 128, :].rearrange("p (r u) -> p r u", r=dil)[:, :, bi * nu:(bi + 1) * nu],
                                hk[hi][:, 0:512].rearrange("p (r u) -> p r u", r=dil), hkDS[hi], R=[hkU[hi], kvW])
                            tiles = [tt for tt in range(4) if (bi * 4 + tt) * 128 >= TP - KEEP[g]]
                            if tiles:
                                pt, ut = psum()
                                for tt in tiles:
                                    TR(pt[:, tt * 128:(tt + 1) * 128], q[:, tt * 128:(tt + 1) * 128], ident[:, :], [qu, cU], [ut], signal=(tt == tiles[-1]))
                                for tt in tiles:
                                    DVE("tensor_copy", [ut], [kstU], out=kst[:, tt, o4 * 128:(o4 + 1) * 128], in_=pt[:, tt * 128:(tt + 1) * 128])
                            if o4 == 3:
                                k_out_dma(g, hq, bi)
                        pend["f"] = tail
            return comp

        def mk_q(g, hq):
            dil = DIL[g]

            def comp(slot, wu):
                w3 = slot3(slot, KC, 512)
                gcol = S_BQ + g
                for bi, (c0, n) in enumerate(BLKS):
                    for o4 in range(4):
                        h_ = hq * 4 + o4
                        q, qu, s_, su = qk_front(w3, wu, o4, bi, c0, n)
                        flush_pending()
                        if bi == 2:
                            r, ru = qk_sqrt(s_, su, 1)
                            DVE("scalar_tensor_tensor", [qu, ru, cU], [smpU], out=qs_f[:, g * 16 + h_:g * 16 + h_ + 1], in0=q[:, :1], scalar=gs[:, gcol:gcol + 1],
                                in1=r[:, :1], op0=ALU.mult, op1=ALU.mult)
                            continue

                        def tail(q=q, qu=qu, s_=s_, su=su, bi=bi, h_=h_, n=n):
                            r, ru = qk_sqrt(s_, su, n)
                            hi = cnt["hk"] % 2; cnt["hk"] += 1
                            nu = 512 // dil
                            hv = hk[hi][:, 0:512].rearrange("p (r u) -> p r u", r=dil)
                            DVE("scalar_tensor_tensor", [qu, ru, cU], [hkU[hi]], out=hv, in0=q[:, :].rearrange("p (u r) -> p r u", r=dil), scalar=gs[:, gcol:gcol + 1],
                                in1=r[:, :].rearrange("p (u r) -> p r u", r=dil), op0=ALU.mult, op1=ALU.mult)
                            row = (g * 16 + h_) * 128
                            dma("sp", qs_d.ap()[row:row + 128, :].rearrange("p (r u) -> p r u", r=dil)[:, :, bi * nu:(bi + 1) * nu], hv, hkDS[hi], R=[hkU[hi], qsW])
                        pend["f"] = tail
            return comp

        def mk_v(g, hq):
            def comp(slot, wu):
                w3 = slot3(slot, KC, 512)
                for t in range(8):
                    vi = cnt["v"] % 2; cnt["v"] += 1
                    p, u = psum()
                    for kc in range(KC):
                        MM(p[:, :], hT[:, kc, t * 128:(t + 1) * 128], w3[:, kc, :], kc == 0, kc == KC - 1, [wu, hU[t // 4]], [u], signal=(kc == KC - 1))
                    DVE("tensor_copy", [u], [vstU[vi]], out=vst[vi][:, :], in_=p[:, :])
                    ACT(vbs[vi][:, :], vst[vi][:, :], AF.Copy, [vstU[vi]], [vbsU[vi]])
                    if t * 128 >= TP - KEEP[g]:
                        r0 = t * 128 - (TP - KEEP[g])
                        dma("sp", bv_p[g].ap()[r0:r0 + 128, hq * 512:(hq + 1) * 512], vst[vi][:, :], vstDS[vi], R=[vstU[vi]])
                    off = VB + ((g * 16 + hq * 4) * 1024 + t * 128) * 128
                    dma("sp", bass.AP(kvi, off, [[128, 128], [1024 * 128, 4], [1, 128]]), vbs[vi][:, :].rearrange("p (o e) -> p o e", e=128), vbsDS[vi], R=[vbsU[vi], kvW])
                for o4 in range(4):
                    p, u = psum()
                    mm_fm(p, u, w3, wu, KC, o4, hT, lambda kc: [hU[2]], TP, 1)
                    DVE("tensor_copy", [u], [smpU], out=vs_f[:, g * 16 + hq * 4 + o4:g * 16 + hq * 4 + o4 + 1], in_=p[:, 0:1])
            return comp

        def qkv_src(g, which, hq):
            return wsrc(w_qkv, 0, KC, g * 6144 + which * 2048 + hq * 512, 512)
        for g in range(3):
            for hq in range(4):
                wstep([(lambda s: slot3(s, KC, 512), qkv_src(g, 1, hq))], mk_k(g, hq))
        def mk_flush(inner):
            def comp(slot, wu):
                flush_pending()
                inner(slot, wu)
            return comp
        nv = 0
        for g in range(3):
            for hq in range(4):
                wstep([(lambda s: slot3(s, KC, 512), qkv_src(g, 2, hq))], mk_flush(mk_v(g, hq)) if nv == 0 else mk_v(g, hq))
                nv += 1
        run_steps()
        def mk_cc(k):
            return lambda: B.cc(kvi.ap()[k * 1024:(k + 1) * 1024, :].opt(), kvo.ap()[k * 2048:(k + 1) * 2048, :].opt(), PAIRS_RUN, ccsem, W=[kvW, kvoU], count=k + 1)
        nq = 0
        for g in range(3):
            for hq in range(4):
                wstep([(lambda s: slot3(s, KC, 512), qkv_src(g, 0, hq))], mk_q(g, hq), post_issue=mk_cc(nq))
                nq += 1
        run_steps()
        flush_pending()

        arena_reset()
        carve(3 * 48)
        hTb = hT2[:, :]
        hoff = [0]

        def hcarve(n_bf):
            o = hoff[0]; hoff[0] = o + n_bf + (n_bf % 2)
            assert hoff[0] <= KC * XC
            return hTb[:, o:o + n_bf]
        q3 = [hcarve(1024) for _ in range(3)]; ko = [hcarve(1024) for _ in range(3)]
        kp = [hcarve(128), hcarve(512), hcarve(1024)]
        vo = [hcarve(8 * 128).rearrange("p (t e) -> p t e", e=128), hcarve(8 * 128).rearrange("p (t e) -> p t e", e=128), hcarve(16 * 128).rearrange("p (t e) -> p t e", e=128)]
        vpv = [hcarve(128).rearrange("p (t e) -> p t e", e=128), hcarve(4 * 128).rearrange("p (t e) -> p t e", e=128), hcarve(16 * 128).rearrange("p (t e) -> p t e", e=128)]
        et = carve(10 * 128).rearrange("p (k i) -> p k i", i=128)
        acc = carve(1024); den = carve(1024); accU = Unit(); denU = Unit()
        ldq = Unit(); ldk = Unit(); ldv = Unit(); lde = Unit()
        qDS, kDS, vDS, eDS = B.new_ds(), B.new_ds(), B.new_ds(), B.new_ds()

        def head(h_):
            for g in range(3):
                dil = DIL[g]; U = TP // dil
                row = (g * 16 + h_) * 128
                dma("sp", q3[g], qs_d.ap()[row:row + 128, :], qDS, R=[qsW], W=[ldq])
                dma("sp", ko[g], kvi.ap()[row:row + 128, :], kDS, R=[kvW], W=[ldk])
                orow = kvo_off(row * 1024) // 1024
                if g == 0:
                    dma("sp", kp[0], kvo.ap()[orow:orow + 128, 896:1024], kDS, R=[kvoU], W=[ldk])
                elif g == 1:
                    dma("sp", kp[1].rearrange("p (r u) -> p r u", r=4), kvo.ap()[orow:orow + 128, :].rearrange("p (r u) -> p r u", r=4)[:, :, 128:256], kDS, R=[kvoU], W=[ldk])
                else:
                    dma("sp", kp[2], kvo.ap()[orow:orow + 128, :], kDS, R=[kvoU], W=[ldk])
                vbase = VB + (g * 16 + h_) * 1024 * 128
                if g == 0:
                    dma("sp", vo[0], bass.AP(kvi, vbase, [[128, 128], [128 * 128, 8], [1, 128]]), vDS, R=[kvW], W=[ldv])
                    dma("sp", vpv[0], bass.AP(kvo, kvo_off(vbase) + 896 * 128, [[128, 128], [128 * 128, 1], [1, 128]]), vDS, R=[kvoU], W=[ldv])
                elif g == 1:
                    for r in range(4):
                        dma("sp", vo[1][:, r * 2:r * 2 + 2, :], bass.AP(kvi, vbase + r * 128, [[4 * 128, 128], [512 * 128, 2], [1, 128]]), vDS, R=[kvW], W=[ldv])
                    dma("sp", vpv[1], bass.AP(kvo, kvo_off(vbase) + 512 * 128, [[4 * 128, 128], [128, 4], [1, 128]]), vDS, R=[kvoU], W=[ldv])
                else:
                    dma("sp", vo[2][0:64, :, :], bass.AP(kvi, vbase, [[16 * 128, 64], [128, 16], [1, 128]]), vDS, R=[kvW], W=[ldv])
                    dma("sp", vpv[2][0:64, :, :], bass.AP(kvo, kvo_off(vbase), [[16 * 128, 64], [128, 16], [1, 128]]), vDS, R=[kvoU], W=[ldv])
                erow = (g * 16 + h_) * 3
                if g < 2:
                    for k_, v in enumerate((1, 0, 2, 0)):
                        dma("sp", et[:, g * 4 + k_, :], bass.AP(re_d, (erow + v) * 128 * 255 + 127, [[254, 128], [1, 128]]), eDS, R=[reU], W=[lde])
                else:
                    dma("sp", et[0:64, 8, 0:64], bass.AP(re_d, (erow + 2) * 128 * 255 + 191, [[254, 64], [1, 64]]), eDS, R=[reU], W=[lde])
                    dma("sp", et[0:64, 9, 0:64], bass.AP(re_d, (erow + 0) * 128 * 255 + 127, [[254, 64], [1, 64]]), eDS, R=[reU], W=[lde])
            units = []
            for g in range(3):
                dil = DIL[g]; U = TP // dil
                QB = 128 if g < 2 else 64
                for r in range(dil):
                    for qb in range(U // QB):
                        units.append((g, r, qb))

            def stageA(un):
                g, r, qb = un
                dil = DIL[g]; U = TP // dil
                QB = 128 if g < 2 else 64
                nqb = U // QB
                qcols = q3[g][:, r * U + qb * QB:r * U + (qb + 1) * QB]
                if qb > 0:
                    kT0 = ko[g][:, r * U + (qb - 1) * QB:r * U + qb * QB]
                    vt0 = vo[g][0:QB, (r * nqb + qb - 1), :]
                    eb = g * 4
                else:
                    if g == 0:
                        kT0 = kp[0][:, :]; vt0 = vpv[0][:, 0, :]
                    elif g == 1:
                        kT0 = kp[1][:, r * 128:(r + 1) * 128]; vt0 = vpv[1][:, r, :]
                    else:
                        kT0 = kp[2][:, r * 64:(r + 1) * 64]; vt0 = vpv[2][0:64, r, :]
                    eb = (g * 4 + 2) if g < 2 else 8
                kT1 = ko[g][:, r * U + qb * QB:r * U + (qb + 1) * QB]
                vt1 = vo[g][0:QB, (r * nqb + qb), :]
                p, u = psum()
                MM(p[0:QB, 0:QB], kT0, qcols, True, True, [ldk, ldq], [u])
                MM(p[0:QB, QB:2 * QB], kT1, qcols, True, True, [ldk, ldq], [u])
                e_, eu = tf()
                ACT(e_[0:QB, 0:2 * QB], p[0:QB, 0:2 * QB], AF.Exp, [u], [eu], scale=SCALE)
                pb, pbu = tb()
                DVE("tensor_tensor", [eu, lde], [pbu], out=pb[0:QB, 0:2 * QB].rearrange("p (k i) -> p k i", k=2), in0=e_[0:QB, 0:2 * QB].rearrange("p (k i) -> p k i", k=2),
                    in1=et[0:QB, eb:eb + 2, 0:QB], op=ALU.mult)
                return (un, pb, pbu, vt0, vt1)

            def stageB(sa):
                un, pb, pbu, vt0, vt1 = sa
                QB = 128 if un[0] < 2 else 64
                po, uo = psum(); pd, ud = psum()
                MM(po[:, 0:QB], vt0, pb[0:QB, 0:QB], True, False, [ldv, pbu], [uo])
                MM(po[:, 0:QB], vt1, pb[0:QB, QB:2 * QB], False, True, [ldv, pbu], [uo])
                MM(pd[:, 0:QB], onesb[0:QB, :], pb[0:QB, 0:QB], True, False, [pbu, cU], [ud])
                MM(pd[:, 0:QB], onesb[0:QB, :], pb[0:QB, QB:2 * QB], False, True, [pbu, cU], [ud])
                return (un, po, uo, pd, ud)

            def stageC(sb):
                (g, r, qb), po, uo, pd, ud = sb
                dil = DIL[g]
                QB = 128 if g < 2 else 64
                a_out = acc.rearrange("p (u r) -> p r u", r=dil)[:, r, qb * QB:(qb + 1) * QB]
                d_out = den.rearrange("p (u r) -> p r u", r=dil)[:, r, qb * QB:(qb + 1) * QB]
                if g == 0:
                    DVE("tensor_copy", [uo], [accU], out=a_out, in_=po[:, 0:QB])
                    ACT(d_out, pd[:, 0:QB], AF.Copy, [ud], [denU])
                else:
                    DVE("tensor_tensor", [uo, accU], [accU], out=a_out, in0=po[:, 0:QB], in1=a_out, op=ALU.add)
                    DVE("tensor_tensor", [ud, denU], [denU], out=d_out, in0=pd[:, 0:QB], in1=d_out, op=ALU.add)
            sa_prev, sb_prev = None, None
            for un in units + [None, None]:
                sa = stageA(un) if un is not None else None
                sb = stageB(sa_prev) if sa_prev is not None else None
                if sb_prev is not None:
                    stageC(sb_prev)
                sa_prev, sb_prev = sa, sb
            DVE("reciprocal", [denU], [denU], out=den[:, :], in_=den[:, :])
            DVE("tensor_tensor", [accU, denU], [mU[h_][0], mU[h_][1]], out=mid[:, h_, 0:TP], in0=acc[:, :], in1=den[:, :], op=ALU.mult)
        for h_ in range(16):
            head(h_)

        sample_attn(qs_f, ks_f, vs_f, smpU, hTb)
        out_proj_steps(w_o, 0, KC)
        run_steps()

    def col_ln(src, srcU, grow, brow, out, outW, scr, scrU):
        sq = scr[:, 0:32]; lo = scr[:, 32:64]; stt = scr[:, 64:70]
        DVE("tensor_copy", list(srcU), [scrU], out=sq[:, 0:16], in_=src)
        DVE("tensor_tensor", list(srcU), [scrU], out=sq[:, 16:32], in0=src, in1=src, op=ALU.mult)
        hb, hbu = tb(); lb_, lbu = tb()
        DVE("tensor_copy", [scrU], [hbu], out=hb[:, 0:32], in_=sq)
        DVE("tensor_tensor", [scrU, hbu], [scrU], out=lo, in0=sq, in1=hb[:, 0:32], op=ALU.subtract)
        DVE("tensor_copy", [scrU], [lbu], out=lb_[:, 0:32], in_=lo)
        p, u = psum()
        MM(p[:, 0:32], onesb[:, :], hb[:, 0:32], True, False, [hbu, cU], [u])
        MM(p[:, 0:32], onesb[:, :], lb_[:, 0:32], False, True, [lbu, cU], [u])
        s1 = stt[:, 0:1]; s2 = stt[:, 1:2]; mu = stt[:, 2:3]; var = stt[:, 3:4]; rs = stt[:, 4:5]; nb = stt[:, 5:6]
        DVE("tensor_reduce", [u], [scrU], out=s1, in_=p[:, 0:16], axis=AX.X, op=ALU.add)
        DVE("tensor_reduce", [u], [scrU], out=s2, in_=p[:, 16:32], axis=AX.X, op=ALU.add)
        DVE("tensor_scalar_mul", [scrU], [scrU], out=mu, in0=s1, scalar1=1.0 / D)
        DVE("tensor_tensor", [scrU], [scrU], out=var, in0=mu, in1=mu, op=ALU.mult)
        DVE("scalar_tensor_tensor", [scrU], [scrU], out=var, in0=s2, scalar=1.0 / D, in1=var, op0=ALU.mult, op1=ALU.subtract)
        ACT(rs, var, AF.Sqrt, [scrU, cU], [scrU], bias=epsT[:, 0:1], scale=1.0)
        DVE("reciprocal", [scrU], [scrU], out=rs, in_=rs)
        DVE("scalar_tensor_tensor", [scrU], [scrU], out=nb, in0=mu, scalar=-1.0, in1=rs, op0=ALU.mult, op1=ALU.mult)
        ACT(out, src, AF.Identity, list(srcU) + [scrU], list(outW), bias=nb, scale=rs)
        DVE("tensor_tensor", list(outW) + [cU], list(outW), out=out, in0=out, in1=vecT[:, :, grow], op=ALU.mult)
        DVE("tensor_tensor", list(outW) + [cU], list(outW), out=out, in0=out, in1=vecT[:, :, brow], op=ALU.add)

    def convmod(i):
        arena_reset()
        cxi = B.dram("cxi", [128, 480], F32); cxo = B.dram("cxo", [256, 480], F32)
        ccs = B.new_sem("cc_cv")
        w_in = c_w_in.ap(); w_out = c_w_out.ap()
        ztail = carve(480).rearrange("p (c t) -> p c t", t=30); ztU = Unit(); ztDS = B.new_ds()
        halo = carve(480).rearrange("p (c t) -> p c t", t=30); haU = Unit(); haDS = B.new_ds()
        zcs = carve(16 * 31).rearrange("p (c t) -> p c t", t=31); zsU = Unit()
        zh = carve(16 * 60 // 2, BF16).rearrange("p (c t) -> p c t", t=60); zhU = Unit()
        stg1 = carve(2048)
        yacc = [stg1[:, 0:1024], stg1[:, 1024:2048]]; yU = [Unit(), Unit()]
        scr = carve(80); scrU = Unit()
        ysm = carve(32); ysU = Unit()
        lnmu = carve(512); lnrs = carve(512)
        rmsnorm(V_GMIX + i)
        rows_to_fm([stg1, stg1], cst.ap(), 30, lambda kc: zcs[:, kc, 0:30], lambda kc: [zsU], single=True)

        def mk_in(c2):
            def comp(slot, wu):
                w3 = slot3(slot, KC, 512)
                for j in range(2):
                    c = c2 + j
                    for bi, (c0, n) in enumerate(BLKS):
                        pa, ua = psum(); pg, ug = psum()
                        mm_fm(pa, ua, w3, wu, KC, j, hT, lambda kc: [hU[bi]], c0, n)
                        mm_fm(pg, ug, w3, wu, KC, 2 + j, hT, lambda kc: [hU[bi]], c0, n)
                        s_, su = tf()
                        ACT(s_[:, :n], pg[:, :n], AF.Sigmoid, [ug, cU], [su], bias=vecT[:, c, V_CBIN + 1:V_CBIN + 2], scale=1.0)
                        if bi == 2:
                            DVE("scalar_tensor_tensor", [ua, su, cU], [zsU], out=zcs[:, c, 30:31], in0=pa[:, :1], scalar=vecT[:, c, V_CBIN:V_CBIN + 1], in1=s_[:, :1], op0=ALU.add, op1=ALU.mult)
                            continue
                        DVE("scalar_tensor_tensor", [ua, su, cU], [mU[c][bi]], out=mid[:, c, c0:c0 + n], in0=pa[:, :n], scalar=vecT[:, c, V_CBIN:V_CBIN + 1], in1=s_[:, :n], op0=ALU.add, op1=ALU.mult)
                        if bi == 1:
                            DVE("scalar_tensor_tensor", [ua, su, cU], [ztU], out=ztail[:, c, :], in0=pa[:, 482:512], scalar=vecT[:, c, V_CBIN:V_CBIN + 1], in1=s_[:, 482:512], op0=ALU.add, op1=ALU.mult)
            return comp
        for c2 in range(0, KC, 2):
            wstep([(lambda s: slot3(s, KC, 512)[:, :, 0:256], wsrc(w_in, 0, KC, c2 * 128, 256)),
                   (lambda s: slot3(s, KC, 512)[:, :, 256:512], wsrc(w_in, 0, KC, D + c2 * 128, 256))], mk_in(c2))
        run_steps()
        fm_to_rows([stg1, stg1], lambda kc: ztail[:, kc, :], lambda kc: [ztU], 30, cconv_p.ap(), single=True)
        fm_to_rows([stg1, stg1], lambda kc: zcs[:, kc, 1:31], lambda kc: [zsU], 30, cconv_s.ap(), single=True)
        cxU = Unit()
        dma("sp", cxi.ap(), ztail[:, :, :].rearrange("p c t -> p (c t)"), ztDS, R=[ztU], W=[cxU])
        B.cc(cxi.ap().opt(), cxo.ap().opt(), PAIRS_RUN, ccs, W=[cxU])
        dma("sp", halo[:, :, :].rearrange("p c t -> p (c t)"), cxo.ap()[0:128, :], haDS, R=[cxU], W=[haU])
        DVE("tensor_scalar_mul", [haU, cU], [haU], out=halo[:, :, :], in0=halo[:, :, :], scalar1=flg[:, 0:1])
        DVE("tensor_copy", [haU], [zhU], out=zh[:, :, 0:30], in_=halo[:, :, :])
        DVE("tensor_copy", [mU[c][0] for c in range(KC)], [zhU], out=zh[:, :, 30:60], in_=mid[:, :, 0:30])

        B.barrier()
        dg = stg1[:, 0:1984].bitcast(BF16).rearrange("p (k m) -> p k m", m=128); dgU = Unit()
        for c in range(KC):
            DVE("tensor_tensor", [cU], [dgU], out=dg, in0=ident[:, :].unsqueeze(1).to_broadcast([128, 31, 128]),
                in1=vecT[:, c, V_CWDW:V_CWDW + 31].unsqueeze(2).to_broadcast([128, 31, 128]), op=ALU.mult)
            zsrc = [mU[c][0], mU[c][1]]
            ph, uh = psum(); p0, u0 = psum(); p1, u1 = psum()
            for k in range(31):
                MM(ph[:, 0:30], dg[:, k, :], zh[:, c, k:k + 30], k == 0, k == 30, [dgU, zhU], [uh], signal=(k == 30))
            for k in range(31):
                MM(p0[:, 0:482], dg[:, k, :], mid[:, c, k:k + 482], k == 0, k == 30, [dgU] + zsrc, [u0], signal=(k == 30))
            for k in range(31):
                MM(p1[:, 0:512], dg[:, k, :], mid[:, c, 482 + k:482 + k + 512], k == 0, k == 30, [dgU] + zsrc, [u1], signal=(k == 30))
            bdw = vecT[:, c, V_CBDW:V_CBDW + 1]
            ACT(hT[:, c, 0:30], ph[:, 0:30], AF.Identity, [uh, cU], [hU[0]], bias=bdw, scale=1.0)
            ACT(hT[:, c, 30:512], p0[:, 0:482], AF.Identity, [u0, cU], [hU[0]], bias=bdw, scale=1.0)
            ACT(hT[:, c, 512:TP], p1[:, 0:512], AF.Identity, [u1, cU], [hU[1]], bias=bdw, scale=1.0)
        for bi, (c0, n) in enumerate(BLKS[:2]):
            p1, u1 = psum()
            for c in range(KC):
                MM(p1[:, :n], onesb[:, :], hT[:, c, c0:c0 + n], c == 0, c == KC - 1, [hU[bi], cU], [u1], signal=(c == KC - 1))
            p2, u2 = sumsq_fm(lambda kc: hT[:, kc, c0:c0 + n], lambda kc: [hU[bi]], KC, n)
            mu, muU, rs, rsU2 = lnmu, Unit(), lnrs, Unit()
            DVE("tensor_scalar_mul", [u1], [muU], out=mu[:, :n], in0=p1[:, :n], scalar1=1.0 / D)
            DVE("tensor_tensor", [muU], [rsU2], out=rs[:, :n], in0=mu[:, :n], in1=mu[:, :n], op=ALU.mult)
            DVE("scalar_tensor_tensor", [u2, rsU2], [rsU2], out=rs[:, :n], in0=p2[:, :n], scalar=1.0 / D, in1=rs[:, :n], op0=ALU.mult, op1=ALU.subtract)
            ACT(rs[:, :n], rs[:, :n], AF.Sqrt, [rsU2, cU], [rsU2], bias=epsT[:, 0:1], scale=1.0)
            DVE("reciprocal", [rsU2], [rsU2], out=rs[:, :n], in_=rs[:, :n])
            for c in range(KC):
                t_, tu_ = tf()
                DVE("tensor_tensor", [hU[bi], muU], [tu_], out=t_[:, :n], in0=hT[:, c, c0:c0 + n], in1=mu[:, :n], op=ALU.subtract)
                DVE("tensor_tensor", [tu_, rsU2], [tu_], out=t_[:, :n], in0=t_[:, :n], in1=rs[:, :n], op=ALU.mult)
                ACT(hT[:, c, c0:c0 + n], t_[:, :n], AF.Silu, [tu_, cU], [hU[bi]], bias=vecT[:, c, V_CLNB:V_CLNB + 1], scale=vecT[:, c, V_CLNG:V_CLNG + 1])
        B.barrier()
        prod31 = yacc[0][:, 0:16 * 31].rearrange("p (c t) -> p c t", t=31)
        DVE("tensor_tensor", [zsU, cU, yU[0]], [yU[0]], out=prod31, in0=zcs[:, :, :], in1=vecT[:, :, V_CWDW:V_CWDW + 31], op=ALU.mult)
        DVE("tensor_reduce", [yU[0]], [ysU], out=ysm[:, 0:16], in_=prod31, axis=AX.X, op=ALU.add)
        DVE("tensor_tensor", [ysU, cU], [ysU], out=ysm[:, 0:16], in0=ysm[:, 0:16], in1=vecT[:, :, V_CBDW], op=ALU.add)
        col_ln(ysm[:, 0:16], [ysU], V_CLNG, V_CLNB, ysm[:, 16:32], [ysU], scr, scrU)
        ACT(hT[:, :, TP], ysm[:, 16:32], AF.Silu, [ysU], [hU[2]])
        out_proj_steps(w_out, 0, KC, src=hT, srcU=lambda kc, bi: hU[bi])
        run_steps()

    for i in range(4):
        if on("mix%d" % i):
            if i % 3 == 0:
                gmlp(i, i // 3)
            elif i % 3 == 1:
                dilattn(i)
            else:
                convmod(i)
        if on("xat%d" % i):
            xattn(i)
        if on("ffn%d" % i):
            ffn(i)

    arena_reset()
    stg = [carve(2048), carve(2048)]
    for t in range(8):
        fm_to_rows(stg, lambda kc, t=t: xT[:, kc, t * 128:(t + 1) * 128], lambda kc, t=t: [xU[kc][t // 4]], 128, y_p.ap()[t * 128:(t + 1) * 128, :])
    if "xs" not in SKIP:
        fm_to_rows(stg, lambda kc: xT[:, kc, TP:TP + 1], lambda kc: [xU[kc][2]], 1, y_s.ap())

    sp = B.engs["sp"]
    for ds in B.dss:
        if ds.count:
            sp.prog.append(("wait", ds.sem, ds.count))
    B.emit()
    return B


def t5_bucket_np(dist):
    import math
    max_exact = 16
    d = np.maximum(dist, 1).astype(np.float32)
    large = max_exact + (np.log(d / max_exact) / math.log(2048 / max_exact) * (32 - max_exact)).astype(np.int32)
    large = np.minimum(large, 31)
    return np.where(dist < max_exact, dist, large)


_CACHE = {}


def kernel(**inp):
    if "B" not in _CACHE:
        _CACHE["B"] = build_program()
    B = _CACHE["B"]
    f = lambda a: np.ascontiguousarray(a, dtype=np.float32)
    vec_rows = [inp["g_mix"], inp["g_xattn"], inp["g_mem"], inp["g_ffn"], inp["a_ln_g"], inp["a_ln_b"],
                inp["c_b_in"].reshape(2, D), inp["c_b_dw"], inp["c_ln_g"], inp["c_ln_b"], inp["c_w_dw"][0]]
    vecs = f(np.concatenate([np.asarray(v).reshape(-1, D) for v in vec_rows], axis=0))
    assert vecs.shape[0] == NV
    gsm = f(np.concatenate([inp["x_q_norm"], inp["x_k_norm"], inp["b_q_norm"][0], inp["b_k_norm"][0]], axis=0).T)
    ident = np.eye(128, dtype=np.float32)
    masku = np.triu(np.ones((128, 128), np.float32))
    selg = np.zeros((33, 9, 255), np.float32)
    u = np.arange(255)
    for g, dil in enumerate((1, 4, 16)):
        cur = np.where(u >= 127, t5_bucket_np(np.maximum(u - 127, 0) * dil), 32)
        prev = np.where(u <= 127, t5_bucket_np((u + 1) * dil), 32)
        third = prev if g < 2 else cur
        for v, idx in enumerate((cur, prev, third)):
            selg[idx, g * 3 + v, u] = 1.0
    selg = selg.reshape(33, 9 * 255)
    sels = np.zeros((32, 3, 128), np.float32)
    jj = np.arange(128)
    for g, dil in enumerate((1, 4, 16)):
        sels[t5_bucket_np((128 - jj) * dil), g, jj] = 1.0
    sels = sels.reshape(32, 384)
    shared = dict(vecs=vecs, gsm=gsm, relb=f(inp["rel_bias"]), ident=ident, masku=masku, selg=selg, sels=sels,
                  a_w_s=f(inp["a_w_s"]), a_b_s=f(inp["a_b_s"]).reshape(2, 2048),
                  a_w_in=f(inp["a_w_in"]), a_w_out=f(inp["a_w_out"]), b_w_qkv=f(inp["b_w_qkv"][0]), b_w_out=f(inp["b_w_out"][0]),
                  c_w_in=f(inp["c_w_in"][0]), c_w_out=f(inp["c_w_out"][0]), x_w_q=f(inp["x_w_q"]), x_w_kv=f(inp["x_w_kv"]),
                  x_w_o=f(inp["x_w_o"]), f_w_in=f(inp["f_w_in"]), f_w_out=f(inp["f_w_out"]))
    in_maps = []
    for c in range(NCORES):
        b, half = c // 2, c % 2
        m = dict(shared)
        m["x_p"] = f(inp["x_prompt"][b, half * TP:(half + 1) * TP])
        m["x_s"] = f(inp["x_sample"][c])
        m["mem"] = f(inp["mem_prompt"][b])
        m["flag"] = np.full((128, 1), float(half), np.float32)
        caches = ((inp["cache_b_k_w128"], inp["cache_b_v_w128"]), (inp["cache_b_k_w512"], inp["cache_b_v_w512"]),
                  (inp["cache_b_k_w2048"], inp["cache_b_v_w2048"]))
        for g, w in enumerate((128, 512, 2048)):
            m["ck%d" % g] = f(caches[g][0][0, c]).reshape(w, D)
            m["cv%d" % g] = f(caches[g][1][0, c]).reshape(w, D)
        m["cst"] = f(inp["state_c_conv"][0, c])
        m["cmk"] = f(inp["cache_mem_k"][:, c]).reshape(4, 256, 512)
        m["cmv"] = f(inp["cache_mem_v"][:, c]).reshape(4, 256, 512)
        in_maps.append(m)
    nrun = int(os.environ.get("MK_NCORES", NCORES))
    in_maps = [{k: m[k] for k in B.used_inputs} for m in in_maps]
    res = run_bass_kernel_spmd(B.nc, in_maps[:nrun], core_ids=list(range(nrun)))
    R = list(res.results) + [res.results[c % nrun] for c in range(nrun, NCORES)]
    o = lambda c, k: np.asarray(R[c][k], dtype=np.float32)
    y_prompt = np.stack([np.concatenate([o(2 * b, "y_p"), o(2 * b + 1, "y_p")], axis=0) for b in range(4)])
    y_sample = np.stack([o(c, "y_s") for c in range(8)])
    outs = [y_prompt, y_sample]
    for g, n in enumerate((128, 512, 2048)):
        for kv in ("bk", "bv"):
            if g < 2:
                a = np.stack([o(2 * b + 1, "%s%d_p" % (kv, g)) for b in range(4)])
            else:
                a = np.stack([np.concatenate([o(2 * b, "%s2_p" % kv), o(2 * b + 1, "%s2_p" % kv)], axis=0) for b in range(4)])
            outs.append(a.reshape(1, 4, n, 16, 128))
    outs.append(np.stack([o(2 * b + 1, "cconv_p") for b in range(4)])[None])
    outs.append(np.stack([o(2 * b, "memk_p") for b in range(4)], axis=1).reshape(4, 4, 256, 4, 128))
    outs.append(np.stack([o(2 * b, "memv_p") for b in range(4)], axis=1).reshape(4, 4, 256, 4, 128))
    for g, w in enumerate((128, 512, 2048)):
        for kv in ("bk", "bv"):
            outs.append(np.stack([o(c, "%s%d_s" % (kv, g)) for c in range(8)]).reshape(1, 8, w, 16, 128))
    outs.append(np.stack([o(c, "cconv_s") for c in range(8)])[None])
    outs.append(np.stack([o(c, "av_s") for c in range(8)], axis=1).reshape(2, 8, 1, D))
    return tuple(outs)
```

```python
import os
import numpy as np
from contextlib import ExitStack
import concourse.bass as bass
import concourse.mybir as mybir
from concourse.bass_utils import run_bass_kernel_spmd

F32 = mybir.dt.float32
BF16 = mybir.dt.bfloat16
AF = mybir.ActivationFunctionType
ALU = mybir.AluOpType
AX = mybir.AxisListType

D = 2048
KC = 16
TP = 1024
XC = TP + 1
NCORES = 8
FFN_H = 5632
EPS = 1e-6
SCALE = 128 ** -0.5
NSLOT = 2
PAIRS = [[0, 1], [2, 3], [4, 5], [6, 7]]
PAIRS_RUN = PAIRS[:int(os.environ.get("MK_NCORES", 8)) // 2]

V_GMIX, V_GXAT, V_GMEM, V_GFFN = 0, 4, 8, 12
V_ALNG, V_ALNB = 16, 18
V_CBIN, V_CBDW, V_CLNG, V_CLNB, V_CWDW = 20, 22, 23, 24, 25
NV = 56
S_XQ, S_XK, S_BQ, S_BK = 0, 4, 8, 11

BLKS = [(0, 512), (512, 512), (1024, 1)]

PLAN = os.environ.get("MK_PLAN", "")
SKIP = os.environ.get("MK_SKIP", "").split(",")


class Unit:
    __slots__ = ("w", "rs", "excl")

    def __init__(self, excl=False):
        self.w = None
        self.rs = {}
        self.excl = excl


class Eng:
    def __init__(self, name, sem):
        self.name = name
        self.sem = sem
        self.n = 0
        self.seen = {}
        self.prog = []


class DS:
    def __init__(self, sem):
        self.sem = sem
        self.count = 0


class Builder:
    def __init__(self):
        self.nc = bass.Bass("TRN2", target_bir_lowering=False)
        self.es = ExitStack()
        self.sems = []
        self.engs = {}
        self.dss = []
        self.nsb = 0

    def new_sem(self, name):
        s = self.es.enter_context(self.nc.semaphore(name))
        self.sems.append(s)
        return len(self.sems) - 1

    def new_ds(self):
        ds = DS(self.new_sem("d%d" % len(self.dss)))
        self.dss.append(ds)
        return ds

    def sb(self, shape, dt, name=None):
        self.nsb += 1
        return self.es.enter_context(self.nc.sbuf_tensor("s_" + (name or ("sb%d" % self.nsb)), list(shape), dt))

    def dram(self, name, shape, dt, kind=None):
        if kind is None:
            return self.nc.dram_tensor(name, list(shape), dt)
        return self.nc.dram_tensor(name, list(shape), dt, kind=kind)

    def _waits(self, eng, R, W):
        need = {}
        for u in R:
            if u.w is not None and need.get(u.w[0], 0) < u.w[1]:
                need[u.w[0]] = u.w[1]
            if u.excl:
                for s, v in u.rs.items():
                    if s != eng.sem and need.get(s, 0) < v:
                        need[s] = v
        for u in W:
            if u.w is not None and need.get(u.w[0], 0) < u.w[1]:
                need[u.w[0]] = u.w[1]
            for s, v in u.rs.items():
                if need.get(s, 0) < v:
                    need[s] = v
        for s, v in need.items():
            if eng.name == "pe" and s == eng.sem:
                continue
            if eng.seen.get(s, 0) < v:
                eng.prog.append(("wait", s, v))
                eng.seen[s] = v

    def _mark(self, tok, R, W):
        for u in R:
            if u.rs.get(tok[0], 0) < tok[1]:
                u.rs[tok[0]] = tok[1]
        for u in W:
            u.w = tok
            u.rs = {}

    def op(self, eng, meth, R=(), W=(), signal=True, **kw):
        eng = self.engs[eng]
        self._waits(eng, R, W)
        if signal:
            eng.n += 1
            tok = (eng.sem, eng.n)
            eng.prog.append(("ins", meth, kw, eng.sem))
        else:
            tok = (eng.sem, eng.n + 1)
            eng.prog.append(("ins", meth, kw, None))
        self._mark(tok, R, W)

    def dma(self, q, out, in_, ds, R=(), W=(), slow=False):
        eng = self.engs[q]
        self._waits(eng, R, W)
        ds.count += 16
        tok = (ds.sem, ds.count)
        eng.prog.append(("dma", out, in_, ds.sem, slow))
        self._mark(tok, R, W)

    def cc(self, ins, outs, groups, sem, R=(), W=(), count=1):
        eng = self.engs["pool"]
        self._waits(eng, R, W)
        eng.prog.append(("cc", ins, outs, groups, sem))
        self._mark((sem, count), R, W)

    def barrier(self):
        sp = self.engs["sp"]
        for ds in self.dss:
            if ds.count and sp.seen.get(ds.sem, 0) < ds.count:
                sp.prog.append(("wait", ds.sem, ds.count))
                sp.seen[ds.sem] = ds.count
        for e in self.engs.values():
            for x in self.engs.values():
                if x is e or x.n == 0:
                    continue
                if e.seen.get(x.sem, 0) < x.n:
                    e.prog.append(("wait", x.sem, x.n))
                    e.seen[x.sem] = x.n
        sp.n += 1
        sp.prog.append(("seminc", sp.sem))
        for e in self.engs.values():
            if e is not sp:
                e.prog.append(("wait", sp.sem, sp.n))
                e.seen[sp.sem] = sp.n

    def emit(self):
        nc = self.nc
        sems = self.sems
        with nc.Block() as block:
            def run(eng):
                def body(h):
                    for it in eng.prog:
                        if it[0] == "wait":
                            h.wait_ge(sems[it[1]], it[2])
                        elif it[0] == "ins":
                            ins = getattr(h, it[1])(**it[2])
                            if it[3] is not None:
                                ins.then_inc(sems[it[3]], 1)
                        elif it[0] == "dma":
                            if it[4]:
                                h.dma_start(out=it[1], in_=it[2], allow_slow_non_contiguous=True).then_inc(sems[it[3]], 16)
                            else:
                                h.dma_start(out=it[1], in_=it[2]).then_inc(sems[it[3]], 16)
                        elif it[0] == "cc":
                            h.collective_compute("AllGather", ALU.bypass, replica_groups=it[3], ins=[it[1]], outs=[it[2]]).then_inc(sems[it[4]])
                        elif it[0] == "seminc":
                            h.sem_inc(sems[it[1]], 1)
                        elif it[0] == "raw":
                            it[1](h)
                return body
            block.sync(run(self.engs["sp"]))
            block.scalar(run(self.engs["act"]))
            block.vector(run(self.engs["dve"]))
            block.tensor(run(self.engs["pe"]))
            block.gpsimd(run(self.engs["pool"]))


def build_program():
    B = Builder()
    nc = B.nc
    for name in ("pe", "act", "dve", "pool", "sp"):
        B.engs[name] = Eng(name, B.new_sem("e_" + name))
    plan = [s for s in PLAN.split(",") if s]

    def on(tag):
        return (not plan) or (tag in plan)

    class LazyIn:
        def __init__(self, name, shape):
            self.name, self.shape, self.t = name, shape, None

        def handle(self):
            if self.t is None:
                self.t = B.dram(self.name, self.shape, F32, kind="ExternalInput")
                B.used_inputs.append(self.name)
            return self.t

        def ap(self):
            return self.handle().ap()

    B.used_inputs = []

    def din(name, shape):
        return LazyIn(name, shape)

    def dout(name, shape):
        return B.dram(name, shape, F32, kind="ExternalOutput")

    x_p = din("x_p", [TP, D]); x_s = din("x_s", [1, D]); mem = din("mem", [256, D])
    vecs = din("vecs", [NV, D]); gsm = din("gsm", [128, 14]); relb = din("relb", [32, 48])
    flag = din("flag", [128, 1]); ident_d = din("ident", [128, 128]); masku_d = din("masku", [128, 128])
    selg = din("selg", [33, 9 * 255]); sels_d = din("sels", [32, 3 * 128])
    a_w_s = din("a_w_s", [2, 16, 128, 128]); a_b_s = din("a_b_s", [2, 16 * 128])
    ck = [din("ck%d" % g, [w, D]) for g, w in enumerate((128, 512, 2048))]
    cv = [din("cv%d" % g, [w, D]) for g, w in enumerate((128, 512, 2048))]
    cst = din("cst", [30, D]); cmk = din("cmk", [4, 256, 512]); cmv = din("cmv", [4, 256, 512])
    a_w_in = din("a_w_in", [2, D, 4096]); a_w_out = din("a_w_out", [2, D, D])
    b_w_qkv = din("b_w_qkv", [D, 18432]); b_w_out = din("b_w_out", [D, D])
    c_w_in = din("c_w_in", [D, 4096]); c_w_out = din("c_w_out", [D, D])
    x_w_q = din("x_w_q", [4, D, 512]); x_w_kv = din("x_w_kv", [4, D, 1024]); x_w_o = din("x_w_o", [4, 512, D])
    f_w_in = din("f_w_in", [4, D, 2 * FFN_H]); f_w_out = din("f_w_out", [4, FFN_H, D])

    y_p = dout("y_p", [TP, D]); y_s = dout("y_s", [1, D])
    bk_p = [dout("bk%d_p" % g, [n, D]) for g, n in enumerate((128, 512, 1024))]
    bv_p = [dout("bv%d_p" % g, [n, D]) for g, n in enumerate((128, 512, 1024))]
    cconv_p = dout("cconv_p", [30, D]); memk_p = dout("memk_p", [4, 256, 512]); memv_p = dout("memv_p", [4, 256, 512])
    bk_s = [dout("bk%d_s" % g, [w, D]) for g, w in enumerate((128, 512, 2048))]
    bv_s = [dout("bv%d_s" % g, [w, D]) for g, w in enumerate((128, 512, 2048))]
    cconv_s = dout("cconv_s", [30, D]); av_s = dout("av_s", [2, D])

    xT = B.sb([128, KC, XC], F32, "xT"); xU = [[Unit() for _ in BLKS] for _ in range(KC)]
    hT2 = B.sb([128, KC * XC], BF16, "hT"); hU = [Unit() for _ in BLKS]
    hT = hT2[:, :].rearrange("p (k t) -> p k t", t=XC)
    mid = B.sb([128, KC, XC], BF16, "mid"); mU = [[Unit() for _ in BLKS] for _ in range(KC)]
    wsl = [B.sb([128, 8192], BF16, "w%d" % i) for i in range(NSLOT)]
    wU = [Unit() for _ in range(NSLOT)]; wDS = [B.new_ds() for _ in range(NSLOT)]
    ident = B.sb([128, 128], F32, "ident"); onesb = B.sb([128, 128], BF16, "onesb")
    masku = B.sb([128, 128], F32, "masku")
    vecT = B.sb([128, KC, NV], F32, "vecT"); gs = B.sb([128, 14], F32, "gs")
    flg = B.sb([128, 1], F32, "flg"); epsT = B.sb([128, 1], F32, "epsT")
    cU = Unit()
    memhat = B.sb([128, KC, 256], BF16, "memhat"); mhU = Unit()
    ps = [B.es.enter_context(nc.psum_tensor("ps%d" % i, [128, 512], F32)) for i in range(8)]
    pU = [Unit(excl=True) for _ in range(8)]
    AW = int(os.environ.get("MK_AW", 8850))
    arena = B.sb([128, AW], F32, "arena")
    NT = 4
    tmpf = [arena[:, i * 512:(i + 1) * 512] for i in range(NT)]; tU = [Unit() for _ in range(NT)]
    tmpb = [arena[:, NT * 512 + i * 256:NT * 512 + (i + 1) * 256].bitcast(BF16) for i in range(NT)]; bU = [Unit() for _ in range(NT)]
    A0 = NT * 768
    st = {"ps": 0, "tf": 0, "tb": 0, "aoff": 0, "stg": 0}

    def psum():
        i = st["ps"]; st["ps"] = (i + 1) % 8
        return ps[i], pU[i]

    def tf():
        i = st["tf"]; st["tf"] = (i + 1) % NT
        return tmpf[i], tU[i]

    def tb():
        i = st["tb"]; st["tb"] = (i + 1) % NT
        return tmpb[i], bU[i]

    def arena_reset():
        B.barrier()
        st["aoff"] = A0

    def carve(words, dt=F32):
        o = st["aoff"]; st["aoff"] = o + words
        assert st["aoff"] <= AW, ("arena overflow", st["aoff"])
        a = arena[:, o:o + words]
        return a if dt == F32 else a.bitcast(dt)

    op, dma = B.op, B.dma

    def MM(out, lhsT, rhs, start, stop, R, W, signal=True):
        op("pe", "matmul", R=R, W=W, signal=signal, out=out, lhsT=lhsT, rhs=rhs, start=start, stop=stop)

    def TR(out, in_, idn, R, W, signal=True):
        op("pe", "transpose", R=R, W=W, signal=signal, out=out, in_=in_, identity=idn)

    def ACT(out, in_, func, R, W, **kw):
        op("act", "activation", R=R, W=W, out=out, in_=in_, func=func, **kw)

    def DVE(meth, R, W, **kw):
        op("dve", meth, R=R, W=W, **kw)

    def COPY(eng, out, in_, R, W):
        if eng == "act":
            ACT(out, in_, AF.Copy, R, W)
        else:
            DVE("tensor_copy", R, W, out=out, in_=in_)

    cds = B.new_ds()
    dma("sp", ident[:], ident_d.ap(), cds, W=[cU])
    dma("sp", masku[:], masku_d.ap(), cds, W=[cU])
    dma("sp", gs[:], gsm.ap(), cds, W=[cU])
    dma("sp", flg[:], flag.ap(), cds, W=[cU])
    DVE("memset", [], [cU], ap=onesb[:], constant=1.0)
    DVE("memset", [], [cU], ap=epsT[:], constant=EPS)

    ldU = [Unit(), Unit()]; ldDS = [B.new_ds(), B.new_ds()]

    def next_stg():
        i = st["stg"]; st["stg"] ^= 1
        return i

    def rows_to_fm(stg, src_ap, R, dst_fn, dstW, single=False):
        i = 0 if single else next_stg()
        s = stg[i]
        dma("sp", s[0:R, :], src_ap, ldDS[i], W=[ldU[i]])
        for g4 in range(4):
            p, u = psum()
            for j in range(4):
                kc = g4 * 4 + j
                TR(p[:, j * 128:j * 128 + R], s[0:R, kc * 128:(kc + 1) * 128], ident[0:R, 0:R], [ldU[i], cU], [u], signal=(j == 3))
            for j in range(4):
                kc = g4 * 4 + j
                COPY("act" if g4 % 2 else "dve", dst_fn(kc), p[:, j * 128:j * 128 + R], [u], dstW(kc))

    def fm_to_rows(stg, src_fn, srcR, R, dst_ap, single=False):
        i = 0 if single else next_stg()
        s = stg[i]
        for g4 in range(4):
            p, u = psum()
            for j in range(4):
                kc = g4 * 4 + j
                TR(p[0:R, j * 128:(j + 1) * 128], src_fn(kc), ident[:, :], list(srcR(kc)) + [cU], [u], signal=(j == 3))
            COPY("act" if g4 % 2 else "dve", s[0:R, g4 * 512:(g4 + 1) * 512], p[0:R, :], [u], [ldU[i]])
        dma("sp", dst_ap, s[0:R, :], ldDS[i], R=[ldU[i]])

    arena_reset()
    stg = [carve(2048), carve(2048)]
    if "vecs" not in SKIP:
        rows_to_fm(stg, vecs.ap(), NV, lambda kc: vecT[:, kc, :], lambda kc: [cU])
    for t in range(8):
        rows_to_fm(stg, x_p.ap()[t * 128:(t + 1) * 128, :], 128,
                   lambda kc, t=t: xT[:, kc, t * 128:(t + 1) * 128], lambda kc, t=t: [xU[kc][t // 4]])
    if "xs" not in SKIP:
        rows_to_fm(stg, x_s.ap(), 1, lambda kc: xT[:, kc, TP:TP + 1], lambda kc: [xU[kc][2]])

    def rstd_from_psum(p, u, n, inv_n):
        r, ru = tf()
        ACT(r[:, :n], p[:, :n], AF.Sqrt, [u, cU], [ru], bias=epsT[:, 0:1], scale=inv_n)
        DVE("reciprocal", [ru], [ru], out=r[:, :n], in_=r[:, :n])
        return r, ru

    def sumsq_fm(src_fn, srcU, nk, n):
        p, u = psum()
        for kc in range(nk):
            s, su = tb()
            ACT(s[:, :n], src_fn(kc), AF.Square, list(srcU(kc)), [su])
            MM(p[:, :n], onesb[:, :], s[:, :n], kc == 0, kc == nk - 1, [su, cU], [u])
        return p, u

    def rmsnorm(vrow):
        for bi, (c0, n) in enumerate(BLKS):
            p, u = sumsq_fm(lambda kc: xT[:, kc, c0:c0 + n], lambda kc: [xU[kc][bi]], KC, n)
            r, ru = rstd_from_psum(p, u, n, 1.0 / D)
            for kc in range(KC):
                DVE("scalar_tensor_tensor", [xU[kc][bi], ru, cU], [hU[bi]], out=hT[:, kc, c0:c0 + n], in0=xT[:, kc, c0:c0 + n],
                    scalar=vecT[:, kc, vrow:vrow + 1], in1=r[:, :n], op0=ALU.mult, op1=ALU.mult)

    def load_memhat():
        mT = carve(KC * 64).rearrange("p (kc t) -> p kc t", t=64); mTU = [Unit() for _ in range(KC)]
        for t in range(4):
            rows_to_fm(stg, mem.ap()[t * 64:(t + 1) * 64, :], 64, lambda kc: mT[:, kc, :], lambda kc: [mTU[kc]])
            p, u = sumsq_fm(lambda kc: mT[:, kc, :], lambda kc: [mTU[kc]], KC, 64)
            r, ru = rstd_from_psum(p, u, 64, 1.0 / D)
            for kc in range(KC):
                DVE("tensor_tensor", [mTU[kc], ru], [mhU], out=memhat[:, kc, t * 64:(t + 1) * 64], in0=mT[:, kc, :], in1=r[:, :64], op=ALU.mult)
    if "mem" not in SKIP:
        load_memhat()

    steps = []

    def wstep(loads, compute, post_issue=None):
        steps.append((loads, compute, post_issue))

    def wsrc(w_ap, k0, nk, c0, ncol):
        return w_ap[k0 * 128:(k0 + nk) * 128, c0:c0 + ncol].rearrange("(kc p) n -> p kc n", p=128)

    def slot3(slot, nk, ncol):
        return slot[:, 0:nk * ncol].rearrange("p (kc n) -> p kc n", n=ncol)

    def run_steps():
        issued = 0
        for k in range(len(steps)):
            while issued < min(len(steps), k + NSLOT):
                si = issued % NSLOT
                for dst_fn, src in steps[issued][0]:
                    dma("pool", dst_fn(wsl[si]), src, wDS[si], W=[wU[si]])
                if steps[issued][2] is not None:
                    steps[issued][2]()
                issued += 1
            steps[k][1](wsl[k % NSLOT], wU[k % NSLOT])
        steps.clear()

    def mm_fm(p, u, w3, wu, nk, oc, in_t, inU, c0, n):
        for kc in range(nk):
            MM(p[:, :n], w3[:, kc, oc * 128:(oc + 1) * 128], in_t[:, kc, c0:c0 + n], kc == 0, kc == nk - 1, [wu] + list(inU(kc)), [u], signal=(kc == nk - 1))

    def resid_add(p, u, oc, bi, c0, n):
        DVE("tensor_tensor", [u], [xU[oc][bi]], out=xT[:, oc, c0:c0 + n], in0=p[:, :n], in1=xT[:, oc, c0:c0 + n], op=ALU.add)

    def out_proj_steps(w_ap, k0, nk, src=None, srcU=None):
        src = mid if src is None else src
        srcU = (lambda kc, bi: mU[kc][bi]) if srcU is None else srcU

        def mk(cb):
            def comp(slot, wu):
                w3 = slot3(slot, nk, 512)
                for o4 in range(4):
                    for bi, (c0, n) in enumerate(BLKS):
                        p, u = psum()
                        mm_fm(p, u, w3, wu, nk, o4, src, lambda kc: [srcU(kc, bi)], c0, n)
                        resid_add(p, u, cb * 4 + o4, bi, c0, n)
            return comp
        for cb in range(4):
            wstep([(lambda s: slot3(s, nk, 512), wsrc(w_ap, k0, nk, cb * 512, 512))], mk(cb))

    def ffn(i):
        rmsnorm(V_GFFN + i)
        w_in = f_w_in.ap()[i]; w_out = f_w_out.ap()[i]

        def mk_in(c2, c_lo):
            def comp(slot, wu):
                w3 = slot3(slot, KC, 512)
                for j in range(2):
                    lc = c2 + j - c_lo
                    for bi, (c0, n) in enumerate(BLKS):
                        pg, ug = psum(); pu, uu = psum()
                        mm_fm(pg, ug, w3, wu, KC, j, hT, lambda kc: [hU[bi]], c0, n)
                        mm_fm(pu, uu, w3, wu, KC, 2 + j, hT, lambda kc: [hU[bi]], c0, n)
                        s, su = tf()
                        ACT(s[:, :n], pg[:, :n], AF.Silu, [ug], [su])
                        DVE("tensor_tensor", [uu, su], [mU[lc][bi]], out=mid[:, lc, c0:c0 + n], in0=pu[:, :n], in1=s[:, :n], op=ALU.mult)
            return comp
        for c_lo, c_hi in ((0, 16), (16, 32), (32, 44)):
            for c2 in range(c_lo, c_hi, 2):
                wstep([(lambda s: slot3(s, KC, 512)[:, :, 0:256], wsrc(w_in, 0, KC, c2 * 128, 256)),
                       (lambda s: slot3(s, KC, 512)[:, :, 256:512], wsrc(w_in, 0, KC, FFN_H + c2 * 128, 256))], mk_in(c2, c_lo))
            out_proj_steps(w_out, c_lo, c_hi - c_lo)
        run_steps()

    def head_norm(p, u, n, gcol, out_bf, outW, out_f32=None, out32W=()):
        q, qu = tf()
        ACT(q[:, :n], p[:, :n], AF.Copy, [u], [qu])
        s, su = tb()
        DVE("tensor_tensor", [qu], [su], out=s[:, :n], in0=q[:, :n], in1=q[:, :n], op=ALU.mult)
        p2, u2 = psum()
        MM(p2[:, :n], onesb[:, :], s[:, :n], True, True, [su, cU], [u2])
        r, ru = rstd_from_psum(p2, u2, n, 1.0 / 128)
        if out_f32 is not None:
            DVE("scalar_tensor_tensor", [qu, ru, cU], list(out32W), out=out_f32, in0=q[:, :n], scalar=gs[:, gcol:gcol + 1], in1=r[:, :n],
                op0=ALU.mult, op1=ALU.mult)
            ACT(out_bf, out_f32, AF.Copy, list(out32W), list(outW))
        else:
            DVE("scalar_tensor_tensor", [qu, ru, cU], list(outW), out=out_bf, in0=q[:, :n], scalar=gs[:, gcol:gcol + 1], in1=r[:, :n],
                op0=ALU.mult, op1=ALU.mult)

    def xattn(i):
        arena_reset()
        def qT(hh, c0, n):
            return mid[:, 4 + hh, c0:c0 + n]
        kTp = mid[:, 8, 0:1024].rearrange("p (h t) -> p h t", t=256); kpU = [mU[8][0], mU[8][1]]
        kTs = mid[:, 9, 0:1024].rearrange("p (h t) -> p h t", t=256); ksU = [mU[9][0], mU[9][1]]
        vp = mid[:, 10, 0:1024].rearrange("p (m c) -> p m c", c=512); vpU = [mU[10][0], mU[10][1]]
        vs = mid[:, 11, 0:1024].rearrange("p (m c) -> p m c", c=512); vsU = [mU[11][0], mU[11][1]]
        kst = carve(1024).rearrange("p (m c) -> p m c", c=512); kstU = Unit(); kstDS = B.new_ds()
        vsDS = B.new_ds()
        kf = carve(4 * 256).rearrange("p (h t) -> p h t", t=256); kfU = [Unit() for _ in range(4)]
        ost = [carve(512), carve(512)]; ostU = [Unit(), Unit()]; ostDS = [B.new_ds(), B.new_ds()]
        oi = [0]

        def next_o():
            oi[0] ^= 1
            return oi[0]

        dma("sp", kst[:, :, :], cmk.ap()[i].rearrange("(m p) c -> p m c", p=128), kstDS, W=[kstU])
        for hh in range(4):
            p, u = psum()
            for m in range(2):
                TR(p[:, m * 128:(m + 1) * 128], kst[:, m, hh * 128:(hh + 1) * 128], ident[:, :], [kstU, cU], [u], signal=(m == 1))
            ACT(kTs[:, hh, :], p[:, 0:256], AF.Copy, [u], ksU)
        dma("pool", vs[:, :, :], cmv.ap()[i].rearrange("(m p) c -> p m c", p=128), vsDS, W=vsU)

        rmsnorm(V_GXAT + i)

        def comp_q(slot, wu):
            w3 = slot3(slot, KC, 512)
            for hh in range(4):
                for bi, (c0, n) in enumerate(BLKS):
                    p, u = psum()
                    mm_fm(p, u, w3, wu, KC, hh, hT, lambda kc: [hU[bi]], c0, n)
                    head_norm(p, u, n, S_XQ + i, qT(hh, c0, n), [mU[4 + hh][bi]])
        wstep([(lambda s: slot3(s, KC, 512), wsrc(x_w_q.ap()[i], 0, KC, 0, 512))], comp_q)

        def fold_gain(w3, wu):
            g = vecT[:, :, V_GMEM + i:V_GMEM + i + 1].to_broadcast([128, KC, 512])
            DVE("tensor_tensor", [wu, cU], [wu], out=w3, in0=w3, in1=g, op=ALU.mult)

        def comp_k(slot, wu):
            w3 = slot3(slot, KC, 512)
            fold_gain(w3, wu)
            for hh in range(4):
                p, u = psum()
                mm_fm(p, u, w3, wu, KC, hh, memhat, lambda kc: [mhU], 0, 256)
                head_norm(p, u, 256, S_XK + i, kTp[:, hh, :], kpU, out_f32=kf[:, hh, :], out32W=[kfU[hh]])
            for m in range(2):
                si = next_o()
                p, u = psum()
                for hh in range(4):
                    TR(p[:, hh * 128:(hh + 1) * 128], kf[:, hh, m * 128:(m + 1) * 128], ident[:, :], [kfU[hh], cU], [u], signal=(hh == 3))
                DVE("tensor_copy", [u], [ostU[si]], out=ost[si][:, :], in_=p[:, :])
                dma("sp", memk_p.ap()[i, m * 128:(m + 1) * 128, :], ost[si][:, :], ostDS[si], R=[ostU[si]])
        wstep([(lambda s: slot3(s, KC, 512), wsrc(x_w_kv.ap()[i], 0, KC, 0, 512))], comp_k)

        def comp_v(slot, wu):
            w3 = slot3(slot, KC, 512)
            fold_gain(w3, wu)
            for m in range(2):
                si = next_o()
                p, u = psum()
                for kc in range(KC):
                    MM(p[:, :], memhat[:, kc, m * 128:(m + 1) * 128], w3[:, kc, :], kc == 0, kc == KC - 1, [wu, mhU], [u], signal=(kc == KC - 1))
                DVE("tensor_copy", [u], [ostU[si]], out=ost[si][:, :], in_=p[:, :])
                ACT(vp[:, m, :], ost[si][:, :], AF.Copy, [ostU[si]], vpU)
                dma("sp", memv_p.ap()[i, m * 128:(m + 1) * 128, :], ost[si][:, :], ostDS[si], R=[ostU[si]])
        wstep([(lambda s: slot3(s, KC, 512), wsrc(x_w_kv.ap()[i], 0, KC, 512, 512))], comp_v)

        def comp_o(slot, wu):
            for bi, (c0, n) in enumerate(BLKS):
                kT, kU, vv, vU = (kTs, ksU, vs, vsU) if bi == 2 else (kTp, kpU, vp, vpU)
                for hh in range(4):
                    po, uo = psum(); pd, ud = psum()
                    for m in range(2):
                        p, u = psum()
                        MM(p[:, :n], kT[:, hh, m * 128:(m + 1) * 128], qT(hh, c0, n), True, True, kU + [mU[4 + hh][bi]], [u])
                        e, eu = tb()
                        ACT(e[:, :n], p[:, :n], AF.Exp, [u], [eu], scale=SCALE)
                        MM(po[:, :n], vv[:, m, hh * 128:(hh + 1) * 128], e[:, :n], m == 0, m == 1, vU + [eu], [uo])
                        MM(pd[:, :n], onesb[:, :], e[:, :n], m == 0, m == 1, [eu, cU], [ud])
                    r, ru = tf()
                    DVE("reciprocal", [ud], [ru], out=r[:, :n], in_=pd[:, :n])
                    DVE("tensor_tensor", [uo, ru], [mU[hh][bi]], out=mid[:, hh, c0:c0 + n], in0=po[:, :n], in1=r[:, :n], op=ALU.mult)
            w3 = slot[:, 0:4 * D].rearrange("p (kc n) -> p kc n", n=D)
            for oc in range(KC):
                for bi, (c0, n) in enumerate(BLKS):
                    p, u = psum()
                    mm_fm(p, u, w3, wu, 4, oc, mid, lambda kc: [mU[kc][bi]], c0, n)
                    resid_add(p, u, oc, bi, c0, n)
        wstep([(lambda s: s[:, 0:4 * D].rearrange("p (kc n) -> p kc n", n=D), wsrc(x_w_o.ap()[i], 0, 4, 0, D))], comp_o)
        run_steps()

    def gmlp(i, j):
        arena_reset()
        w_in = a_w_in.ap()[j]; w_out = a_w_out.ap()[j]
        NTL = 2
        gv = carve(NTL * D // 2, BF16).rearrange("p (t c) -> p t c", c=D); gvU = [Unit() for _ in range(NTL)]
        wsT = carve(16 * 128 // 2, BF16).rearrange("p (g q) -> p g q", q=128); wsU = Unit()
        Cb = carve(16 * 128).rearrange("p (g q) -> p g q", q=128); CU = Unit(); CDS = B.new_ds()
        wst = carve(512).rearrange("p (g q) -> p g q", q=128); wstU = Unit(); wstDS = B.new_ds()
        sm = carve(64); smU = Unit(); smDS = B.new_ds()
        ws00, bs0, gvs, vln = sm[:, 0:16], sm[:, 16:32], sm[:, 32:48], sm[:, 48:64]
        stat = carve(16); statU = Unit()
        rmsnorm(V_GMIX + i)
        dma("sp", Cb[:, :, :], a_b_s.ap()[j].partition_broadcast(128).rearrange("p (g q) -> p g q", q=128), CDS, W=[CU])
        dma("sp", ws00, bass.AP(a_w_s.handle(), j * 16 * 16384, [[0, 128], [16384, 16]]), smDS, W=[smU], slow=True)
        dma("sp", bs0, bass.AP(a_b_s.handle(), j * 2048, [[0, 128], [128, 16]]), smDS, W=[smU], slow=True)
        for g4 in range(4):
            dma("sp", wst[:, :, :], a_w_s.ap()[j, g4 * 4:(g4 + 1) * 4].rearrange("g p q -> p g q"), wstDS, W=[wstU])
            p, u = psum()
            for k in range(4):
                TR(p[:, k * 128:(k + 1) * 128], wst[:, k, :], ident[:, :], [wstU, cU], [u], signal=(k == 3))
            for k in range(4):
                g = g4 * 4 + k
                DVE("tensor_tensor", [u, cU], [wsU], out=wsT[:, g, :], in0=p[:, k * 128:(k + 1) * 128], in1=masku[:, :], op=ALU.mult)
        for g4 in range(4):
            p, u = psum()
            for k in range(4):
                g = g4 * 4 + k
                MM(p[:, k * 128:(k + 1) * 128], onesb[:, :], wsT[:, g, :], True, True, [wsU, cU], [u], signal=(k == 3))
            for k in range(4):
                g = g4 * 4 + k
                DVE("scalar_tensor_tensor", [u, CU, cU], [CU], out=Cb[:, g, :], in0=p[:, k * 128:(k + 1) * 128], scalar=vecT[:, g, V_ALNB + j:V_ALNB + j + 1],
                    in1=Cb[:, g, :], op0=ALU.mult, op1=ALU.add)

        def mk_v(cb, t0, last):
            def comp(slot, wu):
                w3 = slot3(slot, KC, 512)
                for tl in range(NTL):
                    t = t0 + tl
                    p, u = psum()
                    for kc in range(KC):
                        MM(p[:, :], hT[:, kc, t * 128:(t + 1) * 128], w3[:, kc, :], kc == 0, kc == KC - 1, [wu, hU[t // 4]], [u], signal=(kc == KC - 1))
                    ACT(gv[:, tl, cb * 512:(cb + 1) * 512], p[:, :], AF.Gelu_apprx_tanh, [u], [gvU[tl]])
                if not last:
                    return
                for tl in range(NTL):
                    t = t0 + tl
                    bi = t // 4
                    j1, j1u = tb(); j2, j2u = tb()
                    s1 = stat[:, 0:1]; s2 = stat[:, 1:2]; mu = stat[:, 2:3]; var = stat[:, 3:4]; rs = stat[:, 4:5]; nb = stat[:, 5:6]
                    DVE("memset", [], [statU], ap=stat[:, 8:16], constant=0.0)
                    for q4 in range(4):
                        ACT(j1[:, :], gv[:, tl, q4 * 512:(q4 + 1) * 512], AF.Copy, [gvU[tl]], [j1u, statU], accum_out=stat[:, 8 + q4:9 + q4])
                        ACT(j2[:, :], gv[:, tl, q4 * 512:(q4 + 1) * 512], AF.Square, [gvU[tl]], [j2u, statU], accum_out=stat[:, 12 + q4:13 + q4])
                    DVE("tensor_reduce", [statU], [statU], out=s1, in_=stat[:, 8:12], axis=AX.X, op=ALU.add)
                    DVE("tensor_reduce", [statU], [statU], out=s2, in_=stat[:, 12:16], axis=AX.X, op=ALU.add)
                    DVE("tensor_scalar_mul", [statU], [statU], out=mu, in0=s1, scalar1=1.0 / D)
                    DVE("tensor_tensor", [statU], [statU], out=var, in0=mu, in1=mu, op=ALU.mult)
                    DVE("scalar_tensor_tensor", [statU], [statU], out=var, in0=s2, scalar=1.0 / D, in1=var, op0=ALU.mult, op1=ALU.subtract)
                    ACT(rs, var, AF.Sqrt, [statU, cU], [statU], bias=epsT[:, 0:1], scale=1.0)
                    DVE("reciprocal", [statU], [statU], out=rs, in_=rs)
                    DVE("scalar_tensor_tensor", [statU], [statU], out=nb, in0=mu, scalar=-1.0, in1=rs, op0=ALU.mult, op1=ALU.mult)
                    ACT(gv[:, tl, :], gv[:, tl, :], AF.Identity, [gvU[tl], statU], [gvU[tl]], bias=nb, scale=rs)
                    for g4 in range(4):
                        p, u = psum()
                        for k in range(4):
                            g = g4 * 4 + k
                            MM(p[:, k * 128:(k + 1) * 128], gv[:, tl, g * 128:(g + 1) * 128], wsT[:, g, :], True, True, [gvU[tl], wsU], [u], signal=(k == 3))
                        for k in range(4):
                            g = g4 * 4 + k
                            DVE("scalar_tensor_tensor", [u, CU, cU], [mU[g][bi]], out=mid[:, g, t * 128:(t + 1) * 128], in0=p[:, k * 128:(k + 1) * 128],
                                scalar=vecT[:, g, V_ALNG + j:V_ALNG + j + 1], in1=Cb[:, g, :], op0=ALU.mult, op1=ALU.add)
            return comp
        for t0 in range(0, 8, NTL):
            for cb in range(4):
                wstep([(lambda s: slot3(s, KC, 512), wsrc(w_in, 0, KC, D + cb * 512, 512))], mk_v(cb, t0, cb == 3))

        def mk_vs(cb):
            def comp(slot, wu):
                w3 = slot3(slot, KC, 512)
                for o4 in range(4):
                    p, u = psum()
                    mm_fm(p, u, w3, wu, KC, o4, hT, lambda kc: [hU[2]], TP, 1)
                    ACT(gvs[:, cb * 4 + o4:cb * 4 + o4 + 1], p[:, 0:1], AF.Gelu_apprx_tanh, [u], [smU])
                if cb != 3:
                    return
                sq, squ = tf()
                DVE("tensor_copy", [smU], [squ], out=sq[:, 0:16], in_=gvs)
                DVE("tensor_tensor", [smU], [squ], out=sq[:, 16:32], in0=gvs, in1=gvs, op=ALU.mult)
                hb, hbu = tb(); lb_, lbu = tb()
                DVE("tensor_copy", [squ], [hbu], out=hb[:, 0:32], in_=sq[:, 0:32])
                DVE("tensor_tensor", [squ, hbu], [squ], out=sq[:, 32:64], in0=sq[:, 0:32], in1=hb[:, 0:32], op=ALU.subtract)
                DVE("tensor_copy", [squ], [lbu], out=lb_[:, 0:32], in_=sq[:, 32:64])
                p, u = psum()
                MM(p[:, 0:32], onesb[:, :], hb[:, 0:32], True, False, [hbu, cU], [u], signal=False)
                MM(p[:, 0:32], onesb[:, :], lb_[:, 0:32], False, True, [lbu, cU], [u])
                s1 = stat[:, 0:1]; s2 = stat[:, 1:2]; mu = stat[:, 2:3]; var = stat[:, 3:4]; rs = stat[:, 4:5]; nb = stat[:, 5:6]
                DVE("tensor_reduce", [u], [statU], out=s1, in_=p[:, 0:16], axis=AX.X, op=ALU.add)
                DVE("tensor_reduce", [u], [statU], out=s2, in_=p[:, 16:32], axis=AX.X, op=ALU.add)
                DVE("tensor_scalar_mul", [statU], [statU], out=mu, in0=s1, scalar1=1.0 / D)
                DVE("tensor_tensor", [statU], [statU], out=var, in0=mu, in1=mu, op=ALU.mult)
                DVE("scalar_tensor_tensor", [statU], [statU], out=var, in0=s2, scalar=1.0 / D, in1=var, op0=ALU.mult, op1=ALU.subtract)
                ACT(rs, var, AF.Sqrt, [statU, cU], [statU], bias=epsT[:, 0:1], scale=1.0)
                DVE("reciprocal", [statU], [statU], out=rs, in_=rs)
                DVE("scalar_tensor_tensor", [statU], [statU], out=nb, in0=mu, scalar=-1.0, in1=rs, op0=ALU.mult, op1=ALU.mult)
                ACT(vln, gvs, AF.Identity, [smU, statU], [smU], bias=nb, scale=rs)
                DVE("tensor_tensor", [smU, cU], [smU], out=vln, in0=vln, in1=vecT[:, :, V_ALNG + j], op=ALU.mult)
                DVE("tensor_tensor", [smU, cU], [smU], out=vln, in0=vln, in1=vecT[:, :, V_ALNB + j], op=ALU.add)
                p2, u2 = psum()
                TR(p2[0:16, 0:128], vln, ident[:, :], [smU, cU], [u2])
                o, ou = tf()
                DVE("tensor_copy", [u2], [ou], out=o[0:16, 0:128], in_=p2[0:16, 0:128])
                dma("sp", av_s.ap()[j].rearrange("(g e) -> g e", e=128), o[0:16, 0:128], smDS, R=[ou])
                DVE("tensor_tensor", [smU], [smU], out=gvs, in0=vln, in1=ws00, op=ALU.mult)
                DVE("tensor_tensor", [smU], [mU[g][2] for g in range(KC)], out=mid[:, :, TP], in0=gvs, in1=bs0, op=ALU.add)
            return comp
        for cb in range(4):
            wstep([(lambda s: slot3(s, KC, 512), wsrc(w_in, 0, KC, D + cb * 512, 512))], mk_vs(cb))

        def mk_u(cb):
            def comp(slot, wu):
                w3 = slot3(slot, KC, 512)
                for o4 in range(4):
                    oc = cb * 4 + o4
                    for bi, (c0, n) in enumerate(BLKS):
                        p, u = psum()
                        mm_fm(p, u, w3, wu, KC, o4, hT, lambda kc: [hU[bi]], c0, n)
                        g_, gu = tf()
                        ACT(g_[:, :n], p[:, :n], AF.Gelu_apprx_tanh, [u], [gu])
                        DVE("tensor_tensor", [gu], [mU[oc][bi]], out=mid[:, oc, c0:c0 + n], in0=g_[:, :n], in1=mid[:, oc, c0:c0 + n], op=ALU.mult)
            return comp
        for cb in range(4):
            wstep([(lambda s: slot3(s, KC, 512), wsrc(w_in, 0, KC, cb * 512, 512))], mk_u(cb))
        out_proj_steps(w_out, 0, KC)
        run_steps()

    def sample_attn(qs_f, ks_f, vs_f, smpU, hTb):
        B.barrier()
        hTf = hTb[:, 0:16400].bitcast(F32)
        kcf = hTf[:, 0:2048]; prod = hTf[:, 2048:2560]; sc = hTf[:, 2560:2608]; pf = hTf[:, 2608:2656]
        pn = hTf[:, 2656:2704]; t48 = hTf[:, 2704:2752]; t48b = hTf[:, 2752:2800]; b0 = hTf[:, 2800:2848]
        tabs = hTf[:, 2848:2896]; sels = hTf[:, 2896:3280]; o16 = hTf[:, 3280:3312]; rowst = hTf[:, 3312:3440]
        bfv = hTb[:, 6880:16400]
        Qd = bfv[:, 0:2048]; vcb = [bfv[:, 2048 * (1 + g):2048 * (2 + g)] for g in range(3)]
        pb48 = bfv[:, 8192:8240]; hb = bfv[:, 8240:8288]; lb_ = bfv[:, 8288:8336]
        kU_, vU_, sU, cDS_, vDS_, oDS_, rDS_ = Unit(), [Unit(), Unit(), Unit()], Unit(), B.new_ds(), B.new_ds(), B.new_ds(), B.new_ds()
        qdU, prU, rsU = Unit(), Unit(), Unit()
        dma("sp", tabs[0:32, :], relb.ap(), cDS_, W=[sU])
        dma("sp", sels[0:32, :], sels_d.ap(), cDS_, W=[sU])
        dma("sp", b0, relb.ap()[0:1, :].partition_broadcast(128), cDS_, W=[sU])
        for g in range(3):
            dil = DIL[g]
            dma("pool", vcb[g], bass.AP(cv[g].handle(), 0, [[dil * D, 128], [1, D]]), vDS_, W=[vU_[g]])
        for g in range(3):
            dil = DIL[g]
            dma("sp", kcf, bass.AP(ck[g].handle(), 0, [[dil * D, 128], [1, D]]), cDS_, W=[kU_])
            DVE("tensor_tensor", [cU, smpU], [qdU], out=Qd.rearrange("p (h e) -> p h e", e=128), in0=ident[:, :].unsqueeze(1).to_broadcast([128, 16, 128]),
                in1=qs_f[:, g * 16:(g + 1) * 16].unsqueeze(2).to_broadcast([128, 16, 128]), op=ALU.mult)
            for c4 in range(4):
                p, u = psum()
                MM(p[:, :], onesb[:, :], Qd[:, c4 * 512:(c4 + 1) * 512], True, True, [qdU, cU], [u])
                DVE("tensor_tensor", [u, kU_], [prU], out=prod, in0=kcf[:, c4 * 512:(c4 + 1) * 512], in1=p[:, :], op=ALU.mult)
                DVE("tensor_reduce", [prU], [sU], out=sc[:, g * 16 + c4 * 4:g * 16 + c4 * 4 + 4], in_=prod.rearrange("p (h e) -> p h e", e=128), axis=AX.X, op=ALU.add)
            p, u = psum()
            MM(p[:, 0:16], sels[0:32, g * 128:(g + 1) * 128], tabs[0:32, g * 16:(g + 1) * 16], True, True, [sU], [u])
            DVE("scalar_tensor_tensor", [u, sU], [sU], out=sc[:, g * 16:(g + 1) * 16], in0=sc[:, g * 16:(g + 1) * 16], scalar=SCALE, in1=p[:, 0:16], op0=ALU.mult, op1=ALU.add)
        ACT(pf, sc, AF.Exp, [sU], [sU])
        DVE("tensor_copy", [sU], [sU], out=pb48, in_=pf)
        DVE("tensor_tensor", [smpU], [sU], out=t48, in0=qs_f, in1=ks_f, op=ALU.mult)
        DVE("tensor_copy", [sU], [sU], out=hb, in_=t48)
        DVE("tensor_tensor", [sU], [sU], out=t48b, in0=t48, in1=hb, op=ALU.subtract)
        DVE("tensor_copy", [sU], [sU], out=lb_, in_=t48b)
        p, u = psum()
        MM(p[:, 0:48], onesb[:, :], hb, True, False, [sU, cU], [u])
        MM(p[:, 0:48], onesb[:, :], lb_, False, True, [sU, cU], [u])
        DVE("scalar_tensor_tensor", [u, sU], [sU], out=t48, in0=p[:, 0:48], scalar=SCALE, in1=b0, op0=ALU.mult, op1=ALU.add)
        ACT(pn, t48, AF.Exp, [sU], [sU])
        pso, uso = psum(); psd, usd = psum()
        for h_ in range(16):
            for g in range(3):
                MM(pso[:, h_:h_ + 1], vcb[g][:, h_ * 128:(h_ + 1) * 128], pb48[:, g * 16 + h_:g * 16 + h_ + 1], g == 0, g == 2, [vU_[g], sU], [uso])
        for g in range(3):
            MM(psd[:, 0:16], onesb[:, :], pb48[:, g * 16:(g + 1) * 16], g == 0, g == 2, [sU, cU], [usd])
        DVE("tensor_tensor", [sU, smpU], [sU], out=t48, in0=pn, in1=vs_f, op=ALU.mult)
        DVE("tensor_reduce", [sU], [sU], out=o16[:, 0:16], in_=t48.rearrange("p (g h) -> p h g", g=3), axis=AX.X, op=ALU.add)
        DVE("tensor_reduce", [sU], [sU], out=o16[:, 16:32], in_=pn.rearrange("p (g h) -> p h g", g=3), axis=AX.X, op=ALU.add)
        DVE("tensor_tensor", [uso, sU], [sU], out=o16[:, 0:16], in0=pso[:, 0:16], in1=o16[:, 0:16], op=ALU.add)
        DVE("tensor_tensor", [usd, sU], [sU], out=o16[:, 16:32], in0=psd[:, 0:16], in1=o16[:, 16:32], op=ALU.add)
        DVE("reciprocal", [sU], [sU], out=o16[:, 16:32], in_=o16[:, 16:32])
        DVE("tensor_tensor", [sU], [mU[k][2] for k in range(KC)], out=mid[:, :, TP], in0=o16[:, 0:16], in1=o16[:, 16:32], op=ALU.mult)
        for g, W_ in enumerate((128, 512, 2048)):
            n16 = (W_ - 1) * 16
            for src_t, dst_t, col in ((ck[g], bk_s[g], ks_f), (cv[g], bv_s[g], vs_f)):
                dma("sp", bass.AP(dst_t, 0, [[n16, 128], [1, n16]]), bass.AP(src_t.handle(), D, [[n16, 128], [1, n16]]), oDS_)
                p, u = psum()
                TR(p[0:16, 0:128], col[:, g * 16:(g + 1) * 16], ident[:, :], [smpU, cU], [u])
                DVE("tensor_copy", [u, rsU], [rsU], out=rowst[0:16, :], in_=p[0:16, 0:128])
                dma("sp", dst_t.ap()[W_ - 1:W_, :].rearrange("o (h e) -> (o h) e", e=128), rowst[0:16, :], rDS_, R=[rsU])

    DIL = (1, 4, 16)
    KEEP = (128, 512, 1024)

    def dilattn(i):
        qs_d = B.dram("qs_d", [6144, 1024], BF16)
        kvi = B.dram("kvi", [12288, 1024], BF16)
        kvo = B.dram("kvo", [24576, 1024], BF16)

        def kvo_off(elem_off):
            row = elem_off // 1024
            return ((row // 1024) * 2048 + (row % 1024)) * 1024 + (elem_off % 1024)
        eg_d = B.dram("eg_d", [144, 255], F32)
        re_d = B.dram("re_d", [144 * 128, 255], F32)
        VB = 6144 * 1024
        RANK = 12288 * 1024
        kvWc = [Unit() for _ in range(12)]; qsW = Unit(); reU = Unit(); kvoU = [Unit() for _ in range(12)]

        def kvW_of(elem_off):
            return kvWc[elem_off // (1024 * 1024)]

        def kvoU_of(elem_off):
            return kvoU[elem_off // (1024 * 1024)]
        ccsem = B.new_sem("cc_kv")
        w_qkv = b_w_qkv.ap(); w_o = b_w_out.ap()

        arena_reset()
        tabx = carve(48); selS = carve(9 * 255); egs = carve(9 * 255); tU_ = Unit(); tDS = B.new_ds()
        dma("sp", tabx[0:32, :], relb.ap(), tDS, W=[tU_])
        DVE("memset", [], [tU_], ap=tabx[32:33, :], constant=-1e30)
        dma("sp", selS[0:33, :], selg.ap(), tDS, W=[tU_])
        for g in range(3):
            for v in range(3):
                c = (g * 3 + v) * 255
                p, u = psum()
                MM(p[0:16, 0:255], tabx[0:33, g * 16:(g + 1) * 16], selS[0:33, c:c + 255], True, True, [tU_], [u])
                ACT(egs[0:16, c:c + 255], p[0:16, 0:255], AF.Exp, [u], [tU_])
                if v == 2:
                    DVE("tensor_scalar_mul", [tU_, cU], [tU_], out=egs[0:16, c:c + 255], in0=egs[0:16, c:c + 255], scalar1=flg[0:16, 0:1])
        dma("sp", eg_d.ap().rearrange("(g h v) c -> h g v c", g=3, h=16, v=3), egs[0:16, :].rearrange("p (g v c) -> p g v c", g=3, v=3), tDS, R=[tU_], W=[reU])
        for k in range(9):
            dma("sp", re_d.ap()[k * 2048:(k + 1) * 2048, :].rearrange("(r j) c -> r j c", j=128),
                bass.AP(eg_d, k * 16 * 255, [[255, 16], [0, 128], [1, 255]]), tDS, R=[reU], W=[reU])

        arena_reset()
        smp = carve(3 * 48); smpU = Unit()
        rmsnorm(V_GMIX + i)
        hk = [carve(512, BF16), carve(512, BF16)]; hkU = [Unit(), Unit()]; hkDS = [B.new_ds(), B.new_ds()]
        kst = carve(2048).rearrange("p (t c) -> p t c", c=512); kstU = Unit(); kstDS = B.new_ds()
        vreg = carve(1536)
        kq = [vreg[:, i * 512:(i + 1) * 512] for i in range(3)]; kqU = [Unit() for _ in range(3)]
        vst = [vreg[:, 0:512], vreg[:, 512:1024]]; vstU = [Unit(), Unit()]; vstDS = [B.new_ds(), B.new_ds()]
        vbs = [vreg[:, 1024:1280].bitcast(BF16), vreg[:, 1280:1536].bitcast(BF16)]; vbsU = [Unit(), Unit()]; vbsDS = [B.new_ds(), B.new_ds()]
        qs_f, ks_f, vs_f = smp[:, 0:48], smp[:, 48:96], smp[:, 96:144]
        cnt = {"hk": 0, "v": 0, "kq": 0}

        def k_out_dma(g, hq, bi):
            for tt in range(4):
                t = bi * 4 + tt
                if t * 128 >= TP - KEEP[g]:
                    r0 = t * 128 - (TP - KEEP[g])
                    dma("sp", bk_p[g].ap()[r0:r0 + 128, hq * 512:(hq + 1) * 512], kst[:, tt, :], kstDS, R=[kstU])

        pend = {"f": None, "g": None}

        def flush_pending():
            f = pend["f"]; pend["f"] = None
            if f is not None:
                f()

        def flush_all():
            f = pend["f"]; g_ = pend["g"]
            pend["f"] = None; pend["g"] = None
            if f is not None:
                f()
            if g_ is not None:
                g_()
            g2 = pend["g"]; pend["g"] = None
            if g2 is not None:
                g2()

        def qk_front(w3, wu, o4, bi, c0, n, q=None, qu=None):
            p, u = psum()
            mm_fm(p, u, w3, wu, KC, o4, hT, lambda kc: [hU[bi]], c0, n)
            if q is None:
                q, qu = tf()
            ACT(q[:, :n], p[:, :n], AF.Copy, [u], [qu])
            s_, su = tb()
            ACT(s_[:, :n], p[:, :n], AF.Square, [u], [su])
            return q, qu, s_, su

        def qk_sqrt(s_, su, n):
            p2, u2 = psum()
            MM(p2[:, :n], onesb[:, :], s_[:, :n], True, True, [su, cU], [u2])
            r, ru = tf()
            ACT(r[:, :n], p2[:, :n], AF.Sqrt, [u2, cU], [ru], bias=epsT[:, 0:1], scale=1.0 / 128)
            DVE("reciprocal", [ru], [ru], out=r[:, :n], in_=r[:, :n])
            return r, ru

        def mk_k(g, hq):
            dil = DIL[g]

            def comp(slot, wu):
                w3 = slot3(slot, KC, 512)
                gcol = S_BK + g
                for bi, (c0, n) in enumerate(BLKS):
                    for o4 in range(4):
                        h_ = hq * 4 + o4
                        if bi == 2:
                            flush_all()
                            q, qu, s_, su = qk_front(w3, wu, o4, bi, c0, n)
                            r, ru = qk_sqrt(s_, su, 1)
                            DVE("scalar_tensor_tensor", [qu, ru, cU], [smpU], out=ks_f[:, g * 16 + h_:g * 16 + h_ + 1], in0=q[:, :1], scalar=gs[:, gcol:gcol + 1],
                                in1=r[:, :1], op0=ALU.mult, op1=ALU.mult)
                            continue
                        qi = cnt["kq"] % 3; cnt["kq"] += 1
                        q, qu, s_, su = qk_front(w3, wu, o4, bi, c0, n, q=kq[qi], qu=kqU[qi])
                        fA = pend["f"]; fB = pend["g"]
                        pend["f"] = None; pend["g"] = None
                        if fA is not None:
                            fA()
                        if fB is not None:
                            fB()
                        row = (g * 16 + h_) * 128

                        def tailB(q=q, qu=qu, bi=bi, o4=o4):
                            tiles = [tt for tt in range(4) if (bi * 4 + tt) * 128 >= TP - KEEP[g]]
                            if tiles:
                                pt, ut = psum()
                                for tt in tiles:
                                    TR(pt[:, tt * 128:(tt + 1) * 128], q[:, tt * 128:(tt + 1) * 128], ident[:, :], [qu, cU], [ut], signal=(tt == tiles[-1]))
                                for tt in tiles:
                                    DVE("tensor_copy", [ut], [kstU], out=kst[:, tt, o4 * 128:(o4 + 1) * 128], in_=pt[:, tt * 128:(tt + 1) * 128])
                            if o4 == 3:
                                k_out_dma(g, hq, bi)

                        def tailA(q=q, qu=qu, s_=s_, su=su, bi=bi, n=n, row=row, tailB=tailB):
                            r, ru = qk_sqrt(s_, su, n)
                            DVE("scalar_tensor_tensor", [qu, ru, cU], [qu], out=q[:, :n], in0=q[:, :n], scalar=gs[:, gcol:gcol + 1], in1=r[:, :n], op0=ALU.mult, op1=ALU.mult)
                            hi = cnt["hk"] % 2; cnt["hk"] += 1
                            nu = 512 // dil
                            ACT(hk[hi][:, 0:512].rearrange("p (r u) -> p r u", r=dil), q[:, :].rearrange("p (u r) -> p r u", r=dil), AF.Copy, [qu], [hkU[hi]])
                            dma("sp", kvi.ap()[row:row + 128, :].rearrange("p (r u) -> p r u", r=dil)[:, :, bi * nu:(bi + 1) * nu],
                                hk[hi][:, 0:512].rearrange("p (r u) -> p r u", r=dil), hkDS[hi], R=[hkU[hi], kvW_of(row * 1024)])
                            pend["g"] = tailB
                        pend["f"] = tailA
            return comp

        def mk_q(g, hq):
            dil = DIL[g]

            def comp(slot, wu):
                w3 = slot3(slot, KC, 512)
                gcol = S_BQ + g
                for bi, (c0, n) in enumerate(BLKS):
                    for o4 in range(4):
                        h_ = hq * 4 + o4
                        if bi == 2:
                            flush_all()
                        q, qu, s_, su = qk_front(w3, wu, o4, bi, c0, n)
                        flush_pending()
                        if bi == 2:
                            r, ru = qk_sqrt(s_, su, 1)
                            DVE("scalar_tensor_tensor", [qu, ru, cU], [smpU], out=qs_f[:, g * 16 + h_:g * 16 + h_ + 1], in0=q[:, :1], scalar=gs[:, gcol:gcol + 1],
                                in1=r[:, :1], op0=ALU.mult, op1=ALU.mult)
                            continue

                        def tail(q=q, qu=qu, s_=s_, su=su, bi=bi, h_=h_, n=n):
                            r, ru = qk_sqrt(s_, su, n)
                            hi = cnt["hk"] % 2; cnt["hk"] += 1
                            nu = 512 // dil
                            hv = hk[hi][:, 0:512].rearrange("p (r u) -> p r u", r=dil)
                            DVE("scalar_tensor_tensor", [qu, ru, cU], [hkU[hi]], out=hv, in0=q[:, :].rearrange("p (u r) -> p r u", r=dil), scalar=gs[:, gcol:gcol + 1],
                                in1=r[:, :].rearrange("p (u r) -> p r u", r=dil), op0=ALU.mult, op1=ALU.mult)
                            row = (g * 16 + h_) * 128
                            dma("sp", qs_d.ap()[row:row + 128, :].rearrange("p (r u) -> p r u", r=dil)[:, :, bi * nu:(bi + 1) * nu], hv, hkDS[hi], R=[hkU[hi], qsW])
                        pend["f"] = tail
            return comp

        def mk_v(g, hq):
            def comp(slot, wu):
                w3 = slot3(slot, KC, 512)
                for t in range(8):
                    vi = cnt["v"] % 2; cnt["v"] += 1
                    p, u = psum()
                    for kc in range(KC):
                        MM(p[:, :], hT[:, kc, t * 128:(t + 1) * 128], w3[:, kc, :], kc == 0, kc == KC - 1, [wu, hU[t // 4]], [u], signal=(kc == KC - 1))
                    DVE("tensor_copy", [u], [vstU[vi]], out=vst[vi][:, :], in_=p[:, :])
                    ACT(vbs[vi][:, :], vst[vi][:, :], AF.Copy, [vstU[vi]], [vbsU[vi]])
                    if t * 128 >= TP - KEEP[g]:
                        r0 = t * 128 - (TP - KEEP[g])
                        dma("sp", bv_p[g].ap()[r0:r0 + 128, hq * 512:(hq + 1) * 512], vst[vi][:, :], vstDS[vi], R=[vstU[vi]])
                    off = VB + ((g * 16 + hq * 4) * 1024 + t * 128) * 128
                    dma("sp", bass.AP(kvi, off, [[128, 128], [1024 * 128, 4], [1, 128]]), vbs[vi][:, :].rearrange("p (o e) -> p o e", e=128), vbsDS[vi], R=[vbsU[vi], kvW_of(off)])
                for o4 in range(4):
                    p, u = psum()
                    mm_fm(p, u, w3, wu, KC, o4, hT, lambda kc: [hU[2]], TP, 1)
                    DVE("tensor_copy", [u], [smpU], out=vs_f[:, g * 16 + hq * 4 + o4:g * 16 + hq * 4 + o4 + 1], in_=p[:, 0:1])
            return comp

        def qkv_src(g, which, hq):
            return wsrc(w_qkv, 0, KC, g * 6144 + which * 2048 + hq * 512, 512)
        def mk_first_v(inner):
            def comp(slot, wu):
                flush_all()
                B.barrier()
                inner(slot, wu)
            return comp

        def mk_cc(k):
            return lambda: B.cc(kvi.ap()[k * 1024:(k + 1) * 1024, :].opt(), kvo.ap()[k * 2048:(k + 1) * 2048, :].opt(), PAIRS_RUN, ccsem,
                                W=[kvWc[k], kvoU[k]], count=k + 1)
        cc_at = {}
        for g in range(3):
            cc_at[4 * g + 6] = 2 * g; cc_at[4 * g + 7] = 2 * g + 1
            cc_at[12 + 4 * g + 6] = 6 + 2 * g; cc_at[12 + 4 * g + 7] = 7 + 2 * g
        si = 0
        for which in (1, 2, 0):
            for g in range(3):
                for hq in range(4):
                    comp = mk_k(g, hq) if which == 1 else (mk_v(g, hq) if which == 2 else mk_q(g, hq))
                    if which == 2 and g == 0 and hq == 0:
                        comp = mk_first_v(comp)
                    wstep([(lambda s: slot3(s, KC, 512), qkv_src(g, which, hq))], comp, post_issue=(mk_cc(cc_at[si]) if si in cc_at else None))
                    si += 1
        run_steps()
        flush_all()

        arena_reset()
        carve(3 * 48)
        hTb = hT2[:, :]
        hoff = [0]

        def hcarve(n_bf):
            o = hoff[0]; hoff[0] = o + n_bf + (n_bf % 2)
            assert hoff[0] <= KC * XC
            return hTb[:, o:o + n_bf]
        q3 = [hcarve(1024) for _ in range(3)]; ko = [hcarve(1024) for _ in range(3)]
        kp = [hcarve(128), hcarve(512), hcarve(1024)]
        vo = [hcarve(8 * 128).rearrange("p (t e) -> p t e", e=128), hcarve(8 * 128).rearrange("p (t e) -> p t e", e=128), hcarve(16 * 128).rearrange("p (t e) -> p t e", e=128)]
        vpv = [hcarve(128).rearrange("p (t e) -> p t e", e=128), hcarve(4 * 128).rearrange("p (t e) -> p t e", e=128), hcarve(16 * 128).rearrange("p (t e) -> p t e", e=128)]
        et = carve(10 * 128).rearrange("p (k i) -> p k i", i=128)
        accden = carve(2048).rearrange("p (k t) -> p k t", k=2); adU = Unit()
        ldq = Unit(); ldk = Unit(); ldv = Unit(); lde = Unit()
        qDS, kDS, vDS, eDS = B.new_ds(), B.new_ds(), B.new_ds(), B.new_ds()

        def head(h_):
            for g in range(3):
                dil = DIL[g]; U = TP // dil
                row = (g * 16 + h_) * 128
                dma("sp", q3[g], qs_d.ap()[row:row + 128, :], qDS, R=[qsW], W=[ldq])
                dma("sp", ko[g], kvi.ap()[row:row + 128, :], kDS, R=[kvW_of(row * 1024)], W=[ldk])
                orow = kvo_off(row * 1024) // 1024
                if g == 0:
                    dma("sp", kp[0], kvo.ap()[orow:orow + 128, 896:1024], kDS, R=[kvoU_of(row * 1024)], W=[ldk])
                elif g == 1:
                    dma("sp", kp[1].rearrange("p (r u) -> p r u", r=4), kvo.ap()[orow:orow + 128, :].rearrange("p (r u) -> p r u", r=4)[:, :, 128:256], kDS, R=[kvoU_of(row * 1024)], W=[ldk])
                else:
                    dma("sp", kp[2], kvo.ap()[orow:orow + 128, :], kDS, R=[kvoU_of(row * 1024)], W=[ldk])
                vbase = VB + (g * 16 + h_) * 1024 * 128
                if g == 0:
                    dma("sp", vo[0], bass.AP(kvi, vbase, [[128, 128], [128 * 128, 8], [1, 128]]), vDS, R=[kvW_of(vbase)], W=[ldv])
                    dma("sp", vpv[0], bass.AP(kvo, kvo_off(vbase) + 896 * 128, [[128, 128], [128 * 128, 1], [1, 128]]), vDS, R=[kvoU_of(vbase)], W=[ldv])
                elif g == 1:
                    for r in range(4):
                        dma("sp", vo[1][:, r * 2:r * 2 + 2, :], bass.AP(kvi, vbase + r * 128, [[4 * 128, 128], [512 * 128, 2], [1, 128]]), vDS, R=[kvW_of(vbase)], W=[ldv])
                    dma("sp", vpv[1], bass.AP(kvo, kvo_off(vbase) + 512 * 128, [[4 * 128, 128], [128, 4], [1, 128]]), vDS, R=[kvoU_of(vbase)], W=[ldv])
                else:
                    dma("sp", vo[2][0:64, :, :], bass.AP(kvi, vbase, [[16 * 128, 64], [128, 16], [1, 128]]), vDS, R=[kvW_of(vbase)], W=[ldv])
                    dma("sp", vpv[2][0:64, :, :], bass.AP(kvo, kvo_off(vbase), [[16 * 128, 64], [128, 16], [1, 128]]), vDS, R=[kvoU_of(vbase)], W=[ldv])
                erow = (g * 16 + h_) * 3
                if g < 2:
                    for k_, v in enumerate((1, 0, 2, 0)):
                        dma("sp", et[:, g * 4 + k_, :], bass.AP(re_d, (erow + v) * 128 * 255 + 127, [[254, 128], [1, 128]]), eDS, R=[reU], W=[lde])
                else:
                    dma("sp", et[0:64, 8, 0:64], bass.AP(re_d, (erow + 2) * 128 * 255 + 191, [[254, 64], [1, 64]]), eDS, R=[reU], W=[lde])
                    dma("sp", et[0:64, 9, 0:64], bass.AP(re_d, (erow + 0) * 128 * 255 + 127, [[254, 64], [1, 64]]), eDS, R=[reU], W=[lde])
            units = []
            for g in range(3):
                dil = DIL[g]; U = TP // dil
                QB = 128 if g < 2 else 64
                for r in range(dil):
                    for qb in range(U // QB):
                        units.append((g, r, qb))

            def stageA(un):
                g, r, qb = un
                dil = DIL[g]; U = TP // dil
                QB = 128 if g < 2 else 64
                nqb = U // QB
                qcols = q3[g][:, r * U + qb * QB:r * U + (qb + 1) * QB]
                if qb > 0:
                    kT0 = ko[g][:, r * U + (qb - 1) * QB:r * U + qb * QB]
                    vt0 = vo[g][0:QB, (r * nqb + qb - 1), :]
                    eb = g * 4
                else:
                    if g == 0:
                        kT0 = kp[0][:, :]; vt0 = vpv[0][:, 0, :]
                    elif g == 1:
                        kT0 = kp[1][:, r * 128:(r + 1) * 128]; vt0 = vpv[1][:, r, :]
                    else:
                        kT0 = kp[2][:, r * 64:(r + 1) * 64]; vt0 = vpv[2][0:64, r, :]
                    eb = (g * 4 + 2) if g < 2 else 8
                kT1 = ko[g][:, r * U + qb * QB:r * U + (qb + 1) * QB]
                vt1 = vo[g][0:QB, (r * nqb + qb), :]
                p, u = psum()
                MM(p[0:QB, 0:QB], kT0, qcols, True, True, [ldk, ldq], [u])
                MM(p[0:QB, QB:2 * QB], kT1, qcols, True, True, [ldk, ldq], [u])
                e_, eu = tf()
                ACT(e_[0:QB, 0:2 * QB], p[0:QB, 0:2 * QB], AF.Exp, [u], [eu], scale=SCALE)
                pb, pbu = tb()
                DVE("tensor_tensor", [eu, lde], [pbu], out=pb[0:QB, 0:2 * QB].rearrange("p (k i) -> p k i", k=2), in0=e_[0:QB, 0:2 * QB].rearrange("p (k i) -> p k i", k=2),
                    in1=et[0:QB, eb:eb + 2, 0:QB], op=ALU.mult)
                return (un, pb, pbu, vt0, vt1)

            def stageB(sa):
                un, pb, pbu, vt0, vt1 = sa
                QB = 128 if un[0] < 2 else 64
                po, uo = psum()
                MM(po[:, 0:QB], vt0, pb[0:QB, 0:QB], True, False, [ldv, pbu], [uo])
                MM(po[:, 0:QB], vt1, pb[0:QB, QB:2 * QB], False, True, [ldv, pbu], [uo])
                MM(po[:, QB:2 * QB], onesb[0:QB, :], pb[0:QB, 0:QB], True, False, [pbu, cU], [uo])
                MM(po[:, QB:2 * QB], onesb[0:QB, :], pb[0:QB, QB:2 * QB], False, True, [pbu, cU], [uo])
                return (un, po, uo)

            def stageC(sb):
                (g, r, qb), po, uo = sb
                dil = DIL[g]
                QB = 128 if g < 2 else 64
                a_out = accden.rearrange("p k (u r) -> p k r u", r=dil)[:, :, r, qb * QB:(qb + 1) * QB]
                src = po[:, 0:2 * QB].rearrange("p (k i) -> p k i", k=2)
                if g == 0:
                    DVE("tensor_copy", [uo], [adU], out=a_out, in_=src)
                else:
                    DVE("tensor_tensor", [uo, adU], [adU], out=a_out, in0=src, in1=a_out, op=ALU.add)
            pairs = [units[k:k + 2] for k in range(0, len(units), 2)]
            sa_prev, sb_prev = [], []
            for pr in pairs + [[], []]:
                sa = [stageA(un) for un in pr]
                sb = [stageB(x) for x in sa_prev]
                for x in sb_prev:
                    stageC(x)
                sa_prev, sb_prev = sa, sb
            DVE("reciprocal", [adU], [adU], out=accden[:, 1, :], in_=accden[:, 1, :])
            DVE("tensor_tensor", [adU], [mU[h_][0], mU[h_][1]], out=mid[:, h_, 0:TP], in0=accden[:, 0, :], in1=accden[:, 1, :], op=ALU.mult)
        for h_ in range(16):
            head(h_)

        sample_attn(qs_f, ks_f, vs_f, smpU, hTb)
        out_proj_steps(w_o, 0, KC)
        run_steps()

    def col_ln(src, srcU, grow, brow, out, outW, scr, scrU):
        sq = scr[:, 0:32]; lo = scr[:, 32:64]; stt = scr[:, 64:70]
        DVE("tensor_copy", list(srcU), [scrU], out=sq[:, 0:16], in_=src)
        DVE("tensor_tensor", list(srcU), [scrU], out=sq[:, 16:32], in0=src, in1=src, op=ALU.mult)
        hb, hbu = tb(); lb_, lbu = tb()
        DVE("tensor_copy", [scrU], [hbu], out=hb[:, 0:32], in_=sq)
        DVE("tensor_tensor", [scrU, hbu], [scrU], out=lo, in0=sq, in1=hb[:, 0:32], op=ALU.subtract)
        DVE("tensor_copy", [scrU], [lbu], out=lb_[:, 0:32], in_=lo)
        p, u = psum()
        MM(p[:, 0:32], onesb[:, :], hb[:, 0:32], True, False, [hbu, cU], [u])
        MM(p[:, 0:32], onesb[:, :], lb_[:, 0:32], False, True, [lbu, cU], [u])
        s1 = stt[:, 0:1]; s2 = stt[:, 1:2]; mu = stt[:, 2:3]; var = stt[:, 3:4]; rs = stt[:, 4:5]; nb = stt[:, 5:6]
        DVE("tensor_reduce", [u], [scrU], out=s1, in_=p[:, 0:16], axis=AX.X, op=ALU.add)
        DVE("tensor_reduce", [u], [scrU], out=s2, in_=p[:, 16:32], axis=AX.X, op=ALU.add)
        DVE("tensor_scalar_mul", [scrU], [scrU], out=mu, in0=s1, scalar1=1.0 / D)
        DVE("tensor_tensor", [scrU], [scrU], out=var, in0=mu, in1=mu, op=ALU.mult)
        DVE("scalar_tensor_tensor", [scrU], [scrU], out=var, in0=s2, scalar=1.0 / D, in1=var, op0=ALU.mult, op1=ALU.subtract)
        ACT(rs, var, AF.Sqrt, [scrU, cU], [scrU], bias=epsT[:, 0:1], scale=1.0)
        DVE("reciprocal", [scrU], [scrU], out=rs, in_=rs)
        DVE("scalar_tensor_tensor", [scrU], [scrU], out=nb, in0=mu, scalar=-1.0, in1=rs, op0=ALU.mult, op1=ALU.mult)
        ACT(out, src, AF.Identity, list(srcU) + [scrU], list(outW), bias=nb, scale=rs)
        DVE("tensor_tensor", list(outW) + [cU], list(outW), out=out, in0=out, in1=vecT[:, :, grow], op=ALU.mult)
        DVE("tensor_tensor", list(outW) + [cU], list(outW), out=out, in0=out, in1=vecT[:, :, brow], op=ALU.add)

    def convmod(i):
        arena_reset()
        cxi = B.dram("cxi", [128, 480], F32); cxo = B.dram("cxo", [256, 480], F32)
        ccs = B.new_sem("cc_cv")
        w_in = c_w_in.ap(); w_out = c_w_out.ap()
        ztail = carve(480).rearrange("p (c t) -> p c t", t=30); ztU = Unit(); ztDS = B.new_ds()
        halo = carve(480).rearrange("p (c t) -> p c t", t=30); haU = Unit(); haDS = B.new_ds()
        zcs = carve(16 * 31).rearrange("p (c t) -> p c t", t=31); zsU = Unit()
        zh = carve(16 * 60 // 2, BF16).rearrange("p (c t) -> p c t", t=60); zhU = Unit()
        stg1 = carve(2048)
        yacc = [stg1[:, 0:1024], stg1[:, 1024:2048]]; yU = [Unit(), Unit()]
        scr = carve(80); scrU = Unit()
        ysm = carve(32); ysU = Unit()
        lnmu = carve(512); lnrs = carve(512)
        rmsnorm(V_GMIX + i)
        rows_to_fm([stg1, stg1], cst.ap(), 30, lambda kc: zcs[:, kc, 0:30], lambda kc: [zsU], single=True)

        def mk_in(c2):
            def comp(slot, wu):
                w3 = slot3(slot, KC, 512)
                for j in range(2):
                    c = c2 + j
                    for bi, (c0, n) in enumerate(BLKS):
                        pa, ua = psum(); pg, ug = psum()
                        mm_fm(pa, ua, w3, wu, KC, j, hT, lambda kc: [hU[bi]], c0, n)
                        mm_fm(pg, ug, w3, wu, KC, 2 + j, hT, lambda kc: [hU[bi]], c0, n)
                        s_, su = tf()
                        ACT(s_[:, :n], pg[:, :n], AF.Sigmoid, [ug, cU], [su], bias=vecT[:, c, V_CBIN + 1:V_CBIN + 2], scale=1.0)
                        if bi == 2:
                            DVE("scalar_tensor_tensor", [ua, su, cU], [zsU], out=zcs[:, c, 30:31], in0=pa[:, :1], scalar=vecT[:, c, V_CBIN:V_CBIN + 1], in1=s_[:, :1], op0=ALU.add, op1=ALU.mult)
                            continue
                        DVE("scalar_tensor_tensor", [ua, su, cU], [mU[c][bi]], out=mid[:, c, c0:c0 + n], in0=pa[:, :n], scalar=vecT[:, c, V_CBIN:V_CBIN + 1], in1=s_[:, :n], op0=ALU.add, op1=ALU.mult)
                        if bi == 1:
                            DVE("scalar_tensor_tensor", [ua, su, cU], [ztU], out=ztail[:, c, :], in0=pa[:, 482:512], scalar=vecT[:, c, V_CBIN:V_CBIN + 1], in1=s_[:, 482:512], op0=ALU.add, op1=ALU.mult)
            return comp
        for c2 in range(0, KC, 2):
            wstep([(lambda s: slot3(s, KC, 512)[:, :, 0:256], wsrc(w_in, 0, KC, c2 * 128, 256)),
                   (lambda s: slot3(s, KC, 512)[:, :, 256:512], wsrc(w_in, 0, KC, D + c2 * 128, 256))], mk_in(c2))
        run_steps()
        fm_to_rows([stg1, stg1], lambda kc: ztail[:, kc, :], lambda kc: [ztU], 30, cconv_p.ap(), single=True)
        fm_to_rows([stg1, stg1], lambda kc: zcs[:, kc, 1:31], lambda kc: [zsU], 30, cconv_s.ap(), single=True)
        cxU = Unit()
        dma("sp", cxi.ap(), ztail[:, :, :].rearrange("p c t -> p (c t)"), ztDS, R=[ztU], W=[cxU])
        B.cc(cxi.ap().opt(), cxo.ap().opt(), PAIRS_RUN, ccs, W=[cxU])
        dma("sp", halo[:, :, :].rearrange("p c t -> p (c t)"), cxo.ap()[0:128, :], haDS, R=[cxU], W=[haU])
        DVE("tensor_scalar_mul", [haU, cU], [haU], out=halo[:, :, :], in0=halo[:, :, :], scalar1=flg[:, 0:1])
        DVE("tensor_copy", [haU], [zhU], out=zh[:, :, 0:30], in_=halo[:, :, :])
        DVE("tensor_copy", [mU[c][0] for c in range(KC)], [zhU], out=zh[:, :, 30:60], in_=mid[:, :, 0:30])

        B.barrier()
        dg = stg1[:, 0:1984].bitcast(BF16).rearrange("p (k m) -> p k m", m=128); dgU = Unit()
        for c in range(KC):
            DVE("tensor_tensor", [cU], [dgU], out=dg, in0=ident[:, :].unsqueeze(1).to_broadcast([128, 31, 128]),
                in1=vecT[:, c, V_CWDW:V_CWDW + 31].unsqueeze(2).to_broadcast([128, 31, 128]), op=ALU.mult)
            zsrc = [mU[c][0], mU[c][1]]
            ph, uh = psum(); p0, u0 = psum(); p1, u1 = psum()
            for k in range(31):
                MM(ph[:, 0:30], dg[:, k, :], zh[:, c, k:k + 30], k == 0, k == 30, [dgU, zhU], [uh], signal=(k == 30))
            for k in range(31):
                MM(p0[:, 0:482], dg[:, k, :], mid[:, c, k:k + 482], k == 0, k == 30, [dgU] + zsrc, [u0], signal=(k == 30))
            for k in range(31):
                MM(p1[:, 0:512], dg[:, k, :], mid[:, c, 482 + k:482 + k + 512], k == 0, k == 30, [dgU] + zsrc, [u1], signal=(k == 30))
            bdw = vecT[:, c, V_CBDW:V_CBDW + 1]
            ACT(hT[:, c, 0:30], ph[:, 0:30], AF.Identity, [uh, cU], [hU[0]], bias=bdw, scale=1.0)
            ACT(hT[:, c, 30:512], p0[:, 0:482], AF.Identity, [u0, cU], [hU[0]], bias=bdw, scale=1.0)
            ACT(hT[:, c, 512:TP], p1[:, 0:512], AF.Identity, [u1, cU], [hU[1]], bias=bdw, scale=1.0)
        for bi, (c0, n) in enumerate(BLKS[:2]):
            p1, u1 = psum()
            for c in range(KC):
                MM(p1[:, :n], onesb[:, :], hT[:, c, c0:c0 + n], c == 0, c == KC - 1, [hU[bi], cU], [u1], signal=(c == KC - 1))
            p2, u2 = sumsq_fm(lambda kc: hT[:, kc, c0:c0 + n], lambda kc: [hU[bi]], KC, n)
            mu, muU, rs, rsU2 = lnmu, Unit(), lnrs, Unit()
            DVE("tensor_scalar_mul", [u1], [muU], out=mu[:, :n], in0=p1[:, :n], scalar1=1.0 / D)
            DVE("tensor_tensor", [muU], [rsU2], out=rs[:, :n], in0=mu[:, :n], in1=mu[:, :n], op=ALU.mult)
            DVE("scalar_tensor_tensor", [u2, rsU2], [rsU2], out=rs[:, :n], in0=p2[:, :n], scalar=1.0 / D, in1=rs[:, :n], op0=ALU.mult, op1=ALU.subtract)
            ACT(rs[:, :n], rs[:, :n], AF.Sqrt, [rsU2, cU], [rsU2], bias=epsT[:, 0:1], scale=1.0)
            DVE("reciprocal", [rsU2], [rsU2], out=rs[:, :n], in_=rs[:, :n])
            for c in range(KC):
                t_, tu_ = tf()
                DVE("tensor_tensor", [hU[bi], muU], [tu_], out=t_[:, :n], in0=hT[:, c, c0:c0 + n], in1=mu[:, :n], op=ALU.subtract)
                DVE("tensor_tensor", [tu_, rsU2], [tu_], out=t_[:, :n], in0=t_[:, :n], in1=rs[:, :n], op=ALU.mult)
                ACT(hT[:, c, c0:c0 + n], t_[:, :n], AF.Silu, [tu_, cU], [hU[bi]], bias=vecT[:, c, V_CLNB:V_CLNB + 1], scale=vecT[:, c, V_CLNG:V_CLNG + 1])
        B.barrier()
        prod31 = yacc[0][:, 0:16 * 31].rearrange("p (c t) -> p c t", t=31)
        DVE("tensor_tensor", [zsU, cU, yU[0]], [yU[0]], out=prod31, in0=zcs[:, :, :], in1=vecT[:, :, V_CWDW:V_CWDW + 31], op=ALU.mult)
        DVE("tensor_reduce", [yU[0]], [ysU], out=ysm[:, 0:16], in_=prod31, axis=AX.X, op=ALU.add)
        DVE("tensor_tensor", [ysU, cU], [ysU], out=ysm[:, 0:16], in0=ysm[:, 0:16], in1=vecT[:, :, V_CBDW], op=ALU.add)
        col_ln(ysm[:, 0:16], [ysU], V_CLNG, V_CLNB, ysm[:, 16:32], [ysU], scr, scrU)
        ACT(hT[:, :, TP], ysm[:, 16:32], AF.Silu, [ysU], [hU[2]])
        out_proj_steps(w_out, 0, KC, src=hT, srcU=lambda kc, bi: hU[bi])
        run_steps()

    for i in range(4):
        if on("mix%d" % i):
            if i % 3 == 0:
                gmlp(i, i // 3)
            elif i % 3 == 1:
                dilattn(i)
            else:
                convmod(i)
        if on("xat%d" % i):
            xattn(i)
        if on("ffn%d" % i):
            ffn(i)

    arena_reset()
    stg = [carve(2048), carve(2048)]
    for t in range(8):
        fm_to_rows(stg, lambda kc, t=t: xT[:, kc, t * 128:(t + 1) * 128], lambda kc, t=t: [xU[kc][t // 4]], 128, y_p.ap()[t * 128:(t + 1) * 128, :])
    if "xs" not in SKIP:
        fm_to_rows(stg, lambda kc: xT[:, kc, TP:TP + 1], lambda kc: [xU[kc][2]], 1, y_s.ap())

    sp = B.engs["sp"]
    for ds in B.dss:
        if ds.count:
            sp.prog.append(("wait", ds.sem, ds.count))
    B.emit()
    return B


def t5_bucket_np(dist):
    import math
    max_exact = 16
    d = np.maximum(dist, 1).astype(np.float32)
    large = max_exact + (np.log(d / max_exact) / math.log(2048 / max_exact) * (32 - max_exact)).astype(np.int32)
    large = np.minimum(large, 31)
    return np.where(dist < max_exact, dist, large)


_CACHE = {}


def kernel(**inp):
    if "B" not in _CACHE:
        _CACHE["B"] = build_program()
    B = _CACHE["B"]
    f = lambda a: np.ascontiguousarray(a, dtype=np.float32)
    vec_rows = [inp["g_mix"], inp["g_xattn"], inp["g_mem"], inp["g_ffn"], inp["a_ln_g"], inp["a_ln_b"],
                inp["c_b_in"].reshape(2, D), inp["c_b_dw"], inp["c_ln_g"], inp["c_ln_b"], inp["c_w_dw"][0]]
    vecs = f(np.concatenate([np.asarray(v).reshape(-1, D) for v in vec_rows], axis=0))
    assert vecs.shape[0] == NV
    gsm = f(np.concatenate([inp["x_q_norm"], inp["x_k_norm"], inp["b_q_norm"][0], inp["b_k_norm"][0]], axis=0).T)
    ident = np.eye(128, dtype=np.float32)
    masku = np.triu(np.ones((128, 128), np.float32))
    selg = np.zeros((33, 9, 255), np.float32)
    u = np.arange(255)
    for g, dil in enumerate((1, 4, 16)):
        cur = np.where(u >= 127, t5_bucket_np(np.maximum(u - 127, 0) * dil), 32)
        prev = np.where(u <= 127, t5_bucket_np((u + 1) * dil), 32)
        third = prev if g < 2 else cur
        for v, idx in enumerate((cur, prev, third)):
            selg[idx, g * 3 + v, u] = 1.0
    selg = selg.reshape(33, 9 * 255)
    sels = np.zeros((32, 3, 128), np.float32)
    jj = np.arange(128)
    for g, dil in enumerate((1, 4, 16)):
        sels[t5_bucket_np((128 - jj) * dil), g, jj] = 1.0
    sels = sels.reshape(32, 384)
    shared = dict(vecs=vecs, gsm=gsm, relb=f(inp["rel_bias"]), ident=ident, masku=masku, selg=selg, sels=sels,
                  a_w_s=f(inp["a_w_s"]), a_b_s=f(inp["a_b_s"]).reshape(2, 2048),
                  a_w_in=f(inp["a_w_in"]), a_w_out=f(inp["a_w_out"]), b_w_qkv=f(inp["b_w_qkv"][0]), b_w_out=f(inp["b_w_out"][0]),
                  c_w_in=f(inp["c_w_in"][0]), c_w_out=f(inp["c_w_out"][0]), x_w_q=f(inp["x_w_q"]), x_w_kv=f(inp["x_w_kv"]),
                  x_w_o=f(inp["x_w_o"]), f_w_in=f(inp["f_w_in"]), f_w_out=f(inp["f_w_out"]))
    in_maps = []
    for c in range(NCORES):
        b, half = c // 2, c % 2
        m = dict(shared)
        m["x_p"] = f(inp["x_prompt"][b, half * TP:(half + 1) * TP])
        m["x_s"] = f(inp["x_sample"][c])
        m["mem"] = f(inp["mem_prompt"][b])
        m["flag"] = np.full((128, 1), float(half), np.float32)
        caches = ((inp["cache_b_k_w128"], inp["cache_b_v_w128"]), (inp["cache_b_k_w512"], inp["cache_b_v_w512"]),
                  (inp["cache_b_k_w2048"], inp["cache_b_v_w2048"]))
        for g, w in enumerate((128, 512, 2048)):
            m["ck%d" % g] = f(caches[g][0][0, c]).reshape(w, D)
            m["cv%d" % g] = f(caches[g][1][0, c]).reshape(w, D)
        m["cst"] = f(inp["state_c_conv"][0, c])
        m["cmk"] = f(inp["cache_mem_k"][:, c]).reshape(4, 256, 512)
        m["cmv"] = f(inp["cache_mem_v"][:, c]).reshape(4, 256, 512)
        in_maps.append(m)
    nrun = int(os.environ.get("MK_NCORES", NCORES))
    in_maps = [{k: m[k] for k in B.used_inputs} for m in in_maps]
    res = run_bass_kernel_spmd(B.nc, in_maps[:nrun], core_ids=list(range(nrun)))
    R = list(res.results) + [res.results[c % nrun] for c in range(nrun, NCORES)]
    o = lambda c, k: np.asarray(R[c][k], dtype=np.float32)
    y_prompt = np.stack([np.concatenate([o(2 * b, "y_p"), o(2 * b + 1, "y_p")], axis=0) for b in range(4)])
    y_sample = np.stack([o(c, "y_s") for c in range(8)])
    outs = [y_prompt, y_sample]
    for g, n in enumerate((128, 512, 2048)):
        for kv in ("bk", "bv"):
            if g < 2:
                a = np.stack([o(2 * b + 1, "%s%d_p" % (kv, g)) for b in range(4)])
            else:
                a = np.stack([np.concatenate([o(2 * b, "%s2_p" % kv), o(2 * b + 1, "%s2_p" % kv)], axis=0) for b in range(4)])
            outs.append(a.reshape(1, 4, n, 16, 128))
    outs.append(np.stack([o(2 * b + 1, "cconv_p") for b in range(4)])[None])
    outs.append(np.stack([o(2 * b, "memk_p") for b in range(4)], axis=1).reshape(4, 4, 256, 4, 128))
    outs.append(np.stack([o(2 * b, "memv_p") for b in range(4)], axis=1).reshape(4, 4, 256, 4, 128))
    for g, w in enumerate((128, 512, 2048)):
        for kv in ("bk", "bv"):
            outs.append(np.stack([o(c, "%s%d_s" % (kv, g)) for c in range(8)]).reshape(1, 8, w, 16, 128))
    outs.append(np.stack([o(c, "cconv_s") for c in range(8)])[None])
    outs.append(np.stack([o(c, "av_s") for c in range(8)], axis=1).reshape(2, 8, 1, D))
    return tuple(outs)
```

```python
import os
import numpy as np
from contextlib import ExitStack
import concourse.bass as bass
import concourse.mybir as mybir
from concourse.bass_utils import run_bass_kernel_spmd

F32 = mybir.dt.float32
BF16 = mybir.dt.bfloat16
AF = mybir.ActivationFunctionType
ALU = mybir.AluOpType
AX = mybir.AxisListType

D = 2048
KC = 16
TP = 1024
XC = TP + 1
NCORES = 8
FFN_H = 5632
EPS = 1e-6
SCALE = 128 ** -0.5
NSLOT = 2
PAIRS = [[0, 1], [2, 3], [4, 5], [6, 7]]
PAIRS_RUN = PAIRS[:int(os.environ.get("MK_NCORES", 8)) // 2]

V_GMIX, V_GXAT, V_GMEM, V_GFFN = 0, 4, 8, 12
V_ALNG, V_ALNB = 16, 18
V_CBIN, V_CBDW, V_CLNG, V_CLNB, V_CWDW = 20, 22, 23, 24, 25
NV = 56
S_XQ, S_XK, S_BQ, S_BK = 0, 4, 8, 11

BLKS = [(0, 512), (512, 512), (1024, 1)]

PLAN = os.environ.get("MK_PLAN", "")
SKIP = os.environ.get("MK_SKIP", "").split(",")


class Unit:
    __slots__ = ("w", "rs", "excl")

    def __init__(self, excl=False):
        self.w = None
        self.rs = {}
        self.excl = excl


class Eng:
    def __init__(self, name, sem):
        self.name = name
        self.sem = sem
        self.n = 0
        self.seen = {}
        self.prog = []


class DS:
    def __init__(self, sem):
        self.sem = sem
        self.count = 0


class Builder:
    def __init__(self):
        self.nc = bass.Bass("TRN2", target_bir_lowering=False)
        self.es = ExitStack()
        self.sems = []
        self.engs = {}
        self.dss = []
        self.nsb = 0

    def new_sem(self, name):
        s = self.es.enter_context(self.nc.semaphore(name))
        self.sems.append(s)
        return len(self.sems) - 1

    def new_ds(self):
        ds = DS(self.new_sem("d%d" % len(self.dss)))
        self.dss.append(ds)
        return ds

    def sb(self, shape, dt, name=None):
        self.nsb += 1
        return self.es.enter_context(self.nc.sbuf_tensor("s_" + (name or ("sb%d" % self.nsb)), list(shape), dt))

    def dram(self, name, shape, dt, kind=None):
        if kind is None:
            return self.nc.dram_tensor(name, list(shape), dt)
        return self.nc.dram_tensor(name, list(shape), dt, kind=kind)

    def _waits(self, eng, R, W):
        need = {}
        for u in R:
            if u.w is not None and need.get(u.w[0], 0) < u.w[1]:
                need[u.w[0]] = u.w[1]
            if u.excl:
                for s, v in u.rs.items():
                    if s != eng.sem and need.get(s, 0) < v:
                        need[s] = v
        for u in W:
            if u.w is not None and need.get(u.w[0], 0) < u.w[1]:
                need[u.w[0]] = u.w[1]
            for s, v in u.rs.items():
                if need.get(s, 0) < v:
                    need[s] = v
        for s, v in need.items():
            if eng.name == "pe" and s == eng.sem:
                continue
            if eng.seen.get(s, 0) < v:
                eng.prog.append(("wait", s, v))
                eng.seen[s] = v

    def _mark(self, tok, R, W):
        for u in R:
            if u.rs.get(tok[0], 0) < tok[1]:
                u.rs[tok[0]] = tok[1]
        for u in W:
            u.w = tok
            u.rs = {}

    def op(self, eng, meth, R=(), W=(), signal=True, **kw):
        eng = self.engs[eng]
        self._waits(eng, R, W)
        if signal:
            eng.n += 1
            tok = (eng.sem, eng.n)
            eng.prog.append(("ins", meth, kw, eng.sem))
        else:
            tok = (eng.sem, eng.n + 1)
            eng.prog.append(("ins", meth, kw, None))
        self._mark(tok, R, W)

    def dma(self, q, out, in_, ds, R=(), W=(), slow=False):
        eng = self.engs[q]
        self._waits(eng, R, W)
        ds.count += 16
        tok = (ds.sem, ds.count)
        eng.prog.append(("dma", out, in_, ds.sem, slow))
        self._mark(tok, R, W)

    def cc(self, ins, outs, groups, sem, R=(), W=(), count=1):
        eng = self.engs["pool"]
        self._waits(eng, R, W)
        eng.prog.append(("cc", ins, outs, groups, sem))
        self._mark((sem, count), R, W)

    def barrier(self):
        sp = self.engs["sp"]
        for ds in self.dss:
            if ds.count and sp.seen.get(ds.sem, 0) < ds.count:
                sp.prog.append(("wait", ds.sem, ds.count))
                sp.seen[ds.sem] = ds.count
        for e in self.engs.values():
            for x in self.engs.values():
                if x is e or x.n == 0:
                    continue
                if e.seen.get(x.sem, 0) < x.n:
                    e.prog.append(("wait", x.sem, x.n))
                    e.seen[x.sem] = x.n
        sp.n += 1
        sp.prog.append(("seminc", sp.sem))
        for e in self.engs.values():
            if e is not sp:
                e.prog.append(("wait", sp.sem, sp.n))
                e.seen[sp.sem] = sp.n

    def emit(self):
        nc = self.nc
        sems = self.sems
        with nc.Block() as block:
            def run(eng):
                def body(h):
                    for it in eng.prog:
                        if it[0] == "wait":
                            h.wait_ge(sems[it[1]], it[2])
                        elif it[0] == "ins":
                            ins = getattr(h, it[1])(**it[2])
                            if it[3] is not None:
                                ins.then_inc(sems[it[3]], 1)
                        elif it[0] == "dma":
                            if it[4]:
                                h.dma_start(out=it[1], in_=it[2], allow_slow_non_contiguous=True).then_inc(sems[it[3]], 16)
                            else:
                                h.dma_start(out=it[1], in_=it[2]).then_inc(sems[it[3]], 16)
                        elif it[0] == "cc":
                            h.collective_compute("AllGather", ALU.bypass, replica_groups=it[3], ins=[it[1]], outs=[it[2]]).then_inc(sems[it[4]])
                        elif it[0] == "seminc":
                            h.sem_inc(sems[it[1]], 1)
                        elif it[0] == "raw":
                            it[1](h)
                return body
            block.sync(run(self.engs["sp"]))
            block.scalar(run(self.engs["act"]))
            block.vector(run(self.engs["dve"]))
            block.tensor(run(self.engs["pe"]))
            block.gpsimd(run(self.engs["pool"]))


def build_program():
    B = Builder()
    nc = B.nc
    for name in ("pe", "act", "dve", "pool", "sp"):
        B.engs[name] = Eng(name, B.new_sem("e_" + name))
    plan = [s for s in PLAN.split(",") if s]

    def on(tag):
        return (not plan) or (tag in plan)

    class LazyIn:
        def __init__(self, name, shape):
            self.name, self.shape, self.t = name, shape, None

        def handle(self):
            if self.t is None:
                self.t = B.dram(self.name, self.shape, F32, kind="ExternalInput")
                B.used_inputs.append(self.name)
            return self.t

        def ap(self):
            return self.handle().ap()

    B.used_inputs = []

    def din(name, shape):
        return LazyIn(name, shape)

    def dout(name, shape):
        return B.dram(name, shape, F32, kind="ExternalOutput")

    x_p = din("x_p", [TP, D]); x_s = din("x_s", [1, D]); mem = din("mem", [256, D])
    vecs = din("vecs", [NV, D]); gsm = din("gsm", [128, 14]); relb = din("relb", [32, 48])
    flag = din("flag", [128, 1]); ident_d = din("ident", [128, 128]); masku_d = din("masku", [128, 128])
    selg = din("selg", [33, 9 * 255]); sels_d = din("sels", [32, 3 * 128])
    a_w_s = din("a_w_s", [2, 16, 128, 128]); a_b_s = din("a_b_s", [2, 16 * 128])
    ck = [din("ck%d" % g, [w, D]) for g, w in enumerate((128, 512, 2048))]
    cv = [din("cv%d" % g, [w, D]) for g, w in enumerate((128, 512, 2048))]
    cst = din("cst", [30, D]); cmk = din("cmk", [4, 256, 512]); cmv = din("cmv", [4, 256, 512])
    a_w_in = din("a_w_in", [2, D, 4096]); a_w_out = din("a_w_out", [2, D, D])
    b_w_qkv = din("b_w_qkv", [D, 18432]); b_w_out = din("b_w_out", [D, D])
    c_w_in = din("c_w_in", [D, 4096]); c_w_out = din("c_w_out", [D, D])
    x_w_q = din("x_w_q", [4, D, 512]); x_w_kv = din("x_w_kv", [4, D, 1024]); x_w_o = din("x_w_o", [4, 512, D])
    f_w_in = din("f_w_in", [4, D, 2 * FFN_H]); f_w_out = din("f_w_out", [4, FFN_H, D])

    y_p = dout("y_p", [TP, D]); y_s = dout("y_s", [1, D])
    bk_p = [dout("bk%d_p" % g, [n, D]) for g, n in enumerate((128, 512, 1024))]
    bv_p = [dout("bv%d_p" % g, [n, D]) for g, n in enumerate((128, 512, 1024))]
    cconv_p = dout("cconv_p", [30, D]); memk_p = dout("memk_p", [4, 256, 512]); memv_p = dout("memv_p", [4, 256, 512])
    bk_s = [dout("bk%d_s" % g, [w, D]) for g, w in enumerate((128, 512, 2048))]
    bv_s = [dout("bv%d_s" % g, [w, D]) for g, w in enumerate((128, 512, 2048))]
    cconv_s = dout("cconv_s", [30, D]); av_s = dout("av_s", [2, D])

    xT = B.sb([128, KC, XC], F32, "xT"); xU = [[Unit() for _ in BLKS] for _ in range(KC)]
    hT2 = B.sb([128, KC * XC], BF16, "hT"); hU = [Unit() for _ in BLKS]
    hT = hT2[:, :].rearrange("p (k t) -> p k t", t=XC)
    mid = B.sb([128, KC, XC], BF16, "mid"); mU = [[Unit() for _ in BLKS] for _ in range(KC)]
    wsl = [B.sb([128, 8192], BF16, "w%d" % i) for i in range(NSLOT)]
    wU = [Unit() for _ in range(NSLOT)]; wDS = [B.new_ds() for _ in range(NSLOT)]
    ident = B.sb([128, 128], F32, "ident"); onesb = B.sb([128, 128], BF16, "onesb")
    masku = B.sb([128, 128], F32, "masku")
    vecT = B.sb([128, KC, NV], F32, "vecT"); gs = B.sb([128, 14], F32, "gs")
    flg = B.sb([128, 1], F32, "flg"); epsT = B.sb([128, 1], F32, "epsT")
    cU = Unit()
    memhat = B.sb([128, KC, 256], BF16, "memhat"); mhU = Unit()
    ps = [B.es.enter_context(nc.psum_tensor("ps%d" % i, [128, 512], F32)) for i in range(8)]
    pU = [Unit(excl=True) for _ in range(8)]
    AW = int(os.environ.get("MK_AW", 8850))
    arena = B.sb([128, AW], F32, "arena")
    NT = 4
    tmpf = [arena[:, i * 512:(i + 1) * 512] for i in range(NT)]; tU = [Unit() for _ in range(NT)]
    tmpb = [arena[:, NT * 512 + i * 256:NT * 512 + (i + 1) * 256].bitcast(BF16) for i in range(NT)]; bU = [Unit() for _ in range(NT)]
    A0 = NT * 768
    st = {"ps": 0, "tf": 0, "tb": 0, "aoff": 0, "stg": 0}

    def psum():
        i = st["ps"]; st["ps"] = (i + 1) % 8
        return ps[i], pU[i]

    def tf():
        i = st["tf"]; st["tf"] = (i + 1) % NT
        return tmpf[i], tU[i]

    def tb():
        i = st["tb"]; st["tb"] = (i + 1) % NT
        return tmpb[i], bU[i]

    def arena_reset():
        B.barrier()
        st["aoff"] = A0

    def carve(words, dt=F32):
        o = st["aoff"]; st["aoff"] = o + words
        assert st["aoff"] <= AW, ("arena overflow", st["aoff"])
        a = arena[:, o:o + words]
        return a if dt == F32 else a.bitcast(dt)

    op, dma = B.op, B.dma

    def MM(out, lhsT, rhs, start, stop, R, W, signal=True):
        op("pe", "matmul", R=R, W=W, signal=signal, out=out, lhsT=lhsT, rhs=rhs, start=start, stop=stop)

    def TR(out, in_, idn, R, W, signal=True):
        op("pe", "transpose", R=R, W=W, signal=signal, out=out, in_=in_, identity=idn)

    def ACT(out, in_, func, R, W, **kw):
        op("act", "activation", R=R, W=W, out=out, in_=in_, func=func, **kw)

    def DVE(meth, R, W, **kw):
        op("dve", meth, R=R, W=W, **kw)

    def COPY(eng, out, in_, R, W):
        if eng == "act":
            ACT(out, in_, AF.Copy, R, W)
        else:
            DVE("tensor_copy", R, W, out=out, in_=in_)

    cds = B.new_ds()
    dma("sp", ident[:], ident_d.ap(), cds, W=[cU])
    dma("sp", masku[:], masku_d.ap(), cds, W=[cU])
    dma("sp", gs[:], gsm.ap(), cds, W=[cU])
    dma("sp", flg[:], flag.ap(), cds, W=[cU])
    DVE("memset", [], [cU], ap=onesb[:], constant=1.0)
    DVE("memset", [], [cU], ap=epsT[:], constant=EPS)

    ldU = [Unit(), Unit()]; ldDS = [B.new_ds(), B.new_ds()]

    def next_stg():
        i = st["stg"]; st["stg"] ^= 1
        return i

    def rows_to_fm(stg, src_ap, R, dst_fn, dstW, single=False):
        i = 0 if single else next_stg()
        s = stg[i]
        dma("sp", s[0:R, :], src_ap, ldDS[i], W=[ldU[i]])
        for g4 in range(4):
            p, u = psum()
            for j in range(4):
                kc = g4 * 4 + j
                TR(p[:, j * 128:j * 128 + R], s[0:R, kc * 128:(kc + 1) * 128], ident[0:R, 0:R], [ldU[i], cU], [u], signal=(j == 3))
            for j in range(4):
                kc = g4 * 4 + j
                COPY("act" if g4 % 2 else "dve", dst_fn(kc), p[:, j * 128:j * 128 + R], [u], dstW(kc))

    def fm_to_rows(stg, src_fn, srcR, R, dst_ap, single=False):
        i = 0 if single else next_stg()
        s = stg[i]
        for g4 in range(4):
            p, u = psum()
            for j in range(4):
                kc = g4 * 4 + j
                TR(p[0:R, j * 128:(j + 1) * 128], src_fn(kc), ident[:, :], list(srcR(kc)) + [cU], [u], signal=(j == 3))
            COPY("act" if g4 % 2 else "dve", s[0:R, g4 * 512:(g4 + 1) * 512], p[0:R, :], [u], [ldU[i]])
        dma("sp", dst_ap, s[0:R, :], ldDS[i], R=[ldU[i]])

    arena_reset()
    stg = [carve(2048), carve(2048)]
    if "vecs" not in SKIP:
        rows_to_fm(stg, vecs.ap(), NV, lambda kc: vecT[:, kc, :], lambda kc: [cU])
    for t in range(8):
        rows_to_fm(stg, x_p.ap()[t * 128:(t + 1) * 128, :], 128,
                   lambda kc, t=t: xT[:, kc, t * 128:(t + 1) * 128], lambda kc, t=t: [xU[kc][t // 4]])
    if "xs" not in SKIP:
        rows_to_fm(stg, x_s.ap(), 1, lambda kc: xT[:, kc, TP:TP + 1], lambda kc: [xU[kc][2]])

    def rstd_from_psum(p, u, n, inv_n):
        r, ru = tf()
        ACT(r[:, :n], p[:, :n], AF.Ln, [u, cU], [ru], bias=epsT[:, 0:1], scale=inv_n)
        ACT(r[:, :n], r[:, :n], AF.Exp, [ru], [ru], scale=-0.5)
        return r, ru

    def sumsq_fm(src_fn, srcU, nk, n):
        p, u = psum()
        for kc in range(nk):
            s, su = tb()
            ACT(s[:, :n], src_fn(kc), AF.Square, list(srcU(kc)), [su])
            MM(p[:, :n], onesb[:, :], s[:, :n], kc == 0, kc == nk - 1, [su, cU], [u])
        return p, u

    def rmsnorm(vrow):
        for bi, (c0, n) in enumerate(BLKS):
            p, u = sumsq_fm(lambda kc: xT[:, kc, c0:c0 + n], lambda kc: [xU[kc][bi]], KC, n)
            r, ru = rstd_from_psum(p, u, n, 1.0 / D)
            for kc in range(KC):
                DVE("scalar_tensor_tensor", [xU[kc][bi], ru, cU], [hU[bi]], out=hT[:, kc, c0:c0 + n], in0=xT[:, kc, c0:c0 + n],
                    scalar=vecT[:, kc, vrow:vrow + 1], in1=r[:, :n], op0=ALU.mult, op1=ALU.mult)

    def load_memhat():
        mT = carve(KC * 64).rearrange("p (kc t) -> p kc t", t=64); mTU = [Unit() for _ in range(KC)]
        for t in range(4):
            rows_to_fm(stg, mem.ap()[t * 64:(t + 1) * 64, :], 64, lambda kc: mT[:, kc, :], lambda kc: [mTU[kc]])
            p, u = sumsq_fm(lambda kc: mT[:, kc, :], lambda kc: [mTU[kc]], KC, 64)
            r, ru = rstd_from_psum(p, u, 64, 1.0 / D)
            for kc in range(KC):
                DVE("tensor_tensor", [mTU[kc], ru], [mhU], out=memhat[:, kc, t * 64:(t + 1) * 64], in0=mT[:, kc, :], in1=r[:, :64], op=ALU.mult)
    if "mem" not in SKIP:
        load_memhat()

    steps = []

    def wstep(loads, compute, post_issue=None):
        steps.append((loads, compute, post_issue))

    def wsrc(w_ap, k0, nk, c0, ncol):
        return w_ap[k0 * 128:(k0 + nk) * 128, c0:c0 + ncol].rearrange("(kc p) n -> p kc n", p=128)

    def slot3(slot, nk, ncol):
        return slot[:, 0:nk * ncol].rearrange("p (kc n) -> p kc n", n=ncol)

    def run_steps():
        issued = 0
        for k in range(len(steps)):
            while issued < min(len(steps), k + NSLOT):
                si = issued % NSLOT
                for dst_fn, src in steps[issued][0]:
                    dma("pool", dst_fn(wsl[si]), src, wDS[si], W=[wU[si]])
                if steps[issued][2] is not None:
                    steps[issued][2]()
                issued += 1
            steps[k][1](wsl[k % NSLOT], wU[k % NSLOT])
        steps.clear()

    def mm_fm(p, u, w3, wu, nk, oc, in_t, inU, c0, n):
        for kc in range(nk):
            MM(p[:, :n], w3[:, kc, oc * 128:(oc + 1) * 128], in_t[:, kc, c0:c0 + n], kc == 0, kc == nk - 1, [wu] + list(inU(kc)), [u], signal=(kc == nk - 1))

    def resid_add(p, u, oc, bi, c0, n):
        DVE("tensor_tensor", [u], [xU[oc][bi]], out=xT[:, oc, c0:c0 + n], in0=p[:, :n], in1=xT[:, oc, c0:c0 + n], op=ALU.add)

    def out_proj_steps(w_ap, k0, nk, src=None, srcU=None):
        src = mid if src is None else src
        srcU = (lambda kc, bi: mU[kc][bi]) if srcU is None else srcU

        def mk(cb):
            def comp(slot, wu):
                w3 = slot3(slot, nk, 512)
                for o4 in range(4):
                    for bi, (c0, n) in enumerate(BLKS):
                        p, u = psum()
                        mm_fm(p, u, w3, wu, nk, o4, src, lambda kc: [srcU(kc, bi)], c0, n)
                        resid_add(p, u, cb * 4 + o4, bi, c0, n)
            return comp
        for cb in range(4):
            wstep([(lambda s: slot3(s, nk, 512), wsrc(w_ap, k0, nk, cb * 512, 512))], mk(cb))

    def ffn(i):
        rmsnorm(V_GFFN + i)
        w_in = f_w_in.ap()[i]; w_out = f_w_out.ap()[i]

        def mk_in(c2, c_lo):
            def comp(slot, wu):
                w3 = slot3(slot, KC, 512)
                for j in range(2):
                    lc = c2 + j - c_lo
                    for bi, (c0, n) in enumerate(BLKS):
                        pg, ug = psum(); pu, uu = psum()
                        mm_fm(pg, ug, w3, wu, KC, j, hT, lambda kc: [hU[bi]], c0, n)
                        mm_fm(pu, uu, w3, wu, KC, 2 + j, hT, lambda kc: [hU[bi]], c0, n)
                        s, su = tf()
                        ACT(s[:, :n], pg[:, :n], AF.Silu, [ug], [su])
                        DVE("tensor_tensor", [uu, su], [mU[lc][bi]], out=mid[:, lc, c0:c0 + n], in0=pu[:, :n], in1=s[:, :n], op=ALU.mult)
            return comp
        for c_lo, c_hi in ((0, 16), (16, 32), (32, 44)):
            for c2 in range(c_lo, c_hi, 2):
                wstep([(lambda s: slot3(s, KC, 512)[:, :, 0:256], wsrc(w_in, 0, KC, c2 * 128, 256)),
                       (lambda s: slot3(s, KC, 512)[:, :, 256:512], wsrc(w_in, 0, KC, FFN_H + c2 * 128, 256))], mk_in(c2, c_lo))
            out_proj_steps(w_out, c_lo, c_hi - c_lo)
        run_steps()

    def head_norm(p, u, n, gcol, out_bf, outW, out_f32=None, out32W=()):
        q, qu = tf()
        ACT(q[:, :n], p[:, :n], AF.Copy, [u], [qu])
        s, su = tb()
        DVE("tensor_tensor", [qu], [su], out=s[:, :n], in0=q[:, :n], in1=q[:, :n], op=ALU.mult)
        p2, u2 = psum()
        MM(p2[:, :n], onesb[:, :], s[:, :n], True, True, [su, cU], [u2])
        r, ru = rstd_from_psum(p2, u2, n, 1.0 / 128)
        if out_f32 is not None:
            DVE("scalar_tensor_tensor", [qu, ru, cU], list(out32W), out=out_f32, in0=q[:, :n], scalar=gs[:, gcol:gcol + 1], in1=r[:, :n],
                op0=ALU.mult, op1=ALU.mult)
            ACT(out_bf, out_f32, AF.Copy, list(out32W), list(outW))
        else:
            DVE("scalar_tensor_tensor", [qu, ru, cU], list(outW), out=out_bf, in0=q[:, :n], scalar=gs[:, gcol:gcol + 1], in1=r[:, :n],
                op0=ALU.mult, op1=ALU.mult)

    def xattn(i):
        arena_reset()
        def qT(hh, c0, n):
            return mid[:, 4 + hh, c0:c0 + n]
        kTp = mid[:, 8, 0:1024].rearrange("p (h t) -> p h t", t=256); kpU = [mU[8][0], mU[8][1]]
        kTs = mid[:, 9, 0:1024].rearrange("p (h t) -> p h t", t=256); ksU = [mU[9][0], mU[9][1]]
        vp = mid[:, 10, 0:1024].rearrange("p (m c) -> p m c", c=512); vpU = [mU[10][0], mU[10][1]]
        vs = mid[:, 11, 0:1024].rearrange("p (m c) -> p m c", c=512); vsU = [mU[11][0], mU[11][1]]
        kst = carve(1024).rearrange("p (m c) -> p m c", c=512); kstU = Unit(); kstDS = B.new_ds()
        vsDS = B.new_ds()
        kf = carve(4 * 256).rearrange("p (h t) -> p h t", t=256); kfU = [Unit() for _ in range(4)]
        ost = [carve(512), carve(512)]; ostU = [Unit(), Unit()]; ostDS = [B.new_ds(), B.new_ds()]
        oi = [0]

        def next_o():
            oi[0] ^= 1
            return oi[0]

        dma("sp", kst[:, :, :], cmk.ap()[i].rearrange("(m p) c -> p m c", p=128), kstDS, W=[kstU])
        for hh in range(4):
            p, u = psum()
            for m in range(2):
                TR(p[:, m * 128:(m + 1) * 128], kst[:, m, hh * 128:(hh + 1) * 128], ident[:, :], [kstU, cU], [u], signal=(m == 1))
            ACT(kTs[:, hh, :], p[:, 0:256], AF.Copy, [u], ksU)
        dma("pool", vs[:, :, :], cmv.ap()[i].rearrange("(m p) c -> p m c", p=128), vsDS, W=vsU)

        rmsnorm(V_GXAT + i)

        def comp_q(slot, wu):
            w3 = slot3(slot, KC, 512)
            for hh in range(4):
                for bi, (c0, n) in enumerate(BLKS):
                    p, u = psum()
                    mm_fm(p, u, w3, wu, KC, hh, hT, lambda kc: [hU[bi]], c0, n)
                    head_norm(p, u, n, S_XQ + i, qT(hh, c0, n), [mU[4 + hh][bi]])
        wstep([(lambda s: slot3(s, KC, 512), wsrc(x_w_q.ap()[i], 0, KC, 0, 512))], comp_q)

        def fold_gain(w3, wu):
            g = vecT[:, :, V_GMEM + i:V_GMEM + i + 1].to_broadcast([128, KC, 512])
            DVE("tensor_tensor", [wu, cU], [wu], out=w3, in0=w3, in1=g, op=ALU.mult)

        def comp_k(slot, wu):
            w3 = slot3(slot, KC, 512)
            fold_gain(w3, wu)
            for hh in range(4):
                p, u = psum()
                mm_fm(p, u, w3, wu, KC, hh, memhat, lambda kc: [mhU], 0, 256)
                head_norm(p, u, 256, S_XK + i, kTp[:, hh, :], kpU, out_f32=kf[:, hh, :], out32W=[kfU[hh]])
            for m in range(2):
                si = next_o()
                p, u = psum()
                for hh in range(4):
                    TR(p[:, hh * 128:(hh + 1) * 128], kf[:, hh, m * 128:(m + 1) * 128], ident[:, :], [kfU[hh], cU], [u], signal=(hh == 3))
                DVE("tensor_copy", [u], [ostU[si]], out=ost[si][:, :], in_=p[:, :])
                dma("sp", memk_p.ap()[i, m * 128:(m + 1) * 128, :], ost[si][:, :], ostDS[si], R=[ostU[si]])
        wstep([(lambda s: slot3(s, KC, 512), wsrc(x_w_kv.ap()[i], 0, KC, 0, 512))], comp_k)

        def comp_v(slot, wu):
            w3 = slot3(slot, KC, 512)
            fold_gain(w3, wu)
            for m in range(2):
                si = next_o()
                p, u = psum()
                for kc in range(KC):
                    MM(p[:, :], memhat[:, kc, m * 128:(m + 1) * 128], w3[:, kc, :], kc == 0, kc == KC - 1, [wu, mhU], [u], signal=(kc == KC - 1))
                DVE("tensor_copy", [u], [ostU[si]], out=ost[si][:, :], in_=p[:, :])
                ACT(vp[:, m, :], ost[si][:, :], AF.Copy, [ostU[si]], vpU)
                dma("sp", memv_p.ap()[i, m * 128:(m + 1) * 128, :], ost[si][:, :], ostDS[si], R=[ostU[si]])
        wstep([(lambda s: slot3(s, KC, 512), wsrc(x_w_kv.ap()[i], 0, KC, 512, 512))], comp_v)

        def comp_o(slot, wu):
            for bi, (c0, n) in enumerate(BLKS):
                kT, kU, vv, vU = (kTs, ksU, vs, vsU) if bi == 2 else (kTp, kpU, vp, vpU)
                for hh in range(4):
                    po, uo = psum(); pd, ud = psum()
                    for m in range(2):
                        p, u = psum()
                        MM(p[:, :n], kT[:, hh, m * 128:(m + 1) * 128], qT(hh, c0, n), True, True, kU + [mU[4 + hh][bi]], [u])
                        e, eu = tb()
                        ACT(e[:, :n], p[:, :n], AF.Exp, [u], [eu], scale=SCALE)
                        MM(po[:, :n], vv[:, m, hh * 128:(hh + 1) * 128], e[:, :n], m == 0, m == 1, vU + [eu], [uo])
                        MM(pd[:, :n], onesb[:, :], e[:, :n], m == 0, m == 1, [eu, cU], [ud])
                    r, ru = tf()
                    ACT(r[:, :n], pd[:, :n], AF.Ln, [ud], [ru])
                    ACT(r[:, :n], r[:, :n], AF.Exp, [ru], [ru], scale=-1.0)
                    DVE("tensor_tensor", [uo, ru], [mU[hh][bi]], out=mid[:, hh, c0:c0 + n], in0=po[:, :n], in1=r[:, :n], op=ALU.mult)
            w3 = slot[:, 0:4 * D].rearrange("p (kc n) -> p kc n", n=D)
            for oc in range(KC):
                for bi, (c0, n) in enumerate(BLKS):
                    p, u = psum()
                    mm_fm(p, u, w3, wu, 4, oc, mid, lambda kc: [mU[kc][bi]], c0, n)
                    resid_add(p, u, oc, bi, c0, n)
        wstep([(lambda s: s[:, 0:4 * D].rearrange("p (kc n) -> p kc n", n=D), wsrc(x_w_o.ap()[i], 0, 4, 0, D))], comp_o)
        run_steps()

    def gmlp(i, j):
        arena_reset()
        w_in = a_w_in.ap()[j]; w_out = a_w_out.ap()[j]
        NTL = 2
        gv = carve(NTL * D // 2, BF16).rearrange("p (t c) -> p t c", c=D); gvU = [Unit() for _ in range(NTL)]
        wsT = carve(16 * 128 // 2, BF16).rearrange("p (g q) -> p g q", q=128); wsU = Unit()
        Cb = carve(16 * 128).rearrange("p (g q) -> p g q", q=128); CU = Unit(); CDS = B.new_ds()
        wst = carve(512).rearrange("p (g q) -> p g q", q=128); wstU = Unit(); wstDS = B.new_ds()
        sm = carve(64); smU = Unit(); smDS = B.new_ds()
        ws00, bs0, gvs, vln = sm[:, 0:16], sm[:, 16:32], sm[:, 32:48], sm[:, 48:64]
        stat = carve(16); statU = Unit()
        rmsnorm(V_GMIX + i)
        dma("sp", Cb[:, :, :], a_b_s.ap()[j].partition_broadcast(128).rearrange("p (g q) -> p g q", q=128), CDS, W=[CU])
        dma("sp", ws00, bass.AP(a_w_s.handle(), j * 16 * 16384, [[0, 128], [16384, 16]]), smDS, W=[smU], slow=True)
        dma("sp", bs0, bass.AP(a_b_s.handle(), j * 2048, [[0, 128], [128, 16]]), smDS, W=[smU], slow=True)
        for g4 in range(4):
            dma("sp", wst[:, :, :], a_w_s.ap()[j, g4 * 4:(g4 + 1) * 4].rearrange("g p q -> p g q"), wstDS, W=[wstU])
            p, u = psum()
            for k in range(4):
                TR(p[:, k * 128:(k + 1) * 128], wst[:, k, :], ident[:, :], [wstU, cU], [u], signal=(k == 3))
            for k in range(4):
                g = g4 * 4 + k
                DVE("tensor_tensor", [u, cU], [wsU], out=wsT[:, g, :], in0=p[:, k * 128:(k + 1) * 128], in1=masku[:, :], op=ALU.mult)
        for g4 in range(4):
            p, u = psum()
            for k in range(4):
                g = g4 * 4 + k
                MM(p[:, k * 128:(k + 1) * 128], onesb[:, :], wsT[:, g, :], True, True, [wsU, cU], [u], signal=(k == 3))
            for k in range(4):
                g = g4 * 4 + k
                DVE("scalar_tensor_tensor", [u, CU, cU], [CU], out=Cb[:, g, :], in0=p[:, k * 128:(k + 1) * 128], scalar=vecT[:, g, V_ALNB + j:V_ALNB + j + 1],
                    in1=Cb[:, g, :], op0=ALU.mult, op1=ALU.add)

        def mk_v(cb, t0, last):
            def comp(slot, wu):
                w3 = slot3(slot, KC, 512)
                for tl in range(NTL):
                    t = t0 + tl
                    p, u = psum()
                    for kc in range(KC):
                        MM(p[:, :], hT[:, kc, t * 128:(t + 1) * 128], w3[:, kc, :], kc == 0, kc == KC - 1, [wu, hU[t // 4]], [u], signal=(kc == KC - 1))
                    ACT(gv[:, tl, cb * 512:(cb + 1) * 512], p[:, :], AF.Gelu_apprx_tanh, [u], [gvU[tl]])
                if not last:
                    return
                for tl in range(NTL):
                    t = t0 + tl
                    bi = t // 4
                    j1, j1u = tb(); j2, j2u = tb()
                    s1 = stat[:, 0:1]; s2 = stat[:, 1:2]; mu = stat[:, 2:3]; var = stat[:, 3:4]; rs = stat[:, 4:5]; nb = stat[:, 5:6]
                    DVE("memset", [], [statU], ap=stat[:, 8:16], constant=0.0)
                    for q4 in range(4):
                        ACT(j1[:, :], gv[:, tl, q4 * 512:(q4 + 1) * 512], AF.Copy, [gvU[tl]], [j1u, statU], accum_out=stat[:, 8 + q4:9 + q4])
                        ACT(j2[:, :], gv[:, tl, q4 * 512:(q4 + 1) * 512], AF.Square, [gvU[tl]], [j2u, statU], accum_out=stat[:, 12 + q4:13 + q4])
                    DVE("tensor_reduce", [statU], [statU], out=s1, in_=stat[:, 8:12], axis=AX.X, op=ALU.add)
                    DVE("tensor_reduce", [statU], [statU], out=s2, in_=stat[:, 12:16], axis=AX.X, op=ALU.add)
                    DVE("tensor_scalar_mul", [statU], [statU], out=mu, in0=s1, scalar1=1.0 / D)
                    DVE("tensor_tensor", [statU], [statU], out=var, in0=mu, in1=mu, op=ALU.mult)
                    DVE("scalar_tensor_tensor", [statU], [statU], out=var, in0=s2, scalar=1.0 / D, in1=var, op0=ALU.mult, op1=ALU.subtract)
                    ACT(rs, var, AF.Sqrt, [statU, cU], [statU], bias=epsT[:, 0:1], scale=1.0)
                    DVE("reciprocal", [statU], [statU], out=rs, in_=rs)
                    DVE("scalar_tensor_tensor", [statU], [statU], out=nb, in0=mu, scalar=-1.0, in1=rs, op0=ALU.mult, op1=ALU.mult)
                    ACT(gv[:, tl, :], gv[:, tl, :], AF.Identity, [gvU[tl], statU], [gvU[tl]], bias=nb, scale=rs)
                    for g4 in range(4):
                        p, u = psum()
                        for k in range(4):
                            g = g4 * 4 + k
                            MM(p[:, k * 128:(k + 1) * 128], gv[:, tl, g * 128:(g + 1) * 128], wsT[:, g, :], True, True, [gvU[tl], wsU], [u], signal=(k == 3))
                        for k in range(4):
                            g = g4 * 4 + k
                            DVE("scalar_tensor_tensor", [u, CU, cU], [mU[g][bi]], out=mid[:, g, t * 128:(t + 1) * 128], in0=p[:, k * 128:(k + 1) * 128],
                                scalar=vecT[:, g, V_ALNG + j:V_ALNG + j + 1], in1=Cb[:, g, :], op0=ALU.mult, op1=ALU.add)
            return comp
        for t0 in range(0, 8, NTL):
            for cb in range(4):
                wstep([(lambda s: slot3(s, KC, 512), wsrc(w_in, 0, KC, D + cb * 512, 512))], mk_v(cb, t0, cb == 3))

        def mk_vs(cb):
            def comp(slot, wu):
                w3 = slot3(slot, KC, 512)
                for o4 in range(4):
                    p, u = psum()
                    mm_fm(p, u, w3, wu, KC, o4, hT, lambda kc: [hU[2]], TP, 1)
                    ACT(gvs[:, cb * 4 + o4:cb * 4 + o4 + 1], p[:, 0:1], AF.Gelu_apprx_tanh, [u], [smU])
                if cb != 3:
                    return
                sq, squ = tf()
                DVE("tensor_copy", [smU], [squ], out=sq[:, 0:16], in_=gvs)
                DVE("tensor_tensor", [smU], [squ], out=sq[:, 16:32], in0=gvs, in1=gvs, op=ALU.mult)
                hb, hbu = tb(); lb_, lbu = tb()
                DVE("tensor_copy", [squ], [hbu], out=hb[:, 0:32], in_=sq[:, 0:32])
                DVE("tensor_tensor", [squ, hbu], [squ], out=sq[:, 32:64], in0=sq[:, 0:32], in1=hb[:, 0:32], op=ALU.subtract)
                DVE("tensor_copy", [squ], [lbu], out=lb_[:, 0:32], in_=sq[:, 32:64])
                p, u = psum()
                MM(p[:, 0:32], onesb[:, :], hb[:, 0:32], True, False, [hbu, cU], [u], signal=False)
                MM(p[:, 0:32], onesb[:, :], lb_[:, 0:32], False, True, [lbu, cU], [u])
                s1 = stat[:, 0:1]; s2 = stat[:, 1:2]; mu = stat[:, 2:3]; var = stat[:, 3:4]; rs = stat[:, 4:5]; nb = stat[:, 5:6]
                DVE("tensor_reduce", [u], [statU], out=s1, in_=p[:, 0:16], axis=AX.X, op=ALU.add)
                DVE("tensor_reduce", [u], [statU], out=s2, in_=p[:, 16:32], axis=AX.X, op=ALU.add)
                DVE("tensor_scalar_mul", [statU], [statU], out=mu, in0=s1, scalar1=1.0 / D)
                DVE("tensor_tensor", [statU], [statU], out=var, in0=mu, in1=mu, op=ALU.mult)
                DVE("scalar_tensor_tensor", [statU], [statU], out=var, in0=s2, scalar=1.0 / D, in1=var, op0=ALU.mult, op1=ALU.subtract)
                ACT(rs, var, AF.Sqrt, [statU, cU], [statU], bias=epsT[:, 0:1], scale=1.0)
                DVE("reciprocal", [statU], [statU], out=rs, in_=rs)
                DVE("scalar_tensor_tensor", [statU], [statU], out=nb, in0=mu, scalar=-1.0, in1=rs, op0=ALU.mult, op1=ALU.mult)
                ACT(vln, gvs, AF.Identity, [smU, statU], [smU], bias=nb, scale=rs)
                DVE("tensor_tensor", [smU, cU], [smU], out=vln, in0=vln, in1=vecT[:, :, V_ALNG + j], op=ALU.mult)
                DVE("tensor_tensor", [smU, cU], [smU], out=vln, in0=vln, in1=vecT[:, :, V_ALNB + j], op=ALU.add)
                p2, u2 = psum()
                TR(p2[0:16, 0:128], vln, ident[:, :], [smU, cU], [u2])
                o, ou = tf()
                DVE("tensor_copy", [u2], [ou], out=o[0:16, 0:128], in_=p2[0:16, 0:128])
                dma("sp", av_s.ap()[j].rearrange("(g e) -> g e", e=128), o[0:16, 0:128], smDS, R=[ou])
                DVE("tensor_tensor", [smU], [smU], out=gvs, in0=vln, in1=ws00, op=ALU.mult)
                DVE("tensor_tensor", [smU], [mU[g][2] for g in range(KC)], out=mid[:, :, TP], in0=gvs, in1=bs0, op=ALU.add)
            return comp
        for cb in range(4):
            wstep([(lambda s: slot3(s, KC, 512), wsrc(w_in, 0, KC, D + cb * 512, 512))], mk_vs(cb))

        def mk_u(cb):
            def comp(slot, wu):
                w3 = slot3(slot, KC, 512)
                for o4 in range(4):
                    oc = cb * 4 + o4
                    for bi, (c0, n) in enumerate(BLKS):
                        p, u = psum()
                        mm_fm(p, u, w3, wu, KC, o4, hT, lambda kc: [hU[bi]], c0, n)
                        g_, gu = tf()
                        ACT(g_[:, :n], p[:, :n], AF.Gelu_apprx_tanh, [u], [gu])
                        DVE("tensor_tensor", [gu], [mU[oc][bi]], out=mid[:, oc, c0:c0 + n], in0=g_[:, :n], in1=mid[:, oc, c0:c0 + n], op=ALU.mult)
            return comp
        for cb in range(4):
            wstep([(lambda s: slot3(s, KC, 512), wsrc(w_in, 0, KC, cb * 512, 512))], mk_u(cb))
        out_proj_steps(w_out, 0, KC)
        run_steps()

    def sample_attn(qs_f, ks_f, vs_f, smpU, hTb):
        B.barrier()
        hTf = hTb[:, 0:16400].bitcast(F32)
        kcf = hTf[:, 0:2048]; prod = hTf[:, 2048:2560]; sc = hTf[:, 2560:2608]; pf = hTf[:, 2608:2656]
        pn = hTf[:, 2656:2704]; t48 = hTf[:, 2704:2752]; t48b = hTf[:, 2752:2800]; b0 = hTf[:, 2800:2848]
        tabs = hTf[:, 2848:2896]; sels = hTf[:, 2896:3280]; o16 = hTf[:, 3280:3312]; rowst = hTf[:, 3312:3440]
        bfv = hTb[:, 6880:16400]
        Qd = bfv[:, 0:2048]; vcb = [bfv[:, 2048 * (1 + g):2048 * (2 + g)] for g in range(3)]
        pb48 = bfv[:, 8192:8240]; hb = bfv[:, 8240:8288]; lb_ = bfv[:, 8288:8336]
        kU_, vU_, sU, cDS_, vDS_, oDS_, rDS_ = Unit(), [Unit(), Unit(), Unit()], Unit(), B.new_ds(), B.new_ds(), B.new_ds(), B.new_ds()
        qdU, prU, rsU = Unit(), Unit(), Unit()
        dma("sp", tabs[0:32, :], relb.ap(), cDS_, W=[sU])
        dma("sp", sels[0:32, :], sels_d.ap(), cDS_, W=[sU])
        dma("sp", b0, relb.ap()[0:1, :].partition_broadcast(128), cDS_, W=[sU])
        for g in range(3):
            dil = DIL[g]
            dma("pool", vcb[g], bass.AP(cv[g].handle(), 0, [[dil * D, 128], [1, D]]), vDS_, W=[vU_[g]])
        for g in range(3):
            dil = DIL[g]
            dma("sp", kcf, bass.AP(ck[g].handle(), 0, [[dil * D, 128], [1, D]]), cDS_, W=[kU_])
            DVE("tensor_tensor", [cU, smpU], [qdU], out=Qd.rearrange("p (h e) -> p h e", e=128), in0=ident[:, :].unsqueeze(1).to_broadcast([128, 16, 128]),
                in1=qs_f[:, g * 16:(g + 1) * 16].unsqueeze(2).to_broadcast([128, 16, 128]), op=ALU.mult)
            for c4 in range(4):
                p, u = psum()
                MM(p[:, :], onesb[:, :], Qd[:, c4 * 512:(c4 + 1) * 512], True, True, [qdU, cU], [u])
                DVE("tensor_tensor", [u, kU_], [prU], out=prod, in0=kcf[:, c4 * 512:(c4 + 1) * 512], in1=p[:, :], op=ALU.mult)
                DVE("tensor_reduce", [prU], [sU], out=sc[:, g * 16 + c4 * 4:g * 16 + c4 * 4 + 4], in_=prod.rearrange("p (h e) -> p h e", e=128), axis=AX.X, op=ALU.add)
            p, u = psum()
            MM(p[:, 0:16], sels[0:32, g * 128:(g + 1) * 128], tabs[0:32, g * 16:(g + 1) * 16], True, True, [sU], [u])
            DVE("scalar_tensor_tensor", [u, sU], [sU], out=sc[:, g * 16:(g + 1) * 16], in0=sc[:, g * 16:(g + 1) * 16], scalar=SCALE, in1=p[:, 0:16], op0=ALU.mult, op1=ALU.add)
        ACT(pf, sc, AF.Exp, [sU], [sU])
        DVE("tensor_copy", [sU], [sU], out=pb48, in_=pf)
        DVE("tensor_tensor", [smpU], [sU], out=t48, in0=qs_f, in1=ks_f, op=ALU.mult)
        DVE("tensor_copy", [sU], [sU], out=hb, in_=t48)
        DVE("tensor_tensor", [sU], [sU], out=t48b, in0=t48, in1=hb, op=ALU.subtract)
        DVE("tensor_copy", [sU], [sU], out=lb_, in_=t48b)
        p, u = psum()
        MM(p[:, 0:48], onesb[:, :], hb, True, False, [sU, cU], [u])
        MM(p[:, 0:48], onesb[:, :], lb_, False, True, [sU, cU], [u])
        DVE("scalar_tensor_tensor", [u, sU], [sU], out=t48, in0=p[:, 0:48], scalar=SCALE, in1=b0, op0=ALU.mult, op1=ALU.add)
        ACT(pn, t48, AF.Exp, [sU], [sU])
        pso, uso = psum(); psd, usd = psum()
        for h_ in range(16):
            for g in range(3):
                MM(pso[:, h_:h_ + 1], vcb[g][:, h_ * 128:(h_ + 1) * 128], pb48[:, g * 16 + h_:g * 16 + h_ + 1], g == 0, g == 2, [vU_[g], sU], [uso])
        for g in range(3):
            MM(psd[:, 0:16], onesb[:, :], pb48[:, g * 16:(g + 1) * 16], g == 0, g == 2, [sU, cU], [usd])
        DVE("tensor_tensor", [sU, smpU], [sU], out=t48, in0=pn, in1=vs_f, op=ALU.mult)
        DVE("tensor_reduce", [sU], [sU], out=o16[:, 0:16], in_=t48.rearrange("p (g h) -> p h g", g=3), axis=AX.X, op=ALU.add)
        DVE("tensor_reduce", [sU], [sU], out=o16[:, 16:32], in_=pn.rearrange("p (g h) -> p h g", g=3), axis=AX.X, op=ALU.add)
        DVE("tensor_tensor", [uso, sU], [sU], out=o16[:, 0:16], in0=pso[:, 0:16], in1=o16[:, 0:16], op=ALU.add)
        DVE("tensor_tensor", [usd, sU], [sU], out=o16[:, 16:32], in0=psd[:, 0:16], in1=o16[:, 16:32], op=ALU.add)
        DVE("reciprocal", [sU], [sU], out=o16[:, 16:32], in_=o16[:, 16:32])
        DVE("tensor_tensor", [sU], [mU[k][2] for k in range(KC)], out=mid[:, :, TP], in0=o16[:, 0:16], in1=o16[:, 16:32], op=ALU.mult)
        for g, W_ in enumerate((128, 512, 2048)):
            n16 = (W_ - 1) * 16
            for src_t, dst_t, col in ((ck[g], bk_s[g], ks_f), (cv[g], bv_s[g], vs_f)):
                dma("sp", bass.AP(dst_t, 0, [[n16, 128], [1, n16]]), bass.AP(src_t.handle(), D, [[n16, 128], [1, n16]]), oDS_)
                p, u = psum()
                TR(p[0:16, 0:128], col[:, g * 16:(g + 1) * 16], ident[:, :], [smpU, cU], [u])
                DVE("tensor_copy", [u, rsU], [rsU], out=rowst[0:16, :], in_=p[0:16, 0:128])
                dma("sp", dst_t.ap()[W_ - 1:W_, :].rearrange("o (h e) -> (o h) e", e=128), rowst[0:16, :], rDS_, R=[rsU])

    DIL = (1, 4, 16)
    KEEP = (128, 512, 1024)

    def dilattn(i):
        qs_d = B.dram("qs_d", [6144, 1024], BF16)
        kvi = B.dram("kvi", [12288, 1024], BF16)
        kvo = B.dram("kvo", [24576, 1024], BF16)

        def kvo_off(elem_off):
            row = elem_off // 1024
            return ((row // 1024) * 2048 + (row % 1024)) * 1024 + (elem_off % 1024)
        eg_d = B.dram("eg_d", [144, 255], F32)
        re_d = B.dram("re_d", [144 * 128, 255], F32)
        VB = 6144 * 1024
        RANK = 12288 * 1024
        kvWc = [Unit() for _ in range(12)]; qsW = Unit(); reU = Unit(); kvoU = [Unit() for _ in range(12)]

        def kvW_of(elem_off):
            return kvWc[elem_off // (1024 * 1024)]

        def kvoU_of(elem_off):
            return kvoU[elem_off // (1024 * 1024)]
        ccsem = B.new_sem("cc_kv")
        w_qkv = b_w_qkv.ap(); w_o = b_w_out.ap()

        arena_reset()
        tabx = carve(48); selS = carve(9 * 255); egs = carve(9 * 255); tU_ = Unit(); tDS = B.new_ds()
        dma("sp", tabx[0:32, :], relb.ap(), tDS, W=[tU_])
        DVE("memset", [], [tU_], ap=tabx[32:33, :], constant=-1e30)
        dma("sp", selS[0:33, :], selg.ap(), tDS, W=[tU_])
        for g in range(3):
            for v in range(3):
                c = (g * 3 + v) * 255
                p, u = psum()
                MM(p[0:16, 0:255], tabx[0:33, g * 16:(g + 1) * 16], selS[0:33, c:c + 255], True, True, [tU_], [u])
                ACT(egs[0:16, c:c + 255], p[0:16, 0:255], AF.Exp, [u], [tU_])
                if v == 2:
                    DVE("tensor_scalar_mul", [tU_, cU], [tU_], out=egs[0:16, c:c + 255], in0=egs[0:16, c:c + 255], scalar1=flg[0:16, 0:1])
        dma("sp", eg_d.ap().rearrange("(g h v) c -> h g v c", g=3, h=16, v=3), egs[0:16, :].rearrange("p (g v c) -> p g v c", g=3, v=3), tDS, R=[tU_], W=[reU])
        for k in range(9):
            dma("sp", re_d.ap()[k * 2048:(k + 1) * 2048, :].rearrange("(r j) c -> r j c", j=128),
                bass.AP(eg_d, k * 16 * 255, [[255, 16], [0, 128], [1, 255]]), tDS, R=[reU], W=[reU])

        arena_reset()
        smp = carve(3 * 48); smpU = Unit()
        rmsnorm(V_GMIX + i)
        hk = [carve(512, BF16), carve(512, BF16)]; hkU = [Unit(), Unit()]; hkDS = [B.new_ds(), B.new_ds()]
        kst = carve(2048).rearrange("p (t c) -> p t c", c=512); kstU = Unit(); kstDS = B.new_ds()
        vreg = carve(1536)
        kq = [vreg[:, i * 512:(i + 1) * 512] for i in range(3)]; kqU = [Unit() for _ in range(3)]
        vst = [vreg[:, 0:512], vreg[:, 512:1024]]; vstU = [Unit(), Unit()]; vstDS = [B.new_ds(), B.new_ds()]
        vbs = [vreg[:, 1024:1280].bitcast(BF16), vreg[:, 1280:1536].bitcast(BF16)]; vbsU = [Unit(), Unit()]; vbsDS = [B.new_ds(), B.new_ds()]
        qs_f, ks_f, vs_f = smp[:, 0:48], smp[:, 48:96], smp[:, 96:144]
        cnt = {"hk": 0, "v": 0, "kq": 0}

        def k_out_dma(g, hq, bi):
            for tt in range(4):
                t = bi * 4 + tt
                if t * 128 >= TP - KEEP[g]:
                    r0 = t * 128 - (TP - KEEP[g])
                    dma("sp", bk_p[g].ap()[r0:r0 + 128, hq * 512:(hq + 1) * 512], kst[:, tt, :], kstDS, R=[kstU])

        pend = {"f": None, "g": None}

        def flush_pending():
            f = pend["f"]; pend["f"] = None
            if f is not None:
                f()

        def flush_all():
            f = pend["f"]; g_ = pend["g"]
            pend["f"] = None; pend["g"] = None
            if f is not None:
                f()
            if g_ is not None:
                g_()
            g2 = pend["g"]; pend["g"] = None
            if g2 is not None:
                g2()

        def qk_front(w3, wu, o4, bi, c0, n, q=None, qu=None):
            p, u = psum()
            mm_fm(p, u, w3, wu, KC, o4, hT, lambda kc: [hU[bi]], c0, n)
            if q is None:
                q, qu = tf()
            ACT(q[:, :n], p[:, :n], AF.Copy, [u], [qu])
            s_, su = tb()
            ACT(s_[:, :n], p[:, :n], AF.Square, [u], [su])
            return q, qu, s_, su

        def qk_sqrt(s_, su, n):
            p2, u2 = psum()
            MM(p2[:, :n], onesb[:, :], s_[:, :n], True, True, [su, cU], [u2])
            r, ru = tf()
            ACT(r[:, :n], p2[:, :n], AF.Ln, [u2, cU], [ru], bias=epsT[:, 0:1], scale=1.0 / 128)
            ACT(r[:, :n], r[:, :n], AF.Exp, [ru], [ru], scale=-0.5)
            return r, ru

        def mk_k(g, hq):
            dil = DIL[g]

            def comp(slot, wu):
                w3 = slot3(slot, KC, 512)
                gcol = S_BK + g
                for bi, (c0, n) in enumerate(BLKS):
                    for o4 in range(4):
                        h_ = hq * 4 + o4
                        if bi == 2:
                            flush_all()
                            q, qu, s_, su = qk_front(w3, wu, o4, bi, c0, n)
                            r, ru = qk_sqrt(s_, su, 1)
                            DVE("scalar_tensor_tensor", [qu, ru, cU], [smpU], out=ks_f[:, g * 16 + h_:g * 16 + h_ + 1], in0=q[:, :1], scalar=gs[:, gcol:gcol + 1],
                                in1=r[:, :1], op0=ALU.mult, op1=ALU.mult)
                            continue
                        qi = cnt["kq"] % 3; cnt["kq"] += 1
                        q, qu, s_, su = qk_front(w3, wu, o4, bi, c0, n, q=kq[qi], qu=kqU[qi])
                        fA = pend["f"]; fB = pend["g"]
                        pend["f"] = None; pend["g"] = None
                        if fA is not None:
                            fA()
                        if fB is not None:
                            fB()
                        row = (g * 16 + h_) * 128

                        def tailB(q=q, qu=qu, bi=bi, o4=o4):
                            tiles = [tt for tt in range(4) if (bi * 4 + tt) * 128 >= TP - KEEP[g]]
                            if tiles:
                                pt, ut = psum()
                                for tt in tiles:
                                    TR(pt[:, tt * 128:(tt + 1) * 128], q[:, tt * 128:(tt + 1) * 128], ident[:, :], [qu, cU], [ut], signal=(tt == tiles[-1]))
                                t0_, t1_ = tiles[0], tiles[-1] + 1
                                ACT(kst[:, t0_:t1_, o4 * 128:(o4 + 1) * 128], pt[:, t0_ * 128:t1_ * 128].rearrange("p (t e) -> p t e", e=128), AF.Copy, [ut], [kstU])
                            if o4 == 3:
                                k_out_dma(g, hq, bi)

                        def tailA(q=q, qu=qu, s_=s_, su=su, bi=bi, n=n, row=row, tailB=tailB):
                            r, ru = qk_sqrt(s_, su, n)
                            DVE("scalar_tensor_tensor", [qu, ru, cU], [qu], out=q[:, :n], in0=q[:, :n], scalar=gs[:, gcol:gcol + 1], in1=r[:, :n], op0=ALU.mult, op1=ALU.mult)
                            hi = cnt["hk"] % 2; cnt["hk"] += 1
                            nu = 512 // dil
                            DVE("tensor_copy", [qu], [hkU[hi]], out=hk[hi][:, 0:512].rearrange("p (r u) -> p r u", r=dil), in_=q[:, :].rearrange("p (u r) -> p r u", r=dil))
                            dma("sp", kvi.ap()[row:row + 128, :].rearrange("p (r u) -> p r u", r=dil)[:, :, bi * nu:(bi + 1) * nu],
                                hk[hi][:, 0:512].rearrange("p (r u) -> p r u", r=dil), hkDS[hi], R=[hkU[hi], kvW_of(row * 1024)])
                            pend["g"] = tailB
                        pend["f"] = tailA
            return comp

        def mk_q(g, hq):
            dil = DIL[g]

            def comp(slot, wu):
                w3 = slot3(slot, KC, 512)
                gcol = S_BQ + g
                for bi, (c0, n) in enumerate(BLKS):
                    for o4 in range(4):
                        h_ = hq * 4 + o4
                        if bi == 2:
                            flush_all()
                        q, qu, s_, su = qk_front(w3, wu, o4, bi, c0, n)
                        flush_pending()
                        if bi == 2:
                            r, ru = qk_sqrt(s_, su, 1)
                            DVE("scalar_tensor_tensor", [qu, ru, cU], [smpU], out=qs_f[:, g * 16 + h_:g * 16 + h_ + 1], in0=q[:, :1], scalar=gs[:, gcol:gcol + 1],
                                in1=r[:, :1], op0=ALU.mult, op1=ALU.mult)
                            continue

                        def tail(q=q, qu=qu, s_=s_, su=su, bi=bi, h_=h_, n=n):
                            r, ru = qk_sqrt(s_, su, n)
                            hi = cnt["hk"] % 2; cnt["hk"] += 1
                            nu = 512 // dil
                            hv = hk[hi][:, 0:512].rearrange("p (r u) -> p r u", r=dil)
                            DVE("scalar_tensor_tensor", [qu, ru, cU], [hkU[hi]], out=hv, in0=q[:, :].rearrange("p (u r) -> p r u", r=dil), scalar=gs[:, gcol:gcol + 1],
                                in1=r[:, :].rearrange("p (u r) -> p r u", r=dil), op0=ALU.mult, op1=ALU.mult)
                            row = (g * 16 + h_) * 128
                            dma("sp", qs_d.ap()[row:row + 128, :].rearrange("p (r u) -> p r u", r=dil)[:, :, bi * nu:(bi + 1) * nu], hv, hkDS[hi], R=[hkU[hi], qsW])
                        pend["f"] = tail
            return comp

        def mk_v(g, hq):
            def comp(slot, wu):
                w3 = slot3(slot, KC, 512)
                for t in range(8):
                    vi = cnt["v"] % 2; cnt["v"] += 1
                    p, u = psum()
                    for kc in range(KC):
                        MM(p[:, :], hT[:, kc, t * 128:(t + 1) * 128], w3[:, kc, :], kc == 0, kc == KC - 1, [wu, hU[t // 4]], [u], signal=(kc == KC - 1))
                    DVE("tensor_copy", [u], [vstU[vi]], out=vst[vi][:, :], in_=p[:, :])
                    ACT(vbs[vi][:, :], vst[vi][:, :], AF.Copy, [vstU[vi]], [vbsU[vi]])
                    if t * 128 >= TP - KEEP[g]:
                        r0 = t * 128 - (TP - KEEP[g])
                        dma("sp", bv_p[g].ap()[r0:r0 + 128, hq * 512:(hq + 1) * 512], vst[vi][:, :], vstDS[vi], R=[vstU[vi]])
                    off = VB + ((g * 16 + hq * 4) * 1024 + t * 128) * 128
                    dma("sp", bass.AP(kvi, off, [[128, 128], [1024 * 128, 4], [1, 128]]), vbs[vi][:, :].rearrange("p (o e) -> p o e", e=128), vbsDS[vi], R=[vbsU[vi], kvW_of(off)])
                for o4 in range(4):
                    p, u = psum()
                    mm_fm(p, u, w3, wu, KC, o4, hT, lambda kc: [hU[2]], TP, 1)
                    DVE("tensor_copy", [u], [smpU], out=vs_f[:, g * 16 + hq * 4 + o4:g * 16 + hq * 4 + o4 + 1], in_=p[:, 0:1])
            return comp

        def qkv_src(g, which, hq):
            return wsrc(w_qkv, 0, KC, g * 6144 + which * 2048 + hq * 512, 512)
        def mk_first_v(inner):
            def comp(slot, wu):
                flush_all()
                B.barrier()
                inner(slot, wu)
            return comp

        def mk_cc(k):
            return lambda: B.cc(kvi.ap()[k * 1024:(k + 1) * 1024, :].opt(), kvo.ap()[k * 2048:(k + 1) * 2048, :].opt(), PAIRS_RUN, ccsem,
                                W=[kvWc[k], kvoU[k]], count=k + 1)
        cc_at = {}
        for g in range(3):
            cc_at[4 * g + 6] = 2 * g; cc_at[4 * g + 7] = 2 * g + 1
            cc_at[12 + 4 * g + 6] = 6 + 2 * g; cc_at[12 + 4 * g + 7] = 7 + 2 * g
        si = 0
        for which in (1, 2, 0):
            for g in range(3):
                for hq in range(4):
                    comp = mk_k(g, hq) if which == 1 else (mk_v(g, hq) if which == 2 else mk_q(g, hq))
                    if which == 2 and g == 0 and hq == 0:
                        comp = mk_first_v(comp)
                    wstep([(lambda s: slot3(s, KC, 512), qkv_src(g, which, hq))], comp, post_issue=(mk_cc(cc_at[si]) if si in cc_at else None))
                    si += 1
        run_steps()
        flush_all()

        arena_reset()
        carve(3 * 48)
        hTb = hT2[:, :]
        hoff = [0]

        def hcarve(n_bf):
            o = hoff[0]; hoff[0] = o + n_bf + (n_bf % 2)
            assert hoff[0] <= KC * XC
            return hTb[:, o:o + n_bf]
        q3 = [hcarve(1024) for _ in range(3)]; ko = [hcarve(1024) for _ in range(3)]
        kp = [hcarve(128), hcarve(512), hcarve(1024)]
        vo = [hcarve(8 * 128).rearrange("p (t e) -> p t e", e=128), hcarve(8 * 128).rearrange("p (t e) -> p t e", e=128), hcarve(16 * 128).rearrange("p (t e) -> p t e", e=128)]
        vpv = [hcarve(128).rearrange("p (t e) -> p t e", e=128), hcarve(4 * 128).rearrange("p (t e) -> p t e", e=128), hcarve(16 * 128).rearrange("p (t e) -> p t e", e=128)]
        et = carve(10 * 128).rearrange("p (k i) -> p k i", i=128)
        accden = carve(2048).rearrange("p (k t) -> p k t", k=2); adU = Unit()
        ldq = Unit(); ldk = Unit(); ldv = Unit(); lde = Unit()
        qDS, kDS, vDS, eDS = B.new_ds(), B.new_ds(), B.new_ds(), B.new_ds()

        def head(h_):
            for g in range(3):
                dil = DIL[g]; U = TP // dil
                row = (g * 16 + h_) * 128
                dma("sp", q3[g], qs_d.ap()[row:row + 128, :], qDS, R=[qsW], W=[ldq])
                dma("sp", ko[g], kvi.ap()[row:row + 128, :], kDS, R=[kvW_of(row * 1024)], W=[ldk])
                orow = kvo_off(row * 1024) // 1024
                if g == 0:
                    dma("sp", kp[0], kvo.ap()[orow:orow + 128, 896:1024], kDS, R=[kvoU_of(row * 1024)], W=[ldk])
                elif g == 1:
                    dma("sp", kp[1].rearrange("p (r u) -> p r u", r=4), kvo.ap()[orow:orow + 128, :].rearrange("p (r u) -> p r u", r=4)[:, :, 128:256], kDS, R=[kvoU_of(row * 1024)], W=[ldk])
                else:
                    dma("sp", kp[2], kvo.ap()[orow:orow + 128, :], kDS, R=[kvoU_of(row * 1024)], W=[ldk])
                vbase = VB + (g * 16 + h_) * 1024 * 128
                if g == 0:
                    dma("sp", vo[0], bass.AP(kvi, vbase, [[128, 128], [128 * 128, 8], [1, 128]]), vDS, R=[kvW_of(vbase)], W=[ldv])
                    dma("sp", vpv[0], bass.AP(kvo, kvo_off(vbase) + 896 * 128, [[128, 128], [128 * 128, 1], [1, 128]]), vDS, R=[kvoU_of(vbase)], W=[ldv])
                elif g == 1:
                    for r in range(4):
                        dma("sp", vo[1][:, r * 2:r * 2 + 2, :], bass.AP(kvi, vbase + r * 128, [[4 * 128, 128], [512 * 128, 2], [1, 128]]), vDS, R=[kvW_of(vbase)], W=[ldv])
                    dma("sp", vpv[1], bass.AP(kvo, kvo_off(vbase) + 512 * 128, [[4 * 128, 128], [128, 4], [1, 128]]), vDS, R=[kvoU_of(vbase)], W=[ldv])
                else:
                    dma("sp", vo[2][0:64, :, :], bass.AP(kvi, vbase, [[16 * 128, 64], [128, 16], [1, 128]]), vDS, R=[kvW_of(vbase)], W=[ldv])
                    dma("sp", vpv[2][0:64, :, :], bass.AP(kvo, kvo_off(vbase), [[16 * 128, 64], [128, 16], [1, 128]]), vDS, R=[kvoU_of(vbase)], W=[ldv])
                erow = (g * 16 + h_) * 3
                if g < 2:
                    for k_, v in enumerate((1, 0, 2, 0)):
                        dma("sp", et[:, g * 4 + k_, :], bass.AP(re_d, (erow + v) * 128 * 255 + 127, [[254, 128], [1, 128]]), eDS, R=[reU], W=[lde])
                else:
                    dma("sp", et[0:64, 8, 0:64], bass.AP(re_d, (erow + 2) * 128 * 255 + 191, [[254, 64], [1, 64]]), eDS, R=[reU], W=[lde])
                    dma("sp", et[0:64, 9, 0:64], bass.AP(re_d, (erow + 0) * 128 * 255 + 127, [[254, 64], [1, 64]]), eDS, R=[reU], W=[lde])
            units = []
            for g in range(3):
                dil = DIL[g]; U = TP // dil
                QB = 128 if g < 2 else 64
                for r in range(dil):
                    for qb in range(U // QB):
                        units.append((g, r, qb))

            def stageA(un):
                g, r, qb = un
                dil = DIL[g]; U = TP // dil
                QB = 128 if g < 2 else 64
                nqb = U // QB
                qcols = q3[g][:, r * U + qb * QB:r * U + (qb + 1) * QB]
                if qb > 0:
                    kT0 = ko[g][:, r * U + (qb - 1) * QB:r * U + qb * QB]
                    vt0 = vo[g][0:QB, (r * nqb + qb - 1), :]
                    eb = g * 4
                else:
                    if g == 0:
                        kT0 = kp[0][:, :]; vt0 = vpv[0][:, 0, :]
                    elif g == 1:
                        kT0 = kp[1][:, r * 128:(r + 1) * 128]; vt0 = vpv[1][:, r, :]
                    else:
                        kT0 = kp[2][:, r * 64:(r + 1) * 64]; vt0 = vpv[2][0:64, r, :]
                    eb = (g * 4 + 2) if g < 2 else 8
                kT1 = ko[g][:, r * U + qb * QB:r * U + (qb + 1) * QB]
                vt1 = vo[g][0:QB, (r * nqb + qb), :]
                p, u = psum()
                MM(p[0:QB, 0:QB], kT0, qcols, True, True, [ldk, ldq], [u])
                MM(p[0:QB, QB:2 * QB], kT1, qcols, True, True, [ldk, ldq], [u])
                e_, eu = tf()
                ACT(e_[0:QB, 0:2 * QB], p[0:QB, 0:2 * QB], AF.Exp, [u], [eu], scale=SCALE)
                pb, pbu = tb()
                DVE("tensor_tensor", [eu, lde], [pbu], out=pb[0:QB, 0:2 * QB].rearrange("p (k i) -> p k i", k=2), in0=e_[0:QB, 0:2 * QB].rearrange("p (k i) -> p k i", k=2),
                    in1=et[0:QB, eb:eb + 2, 0:QB], op=ALU.mult)
                return (un, pb, pbu, vt0, vt1)

            def stageB(sa):
                un, pb, pbu, vt0, vt1 = sa
                QB = 128 if un[0] < 2 else 64
                po, uo = psum()
                MM(po[:, 0:QB], vt0, pb[0:QB, 0:QB], True, False, [ldv, pbu], [uo])
                MM(po[:, 0:QB], vt1, pb[0:QB, QB:2 * QB], False, True, [ldv, pbu], [uo])
                MM(po[:, QB:2 * QB], onesb[0:QB, :], pb[0:QB, 0:QB], True, False, [pbu, cU], [uo])
                MM(po[:, QB:2 * QB], onesb[0:QB, :], pb[0:QB, QB:2 * QB], False, True, [pbu, cU], [uo])
                return (un, po, uo)

            def stageC(sb):
                (g, r, qb), po, uo = sb
                dil = DIL[g]
                QB = 128 if g < 2 else 64
                a_out = accden.rearrange("p k (u r) -> p k r u", r=dil)[:, :, r, qb * QB:(qb + 1) * QB]
                src = po[:, 0:2 * QB].rearrange("p (k i) -> p k i", k=2)
                if g == 0:
                    DVE("tensor_copy", [uo], [adU], out=a_out, in_=src)
                else:
                    DVE("tensor_tensor", [uo, adU], [adU], out=a_out, in0=src, in1=a_out, op=ALU.add)
            pairs = [units[k:k + 2] for k in range(0, len(units), 2)]
            sa_prev, sb_prev = [], []
            for pr in pairs + [[], []]:
                sa = [stageA(un) for un in pr]
                sb = [stageB(x) for x in sa_prev]
                for x in sb_prev:
                    stageC(x)
                sa_prev, sb_prev = sa, sb
            ACT(accden[:, 1, :], accden[:, 1, :], AF.Ln, [adU], [adU])
            ACT(accden[:, 1, :], accden[:, 1, :], AF.Exp, [adU], [adU], scale=-1.0)
            DVE("tensor_tensor", [adU], [mU[h_][0], mU[h_][1]], out=mid[:, h_, 0:TP], in0=accden[:, 0, :], in1=accden[:, 1, :], op=ALU.mult)
        for h_ in range(16):
            head(h_)

        sample_attn(qs_f, ks_f, vs_f, smpU, hTb)
        out_proj_steps(w_o, 0, KC)
        run_steps()

    def col_ln(src, srcU, grow, brow, out, outW, scr, scrU):
        sq = scr[:, 0:32]; lo = scr[:, 32:64]; stt = scr[:, 64:70]
        DVE("tensor_copy", list(srcU), [scrU], out=sq[:, 0:16], in_=src)
        DVE("tensor_tensor", list(srcU), [scrU], out=sq[:, 16:32], in0=src, in1=src, op=ALU.mult)
        hb, hbu = tb(); lb_, lbu = tb()
        DVE("tensor_copy", [scrU], [hbu], out=hb[:, 0:32], in_=sq)
        DVE("tensor_tensor", [scrU, hbu], [scrU], out=lo, in0=sq, in1=hb[:, 0:32], op=ALU.subtract)
        DVE("tensor_copy", [scrU], [lbu], out=lb_[:, 0:32], in_=lo)
        p, u = psum()
        MM(p[:, 0:32], onesb[:, :], hb[:, 0:32], True, False, [hbu, cU], [u])
        MM(p[:, 0:32], onesb[:, :], lb_[:, 0:32], False, True, [lbu, cU], [u])
        s1 = stt[:, 0:1]; s2 = stt[:, 1:2]; mu = stt[:, 2:3]; var = stt[:, 3:4]; rs = stt[:, 4:5]; nb = stt[:, 5:6]
        DVE("tensor_reduce", [u], [scrU], out=s1, in_=p[:, 0:16], axis=AX.X, op=ALU.add)
        DVE("tensor_reduce", [u], [scrU], out=s2, in_=p[:, 16:32], axis=AX.X, op=ALU.add)
        DVE("tensor_scalar_mul", [scrU], [scrU], out=mu, in0=s1, scalar1=1.0 / D)
        DVE("tensor_tensor", [scrU], [scrU], out=var, in0=mu, in1=mu, op=ALU.mult)
        DVE("scalar_tensor_tensor", [scrU], [scrU], out=var, in0=s2, scalar=1.0 / D, in1=var, op0=ALU.mult, op1=ALU.subtract)
        ACT(rs, var, AF.Sqrt, [scrU, cU], [scrU], bias=epsT[:, 0:1], scale=1.0)
        DVE("reciprocal", [scrU], [scrU], out=rs, in_=rs)
        DVE("scalar_tensor_tensor", [scrU], [scrU], out=nb, in0=mu, scalar=-1.0, in1=rs, op0=ALU.mult, op1=ALU.mult)
        ACT(out, src, AF.Identity, list(srcU) + [scrU], list(outW), bias=nb, scale=rs)
        DVE("tensor_tensor", list(outW) + [cU], list(outW), out=out, in0=out, in1=vecT[:, :, grow], op=ALU.mult)
        DVE("tensor_tensor", list(outW) + [cU], list(outW), out=out, in0=out, in1=vecT[:, :, brow], op=ALU.add)

    def convmod(i):
        arena_reset()
        cxi = B.dram("cxi", [128, 480], F32); cxo = B.dram("cxo", [256, 480], F32)
        ccs = B.new_sem("cc_cv")
        w_in = c_w_in.ap(); w_out = c_w_out.ap()
        ztail = carve(480).rearrange("p (c t) -> p c t", t=30); ztU = Unit(); ztDS = B.new_ds()
        halo = carve(480).rearrange("p (c t) -> p c t", t=30); haU = Unit(); haDS = B.new_ds()
        zcs = carve(16 * 31).rearrange("p (c t) -> p c t", t=31); zsU = Unit()
        zh = carve(16 * 60 // 2, BF16).rearrange("p (c t) -> p c t", t=60); zhU = Unit()
        stg1 = carve(2048)
        yacc = [stg1[:, 0:1024], stg1[:, 1024:2048]]; yU = [Unit(), Unit()]
        scr = carve(80); scrU = Unit()
        ysm = carve(32); ysU = Unit()
        lnmu = carve(512); lnrs = carve(512)
        rmsnorm(V_GMIX + i)
        rows_to_fm([stg1, stg1], cst.ap(), 30, lambda kc: zcs[:, kc, 0:30], lambda kc: [zsU], single=True)

        def mk_in(c2):
            def comp(slot, wu):
                w3 = slot3(slot, KC, 512)
                for j in range(2):
                    c = c2 + j
                    for bi, (c0, n) in enumerate(BLKS):
                        pa, ua = psum(); pg, ug = psum()
                        mm_fm(pa, ua, w3, wu, KC, j, hT, lambda kc: [hU[bi]], c0, n)
                        mm_fm(pg, ug, w3, wu, KC, 2 + j, hT, lambda kc: [hU[bi]], c0, n)
                        s_, su = tf()
                        ACT(s_[:, :n], pg[:, :n], AF.Sigmoid, [ug, cU], [su], bias=vecT[:, c, V_CBIN + 1:V_CBIN + 2], scale=1.0)
                        if bi == 2:
                            DVE("scalar_tensor_tensor", [ua, su, cU], [zsU], out=zcs[:, c, 30:31], in0=pa[:, :1], scalar=vecT[:, c, V_CBIN:V_CBIN + 1], in1=s_[:, :1], op0=ALU.add, op1=ALU.mult)
                            continue
                        DVE("scalar_tensor_tensor", [ua, su, cU], [mU[c][bi]], out=mid[:, c, c0:c0 + n], in0=pa[:, :n], scalar=vecT[:, c, V_CBIN:V_CBIN + 1], in1=s_[:, :n], op0=ALU.add, op1=ALU.mult)
                        if bi == 1:
                            DVE("scalar_tensor_tensor", [ua, su, cU], [ztU], out=ztail[:, c, :], in0=pa[:, 482:512], scalar=vecT[:, c, V_CBIN:V_CBIN + 1], in1=s_[:, 482:512], op0=ALU.add, op1=ALU.mult)
            return comp
        for c2 in range(0, KC, 2):
            wstep([(lambda s: slot3(s, KC, 512)[:, :, 0:256], wsrc(w_in, 0, KC, c2 * 128, 256)),
                   (lambda s: slot3(s, KC, 512)[:, :, 256:512], wsrc(w_in, 0, KC, D + c2 * 128, 256))], mk_in(c2))
        run_steps()
        fm_to_rows([stg1, stg1], lambda kc: ztail[:, kc, :], lambda kc: [ztU], 30, cconv_p.ap(), single=True)
        fm_to_rows([stg1, stg1], lambda kc: zcs[:, kc, 1:31], lambda kc: [zsU], 30, cconv_s.ap(), single=True)
        cxU = Unit()
        dma("sp", cxi.ap(), ztail[:, :, :].rearrange("p c t -> p (c t)"), ztDS, R=[ztU], W=[cxU])
        B.cc(cxi.ap().opt(), cxo.ap().opt(), PAIRS_RUN, ccs, W=[cxU])
        dma("sp", halo[:, :, :].rearrange("p c t -> p (c t)"), cxo.ap()[0:128, :], haDS, R=[cxU], W=[haU])
        DVE("tensor_scalar_mul", [haU, cU], [haU], out=halo[:, :, :], in0=halo[:, :, :], scalar1=flg[:, 0:1])
        DVE("tensor_copy", [haU], [zhU], out=zh[:, :, 0:30], in_=halo[:, :, :])
        DVE("tensor_copy", [mU[c][0] for c in range(KC)], [zhU], out=zh[:, :, 30:60], in_=mid[:, :, 0:30])

        B.barrier()
        dg = stg1[:, 0:1984].bitcast(BF16).rearrange("p (k m) -> p k m", m=128); dgU = Unit()
        for c in range(KC):
            DVE("tensor_tensor", [cU], [dgU], out=dg, in0=ident[:, :].unsqueeze(1).to_broadcast([128, 31, 128]),
                in1=vecT[:, c, V_CWDW:V_CWDW + 31].unsqueeze(2).to_broadcast([128, 31, 128]), op=ALU.mult)
            zsrc = [mU[c][0], mU[c][1]]
            ph, uh = psum(); p0, u0 = psum(); p1, u1 = psum()
            for k in range(31):
                MM(ph[:, 0:30], dg[:, k, :], zh[:, c, k:k + 30], k == 0, k == 30, [dgU, zhU], [uh], signal=(k == 30))
            for k in range(31):
                MM(p0[:, 0:482], dg[:, k, :], mid[:, c, k:k + 482], k == 0, k == 30, [dgU] + zsrc, [u0], signal=(k == 30))
            for k in range(31):
                MM(p1[:, 0:512], dg[:, k, :], mid[:, c, 482 + k:482 + k + 512], k == 0, k == 30, [dgU] + zsrc, [u1], signal=(k == 30))
            bdw = vecT[:, c, V_CBDW:V_CBDW + 1]
            ACT(hT[:, c, 0:30], ph[:, 0:30], AF.Identity, [uh, cU], [hU[0]], bias=bdw, scale=1.0)
            ACT(hT[:, c, 30:512], p0[:, 0:482], AF.Identity, [u0, cU], [hU[0]], bias=bdw, scale=1.0)
            ACT(hT[:, c, 512:TP], p1[:, 0:512], AF.Identity, [u1, cU], [hU[1]], bias=bdw, scale=1.0)
        for bi, (c0, n) in enumerate(BLKS[:2]):
            p1, u1 = psum()
            for c in range(KC):
                MM(p1[:, :n], onesb[:, :], hT[:, c, c0:c0 + n], c == 0, c == KC - 1, [hU[bi], cU], [u1], signal=(c == KC - 1))
            p2, u2 = sumsq_fm(lambda kc: hT[:, kc, c0:c0 + n], lambda kc: [hU[bi]], KC, n)
            mu, muU, rs, rsU2 = lnmu, Unit(), lnrs, Unit()
            DVE("tensor_scalar_mul", [u1], [muU], out=mu[:, :n], in0=p1[:, :n], scalar1=1.0 / D)
            DVE("tensor_tensor", [muU], [rsU2], out=rs[:, :n], in0=mu[:, :n], in1=mu[:, :n], op=ALU.mult)
            DVE("scalar_tensor_tensor", [u2, rsU2], [rsU2], out=rs[:, :n], in0=p2[:, :n], scalar=1.0 / D, in1=rs[:, :n], op0=ALU.mult, op1=ALU.subtract)
            ACT(rs[:, :n], rs[:, :n], AF.Ln, [rsU2, cU], [rsU2], bias=epsT[:, 0:1], scale=1.0)
            ACT(rs[:, :n], rs[:, :n], AF.Exp, [rsU2], [rsU2], scale=-0.5)
            for c in range(KC):
                t_, tu_ = tf()
                DVE("tensor_tensor", [hU[bi], muU], [tu_], out=t_[:, :n], in0=hT[:, c, c0:c0 + n], in1=mu[:, :n], op=ALU.subtract)
                DVE("tensor_tensor", [tu_, rsU2], [tu_], out=t_[:, :n], in0=t_[:, :n], in1=rs[:, :n], op=ALU.mult)
                ACT(hT[:, c, c0:c0 + n], t_[:, :n], AF.Silu, [tu_, cU], [hU[bi]], bias=vecT[:, c, V_CLNB:V_CLNB + 1], scale=vecT[:, c, V_CLNG:V_CLNG + 1])
        B.barrier()
        prod31 = yacc[0][:, 0:16 * 31].rearrange("p (c t) -> p c t", t=31)
        DVE("tensor_tensor", [zsU, cU, yU[0]], [yU[0]], out=prod31, in0=zcs[:, :, :], in1=vecT[:, :, V_CWDW:V_CWDW + 31], op=ALU.mult)
        DVE("tensor_reduce", [yU[0]], [ysU], out=ysm[:, 0:16], in_=prod31, axis=AX.X, op=ALU.add)
        DVE("tensor_tensor", [ysU, cU], [ysU], out=ysm[:, 0:16], in0=ysm[:, 0:16], in1=vecT[:, :, V_CBDW], op=ALU.add)
        col_ln(ysm[:, 0:16], [ysU], V_CLNG, V_CLNB, ysm[:, 16:32], [ysU], scr, scrU)
        ACT(hT[:, :, TP], ysm[:, 16:32], AF.Silu, [ysU], [hU[2]])
        out_proj_steps(w_out, 0, KC, src=hT, srcU=lambda kc, bi: hU[bi])
        run_steps()

    for i in range(4):
        if on("mix%d" % i):
            if i % 3 == 0:
                gmlp(i, i // 3)
            elif i % 3 == 1:
                dilattn(i)
            else:
                convmod(i)
        if on("xat%d" % i):
            xattn(i)
        if on("ffn%d" % i):
            ffn(i)

    arena_reset()
    stg = [carve(2048), carve(2048)]
    for t in range(8):
        fm_to_rows(stg, lambda kc, t=t: xT[:, kc, t * 128:(t + 1) * 128], lambda kc, t=t: [xU[kc][t // 4]], 128, y_p.ap()[t * 128:(t + 1) * 128, :])
    if "xs" not in SKIP:
        fm_to_rows(stg, lambda kc: xT[:, kc, TP:TP + 1], lambda kc: [xU[kc][2]], 1, y_s.ap())

    sp = B.engs["sp"]
    for ds in B.dss:
        if ds.count:
            sp.prog.append(("wait", ds.sem, ds.count))
    B.emit()
    return B


def t5_bucket_np(dist):
    import math
    max_exact = 16
    d = np.maximum(dist, 1).astype(np.float32)
    large = max_exact + (np.log(d / max_exact) / math.log(2048 / max_exact) * (32 - max_exact)).astype(np.int32)
    large = np.minimum(large, 31)
    return np.where(dist < max_exact, dist, large)


_CACHE = {}


def kernel(**inp):
    if "B" not in _CACHE:
        _CACHE["B"] = build_program()
    B = _CACHE["B"]
    f = lambda a: np.ascontiguousarray(a, dtype=np.float32)
    vec_rows = [inp["g_mix"], inp["g_xattn"], inp["g_mem"], inp["g_ffn"], inp["a_ln_g"], inp["a_ln_b"],
                inp["c_b_in"].reshape(2, D), inp["c_b_dw"], inp["c_ln_g"], inp["c_ln_b"], inp["c_w_dw"][0]]
    vecs = f(np.concatenate([np.asarray(v).reshape(-1, D) for v in vec_rows], axis=0))
    assert vecs.shape[0] == NV
    gsm = f(np.concatenate([inp["x_q_norm"], inp["x_k_norm"], inp["b_q_norm"][0], inp["b_k_norm"][0]], axis=0).T)
    ident = np.eye(128, dtype=np.float32)
    masku = np.triu(np.ones((128, 128), np.float32))
    selg = np.zeros((33, 9, 255), np.float32)
    u = np.arange(255)
    for g, dil in enumerate((1, 4, 16)):
        cur = np.where(u >= 127, t5_bucket_np(np.maximum(u - 127, 0) * dil), 32)
        prev = np.where(u <= 127, t5_bucket_np((u + 1) * dil), 32)
        third = prev if g < 2 else cur
        for v, idx in enumerate((cur, prev, third)):
            selg[idx, g * 3 + v, u] = 1.0
    selg = selg.reshape(33, 9 * 255)
    sels = np.zeros((32, 3, 128), np.float32)
    jj = np.arange(128)
    for g, dil in enumerate((1, 4, 16)):
        sels[t5_bucket_np((128 - jj) * dil), g, jj] = 1.0
    sels = sels.reshape(32, 384)
    shared = dict(vecs=vecs, gsm=gsm, relb=f(inp["rel_bias"]), ident=ident, masku=masku, selg=selg, sels=sels,
                  a_w_s=f(inp["a_w_s"]), a_b_s=f(inp["a_b_s"]).reshape(2, 2048),
                  a_w_in=f(inp["a_w_in"]), a_w_out=f(inp["a_w_out"]), b_w_qkv=f(inp["b_w_qkv"][0]), b_w_out=f(inp["b_w_out"][0]),
                  c_w_in=f(inp["c_w_in"][0]), c_w_out=f(inp["c_w_out"][0]), x_w_q=f(inp["x_w_q"]), x_w_kv=f(inp["x_w_kv"]),
                  x_w_o=f(inp["x_w_o"]), f_w_in=f(inp["f_w_in"]), f_w_out=f(inp["f_w_out"]))
    in_maps = []
    for c in range(NCORES):
        b, half = c // 2, c % 2
        m = dict(shared)
        m["x_p"] = f(inp["x_prompt"][b, half * TP:(half + 1) * TP])
        m["x_s"] = f(inp["x_sample"][c])
        m["mem"] = f(inp["mem_prompt"][b])
        m["flag"] = np.full((128, 1), float(half), np.float32)
        caches = ((inp["cache_b_k_w128"], inp["cache_b_v_w128"]), (inp["cache_b_k_w512"], inp["cache_b_v_w512"]),
                  (inp["cache_b_k_w2048"], inp["cache_b_v_w2048"]))
        for g, w in enumerate((128, 512, 2048)):
            m["ck%d" % g] = f(caches[g][0][0, c]).reshape(w, D)
            m["cv%d" % g] = f(caches[g][1][0, c]).reshape(w, D)
        m["cst"] = f(inp["state_c_conv"][0, c])
        m["cmk"] = f(inp["cache_mem_k"][:, c]).reshape(4, 256, 512)
        m["cmv"] = f(inp["cache_mem_v"][:, c]).reshape(4, 256, 512)
        in_maps.append(m)
    nrun = int(os.environ.get("MK_NCORES", NCORES))
    in_maps = [{k: m[k] for k in B.used_inputs} for m in in_maps]
    res = run_bass_kernel_spmd(B.nc, in_maps[:nrun], core_ids=list(range(nrun)))
    R = list(res.results) + [res.results[c % nrun] for c in range(nrun, NCORES)]
    o = lambda c, k: np.asarray(R[c][k], dtype=np.float32)
    y_prompt = np.stack([np.concatenate([o(2 * b, "y_p"), o(2 * b + 1, "y_p")], axis=0) for b in range(4)])
    y_sample = np.stack([o(c, "y_s") for c in range(8)])
    outs = [y_prompt, y_sample]
    for g, n in enumerate((128, 512, 2048)):
        for kv in ("bk", "bv"):
            if g < 2:
                a = np.stack([o(2 * b + 1, "%s%d_p" % (kv, g)) for b in range(4)])
            else:
                a = np.stack([np.concatenate([o(2 * b, "%s2_p" % kv), o(2 * b + 1, "%s2_p" % kv)], axis=0) for b in range(4)])
            outs.append(a.reshape(1, 4, n, 16, 128))
    outs.append(np.stack([o(2 * b + 1, "cconv_p") for b in range(4)])[None])
    outs.append(np.stack([o(2 * b, "memk_p") for b in range(4)], axis=1).reshape(4, 4, 256, 4, 128))
    outs.append(np.stack([o(2 * b, "memv_p") for b in range(4)], axis=1).reshape(4, 4, 256, 4, 128))
    for g, w in enumerate((128, 512, 2048)):
        for kv in ("bk", "bv"):
            outs.append(np.stack([o(c, "%s%d_s" % (kv, g)) for c in range(8)]).reshape(1, 8, w, 16, 128))
    outs.append(np.stack([o(c, "cconv_s") for c in range(8)])[None])
    outs.append(np.stack([o(c, "av_s") for c in range(8)], axis=1).reshape(2, 8, 1, D))
    return tuple(outs)
```

```python
import os
import numpy as np
from contextlib import ExitStack
import concourse.bass as bass
import concourse.mybir as mybir
from concourse.bass_utils import run_bass_kernel_spmd

F32 = mybir.dt.float32
BF16 = mybir.dt.bfloat16
AF = mybir.ActivationFunctionType
ALU = mybir.AluOpType
AX = mybir.AxisListType

D = 2048
KC = 16
TP = 1024
XC = TP + 1
NCORES = 8
FFN_H = 5632
EPS = 1e-6
SCALE = 128 ** -0.5
NSLOT = 2
PAIRS = [[0, 1], [2, 3], [4, 5], [6, 7]]
PAIRS_RUN = PAIRS[:int(os.environ.get("MK_NCORES", 8)) // 2]

V_GMIX, V_GXAT, V_GMEM, V_GFFN = 0, 4, 8, 12
V_ALNG, V_ALNB = 16, 18
V_CBIN, V_CBDW, V_CLNG, V_CLNB, V_CWDW = 20, 22, 23, 24, 25
NV = 56
S_XQ, S_XK, S_BQ, S_BK = 0, 4, 8, 11

BLKS = [(0, 512), (512, 512), (1024, 1)]

PLAN = os.environ.get("MK_PLAN", "")
SKIP = os.environ.get("MK_SKIP", "").split(",")


class Unit:
    __slots__ = ("w", "rs", "excl")

    def __init__(self, excl=False):
        self.w = None
        self.rs = {}
        self.excl = excl


class Eng:
    def __init__(self, name, sem):
        self.name = name
        self.sem = sem
        self.n = 0
        self.seen = {}
        self.prog = []


class DS:
    def __init__(self, sem, nb=False):
        self.sem = sem
        self.count = 0
        self.nb = nb


class Builder:
    def __init__(self):
        self.nc = bass.Bass("TRN2", target_bir_lowering=False)
        self.es = ExitStack()
        self.sems = []
        self.engs = {}
        self.dss = []
        self.nsb = 0

    def new_sem(self, name):
        s = self.es.enter_context(self.nc.semaphore(name))
        self.sems.append(s)
        return len(self.sems) - 1

    def new_ds(self, nb=False):
        ds = DS(self.new_sem("d%d" % len(self.dss)), nb)
        self.dss.append(ds)
        return ds

    def sb(self, shape, dt, name=None):
        self.nsb += 1
        return self.es.enter_context(self.nc.sbuf_tensor("s_" + (name or ("sb%d" % self.nsb)), list(shape), dt))

    def dram(self, name, shape, dt, kind=None):
        if kind is None:
            return self.nc.dram_tensor(name, list(shape), dt)
        return self.nc.dram_tensor(name, list(shape), dt, kind=kind)

    def _waits(self, eng, R, W):
        need = {}
        for u in R:
            if u.w is not None and need.get(u.w[0], 0) < u.w[1]:
                need[u.w[0]] = u.w[1]
            if u.excl:
                for s, v in u.rs.items():
                    if s != eng.sem and need.get(s, 0) < v:
                        need[s] = v
        for u in W:
            if u.w is not None and need.get(u.w[0], 0) < u.w[1]:
                need[u.w[0]] = u.w[1]
            for s, v in u.rs.items():
                if need.get(s, 0) < v:
                    need[s] = v
        for s, v in need.items():
            if eng.name == "pe" and s == eng.sem:
                continue
            if eng.seen.get(s, 0) < v:
                eng.prog.append(("wait", s, v))
                eng.seen[s] = v

    def _mark(self, tok, R, W):
        for u in R:
            if u.rs.get(tok[0], 0) < tok[1]:
                u.rs[tok[0]] = tok[1]
        for u in W:
            u.w = tok
            u.rs = {}

    def op(self, eng, meth, R=(), W=(), signal=True, **kw):
        eng = self.engs[eng]
        self._waits(eng, R, W)
        if signal:
            eng.n += 1
            tok = (eng.sem, eng.n)
            eng.prog.append(("ins", meth, kw, eng.sem))
        else:
            tok = (eng.sem, eng.n + 1)
            eng.prog.append(("ins", meth, kw, None))
        self._mark(tok, R, W)

    def dma(self, q, out, in_, ds, R=(), W=(), slow=False):
        eng = self.engs[q]
        self._waits(eng, R, W)
        ds.count += 16
        tok = (ds.sem, ds.count)
        eng.prog.append(("dma", out, in_, ds.sem, slow))
        self._mark(tok, R, W)

    def cc(self, ins, outs, groups, sem, R=(), W=(), count=1):
        eng = self.engs["pool"]
        self._waits(eng, R, W)
        eng.prog.append(("cc", ins, outs, groups, sem))
        self._mark((sem, count), R, W)

    def barrier(self):
        sp = self.engs["sp"]
        for ds in self.dss:
            if ds.count and not ds.nb and sp.seen.get(ds.sem, 0) < ds.count:
                sp.prog.append(("wait", ds.sem, ds.count))
                sp.seen[ds.sem] = ds.count
        for e in self.engs.values():
            for x in self.engs.values():
                if x is e or x.n == 0:
                    continue
                if e.seen.get(x.sem, 0) < x.n:
                    e.prog.append(("wait", x.sem, x.n))
                    e.seen[x.sem] = x.n
        sp.n += 1
        sp.prog.append(("seminc", sp.sem))
        for e in self.engs.values():
            if e is not sp:
                e.prog.append(("wait", sp.sem, sp.n))
                e.seen[sp.sem] = sp.n

    def emit(self):
        nc = self.nc
        sems = self.sems
        with nc.Block() as block:
            def run(eng):
                def body(h):
                    for it in eng.prog:
                        if it[0] == "wait":
                            h.wait_ge(sems[it[1]], it[2])
                        elif it[0] == "ins":
                            ins = getattr(h, it[1])(**it[2])
                            if it[3] is not None:
                                ins.then_inc(sems[it[3]], 1)
                        elif it[0] == "dma":
                            if it[4]:
                                h.dma_start(out=it[1], in_=it[2], allow_slow_non_contiguous=True).then_inc(sems[it[3]], 16)
                            else:
                                h.dma_start(out=it[1], in_=it[2]).then_inc(sems[it[3]], 16)
                        elif it[0] == "cc":
                            h.collective_compute("AllGather", ALU.bypass, replica_groups=it[3], ins=[it[1]], outs=[it[2]]).then_inc(sems[it[4]])
                        elif it[0] == "seminc":
                            h.sem_inc(sems[it[1]], 1)
                        elif it[0] == "raw":
                            it[1](h)
                return body
            block.sync(run(self.engs["sp"]))
            block.scalar(run(self.engs["act"]))
            block.vector(run(self.engs["dve"]))
            block.tensor(run(self.engs["pe"]))
            block.gpsimd(run(self.engs["pool"]))


def build_program():
    B = Builder()
    nc = B.nc
    for name in ("pe", "act", "dve", "pool", "sp"):
        B.engs[name] = Eng(name, B.new_sem("e_" + name))
    plan = [s for s in PLAN.split(",") if s]

    def on(tag):
        return (not plan) or (tag in plan)

    class LazyIn:
        def __init__(self, name, shape):
            self.name, self.shape, self.t = name, shape, None

        def handle(self):
            if self.t is None:
                self.t = B.dram(self.name, self.shape, F32, kind="ExternalInput")
                B.used_inputs.append(self.name)
            return self.t

        def ap(self):
            return self.handle().ap()

    B.used_inputs = []

    def din(name, shape):
        return LazyIn(name, shape)

    def dout(name, shape):
        return B.dram(name, shape, F32, kind="ExternalOutput")

    x_p = din("x_p", [TP, D]); x_s = din("x_s", [1, D]); mem = din("mem", [256, D])
    vecs = din("vecs", [NV, D]); gsm = din("gsm", [128, 14]); relb = din("relb", [32, 48])
    flag = din("flag", [128, 1]); ident_d = din("ident", [128, 128]); masku_d = din("masku", [128, 128])
    selg = din("selg", [33, 9 * 255]); sels_d = din("sels", [32, 3 * 128])
    a_w_s = din("a_w_s", [2, 16, 128, 128]); a_b_s = din("a_b_s", [2, 16 * 128])
    ck = [din("ck%d" % g, [w, D]) for g, w in enumerate((128, 512, 2048))]
    cv = [din("cv%d" % g, [w, D]) for g, w in enumerate((128, 512, 2048))]
    cst = din("cst", [30, D]); cmk = din("cmk", [4, 256, 512]); cmv = din("cmv", [4, 256, 512])
    a_w_in = din("a_w_in", [2, D, 4096]); a_w_out = din("a_w_out", [2, D, D])
    b_w_qkv = din("b_w_qkv", [D, 18432]); b_w_out = din("b_w_out", [D, D])
    c_w_in = din("c_w_in", [D, 4096]); c_w_out = din("c_w_out", [D, D])
    x_w_q = din("x_w_q", [4, D, 512]); x_w_kv = din("x_w_kv", [4, D, 1024]); x_w_o = din("x_w_o", [4, 512, D])
    f_w_in = din("f_w_in", [4, D, 2 * FFN_H]); f_w_out = din("f_w_out", [4, FFN_H, D])

    y_p = dout("y_p", [TP, D]); y_s = dout("y_s", [1, D])
    bk_p = [dout("bk%d_p" % g, [n, D]) for g, n in enumerate((128, 512, 1024))]
    bv_p = [dout("bv%d_p" % g, [n, D]) for g, n in enumerate((128, 512, 1024))]
    cconv_p = dout("cconv_p", [30, D]); memk_p = dout("memk_p", [4, 256, 512]); memv_p = dout("memv_p", [4, 256, 512])
    bk_s = [dout("bk%d_s" % g, [w, D]) for g, w in enumerate((128, 512, 2048))]
    bv_s = [dout("bv%d_s" % g, [w, D]) for g, w in enumerate((128, 512, 2048))]
    cconv_s = dout("cconv_s", [30, D]); av_s = dout("av_s", [2, D])

    xT = B.sb([128, KC, XC], F32, "xT"); xU = [[Unit() for _ in BLKS] for _ in range(KC)]
    hT2 = B.sb([128, KC * XC], BF16, "hT"); hU = [Unit() for _ in BLKS]
    hT = hT2[:, :].rearrange("p (k t) -> p k t", t=XC)
    mid = B.sb([128, KC, XC], BF16, "mid"); mU = [[Unit() for _ in BLKS] for _ in range(KC)]
    wsl = [B.sb([128, 8192], BF16, "w%d" % i) for i in range(NSLOT)]
    wU = [Unit() for _ in range(NSLOT)]; wDS = [B.new_ds() for _ in range(NSLOT)]
    ident = B.sb([128, 128], F32, "ident"); onesb = B.sb([128, 128], BF16, "onesb")
    masku = B.sb([128, 128], F32, "masku")
    vecT = B.sb([128, KC, NV], F32, "vecT"); gs = B.sb([128, 14], F32, "gs")
    flg = B.sb([128, 1], F32, "flg"); epsT = B.sb([128, 1], F32, "epsT")
    cU = Unit()
    memhat = B.sb([128, KC, 256], BF16, "memhat"); mhU = Unit()
    ps = [B.es.enter_context(nc.psum_tensor("ps%d" % i, [128, 512], F32)) for i in range(8)]
    pU = [Unit(excl=True) for _ in range(8)]
    AW = int(os.environ.get("MK_AW", 8850))
    arena = B.sb([128, AW], F32, "arena")
    NT = 4
    tmpf = [arena[:, i * 512:(i + 1) * 512] for i in range(NT)]; tU = [Unit() for _ in range(NT)]
    tmpb = [arena[:, NT * 512 + i * 256:NT * 512 + (i + 1) * 256].bitcast(BF16) for i in range(NT)]; bU = [Unit() for _ in range(NT)]
    A0 = NT * 768
    st = {"ps": 0, "tf": 0, "tb": 0, "aoff": 0, "stg": 0}

    def psum():
        i = st["ps"]; st["ps"] = (i + 1) % 8
        return ps[i], pU[i]

    def tf():
        i = st["tf"]; st["tf"] = (i + 1) % NT
        return tmpf[i], tU[i]

    def tb():
        i = st["tb"]; st["tb"] = (i + 1) % NT
        return tmpb[i], bU[i]

    def arena_reset():
        B.barrier()
        st["aoff"] = A0

    def carve(words, dt=F32):
        o = st["aoff"]; st["aoff"] = o + words
        assert st["aoff"] <= AW, ("arena overflow", st["aoff"])
        a = arena[:, o:o + words]
        return a if dt == F32 else a.bitcast(dt)

    op, dma = B.op, B.dma

    def MM(out, lhsT, rhs, start, stop, R, W, signal=True):
        op("pe", "matmul", R=R, W=W, signal=signal, out=out, lhsT=lhsT, rhs=rhs, start=start, stop=stop)

    def TR(out, in_, idn, R, W, signal=True):
        op("pe", "transpose", R=R, W=W, signal=signal, out=out, in_=in_, identity=idn)

    def ACT(out, in_, func, R, W, **kw):
        op("act", "activation", R=R, W=W, out=out, in_=in_, func=func, **kw)

    def DVE(meth, R, W, **kw):
        op("dve", meth, R=R, W=W, **kw)

    def COPY(eng, out, in_, R, W):
        if eng == "act":
            ACT(out, in_, AF.Copy, R, W)
        else:
            DVE("tensor_copy", R, W, out=out, in_=in_)

    cds = B.new_ds()
    dma("sp", ident[:], ident_d.ap(), cds, W=[cU])
    dma("sp", masku[:], masku_d.ap(), cds, W=[cU])
    dma("sp", gs[:], gsm.ap(), cds, W=[cU])
    dma("sp", flg[:], flag.ap(), cds, W=[cU])
    DVE("memset", [], [cU], ap=onesb[:], constant=1.0)
    DVE("memset", [], [cU], ap=epsT[:], constant=EPS)

    ldU = [Unit(), Unit()]; ldDS = [B.new_ds(), B.new_ds()]
    def cache_shift_copies():
        shDS = B.new_ds(nb=True)
        for g, W_ in enumerate((128, 512, 2048)):
            n16 = (W_ - 1) * 16
            for src_t, dst_t in ((ck[g], bk_s[g]), (cv[g], bv_s[g])):
                dma("sp", bass.AP(dst_t, 0, [[n16, 128], [1, n16]]), bass.AP(src_t.handle(), D, [[n16, 128], [1, n16]]), shDS)

    def next_stg():
        i = st["stg"]; st["stg"] ^= 1
        return i

    def rows_to_fm(stg, src_ap, R, dst_fn, dstW, single=False):
        i = 0 if single else next_stg()
        s = stg[i]
        dma("sp", s[0:R, :], src_ap, ldDS[i], W=[ldU[i]])
        for g4 in range(4):
            p, u = psum()
            for j in range(4):
                kc = g4 * 4 + j
                TR(p[:, j * 128:j * 128 + R], s[0:R, kc * 128:(kc + 1) * 128], ident[0:R, 0:R], [ldU[i], cU], [u], signal=(j == 3))
            for j in range(4):
                kc = g4 * 4 + j
                COPY("act" if g4 % 2 else "dve", dst_fn(kc), p[:, j * 128:j * 128 + R], [u], dstW(kc))

    def fm_to_rows(stg, src_fn, srcR, R, dst_ap, single=False):
        i = 0 if single else next_stg()
        s = stg[i]
        for g4 in range(4):
            p, u = psum()
            for j in range(4):
                kc = g4 * 4 + j
                TR(p[0:R, j * 128:(j + 1) * 128], src_fn(kc), ident[:, :], list(srcR(kc)) + [cU], [u], signal=(j == 3))
            COPY("act" if g4 % 2 else "dve", s[0:R, g4 * 512:(g4 + 1) * 512], p[0:R, :], [u], [ldU[i]])
        dma("sp", dst_ap, s[0:R, :], ldDS[i], R=[ldU[i]])

    arena_reset()
    stg = [carve(2048), carve(2048)]
    if "vecs" not in SKIP:
        rows_to_fm(stg, vecs.ap(), NV, lambda kc: vecT[:, kc, :], lambda kc: [cU])
    for t in range(8):
        rows_to_fm(stg, x_p.ap()[t * 128:(t + 1) * 128, :], 128,
                   lambda kc, t=t: xT[:, kc, t * 128:(t + 1) * 128], lambda kc, t=t: [xU[kc][t // 4]])
    if "xs" not in SKIP:
        rows_to_fm(stg, x_s.ap(), 1, lambda kc: xT[:, kc, TP:TP + 1], lambda kc: [xU[kc][2]])

    def rstd_from_psum(p, u, n, inv_n):
        r, ru = tf()
        ACT(r[:, :n], p[:, :n], AF.Ln, [u, cU], [ru], bias=epsT[:, 0:1], scale=inv_n)
        ACT(r[:, :n], r[:, :n], AF.Exp, [ru], [ru], scale=-0.5)
        return r, ru

    def sumsq_fm(src_fn, srcU, nk, n):
        p, u = psum()
        for kc in range(nk):
            s, su = tb()
            ACT(s[:, :n], src_fn(kc), AF.Square, list(srcU(kc)), [su])
            MM(p[:, :n], onesb[:, :], s[:, :n], kc == 0, kc == nk - 1, [su, cU], [u])
        return p, u

    def rmsnorm(vrow):
        for bi, (c0, n) in enumerate(BLKS):
            p, u = sumsq_fm(lambda kc: xT[:, kc, c0:c0 + n], lambda kc: [xU[kc][bi]], KC, n)
            r, ru = rstd_from_psum(p, u, n, 1.0 / D)
            for kc in range(KC):
                DVE("scalar_tensor_tensor", [xU[kc][bi], ru, cU], [hU[bi]], out=hT[:, kc, c0:c0 + n], in0=xT[:, kc, c0:c0 + n],
                    scalar=vecT[:, kc, vrow:vrow + 1], in1=r[:, :n], op0=ALU.mult, op1=ALU.mult)

    def load_memhat():
        mT = carve(KC * 64).rearrange("p (kc t) -> p kc t", t=64); mTU = [Unit() for _ in range(KC)]
        for t in range(4):
            rows_to_fm(stg, mem.ap()[t * 64:(t + 1) * 64, :], 64, lambda kc: mT[:, kc, :], lambda kc: [mTU[kc]])
            p, u = sumsq_fm(lambda kc: mT[:, kc, :], lambda kc: [mTU[kc]], KC, 64)
            r, ru = rstd_from_psum(p, u, 64, 1.0 / D)
            for kc in range(KC):
                DVE("tensor_tensor", [mTU[kc], ru], [mhU], out=memhat[:, kc, t * 64:(t + 1) * 64], in0=mT[:, kc, :], in1=r[:, :64], op=ALU.mult)
    if "mem" not in SKIP:
        load_memhat()

    steps = []

    def wstep(loads, compute, post_issue=None):
        steps.append((loads, compute, post_issue))

    def wsrc(w_ap, k0, nk, c0, ncol):
        return w_ap[k0 * 128:(k0 + nk) * 128, c0:c0 + ncol].rearrange("(kc p) n -> p kc n", p=128)

    def slot3(slot, nk, ncol):
        return slot[:, 0:nk * ncol].rearrange("p (kc n) -> p kc n", n=ncol)

    def run_steps():
        issued = 0
        for k in range(len(steps)):
            while issued < min(len(steps), k + NSLOT):
                si = issued % NSLOT
                for dst_fn, src in steps[issued][0]:
                    dma("pool", dst_fn(wsl[si]), src, wDS[si], W=[wU[si]])
                if steps[issued][2] is not None:
                    steps[issued][2]()
                issued += 1
            steps[k][1](wsl[k % NSLOT], wU[k % NSLOT])
        steps.clear()

    def mm_fm(p, u, w3, wu, nk, oc, in_t, inU, c0, n):
        for kc in range(nk):
            MM(p[:, :n], w3[:, kc, oc * 128:(oc + 1) * 128], in_t[:, kc, c0:c0 + n], kc == 0, kc == nk - 1, [wu] + list(inU(kc)), [u], signal=(kc == nk - 1))

    def resid_add(p, u, oc, bi, c0, n):
        DVE("tensor_tensor", [u], [xU[oc][bi]], out=xT[:, oc, c0:c0 + n], in0=p[:, :n], in1=xT[:, oc, c0:c0 + n], op=ALU.add)

    def out_proj_steps(w_ap, k0, nk, src=None, srcU=None):
        src = mid if src is None else src
        srcU = (lambda kc, bi: mU[kc][bi]) if srcU is None else srcU

        def mk(cb):
            def comp(slot, wu):
                w3 = slot3(slot, nk, 512)
                for o4 in range(4):
                    for bi, (c0, n) in enumerate(BLKS):
                        p, u = psum()
                        mm_fm(p, u, w3, wu, nk, o4, src, lambda kc: [srcU(kc, bi)], c0, n)
                        resid_add(p, u, cb * 4 + o4, bi, c0, n)
            return comp
        for cb in range(4):
            wstep([(lambda s: slot3(s, nk, 512), wsrc(w_ap, k0, nk, cb * 512, 512))], mk(cb))

    def ffn(i):
        rmsnorm(V_GFFN + i)
        if i == 0 and on("mix1"):
            cache_shift_copies()
        w_in = f_w_in.ap()[i]; w_out = f_w_out.ap()[i]

        def mk_in(c2, c_lo):
            def comp(slot, wu):
                w3 = slot3(slot, KC, 512)
                for j in range(2):
                    lc = c2 + j - c_lo
                    for bi, (c0, n) in enumerate(BLKS):
                        pg, ug = psum(); pu, uu = psum()
                        mm_fm(pg, ug, w3, wu, KC, j, hT, lambda kc: [hU[bi]], c0, n)
                        mm_fm(pu, uu, w3, wu, KC, 2 + j, hT, lambda kc: [hU[bi]], c0, n)
                        s, su = tf()
                        ACT(s[:, :n], pg[:, :n], AF.Silu, [ug], [su])
                        DVE("tensor_tensor", [uu, su], [mU[lc][bi]], out=mid[:, lc, c0:c0 + n], in0=pu[:, :n], in1=s[:, :n], op=ALU.mult)
            return comp
        for c_lo, c_hi in ((0, 16), (16, 32), (32, 44)):
            for c2 in range(c_lo, c_hi, 2):
                wstep([(lambda s: slot3(s, KC, 512)[:, :, 0:256], wsrc(w_in, 0, KC, c2 * 128, 256)),
                       (lambda s: slot3(s, KC, 512)[:, :, 256:512], wsrc(w_in, 0, KC, FFN_H + c2 * 128, 256))], mk_in(c2, c_lo))
            out_proj_steps(w_out, c_lo, c_hi - c_lo)
        run_steps()

    def head_norm(p, u, n, gcol, out_bf, outW, out_f32=None, out32W=()):
        q, qu = tf()
        ACT(q[:, :n], p[:, :n], AF.Copy, [u], [qu])
        s, su = tb()
        DVE("tensor_tensor", [qu], [su], out=s[:, :n], in0=q[:, :n], in1=q[:, :n], op=ALU.mult)
        p2, u2 = psum()
        MM(p2[:, :n], onesb[:, :], s[:, :n], True, True, [su, cU], [u2])
        r, ru = rstd_from_psum(p2, u2, n, 1.0 / 128)
        if out_f32 is not None:
            DVE("scalar_tensor_tensor", [qu, ru, cU], list(out32W), out=out_f32, in0=q[:, :n], scalar=gs[:, gcol:gcol + 1], in1=r[:, :n],
                op0=ALU.mult, op1=ALU.mult)
            ACT(out_bf, out_f32, AF.Copy, list(out32W), list(outW))
        else:
            DVE("scalar_tensor_tensor", [qu, ru, cU], list(outW), out=out_bf, in0=q[:, :n], scalar=gs[:, gcol:gcol + 1], in1=r[:, :n],
                op0=ALU.mult, op1=ALU.mult)

    def xattn(i):
        arena_reset()
        def qT(hh, c0, n):
            return mid[:, 4 + hh, c0:c0 + n]
        kTp = mid[:, 8, 0:1024].rearrange("p (h t) -> p h t", t=256); kpU = [mU[8][0], mU[8][1]]
        kTs = mid[:, 9, 0:1024].rearrange("p (h t) -> p h t", t=256); ksU = [mU[9][0], mU[9][1]]
        vp = mid[:, 10, 0:1024].rearrange("p (m c) -> p m c", c=512); vpU = [mU[10][0], mU[10][1]]
        vs = mid[:, 11, 0:1024].rearrange("p (m c) -> p m c", c=512); vsU = [mU[11][0], mU[11][1]]
        kst = carve(1024).rearrange("p (m c) -> p m c", c=512); kstU = Unit(); kstDS = B.new_ds()
        vsDS = B.new_ds()
        kf = carve(4 * 256).rearrange("p (h t) -> p h t", t=256); kfU = [Unit() for _ in range(4)]
        ost = [carve(512), carve(512)]; ostU = [Unit(), Unit()]; ostDS = [B.new_ds(), B.new_ds()]
        oi = [0]

        def next_o():
            oi[0] ^= 1
            return oi[0]

        dma("sp", kst[:, :, :], cmk.ap()[i].rearrange("(m p) c -> p m c", p=128), kstDS, W=[kstU])
        for hh in range(4):
            p, u = psum()
            for m in range(2):
                TR(p[:, m * 128:(m + 1) * 128], kst[:, m, hh * 128:(hh + 1) * 128], ident[:, :], [kstU, cU], [u], signal=(m == 1))
            ACT(kTs[:, hh, :], p[:, 0:256], AF.Copy, [u], ksU)
        dma("pool", vs[:, :, :], cmv.ap()[i].rearrange("(m p) c -> p m c", p=128), vsDS, W=vsU)

        rmsnorm(V_GXAT + i)

        def comp_q(slot, wu):
            w3 = slot3(slot, KC, 512)
            for hh in range(4):
                for bi, (c0, n) in enumerate(BLKS):
                    p, u = psum()
                    mm_fm(p, u, w3, wu, KC, hh, hT, lambda kc: [hU[bi]], c0, n)
                    head_norm(p, u, n, S_XQ + i, qT(hh, c0, n), [mU[4 + hh][bi]])
        wstep([(lambda s: slot3(s, KC, 512), wsrc(x_w_q.ap()[i], 0, KC, 0, 512))], comp_q)

        def fold_gain(w3, wu):
            g = vecT[:, :, V_GMEM + i:V_GMEM + i + 1].to_broadcast([128, KC, 512])
            DVE("tensor_tensor", [wu, cU], [wu], out=w3, in0=w3, in1=g, op=ALU.mult)

        def comp_k(slot, wu):
            w3 = slot3(slot, KC, 512)
            fold_gain(w3, wu)
            for hh in range(4):
                p, u = psum()
                mm_fm(p, u, w3, wu, KC, hh, memhat, lambda kc: [mhU], 0, 256)
                head_norm(p, u, 256, S_XK + i, kTp[:, hh, :], kpU, out_f32=kf[:, hh, :], out32W=[kfU[hh]])
            for m in range(2):
                si = next_o()
                p, u = psum()
                for hh in range(4):
                    TR(p[:, hh * 128:(hh + 1) * 128], kf[:, hh, m * 128:(m + 1) * 128], ident[:, :], [kfU[hh], cU], [u], signal=(hh == 3))
                DVE("tensor_copy", [u], [ostU[si]], out=ost[si][:, :], in_=p[:, :])
                dma("sp", memk_p.ap()[i, m * 128:(m + 1) * 128, :], ost[si][:, :], ostDS[si], R=[ostU[si]])
        wstep([(lambda s: slot3(s, KC, 512), wsrc(x_w_kv.ap()[i], 0, KC, 0, 512))], comp_k)

        def comp_v(slot, wu):
            w3 = slot3(slot, KC, 512)
            fold_gain(w3, wu)
            for m in range(2):
                si = next_o()
                p, u = psum()
                for kc in range(KC):
                    MM(p[:, :], memhat[:, kc, m * 128:(m + 1) * 128], w3[:, kc, :], kc == 0, kc == KC - 1, [wu, mhU], [u], signal=(kc == KC - 1))
                DVE("tensor_copy", [u], [ostU[si]], out=ost[si][:, :], in_=p[:, :])
                ACT(vp[:, m, :], ost[si][:, :], AF.Copy, [ostU[si]], vpU)
                dma("sp", memv_p.ap()[i, m * 128:(m + 1) * 128, :], ost[si][:, :], ostDS[si], R=[ostU[si]])
        wstep([(lambda s: slot3(s, KC, 512), wsrc(x_w_kv.ap()[i], 0, KC, 512, 512))], comp_v)

        def comp_o(slot, wu):
            for bi, (c0, n) in enumerate(BLKS):
                kT, kU, vv, vU = (kTs, ksU, vs, vsU) if bi == 2 else (kTp, kpU, vp, vpU)
                for hh in range(4):
                    po, uo = psum(); pd, ud = psum()
                    for m in range(2):
                        p, u = psum()
                        MM(p[:, :n], kT[:, hh, m * 128:(m + 1) * 128], qT(hh, c0, n), True, True, kU + [mU[4 + hh][bi]], [u])
                        e, eu = tb()
                        ACT(e[:, :n], p[:, :n], AF.Exp, [u], [eu], scale=SCALE)
                        MM(po[:, :n], vv[:, m, hh * 128:(hh + 1) * 128], e[:, :n], m == 0, m == 1, vU + [eu], [uo])
                        MM(pd[:, :n], onesb[:, :], e[:, :n], m == 0, m == 1, [eu, cU], [ud])
                    r, ru = tf()
                    ACT(r[:, :n], pd[:, :n], AF.Ln, [ud], [ru])
                    ACT(r[:, :n], r[:, :n], AF.Exp, [ru], [ru], scale=-1.0)
                    DVE("tensor_tensor", [uo, ru], [mU[hh][bi]], out=mid[:, hh, c0:c0 + n], in0=po[:, :n], in1=r[:, :n], op=ALU.mult)
            w3 = slot[:, 0:4 * D].rearrange("p (kc n) -> p kc n", n=D)
            for oc in range(KC):
                for bi, (c0, n) in enumerate(BLKS):
                    p, u = psum()
                    mm_fm(p, u, w3, wu, 4, oc, mid, lambda kc: [mU[kc][bi]], c0, n)
                    resid_add(p, u, oc, bi, c0, n)
        wstep([(lambda s: s[:, 0:4 * D].rearrange("p (kc n) -> p kc n", n=D), wsrc(x_w_o.ap()[i], 0, 4, 0, D))], comp_o)
        run_steps()

    def gmlp(i, j):
        arena_reset()
        w_in = a_w_in.ap()[j]; w_out = a_w_out.ap()[j]
        NTL = 2
        gv = carve(NTL * D // 2, BF16).rearrange("p (t c) -> p t c", c=D); gvU = [Unit() for _ in range(NTL)]
        wsT = carve(16 * 128 // 2, BF16).rearrange("p (g q) -> p g q", q=128); wsU = Unit()
        Cb = carve(16 * 128).rearrange("p (g q) -> p g q", q=128); CU = Unit(); CDS = B.new_ds()
        wst = carve(512).rearrange("p (g q) -> p g q", q=128); wstU = Unit(); wstDS = B.new_ds()
        sm = carve(64); smU = Unit(); smDS = B.new_ds()
        ws00, bs0, gvs, vln = sm[:, 0:16], sm[:, 16:32], sm[:, 32:48], sm[:, 48:64]
        stat = carve(16); statU = Unit()
        rmsnorm(V_GMIX + i)
        dma("sp", Cb[:, :, :], a_b_s.ap()[j].partition_broadcast(128).rearrange("p (g q) -> p g q", q=128), CDS, W=[CU])
        dma("sp", ws00, bass.AP(a_w_s.handle(), j * 16 * 16384, [[0, 128], [16384, 16]]), smDS, W=[smU], slow=True)
        dma("sp", bs0, bass.AP(a_b_s.handle(), j * 2048, [[0, 128], [128, 16]]), smDS, W=[smU], slow=True)
        for g4 in range(4):
            dma("sp", wst[:, :, :], a_w_s.ap()[j, g4 * 4:(g4 + 1) * 4].rearrange("g p q -> p g q"), wstDS, W=[wstU])
            p, u = psum()
            for k in range(4):
                TR(p[:, k * 128:(k + 1) * 128], wst[:, k, :], ident[:, :], [wstU, cU], [u], signal=(k == 3))
            for k in range(4):
                g = g4 * 4 + k
                DVE("tensor_tensor", [u, cU], [wsU], out=wsT[:, g, :], in0=p[:, k * 128:(k + 1) * 128], in1=masku[:, :], op=ALU.mult)
        for g4 in range(4):
            p, u = psum()
            for k in range(4):
                g = g4 * 4 + k
                MM(p[:, k * 128:(k + 1) * 128], onesb[:, :], wsT[:, g, :], True, True, [wsU, cU], [u], signal=(k == 3))
            for k in range(4):
                g = g4 * 4 + k
                DVE("scalar_tensor_tensor", [u, CU, cU], [CU], out=Cb[:, g, :], in0=p[:, k * 128:(k + 1) * 128], scalar=vecT[:, g, V_ALNB + j:V_ALNB + j + 1],
                    in1=Cb[:, g, :], op0=ALU.mult, op1=ALU.add)

        def mk_v(cb, t0, last):
            def comp(slot, wu):
                w3 = slot3(slot, KC, 512)
                for tl in range(NTL):
                    t = t0 + tl
                    p, u = psum()
                    for kc in range(KC):
                        MM(p[:, :], hT[:, kc, t * 128:(t + 1) * 128], w3[:, kc, :], kc == 0, kc == KC - 1, [wu, hU[t // 4]], [u], signal=(kc == KC - 1))
                    ACT(gv[:, tl, cb * 512:(cb + 1) * 512], p[:, :], AF.Gelu_apprx_tanh, [u], [gvU[tl]])
                if not last:
                    return
                for tl in range(NTL):
                    t = t0 + tl
                    bi = t // 4
                    j1, j1u = tb(); j2, j2u = tb()
                    s1 = stat[:, 0:1]; s2 = stat[:, 1:2]; mu = stat[:, 2:3]; var = stat[:, 3:4]; rs = stat[:, 4:5]; nb = stat[:, 5:6]
                    DVE("memset", [], [statU], ap=stat[:, 8:16], constant=0.0)
                    for q4 in range(4):
                        ACT(j1[:, :], gv[:, tl, q4 * 512:(q4 + 1) * 512], AF.Copy, [gvU[tl]], [j1u, statU], accum_out=stat[:, 8 + q4:9 + q4])
                        ACT(j2[:, :], gv[:, tl, q4 * 512:(q4 + 1) * 512], AF.Square, [gvU[tl]], [j2u, statU], accum_out=stat[:, 12 + q4:13 + q4])
                    DVE("tensor_reduce", [statU], [statU], out=s1, in_=stat[:, 8:12], axis=AX.X, op=ALU.add)
                    DVE("tensor_reduce", [statU], [statU], out=s2, in_=stat[:, 12:16], axis=AX.X, op=ALU.add)
                    DVE("tensor_scalar_mul", [statU], [statU], out=mu, in0=s1, scalar1=1.0 / D)
                    DVE("tensor_tensor", [statU], [statU], out=var, in0=mu, in1=mu, op=ALU.mult)
                    DVE("scalar_tensor_tensor", [statU], [statU], out=var, in0=s2, scalar=1.0 / D, in1=var, op0=ALU.mult, op1=ALU.subtract)
                    ACT(rs, var, AF.Sqrt, [statU, cU], [statU], bias=epsT[:, 0:1], scale=1.0)
                    DVE("reciprocal", [statU], [statU], out=rs, in_=rs)
                    DVE("scalar_tensor_tensor", [statU], [statU], out=nb, in0=mu, scalar=-1.0, in1=rs, op0=ALU.mult, op1=ALU.mult)
                    ACT(gv[:, tl, :], gv[:, tl, :], AF.Identity, [gvU[tl], statU], [gvU[tl]], bias=nb, scale=rs)
                    for g4 in range(4):
                        p, u = psum()
                        for k in range(4):
                            g = g4 * 4 + k
                            MM(p[:, k * 128:(k + 1) * 128], gv[:, tl, g * 128:(g + 1) * 128], wsT[:, g, :], True, True, [gvU[tl], wsU], [u], signal=(k == 3))
                        for k in range(4):
                            g = g4 * 4 + k
                            DVE("scalar_tensor_tensor", [u, CU, cU], [mU[g][bi]], out=mid[:, g, t * 128:(t + 1) * 128], in0=p[:, k * 128:(k + 1) * 128],
                                scalar=vecT[:, g, V_ALNG + j:V_ALNG + j + 1], in1=Cb[:, g, :], op0=ALU.mult, op1=ALU.add)
            return comp
        for t0 in range(0, 8, NTL):
            for cb in range(4):
                wstep([(lambda s: slot3(s, KC, 512), wsrc(w_in, 0, KC, D + cb * 512, 512))], mk_v(cb, t0, cb == 3))

        def mk_vs(cb):
            def comp(slot, wu):
                w3 = slot3(slot, KC, 512)
                for o4 in range(4):
                    p, u = psum()
                    mm_fm(p, u, w3, wu, KC, o4, hT, lambda kc: [hU[2]], TP, 1)
                    ACT(gvs[:, cb * 4 + o4:cb * 4 + o4 + 1], p[:, 0:1], AF.Gelu_apprx_tanh, [u], [smU])
                if cb != 3:
                    return
                sq, squ = tf()
                DVE("tensor_copy", [smU], [squ], out=sq[:, 0:16], in_=gvs)
                DVE("tensor_tensor", [smU], [squ], out=sq[:, 16:32], in0=gvs, in1=gvs, op=ALU.mult)
                hb, hbu = tb(); lb_, lbu = tb()
                DVE("tensor_copy", [squ], [hbu], out=hb[:, 0:32], in_=sq[:, 0:32])
                DVE("tensor_tensor", [squ, hbu], [squ], out=sq[:, 32:64], in0=sq[:, 0:32], in1=hb[:, 0:32], op=ALU.subtract)
                DVE("tensor_copy", [squ], [lbu], out=lb_[:, 0:32], in_=sq[:, 32:64])
                p, u = psum()
                MM(p[:, 0:32], onesb[:, :], hb[:, 0:32], True, False, [hbu, cU], [u], signal=False)
                MM(p[:, 0:32], onesb[:, :], lb_[:, 0:32], False, True, [lbu, cU], [u])
                s1 = stat[:, 0:1]; s2 = stat[:, 1:2]; mu = stat[:, 2:3]; var = stat[:, 3:4]; rs = stat[:, 4:5]; nb = stat[:, 5:6]
                DVE("tensor_reduce", [u], [statU], out=s1, in_=p[:, 0:16], axis=AX.X, op=ALU.add)
                DVE("tensor_reduce", [u], [statU], out=s2, in_=p[:, 16:32], axis=AX.X, op=ALU.add)
                DVE("tensor_scalar_mul", [statU], [statU], out=mu, in0=s1, scalar1=1.0 / D)
                DVE("tensor_tensor", [statU], [statU], out=var, in0=mu, in1=mu, op=ALU.mult)
                DVE("scalar_tensor_tensor", [statU], [statU], out=var, in0=s2, scalar=1.0 / D, in1=var, op0=ALU.mult, op1=ALU.subtract)
                ACT(rs, var, AF.Sqrt, [statU, cU], [statU], bias=epsT[:, 0:1], scale=1.0)
                DVE("reciprocal", [statU], [statU], out=rs, in_=rs)
                DVE("scalar_tensor_tensor", [statU], [statU], out=nb, in0=mu, scalar=-1.0, in1=rs, op0=ALU.mult, op1=ALU.mult)
                ACT(vln, gvs, AF.Identity, [smU, statU], [smU], bias=nb, scale=rs)
                DVE("tensor_tensor", [smU, cU], [smU], out=vln, in0=vln, in1=vecT[:, :, V_ALNG + j], op=ALU.mult)
                DVE("tensor_tensor", [smU, cU], [smU], out=vln, in0=vln, in1=vecT[:, :, V_ALNB + j], op=ALU.add)
                p2, u2 = psum()
                TR(p2[0:16, 0:128], vln, ident[:, :], [smU, cU], [u2])
                o, ou = tf()
                DVE("tensor_copy", [u2], [ou], out=o[0:16, 0:128], in_=p2[0:16, 0:128])
                dma("sp", av_s.ap()[j].rearrange("(g e) -> g e", e=128), o[0:16, 0:128], smDS, R=[ou])
                DVE("tensor_tensor", [smU], [smU], out=gvs, in0=vln, in1=ws00, op=ALU.mult)
                DVE("tensor_tensor", [smU], [mU[g][2] for g in range(KC)], out=mid[:, :, TP], in0=gvs, in1=bs0, op=ALU.add)
            return comp
        for cb in range(4):
            wstep([(lambda s: slot3(s, KC, 512), wsrc(w_in, 0, KC, D + cb * 512, 512))], mk_vs(cb))

        def mk_u(cb):
            def comp(slot, wu):
                w3 = slot3(slot, KC, 512)
                for o4 in range(4):
                    oc = cb * 4 + o4
                    for bi, (c0, n) in enumerate(BLKS):
                        p, u = psum()
                        mm_fm(p, u, w3, wu, KC, o4, hT, lambda kc: [hU[bi]], c0, n)
                        g_, gu = tf()
                        ACT(g_[:, :n], p[:, :n], AF.Gelu_apprx_tanh, [u], [gu])
                        DVE("tensor_tensor", [gu], [mU[oc][bi]], out=mid[:, oc, c0:c0 + n], in0=g_[:, :n], in1=mid[:, oc, c0:c0 + n], op=ALU.mult)
            return comp
        for cb in range(4):
            wstep([(lambda s: slot3(s, KC, 512), wsrc(w_in, 0, KC, cb * 512, 512))], mk_u(cb))
        out_proj_steps(w_out, 0, KC)
        run_steps()

    def sample_attn(qs_f, ks_f, vs_f, smpU, hTb):
        B.barrier()
        hTf = hTb[:, 0:16400].bitcast(F32)
        kcf = hTf[:, 0:2048]; prod = hTf[:, 2048:2560]; sc = hTf[:, 2560:2608]; pf = hTf[:, 2608:2656]
        pn = hTf[:, 2656:2704]; t48 = hTf[:, 2704:2752]; t48b = hTf[:, 2752:2800]; b0 = hTf[:, 2800:2848]
        tabs = hTf[:, 2848:2896]; sels = hTf[:, 2896:3280]; o16 = hTf[:, 3280:3312]; rowst = hTf[:, 3312:3440]
        bfv = hTb[:, 6880:16400]
        Qd = bfv[:, 0:2048]; vcb = [bfv[:, 2048 * (1 + g):2048 * (2 + g)] for g in range(3)]
        pb48 = bfv[:, 8192:8240]; hb = bfv[:, 8240:8288]; lb_ = bfv[:, 8288:8336]
        kU_, vU_, sU, cDS_, vDS_, oDS_, rDS_ = Unit(), [Unit(), Unit(), Unit()], Unit(), B.new_ds(), B.new_ds(), B.new_ds(), B.new_ds()
        qdU, prU, rsU = Unit(), Unit(), Unit()
        dma("sp", tabs[0:32, :], relb.ap(), cDS_, W=[sU])
        dma("sp", sels[0:32, :], sels_d.ap(), cDS_, W=[sU])
        dma("sp", b0, relb.ap()[0:1, :].partition_broadcast(128), cDS_, W=[sU])
        for g in range(3):
            dil = DIL[g]
            dma("pool", vcb[g], bass.AP(cv[g].handle(), 0, [[dil * D, 128], [1, D]]), vDS_, W=[vU_[g]])
        for g in range(3):
            dil = DIL[g]
            dma("sp", kcf, bass.AP(ck[g].handle(), 0, [[dil * D, 128], [1, D]]), cDS_, W=[kU_])
            DVE("tensor_tensor", [cU, smpU], [qdU], out=Qd.rearrange("p (h e) -> p h e", e=128), in0=ident[:, :].unsqueeze(1).to_broadcast([128, 16, 128]),
                in1=qs_f[:, g * 16:(g + 1) * 16].unsqueeze(2).to_broadcast([128, 16, 128]), op=ALU.mult)
            for c4 in range(4):
                p, u = psum()
                MM(p[:, :], onesb[:, :], Qd[:, c4 * 512:(c4 + 1) * 512], True, True, [qdU, cU], [u])
                DVE("tensor_tensor", [u, kU_], [prU], out=prod, in0=kcf[:, c4 * 512:(c4 + 1) * 512], in1=p[:, :], op=ALU.mult)
                DVE("tensor_reduce", [prU], [sU], out=sc[:, g * 16 + c4 * 4:g * 16 + c4 * 4 + 4], in_=prod.rearrange("p (h e) -> p h e", e=128), axis=AX.X, op=ALU.add)
            p, u = psum()
            MM(p[:, 0:16], sels[0:32, g * 128:(g + 1) * 128], tabs[0:32, g * 16:(g + 1) * 16], True, True, [sU], [u])
            DVE("scalar_tensor_tensor", [u, sU], [sU], out=sc[:, g * 16:(g + 1) * 16], in0=sc[:, g * 16:(g + 1) * 16], scalar=SCALE, in1=p[:, 0:16], op0=ALU.mult, op1=ALU.add)
        ACT(pf, sc, AF.Exp, [sU], [sU])
        DVE("tensor_copy", [sU], [sU], out=pb48, in_=pf)
        DVE("tensor_tensor", [smpU], [sU], out=t48, in0=qs_f, in1=ks_f, op=ALU.mult)
        DVE("tensor_copy", [sU], [sU], out=hb, in_=t48)
        DVE("tensor_tensor", [sU], [sU], out=t48b, in0=t48, in1=hb, op=ALU.subtract)
        DVE("tensor_copy", [sU], [sU], out=lb_, in_=t48b)
        p, u = psum()
        MM(p[:, 0:48], onesb[:, :], hb, True, False, [sU, cU], [u])
        MM(p[:, 0:48], onesb[:, :], lb_, False, True, [sU, cU], [u])
        DVE("scalar_tensor_tensor", [u, sU], [sU], out=t48, in0=p[:, 0:48], scalar=SCALE, in1=b0, op0=ALU.mult, op1=ALU.add)
        ACT(pn, t48, AF.Exp, [sU], [sU])
        pso, uso = psum(); psd, usd = psum()
        for h_ in range(16):
            for g in range(3):
                MM(pso[:, h_:h_ + 1], vcb[g][:, h_ * 128:(h_ + 1) * 128], pb48[:, g * 16 + h_:g * 16 + h_ + 1], g == 0, g == 2, [vU_[g], sU], [uso])
        for g in range(3):
            MM(psd[:, 0:16], onesb[:, :], pb48[:, g * 16:(g + 1) * 16], g == 0, g == 2, [sU, cU], [usd])
        DVE("tensor_tensor", [sU, smpU], [sU], out=t48, in0=pn, in1=vs_f, op=ALU.mult)
        DVE("tensor_reduce", [sU], [sU], out=o16[:, 0:16], in_=t48.rearrange("p (g h) -> p h g", g=3), axis=AX.X, op=ALU.add)
        DVE("tensor_reduce", [sU], [sU], out=o16[:, 16:32], in_=pn.rearrange("p (g h) -> p h g", g=3), axis=AX.X, op=ALU.add)
        DVE("tensor_tensor", [uso, sU], [sU], out=o16[:, 0:16], in0=pso[:, 0:16], in1=o16[:, 0:16], op=ALU.add)
        DVE("tensor_tensor", [usd, sU], [sU], out=o16[:, 16:32], in0=psd[:, 0:16], in1=o16[:, 16:32], op=ALU.add)
        DVE("reciprocal", [sU], [sU], out=o16[:, 16:32], in_=o16[:, 16:32])
        DVE("tensor_tensor", [sU], [mU[k][2] for k in range(KC)], out=mid[:, :, TP], in0=o16[:, 0:16], in1=o16[:, 16:32], op=ALU.mult)
        for g, W_ in enumerate((128, 512, 2048)):
            n16 = (W_ - 1) * 16
            for src_t, dst_t, col in ((ck[g], bk_s[g], ks_f), (cv[g], bv_s[g], vs_f)):
                p, u = psum()
                TR(p[0:16, 0:128], col[:, g * 16:(g + 1) * 16], ident[:, :], [smpU, cU], [u])
                DVE("tensor_copy", [u, rsU], [rsU], out=rowst[0:16, :], in_=p[0:16, 0:128])
                dma("sp", dst_t.ap()[W_ - 1:W_, :].rearrange("o (h e) -> (o h) e", e=128), rowst[0:16, :], rDS_, R=[rsU])

    DIL = (1, 4, 16)
    KEEP = (128, 512, 1024)

    def dilattn(i):
        qs_d = B.dram("qs_d", [6144, 1024], BF16)
        kvi = B.dram("kvi", [12288, 1024], BF16)
        kvo = B.dram("kvo", [24576, 1024], BF16)

        def kvo_off(elem_off):
            row = elem_off // 1024
            return ((row // 1024) * 2048 + (row % 1024)) * 1024 + (elem_off % 1024)
        eg_d = B.dram("eg_d", [144, 255], F32)
        re_d = B.dram("re_d", [144 * 128, 255], F32)
        VB = 6144 * 1024
        RANK = 12288 * 1024
        kvWc = [Unit() for _ in range(12)]; qsW = Unit(); reU = Unit(); kvoU = [Unit() for _ in range(12)]

        def kvW_of(elem_off):
            return kvWc[elem_off // (1024 * 1024)]

        def kvoU_of(elem_off):
            return kvoU[elem_off // (1024 * 1024)]
        ccsem = B.new_sem("cc_kv")
        w_qkv = b_w_qkv.ap(); w_o = b_w_out.ap()

        arena_reset()
        tabx = carve(48); selS = carve(9 * 255); egs = carve(9 * 255); tU_ = Unit(); tDS = B.new_ds()
        dma("sp", tabx[0:32, :], relb.ap(), tDS, W=[tU_])
        DVE("memset", [], [tU_], ap=tabx[32:33, :], constant=-1e30)
        dma("sp", selS[0:33, :], selg.ap(), tDS, W=[tU_])
        for g in range(3):
            for v in range(3):
                c = (g * 3 + v) * 255
                p, u = psum()
                MM(p[0:16, 0:255], tabx[0:33, g * 16:(g + 1) * 16], selS[0:33, c:c + 255], True, True, [tU_], [u])
                ACT(egs[0:16, c:c + 255], p[0:16, 0:255], AF.Exp, [u], [tU_])
                if v == 2:
                    DVE("tensor_scalar_mul", [tU_, cU], [tU_], out=egs[0:16, c:c + 255], in0=egs[0:16, c:c + 255], scalar1=flg[0:16, 0:1])
        dma("sp", eg_d.ap().rearrange("(g h v) c -> h g v c", g=3, h=16, v=3), egs[0:16, :].rearrange("p (g v c) -> p g v c", g=3, v=3), tDS, R=[tU_], W=[reU])
        bDS = B.new_ds(nb=True)
        for k in range(9):
            dma("sp", re_d.ap()[k * 2048:(k + 1) * 2048, :].rearrange("(r j) c -> r j c", j=128),
                bass.AP(eg_d, k * 16 * 255, [[255, 16], [0, 128], [1, 255]]), bDS, R=[reU], W=[reU])

        arena_reset()
        smp = carve(3 * 48); smpU = Unit()
        rmsnorm(V_GMIX + i)
        hk = [carve(512, BF16), carve(512, BF16)]; hkU = [Unit(), Unit()]; hkDS = [B.new_ds(), B.new_ds()]
        kst = carve(2048).rearrange("p (t c) -> p t c", c=512); kstU = Unit(); kstDS = B.new_ds()
        vreg = carve(1536)
        kq = [vreg[:, i * 512:(i + 1) * 512] for i in range(3)]; kqU = [Unit() for _ in range(3)]
        vst = [vreg[:, 0:512], vreg[:, 512:1024]]; vstU = [Unit(), Unit()]; vstDS = [B.new_ds(), B.new_ds()]
        vbs = [vreg[:, 1024:1280].bitcast(BF16), vreg[:, 1280:1536].bitcast(BF16)]; vbsU = [Unit(), Unit()]; vbsDS = [B.new_ds(), B.new_ds()]
        qs_f, ks_f, vs_f = smp[:, 0:48], smp[:, 48:96], smp[:, 96:144]
        cnt = {"hk": 0, "v": 0, "kq": 0}

        def k_out_dma(g, hq, bi):
            for tt in range(4):
                t = bi * 4 + tt
                if t * 128 >= TP - KEEP[g]:
                    r0 = t * 128 - (TP - KEEP[g])
                    dma("sp", bk_p[g].ap()[r0:r0 + 128, hq * 512:(hq + 1) * 512], kst[:, tt, :], kstDS, R=[kstU])

        pend = {"f": None, "g": None}

        def flush_pending():
            f = pend["f"]; pend["f"] = None
            if f is not None:
                f()

        def flush_all():
            f = pend["f"]; g_ = pend["g"]
            pend["f"] = None; pend["g"] = None
            if f is not None:
                f()
            if g_ is not None:
                g_()
            g2 = pend["g"]; pend["g"] = None
            if g2 is not None:
                g2()

        def qk_front(w3, wu, o4, bi, c0, n, q=None, qu=None):
            p, u = psum()
            mm_fm(p, u, w3, wu, KC, o4, hT, lambda kc: [hU[bi]], c0, n)
            if q is None:
                q, qu = tf()
            ACT(q[:, :n], p[:, :n], AF.Copy, [u], [qu])
            s_, su = tb()
            ACT(s_[:, :n], p[:, :n], AF.Square, [u], [su])
            return q, qu, s_, su

        def qk_sqrt(s_, su, n):
            p2, u2 = psum()
            MM(p2[:, :n], onesb[:, :], s_[:, :n], True, True, [su, cU], [u2])
            r, ru = tf()
            ACT(r[:, :n], p2[:, :n], AF.Ln, [u2, cU], [ru], bias=epsT[:, 0:1], scale=1.0 / 128)
            ACT(r[:, :n], r[:, :n], AF.Exp, [ru], [ru], scale=-0.5)
            return r, ru

        def mk_k(g, hq):
            dil = DIL[g]

            def comp(slot, wu):
                w3 = slot3(slot, KC, 512)
                gcol = S_BK + g
                for bi, (c0, n) in enumerate(BLKS):
                    for o4 in range(4):
                        h_ = hq * 4 + o4
                        if bi == 2:
                            flush_all()
                            q, qu, s_, su = qk_front(w3, wu, o4, bi, c0, n)
                            r, ru = qk_sqrt(s_, su, 1)
                            DVE("scalar_tensor_tensor", [qu, ru, cU], [smpU], out=ks_f[:, g * 16 + h_:g * 16 + h_ + 1], in0=q[:, :1], scalar=gs[:, gcol:gcol + 1],
                                in1=r[:, :1], op0=ALU.mult, op1=ALU.mult)
                            continue
                        qi = cnt["kq"] % 3; cnt["kq"] += 1
                        q, qu, s_, su = qk_front(w3, wu, o4, bi, c0, n, q=kq[qi], qu=kqU[qi])
                        fA = pend["f"]; fB = pend["g"]
                        pend["f"] = None; pend["g"] = None
                        if fA is not None:
                            fA()
                        if fB is not None:
                            fB()
                        row = (g * 16 + h_) * 128

                        def tailB(q=q, qu=qu, bi=bi, o4=o4):
                            tiles = [tt for tt in range(4) if (bi * 4 + tt) * 128 >= TP - KEEP[g]]
                            if tiles:
                                pt, ut = psum()
                                for tt in tiles:
                                    TR(pt[:, tt * 128:(tt + 1) * 128], q[:, tt * 128:(tt + 1) * 128], ident[:, :], [qu, cU], [ut], signal=(tt == tiles[-1]))
                                t0_, t1_ = tiles[0], tiles[-1] + 1
                                ACT(kst[:, t0_:t1_, o4 * 128:(o4 + 1) * 128], pt[:, t0_ * 128:t1_ * 128].rearrange("p (t e) -> p t e", e=128), AF.Copy, [ut], [kstU])
                            if o4 == 3:
                                k_out_dma(g, hq, bi)

                        def tailA(q=q, qu=qu, s_=s_, su=su, bi=bi, n=n, row=row, tailB=tailB):
                            r, ru = qk_sqrt(s_, su, n)
                            DVE("scalar_tensor_tensor", [qu, ru, cU], [qu], out=q[:, :n], in0=q[:, :n], scalar=gs[:, gcol:gcol + 1], in1=r[:, :n], op0=ALU.mult, op1=ALU.mult)
                            hi = cnt["hk"] % 2; cnt["hk"] += 1
                            nu = 512 // dil
                            DVE("tensor_copy", [qu], [hkU[hi]], out=hk[hi][:, 0:512].rearrange("p (r u) -> p r u", r=dil), in_=q[:, :].rearrange("p (u r) -> p r u", r=dil))
                            dma("sp", kvi.ap()[row:row + 128, :].rearrange("p (r u) -> p r u", r=dil)[:, :, bi * nu:(bi + 1) * nu],
                                hk[hi][:, 0:512].rearrange("p (r u) -> p r u", r=dil), hkDS[hi], R=[hkU[hi], kvW_of(row * 1024)])
                            pend["g"] = tailB
                        pend["f"] = tailA
            return comp

        def mk_q(g, hq):
            dil = DIL[g]

            def comp(slot, wu):
                w3 = slot3(slot, KC, 512)
                gcol = S_BQ + g
                for bi, (c0, n) in enumerate(BLKS):
                    for o4 in range(4):
                        h_ = hq * 4 + o4
                        if bi == 2:
                            flush_all()
                        q, qu, s_, su = qk_front(w3, wu, o4, bi, c0, n)
                        flush_pending()
                        if bi == 2:
                            r, ru = qk_sqrt(s_, su, 1)
                            DVE("scalar_tensor_tensor", [qu, ru, cU], [smpU], out=qs_f[:, g * 16 + h_:g * 16 + h_ + 1], in0=q[:, :1], scalar=gs[:, gcol:gcol + 1],
                                in1=r[:, :1], op0=ALU.mult, op1=ALU.mult)
                            continue

                        def tail(q=q, qu=qu, s_=s_, su=su, bi=bi, h_=h_, n=n):
                            r, ru = qk_sqrt(s_, su, n)
                            hi = cnt["hk"] % 2; cnt["hk"] += 1
                            nu = 512 // dil
                            hv = hk[hi][:, 0:512].rearrange("p (r u) -> p r u", r=dil)
                            DVE("scalar_tensor_tensor", [qu, ru, cU], [hkU[hi]], out=hv, in0=q[:, :].rearrange("p (u r) -> p r u", r=dil), scalar=gs[:, gcol:gcol + 1],
                                in1=r[:, :].rearrange("p (u r) -> p r u", r=dil), op0=ALU.mult, op1=ALU.mult)
                            row = (g * 16 + h_) * 128
                            dma("sp", qs_d.ap()[row:row + 128, :].rearrange("p (r u) -> p r u", r=dil)[:, :, bi * nu:(bi + 1) * nu], hv, hkDS[hi], R=[hkU[hi], qsW])
                        pend["f"] = tail
            return comp

        def mk_v(g, hq):
            def comp(slot, wu):
                w3 = slot3(slot, KC, 512)
                for t in range(8):
                    vi = cnt["v"] % 2; cnt["v"] += 1
                    p, u = psum()
                    for kc in range(KC):
                        MM(p[:, :], hT[:, kc, t * 128:(t + 1) * 128], w3[:, kc, :], kc == 0, kc == KC - 1, [wu, hU[t // 4]], [u], signal=(kc == KC - 1))
                    DVE("tensor_copy", [u], [vstU[vi]], out=vst[vi][:, :], in_=p[:, :])
                    ACT(vbs[vi][:, :], vst[vi][:, :], AF.Copy, [vstU[vi]], [vbsU[vi]])
                    if t * 128 >= TP - KEEP[g]:
                        r0 = t * 128 - (TP - KEEP[g])
                        dma("sp", bv_p[g].ap()[r0:r0 + 128, hq * 512:(hq + 1) * 512], vst[vi][:, :], vstDS[vi], R=[vstU[vi]])
                    off = VB + ((g * 16 + hq * 4) * 1024 + t * 128) * 128
                    dma("sp", bass.AP(kvi, off, [[128, 128], [1024 * 128, 4], [1, 128]]), vbs[vi][:, :].rearrange("p (o e) -> p o e", e=128), vbsDS[vi], R=[vbsU[vi], kvW_of(off)])
                for o4 in range(4):
                    p, u = psum()
                    mm_fm(p, u, w3, wu, KC, o4, hT, lambda kc: [hU[2]], TP, 1)
                    DVE("tensor_copy", [u], [smpU], out=vs_f[:, g * 16 + hq * 4 + o4:g * 16 + hq * 4 + o4 + 1], in_=p[:, 0:1])
            return comp

        def qkv_src(g, which, hq):
            return wsrc(w_qkv, 0, KC, g * 6144 + which * 2048 + hq * 512, 512)
        def mk_first_v(inner):
            def comp(slot, wu):
                flush_all()
                B.barrier()
                inner(slot, wu)
            return comp

        def mk_cc(k):
            return lambda: B.cc(kvi.ap()[k * 1024:(k + 1) * 1024, :].opt(), kvo.ap()[k * 2048:(k + 1) * 2048, :].opt(), PAIRS_RUN, ccsem,
                                W=[kvWc[k], kvoU[k]], count=k + 1)
        cc_at = {}
        for g in range(3):
            cc_at[4 * g + 6] = 2 * g; cc_at[4 * g + 7] = 2 * g + 1
            cc_at[12 + 4 * g + 6] = 6 + 2 * g; cc_at[12 + 4 * g + 7] = 7 + 2 * g
        si = 0
        for which in (1, 2, 0):
            for g in range(3):
                for hq in range(4):
                    comp = mk_k(g, hq) if which == 1 else (mk_v(g, hq) if which == 2 else mk_q(g, hq))
                    if which == 2 and g == 0 and hq == 0:
                        comp = mk_first_v(comp)
                    wstep([(lambda s: slot3(s, KC, 512), qkv_src(g, which, hq))], comp, post_issue=(mk_cc(cc_at[si]) if si in cc_at else None))
                    si += 1
        run_steps()
        flush_all()

        arena_reset()
        carve(3 * 48)
        hTb = hT2[:, :]
        hoff = [0]

        def hcarve(n_bf):
            o = hoff[0]; hoff[0] = o + n_bf + (n_bf % 2)
            assert hoff[0] <= KC * XC
            return hTb[:, o:o + n_bf]
        q3 = [hcarve(1024) for _ in range(3)]; ko = [hcarve(1024) for _ in range(3)]
        kp = [hcarve(128), hcarve(512), hcarve(1024)]
        vo = [hcarve(8 * 128).rearrange("p (t e) -> p t e", e=128), hcarve(8 * 128).rearrange("p (t e) -> p t e", e=128), hcarve(16 * 128).rearrange("p (t e) -> p t e", e=128)]
        vpv = [hcarve(128).rearrange("p (t e) -> p t e", e=128), hcarve(4 * 128).rearrange("p (t e) -> p t e", e=128), hcarve(16 * 128).rearrange("p (t e) -> p t e", e=128)]
        et = carve(10 * 128).rearrange("p (k i) -> p k i", i=128)
        accden = carve(2048).rearrange("p (k t) -> p k t", k=2); adU = Unit()
        ldq = Unit(); ldk = Unit(); ldv = Unit(); lde = Unit()
        qDS, kDS, vDS, eDS = B.new_ds(), B.new_ds(), B.new_ds(), B.new_ds()

        def head(h_):
            for g in range(3):
                dil = DIL[g]; U = TP // dil
                row = (g * 16 + h_) * 128
                dma("sp", q3[g], qs_d.ap()[row:row + 128, :], qDS, R=[qsW], W=[ldq])
                dma("sp", ko[g], kvi.ap()[row:row + 128, :], kDS, R=[kvW_of(row * 1024)], W=[ldk])
                orow = kvo_off(row * 1024) // 1024
                if g == 0:
                    dma("sp", kp[0], kvo.ap()[orow:orow + 128, 896:1024], kDS, R=[kvoU_of(row * 1024)], W=[ldk])
                elif g == 1:
                    dma("sp", kp[1].rearrange("p (r u) -> p r u", r=4), kvo.ap()[orow:orow + 128, :].rearrange("p (r u) -> p r u", r=4)[:, :, 128:256], kDS, R=[kvoU_of(row * 1024)], W=[ldk])
                else:
                    dma("sp", kp[2], kvo.ap()[orow:orow + 128, :], kDS, R=[kvoU_of(row * 1024)], W=[ldk])
                vbase = VB + (g * 16 + h_) * 1024 * 128
                if g == 0:
                    dma("sp", vo[0], bass.AP(kvi, vbase, [[128, 128], [128 * 128, 8], [1, 128]]), vDS, R=[kvW_of(vbase)], W=[ldv])
                    dma("sp", vpv[0], bass.AP(kvo, kvo_off(vbase) + 896 * 128, [[128, 128], [128 * 128, 1], [1, 128]]), vDS, R=[kvoU_of(vbase)], W=[ldv])
                elif g == 1:
                    for r in range(4):
                        dma("sp", vo[1][:, r * 2:r * 2 + 2, :], bass.AP(kvi, vbase + r * 128, [[4 * 128, 128], [512 * 128, 2], [1, 128]]), vDS, R=[kvW_of(vbase)], W=[ldv])
                    dma("sp", vpv[1], bass.AP(kvo, kvo_off(vbase) + 512 * 128, [[4 * 128, 128], [128, 4], [1, 128]]), vDS, R=[kvoU_of(vbase)], W=[ldv])
                else:
                    dma("sp", vo[2][0:64, :, :], bass.AP(kvi, vbase, [[16 * 128, 64], [128, 16], [1, 128]]), vDS, R=[kvW_of(vbase)], W=[ldv])
                    dma("sp", vpv[2][0:64, :, :], bass.AP(kvo, kvo_off(vbase), [[16 * 128, 64], [128, 16], [1, 128]]), vDS, R=[kvoU_of(vbase)], W=[ldv])
                erow = (g * 16 + h_) * 3
                if g < 2:
                    for k_, v in enumerate((1, 0, 2, 0)):
                        dma("sp", et[:, g * 4 + k_, :], bass.AP(re_d, (erow + v) * 128 * 255 + 127, [[254, 128], [1, 128]]), eDS, R=[reU], W=[lde])
                else:
                    dma("sp", et[0:64, 8, 0:64], bass.AP(re_d, (erow + 2) * 128 * 255 + 191, [[254, 64], [1, 64]]), eDS, R=[reU], W=[lde])
                    dma("sp", et[0:64, 9, 0:64], bass.AP(re_d, (erow + 0) * 128 * 255 + 127, [[254, 64], [1, 64]]), eDS, R=[reU], W=[lde])
            units = []
            for g in range(3):
                dil = DIL[g]; U = TP // dil
                QB = 128 if g < 2 else 64
                for r in range(dil):
                    for qb in range(U // QB):
                        units.append((g, r, qb))

            ucnt = [0]

            def stageA(un):
                g, r, qb = un
                dil = DIL[g]; U = TP // dil
                QB = 128 if g < 2 else 64
                nqb = U // QB
                qcols = q3[g][:, r * U + qb * QB:r * U + (qb + 1) * QB]
                if qb > 0:
                    kT0 = ko[g][:, r * U + (qb - 1) * QB:r * U + qb * QB]
                    vt0 = vo[g][0:QB, (r * nqb + qb - 1), :]
                    eb = g * 4
                else:
                    if g == 0:
                        kT0 = kp[0][:, :]; vt0 = vpv[0][:, 0, :]
                    elif g == 1:
                        kT0 = kp[1][:, r * 128:(r + 1) * 128]; vt0 = vpv[1][:, r, :]
                    else:
                        kT0 = kp[2][:, r * 64:(r + 1) * 64]; vt0 = vpv[2][0:64, r, :]
                    eb = (g * 4 + 2) if g < 2 else 8
                kT1 = ko[g][:, r * U + qb * QB:r * U + (qb + 1) * QB]
                vt1 = vo[g][0:QB, (r * nqb + qb), :]
                p, u = psum()
                MM(p[0:QB, 0:QB], kT0, qcols, True, True, [ldk, ldq], [u])
                MM(p[0:QB, QB:2 * QB], kT1, qcols, True, True, [ldk, ldq], [u])
                e_, eu = tf()
                ACT(e_[0:QB, 0:2 * QB], p[0:QB, 0:2 * QB], AF.Exp, [u], [eu], scale=SCALE)
                pb, pbu = tb()
                ucnt[0] += 1
                op("dve", "tensor_tensor", R=[eu, lde], W=[pbu], out=pb[0:QB, 0:2 * QB].rearrange("p (k i) -> p k i", k=2),
                   in0=e_[0:QB, 0:2 * QB].rearrange("p (k i) -> p k i", k=2), in1=et[0:QB, eb:eb + 2, 0:QB], op=ALU.mult)
                return (un, pb, pbu, vt0, vt1)

            def stageB(sa):
                un, pb, pbu, vt0, vt1 = sa
                QB = 128 if un[0] < 2 else 64
                po, uo = psum()
                MM(po[:, 0:QB], vt0, pb[0:QB, 0:QB], True, False, [ldv, pbu], [uo])
                MM(po[:, 0:QB], vt1, pb[0:QB, QB:2 * QB], False, True, [ldv, pbu], [uo])
                MM(po[:, QB:2 * QB], onesb[0:QB, :], pb[0:QB, 0:QB], True, False, [pbu, cU], [uo])
                MM(po[:, QB:2 * QB], onesb[0:QB, :], pb[0:QB, QB:2 * QB], False, True, [pbu, cU], [uo])
                return (un, po, uo)

            def stageC(sb):
                (g, r, qb), po, uo = sb
                dil = DIL[g]
                QB = 128 if g < 2 else 64
                a_out = accden.rearrange("p k (u r) -> p k r u", r=dil)[:, :, r, qb * QB:(qb + 1) * QB]
                src = po[:, 0:2 * QB].rearrange("p (k i) -> p k i", k=2)
                if g == 0:
                    DVE("tensor_copy", [uo], [adU], out=a_out, in_=src)
                else:
                    DVE("tensor_tensor", [uo, adU], [adU], out=a_out, in0=src, in1=a_out, op=ALU.add)
            pairs = [units[k:k + 2] for k in range(0, len(units), 2)]
            sa_prev, sb_prev = [], []
            for pr in pairs + [[], []]:
                sa = [stageA(un) for un in pr]
                sb = [stageB(x) for x in sa_prev]
                for x in sb_prev:
                    stageC(x)
                sa_prev, sb_prev = sa, sb
            ACT(accden[:, 1, :], accden[:, 1, :], AF.Ln, [adU], [adU])
            ACT(accden[:, 1, :], accden[:, 1, :], AF.Exp, [adU], [adU], scale=-1.0)
            DVE("tensor_tensor", [adU], [mU[h_][0], mU[h_][1]], out=mid[:, h_, 0:TP], in0=accden[:, 0, :], in1=accden[:, 1, :], op=ALU.mult)
        for h_ in range(16):
            head(h_)

        sample_attn(qs_f, ks_f, vs_f, smpU, hTb)
        out_proj_steps(w_o, 0, KC)
        run_steps()

    def col_ln(src, srcU, grow, brow, out, outW, scr, scrU):
        sq = scr[:, 0:32]; lo = scr[:, 32:64]; stt = scr[:, 64:70]
        DVE("tensor_copy", list(srcU), [scrU], out=sq[:, 0:16], in_=src)
        DVE("tensor_tensor", list(srcU), [scrU], out=sq[:, 16:32], in0=src, in1=src, op=ALU.mult)
        hb, hbu = tb(); lb_, lbu = tb()
        DVE("tensor_copy", [scrU], [hbu], out=hb[:, 0:32], in_=sq)
        DVE("tensor_tensor", [scrU, hbu], [scrU], out=lo, in0=sq, in1=hb[:, 0:32], op=ALU.subtract)
        DVE("tensor_copy", [scrU], [lbu], out=lb_[:, 0:32], in_=lo)
        p, u = psum()
        MM(p[:, 0:32], onesb[:, :], hb[:, 0:32], True, False, [hbu, cU], [u])
        MM(p[:, 0:32], onesb[:, :], lb_[:, 0:32], False, True, [lbu, cU], [u])
        s1 = stt[:, 0:1]; s2 = stt[:, 1:2]; mu = stt[:, 2:3]; var = stt[:, 3:4]; rs = stt[:, 4:5]; nb = stt[:, 5:6]
        DVE("tensor_reduce", [u], [scrU], out=s1, in_=p[:, 0:16], axis=AX.X, op=ALU.add)
        DVE("tensor_reduce", [u], [scrU], out=s2, in_=p[:, 16:32], axis=AX.X, op=ALU.add)
        DVE("tensor_scalar_mul", [scrU], [scrU], out=mu, in0=s1, scalar1=1.0 / D)
        DVE("tensor_tensor", [scrU], [scrU], out=var, in0=mu, in1=mu, op=ALU.mult)
        DVE("scalar_tensor_tensor", [scrU], [scrU], out=var, in0=s2, scalar=1.0 / D, in1=var, op0=ALU.mult, op1=ALU.subtract)
        ACT(rs, var, AF.Sqrt, [scrU, cU], [scrU], bias=epsT[:, 0:1], scale=1.0)
        DVE("reciprocal", [scrU], [scrU], out=rs, in_=rs)
        DVE("scalar_tensor_tensor", [scrU], [scrU], out=nb, in0=mu, scalar=-1.0, in1=rs, op0=ALU.mult, op1=ALU.mult)
        ACT(out, src, AF.Identity, list(srcU) + [scrU], list(outW), bias=nb, scale=rs)
        DVE("tensor_tensor", list(outW) + [cU], list(outW), out=out, in0=out, in1=vecT[:, :, grow], op=ALU.mult)
        DVE("tensor_tensor", list(outW) + [cU], list(outW), out=out, in0=out, in1=vecT[:, :, brow], op=ALU.add)

    def convmod(i):
        arena_reset()
        cxi = B.dram("cxi", [128, 480], F32); cxo = B.dram("cxo", [256, 480], F32)
        ccs = B.new_sem("cc_cv")
        w_in = c_w_in.ap(); w_out = c_w_out.ap()
        ztail = carve(480).rearrange("p (c t) -> p c t", t=30); ztU = Unit(); ztDS = B.new_ds()
        halo = carve(480).rearrange("p (c t) -> p c t", t=30); haU = Unit(); haDS = B.new_ds()
        zcs = carve(16 * 31).rearrange("p (c t) -> p c t", t=31); zsU = Unit()
        zh = carve(16 * 60 // 2, BF16).rearrange("p (c t) -> p c t", t=60); zhU = Unit()
        stg1 = carve(2048)
        yacc = [stg1[:, 0:1024], stg1[:, 1024:2048]]; yU = [Unit(), Unit()]
        scr = carve(80); scrU = Unit()
        ysm = carve(32); ysU = Unit()
        lnmu = carve(512); lnrs = carve(512)
        rmsnorm(V_GMIX + i)
        rows_to_fm([stg1, stg1], cst.ap(), 30, lambda kc: zcs[:, kc, 0:30], lambda kc: [zsU], single=True)

        def mk_in(c2):
            def comp(slot, wu):
                w3 = slot3(slot, KC, 512)
                for j in range(2):
                    c = c2 + j
                    for bi, (c0, n) in enumerate(BLKS):
                        pa, ua = psum(); pg, ug = psum()
                        mm_fm(pa, ua, w3, wu, KC, j, hT, lambda kc: [hU[bi]], c0, n)
                        mm_fm(pg, ug, w3, wu, KC, 2 + j, hT, lambda kc: [hU[bi]], c0, n)
                        s_, su = tf()
                        ACT(s_[:, :n], pg[:, :n], AF.Sigmoid, [ug, cU], [su], bias=vecT[:, c, V_CBIN + 1:V_CBIN + 2], scale=1.0)
                        if bi == 2:
                            DVE("scalar_tensor_tensor", [ua, su, cU], [zsU], out=zcs[:, c, 30:31], in0=pa[:, :1], scalar=vecT[:, c, V_CBIN:V_CBIN + 1], in1=s_[:, :1], op0=ALU.add, op1=ALU.mult)
                            continue
                        DVE("scalar_tensor_tensor", [ua, su, cU], [mU[c][bi]], out=mid[:, c, c0:c0 + n], in0=pa[:, :n], scalar=vecT[:, c, V_CBIN:V_CBIN + 1], in1=s_[:, :n], op0=ALU.add, op1=ALU.mult)
                        if bi == 1:
                            DVE("scalar_tensor_tensor", [ua, su, cU], [ztU], out=ztail[:, c, :], in0=pa[:, 482:512], scalar=vecT[:, c, V_CBIN:V_CBIN + 1], in1=s_[:, 482:512], op0=ALU.add, op1=ALU.mult)
            return comp
        for c2 in range(0, KC, 2):
            wstep([(lambda s: slot3(s, KC, 512)[:, :, 0:256], wsrc(w_in, 0, KC, c2 * 128, 256)),
                   (lambda s: slot3(s, KC, 512)[:, :, 256:512], wsrc(w_in, 0, KC, D + c2 * 128, 256))], mk_in(c2))
        run_steps()
        fm_to_rows([stg1, stg1], lambda kc: ztail[:, kc, :], lambda kc: [ztU], 30, cconv_p.ap(), single=True)
        fm_to_rows([stg1, stg1], lambda kc: zcs[:, kc, 1:31], lambda kc: [zsU], 30, cconv_s.ap(), single=True)
        cxU = Unit()
        dma("sp", cxi.ap(), ztail[:, :, :].rearrange("p c t -> p (c t)"), ztDS, R=[ztU], W=[cxU])
        B.cc(cxi.ap().opt(), cxo.ap().opt(), PAIRS_RUN, ccs, W=[cxU])
        dma("sp", halo[:, :, :].rearrange("p c t -> p (c t)"), cxo.ap()[0:128, :], haDS, R=[cxU], W=[haU])
        DVE("tensor_scalar_mul", [haU, cU], [haU], out=halo[:, :, :], in0=halo[:, :, :], scalar1=flg[:, 0:1])
        DVE("tensor_copy", [haU], [zhU], out=zh[:, :, 0:30], in_=halo[:, :, :])
        DVE("tensor_copy", [mU[c][0] for c in range(KC)], [zhU], out=zh[:, :, 30:60], in_=mid[:, :, 0:30])

        B.barrier()
        dg = stg1[:, 0:1984].bitcast(BF16).rearrange("p (k m) -> p k m", m=128); dgU = Unit()
        for c in range(KC):
            DVE("tensor_tensor", [cU], [dgU], out=dg, in0=ident[:, :].unsqueeze(1).to_broadcast([128, 31, 128]),
                in1=vecT[:, c, V_CWDW:V_CWDW + 31].unsqueeze(2).to_broadcast([128, 31, 128]), op=ALU.mult)
            zsrc = [mU[c][0], mU[c][1]]
            ph, uh = psum(); p0, u0 = psum(); p1, u1 = psum()
            for k in range(31):
                MM(ph[:, 0:30], dg[:, k, :], zh[:, c, k:k + 30], k == 0, k == 30, [dgU, zhU], [uh], signal=(k == 30))
            for k in range(31):
                MM(p0[:, 0:482], dg[:, k, :], mid[:, c, k:k + 482], k == 0, k == 30, [dgU] + zsrc, [u0], signal=(k == 30))
            for k in range(31):
                MM(p1[:, 0:512], dg[:, k, :], mid[:, c, 482 + k:482 + k + 512], k == 0, k == 30, [dgU] + zsrc, [u1], signal=(k == 30))
            bdw = vecT[:, c, V_CBDW:V_CBDW + 1]
            ACT(hT[:, c, 0:30], ph[:, 0:30], AF.Identity, [uh, cU], [hU[0]], bias=bdw, scale=1.0)
            ACT(hT[:, c, 30:512], p0[:, 0:482], AF.Identity, [u0, cU], [hU[0]], bias=bdw, scale=1.0)
            ACT(hT[:, c, 512:TP], p1[:, 0:512], AF.Identity, [u1, cU], [hU[1]], bias=bdw, scale=1.0)
        for bi, (c0, n) in enumerate(BLKS[:2]):
            p1, u1 = psum()
            for c in range(KC):
                MM(p1[:, :n], onesb[:, :], hT[:, c, c0:c0 + n], c == 0, c == KC - 1, [hU[bi], cU], [u1], signal=(c == KC - 1))
            p2, u2 = sumsq_fm(lambda kc: hT[:, kc, c0:c0 + n], lambda kc: [hU[bi]], KC, n)
            mu, muU, rs, rsU2 = lnmu, Unit(), lnrs, Unit()
            DVE("tensor_scalar_mul", [u1], [muU], out=mu[:, :n], in0=p1[:, :n], scalar1=1.0 / D)
            DVE("tensor_tensor", [muU], [rsU2], out=rs[:, :n], in0=mu[:, :n], in1=mu[:, :n], op=ALU.mult)
            DVE("scalar_tensor_tensor", [u2, rsU2], [rsU2], out=rs[:, :n], in0=p2[:, :n], scalar=1.0 / D, in1=rs[:, :n], op0=ALU.mult, op1=ALU.subtract)
            ACT(rs[:, :n], rs[:, :n], AF.Ln, [rsU2, cU], [rsU2], bias=epsT[:, 0:1], scale=1.0)
            ACT(rs[:, :n], rs[:, :n], AF.Exp, [rsU2], [rsU2], scale=-0.5)
            for c in range(KC):
                t_, tu_ = tf()
                DVE("tensor_tensor", [hU[bi], muU], [tu_], out=t_[:, :n], in0=hT[:, c, c0:c0 + n], in1=mu[:, :n], op=ALU.subtract)
                DVE("tensor_tensor", [tu_, rsU2], [tu_], out=t_[:, :n], in0=t_[:, :n], in1=rs[:, :n], op=ALU.mult)
                ACT(hT[:, c, c0:c0 + n], t_[:, :n], AF.Silu, [tu_, cU], [hU[bi]], bias=vecT[:, c, V_CLNB:V_CLNB + 1], scale=vecT[:, c, V_CLNG:V_CLNG + 1])
        B.barrier()
        prod31 = yacc[0][:, 0:16 * 31].rearrange("p (c t) -> p c t", t=31)
        DVE("tensor_tensor", [zsU, cU, yU[0]], [yU[0]], out=prod31, in0=zcs[:, :, :], in1=vecT[:, :, V_CWDW:V_CWDW + 31], op=ALU.mult)
        DVE("tensor_reduce", [yU[0]], [ysU], out=ysm[:, 0:16], in_=prod31, axis=AX.X, op=ALU.add)
        DVE("tensor_tensor", [ysU, cU], [ysU], out=ysm[:, 0:16], in0=ysm[:, 0:16], in1=vecT[:, :, V_CBDW], op=ALU.add)
        col_ln(ysm[:, 0:16], [ysU], V_CLNG, V_CLNB, ysm[:, 16:32], [ysU], scr, scrU)
        ACT(hT[:, :, TP], ysm[:, 16:32], AF.Silu, [ysU], [hU[2]])
        out_proj_steps(w_out, 0, KC, src=hT, srcU=lambda kc, bi: hU[bi])
        run_steps()

    for i in range(4):
        if on("mix%d" % i):
            if i % 3 == 0:
                gmlp(i, i // 3)
            elif i % 3 == 1:
                dilattn(i)
            else:
                convmod(i)
        if on("xat%d" % i):
            xattn(i)
        if on("ffn%d" % i):
            ffn(i)

    arena_reset()
    stg = [carve(2048), carve(2048)]
    for t in range(8):
        fm_to_rows(stg, lambda kc, t=t: xT[:, kc, t * 128:(t + 1) * 128], lambda kc, t=t: [xU[kc][t // 4]], 128, y_p.ap()[t * 128:(t + 1) * 128, :])
    if "xs" not in SKIP:
        fm_to_rows(stg, lambda kc: xT[:, kc, TP:TP + 1], lambda kc: [xU[kc][2]], 1, y_s.ap())

    sp = B.engs["sp"]
    for ds in B.dss:
        if ds.count:
            sp.prog.append(("wait", ds.sem, ds.count))
    B.emit()
    return B


def t5_bucket_np(dist):
    import math
    max_exact = 16
    d = np.maximum(dist, 1).astype(np.float32)
    large = max_exact + (np.log(d / max_exact) / math.log(2048 / max_exact) * (32 - max_exact)).astype(np.int32)
    large = np.minimum(large, 31)
    return np.where(dist < max_exact, dist, large)


_CACHE = {}


def kernel(**inp):
    if "B" not in _CACHE:
        _CACHE["B"] = build_program()
    B = _CACHE["B"]
    f = lambda a: np.ascontiguousarray(a, dtype=np.float32)
    vec_rows = [inp["g_mix"], inp["g_xattn"], inp["g_mem"], inp["g_ffn"], inp["a_ln_g"], inp["a_ln_b"],
                inp["c_b_in"].reshape(2, D), inp["c_b_dw"], inp["c_ln_g"], inp["c_ln_b"], inp["c_w_dw"][0]]
    vecs = f(np.concatenate([np.asarray(v).reshape(-1, D) for v in vec_rows], axis=0))
    assert vecs.shape[0] == NV
    gsm = f(np.concatenate([inp["x_q_norm"], inp["x_k_norm"], inp["b_q_norm"][0], inp["b_k_norm"][0]], axis=0).T)
    ident = np.eye(128, dtype=np.float32)
    masku = np.triu(np.ones((128, 128), np.float32))
    selg = np.zeros((33, 9, 255), np.float32)
    u = np.arange(255)
    for g, dil in enumerate((1, 4, 16)):
        cur = np.where(u >= 127, t5_bucket_np(np.maximum(u - 127, 0) * dil), 32)
        prev = np.where(u <= 127, t5_bucket_np((u + 1) * dil), 32)
        third = prev if g < 2 else cur
        for v, idx in enumerate((cur, prev, third)):
            selg[idx, g * 3 + v, u] = 1.0
    selg = selg.reshape(33, 9 * 255)
    sels = np.zeros((32, 3, 128), np.float32)
    jj = np.arange(128)
    for g, dil in enumerate((1, 4, 16)):
        sels[t5_bucket_np((128 - jj) * dil), g, jj] = 1.0
    sels = sels.reshape(32, 384)
    shared = dict(vecs=vecs, gsm=gsm, relb=f(inp["rel_bias"]), ident=ident, masku=masku, selg=selg, sels=sels,
                  a_w_s=f(inp["a_w_s"]), a_b_s=f(inp["a_b_s"]).reshape(2, 2048),
                  a_w_in=f(inp["a_w_in"]), a_w_out=f(inp["a_w_out"]), b_w_qkv=f(inp["b_w_qkv"][0]), b_w_out=f(inp["b_w_out"][0]),
                  c_w_in=f(inp["c_w_in"][0]), c_w_out=f(inp["c_w_out"][0]), x_w_q=f(inp["x_w_q"]), x_w_kv=f(inp["x_w_kv"]),
                  x_w_o=f(inp["x_w_o"]), f_w_in=f(inp["f_w_in"]), f_w_out=f(inp["f_w_out"]))
    in_maps = []
    for c in range(NCORES):
        b, half = c // 2, c % 2
        m = dict(shared)
        m["x_p"] = f(inp["x_prompt"][b, half * TP:(half + 1) * TP])
        m["x_s"] = f(inp["x_sample"][c])
        m["mem"] = f(inp["mem_prompt"][b])
        m["flag"] = np.full((128, 1), float(half), np.float32)
        caches = ((inp["cache_b_k_w128"], inp["cache_b_v_w128"]), (inp["cache_b_k_w512"], inp["cache_b_v_w512"]),
                  (inp["cache_b_k_w2048"], inp["cache_b_v_w2048"]))
        for g, w in enumerate((128, 512, 2048)):
            m["ck%d" % g] = f(caches[g][0][0, c]).reshape(w, D)
            m["cv%d" % g] = f(caches[g][1][0, c]).reshape(w, D)
        m["cst"] = f(inp["state_c_conv"][0, c])
        m["cmk"] = f(inp["cache_mem_k"][:, c]).reshape(4, 256, 512)
        m["cmv"] = f(inp["cache_mem_v"][:, c]).reshape(4, 256, 512)
        in_maps.append(m)
    nrun = int(os.environ.get("MK_NCORES", NCORES))
    in_maps = [{k: m[k] for k in B.used_inputs} for m in in_maps]
    res = run_bass_kernel_spmd(B.nc, in_maps[:nrun], core_ids=list(range(nrun)))
    R = list(res.results) + [res.results[c % nrun] for c in range(nrun, NCORES)]
    o = lambda c, k: np.asarray(R[c][k], dtype=np.float32)
    y_prompt = np.stack([np.concatenate([o(2 * b, "y_p"), o(2 * b + 1, "y_p")], axis=0) for b in range(4)])
    y_sample = np.stack([o(c, "y_s") for c in range(8)])
    outs = [y_prompt, y_sample]
    for g, n in enumerate((128, 512, 2048)):
        for kv in ("bk", "bv"):
            if g < 2:
                a = np.stack([o(2 * b + 1, "%s%d_p" % (kv, g)) for b in range(4)])
            else:
                a = np.stack([np.concatenate([o(2 * b, "%s2_p" % kv), o(2 * b + 1, "%s2_p" % kv)], axis=0) for b in range(4)])
            outs.append(a.reshape(1, 4, n, 16, 128))
    outs.append(np.stack([o(2 * b + 1, "cconv_p") for b in range(4)])[None])
    outs.append(np.stack([o(2 * b, "memk_p") for b in range(4)], axis=1).reshape(4, 4, 256, 4, 128))
    outs.append(np.stack([o(2 * b, "memv_p") for b in range(4)], axis=1).reshape(4, 4, 256, 4, 128))
    for g, w in enumerate((128, 512, 2048)):
        for kv in ("bk", "bv"):
            outs.append(np.stack([o(c, "%s%d_s" % (kv, g)) for c in range(8)]).reshape(1, 8, w, 16, 128))
    outs.append(np.stack([o(c, "cconv_s") for c in range(8)])[None])
    outs.append(np.stack([o(c, "av_s") for c in range(8)], axis=1).reshape(2, 8, 1, D))
    return tuple(outs)
```
